# Optimizing a Trainium2 kernel written in Bass

```python
import math
import jax, jax.numpy as jnp
from jax import lax
import numpy as np

D_MODEL = 4096
BATCH = 2
SEQ = 4096
DEPTH = 1

HEAD_SIZE = 64
D_A = D_MODEL // 2
D_B = D_MODEL - D_A
H_A = D_A // HEAD_SIZE
H_Q = D_B // HEAD_SIZE
GQA_RATIO = 8
H_KV = H_Q // GQA_RATIO
GQA_GROUP = H_Q // H_KV
WINDOW = 128
BLOCK = 128
RPB_BUCKETS = 32
RPB_MAX_EXACT = RPB_BUCKETS // 2
RPB_MAX_DIST = 128
DECAY_LORA = max(32, int(round(D_A ** 0.5 * 1.8 / 32)) * 32)
ICLR_LORA = max(32, int(round(D_A ** 0.5 * 1.8 / 32)) * 32)
GATE_LORA = max(32, int(round(D_A ** 0.6 * 0.8 / 32)) * 32)
D_FF = 4 * D_MODEL
ALPHA = (2.0 * DEPTH) ** 0.25
BETA = (8.0 * DEPTH) ** -0.25
LN_EPS = 1e-5
LNX_EPS = 64e-5
OFF_R = 0
OFF_K = OFF_R + D_A
OFF_V = OFF_K + D_A
OFF_W = OFF_V + D_A
OFF_A = OFF_W + DECAY_LORA
OFF_G = OFF_A + ICLR_LORA
RWKV_COLS = OFF_G + GATE_LORA
OFF_Q = RWKV_COLS
OFF_KB = OFF_Q + D_B
OFF_VB = OFF_KB + H_KV * HEAD_SIZE
N_IN = OFF_VB + H_KV * HEAD_SIZE

kernel_name = "hybrid_rwkv7_swa_sink_block"


def layer_norm(x, g, b, eps=LN_EPS):
    xf = x.astype(jnp.float32)
    mu = xf.mean(-1, keepdims=True)
    var = jnp.square(xf - mu).mean(-1, keepdims=True)
    return ((xf - mu) * lax.rsqrt(var + eps) * g + b).astype(x.dtype)


def token_shift(p, mu):
    prev = jnp.pad(p, ((0, 0), (1, 0), (0, 0)))[:, :-1]
    return p + (prev - p) * mu


def rwkv7_recurrence(r, decay, k, v, kk, a):
    b, _, h, n = r.shape

    def step(state, inp):
        r_t, w_t, k_t, v_t, kk_t, a_t = inp
        sa = jnp.einsum('bhvk,bhk->bhv', state, -kk_t)
        state = (state * w_t[:, :, None, :]
                 + sa[..., None] * (kk_t * a_t)[:, :, None, :]
                 + v_t[..., None] * k_t[:, :, None, :])
        return state, jnp.einsum('bhvk,bhk->bhv', state, r_t)

    xs = tuple(jnp.moveaxis(t, 1, 0) for t in (r, decay, k, v, kk, a))
    s0 = jnp.zeros((b, h, n, n), jnp.float32)
    _, ys = lax.scan(step, s0, xs)
    return jnp.moveaxis(ys, 0, 1)


def rwkv7_mixer(p, mu, w0, w_decay_up, a0, w_iclr_up, w_gate_up, k_k, k_a, r_k, lnx_g, lnx_b):
    bsz, seq, _ = p.shape
    f32 = jnp.float32
    p = token_shift(p, mu)
    r = p[..., OFF_R:OFF_K]
    k = p[..., OFF_K:OFF_V]
    v = p[..., OFF_V:OFF_W]
    xw = p[..., OFF_W:OFF_A]
    xa = p[..., OFF_A:OFF_G]
    xg = p[..., OFF_G:RWKV_COLS]
    w = -jax.nn.softplus(-(w0 + jnp.tanh(xw) @ w_decay_up)) - 0.5
    decay = jnp.exp(-jnp.exp(w.astype(f32)))
    a = jax.nn.sigmoid(a0 + xa @ w_iclr_up)
    g = jax.nn.sigmoid(xg) @ w_gate_up
    heads = lambda t: t.astype(f32).reshape(bsz, seq, H_A, HEAD_SIZE)
    kk = heads(k * k_k)
    kk = kk / jnp.maximum(jnp.linalg.norm(kk, axis=-1, keepdims=True), 1e-12)
    k = k * (1 + (a - 1) * k_a)
    r_h, k_h, v_h, a_h = heads(r), heads(k), heads(v), heads(a)
    y = rwkv7_recurrence(r_h, heads(decay), k_h, v_h, kk, a_h)
    mu_y = y.mean(-1, keepdims=True)
    var_y = jnp.square(y - mu_y).mean(-1, keepdims=True)
    y = ((y - mu_y) * lax.rsqrt(var_y + LNX_EPS)).reshape(bsz, seq, D_A) * lnx_g + lnx_b
    bonus = jnp.sum(r_h * k_h * r_k, axis=-1, keepdims=True) * v_h
    y = (y + bonus.reshape(bsz, seq, D_A)) * g
    return y.astype(p.dtype)


def t5_causal_bucket(dist):
    n = jnp.maximum(dist, 0)
    nf = jnp.maximum(n, 1).astype(jnp.float32)
    large = RPB_MAX_EXACT + (jnp.log(nf / RPB_MAX_EXACT)
                             / math.log(RPB_MAX_DIST / RPB_MAX_EXACT)
                             * (RPB_BUCKETS - RPB_MAX_EXACT)).astype(jnp.int32)
    large = jnp.minimum(large, RPB_BUCKETS - 1)
    return jnp.where(n < RPB_MAX_EXACT, n, large)


def swa_sink_attention(q, k, v, rpb_table, sinks):
    bsz, seq = q.shape[:2]
    nb = seq // BLOCK
    f32 = jnp.float32
    qb = q.reshape(bsz, nb, BLOCK, H_KV, GQA_GROUP, HEAD_SIZE)

    def band(t):
        tp = jnp.pad(t, ((0, 0), (BLOCK, 0), (0, 0), (0, 0)))
        prev = tp[:, :seq].reshape(bsz, nb, BLOCK, H_KV, HEAD_SIZE)
        cur = t.reshape(bsz, nb, BLOCK, H_KV, HEAD_SIZE)
        return jnp.concatenate([prev, cur], axis=2)

    kb, vb = band(k), band(v)
    qi = jnp.arange(BLOCK)[:, None]
    kj = jnp.arange(2 * BLOCK)[None, :]
    dist = qi + BLOCK - kj
    bias = rpb_table[t5_causal_bucket(dist)]
    bias = jnp.transpose(bias, (2, 0, 1)).reshape(H_KV, GQA_GROUP, BLOCK, 2 * BLOCK).astype(f32)
    blk = jnp.arange(nb)[:, None, None]
    valid = (dist >= 0) & (dist < WINDOW) & ((blk > 0) | (kj >= BLOCK))
    s = jnp.einsum('bnqhgd,bnkhd->bnhgqk', qb, kb).astype(f32) * (HEAD_SIZE ** -0.5) + bias
    s = jnp.where(valid[None, :, None, None], s, -1e30)
    sink = sinks.astype(f32).reshape(H_KV, GQA_GROUP)[None, None, :, :, None, None]
    m = jnp.maximum(s.max(-1, keepdims=True), sink)
    e = jnp.exp(s - m)
    probs = e / (e.sum(-1, keepdims=True) + jnp.exp(sink - m))
    o = jnp.einsum('bnhgqk,bnkhd->bnqhgd', probs.astype(v.dtype), vb)
    return o.reshape(bsz, seq, H_Q * HEAD_SIZE)


def setup_inputs(seed: int = 0) -> dict:
    key = jax.random.key(seed)
    ks = jax.random.split(key, 28)
    nrm = lambda k, shape, s: s * jax.random.normal(k, shape, jnp.float32)
    L = DEPTH
    col_scale = np.ones((N_IN,), np.float32)
    col_scale[OFF_V:OFF_W] = BETA
    col_scale[OFF_VB:N_IN] = BETA
    return {
        "x": nrm(ks[0], (BATCH, SEQ, D_MODEL), 1.0),
        "c": nrm(ks[1], (BATCH, D_MODEL), 1.0),
        "ln_emb_g": 1.0 + nrm(ks[2], (D_MODEL,), 0.02),
        "ln_emb_b": nrm(ks[3], (D_MODEL,), 0.02),
        "rpb_table": nrm(ks[4], (RPB_BUCKETS, H_Q), 0.5),
        "w_mod": nrm(ks[5], (L, D_MODEL, 6 * D_MODEL), 0.2 * D_MODEL ** -0.5),
        "b_mod": nrm(ks[6], (L, 6 * D_MODEL), 0.01),
        "w_in": nrm(ks[7], (L, D_MODEL, N_IN), D_MODEL ** -0.5) * jnp.asarray(col_scale),
        "mu_shift": jax.random.uniform(ks[8], (L, RWKV_COLS), jnp.float32),
        "w0": jax.random.uniform(ks[9], (L, D_A), jnp.float32, -6.0, -1.0),
        "w_decay_up": nrm(ks[10], (L, DECAY_LORA, D_A), 0.5 * DECAY_LORA ** -0.5),
        "a0": nrm(ks[11], (L, D_A), 0.1),
        "w_iclr_up": nrm(ks[12], (L, ICLR_LORA, D_A), 0.5 * ICLR_LORA ** -0.5),
        "w_gate_up": nrm(ks[13], (L, GATE_LORA, D_A), GATE_LORA ** -0.5),
        "k_k": 0.85 + nrm(ks[14], (L, D_A), 0.05),
        "k_a": 1.0 + nrm(ks[15], (L, D_A), 0.05),
        "r_k": nrm(ks[16], (L, H_A, HEAD_SIZE), 0.1),
        "lnx_g": 1.0 + nrm(ks[17], (L, D_A), 0.02),
        "lnx_b": nrm(ks[18], (L, D_A), 0.02),
        "attn_sinks": nrm(ks[19], (L, H_Q), 1.0),
        "w_out": nrm(ks[20], (L, D_MODEL, D_MODEL), BETA * D_MODEL ** -0.5),
        "ln1_g": 1.0 + nrm(ks[21], (L, D_MODEL), 0.02),
        "ln1_b": nrm(ks[22], (L, D_MODEL), 0.02),
        "w_up": nrm(ks[23], (L, D_MODEL, D_FF), D_MODEL ** -0.5),
        "w_down": nrm(ks[24], (L, D_FF, D_MODEL), BETA * D_FF ** -0.5),
        "ln2_g": 1.0 + nrm(ks[25], (L, D_MODEL), 0.02),
        "ln2_b": nrm(ks[26], (L, D_MODEL), 0.02),
    }


def reference(x, c, ln_emb_g, ln_emb_b, rpb_table, w_mod, b_mod, w_in, mu_shift, w0,
              w_decay_up, a0, w_iclr_up, w_gate_up, k_k, k_a, r_k, lnx_g, lnx_b,
              attn_sinks, w_out, ln1_g, ln1_b, w_up, w_down, ln2_g, ln2_b):
    bsz, seq, _ = x.shape
    cond = jax.nn.silu(c)
    x = layer_norm(x, ln_emb_g, ln_emb_b)
    for l in range(DEPTH):
        mod = (cond @ w_mod[l] + b_mod[l]).reshape(bsz, 6, D_MODEL)[:, :, None, :]
        sh1, sc1, g1, sh2, sc2, g2 = (mod[:, i] for i in range(6))
        u = x * (1 + sc1) + sh1
        p = u @ w_in[l]
        y_a = rwkv7_mixer(p[..., :RWKV_COLS], mu_shift[l], w0[l], w_decay_up[l], a0[l],
                          w_iclr_up[l], w_gate_up[l], k_k[l], k_a[l], r_k[l],
                          lnx_g[l], lnx_b[l])
        q = p[..., OFF_Q:OFF_KB].reshape(bsz, seq, H_Q, HEAD_SIZE)
        kb = p[..., OFF_KB:OFF_VB].reshape(bsz, seq, H_KV, HEAD_SIZE)
        vb = p[..., OFF_VB:N_IN].reshape(bsz, seq, H_KV, HEAD_SIZE)
        y_b = swa_sink_attention(q, kb, vb, rpb_table, attn_sinks[l])
        mix = jnp.concatenate([y_a, y_b], axis=-1) @ w_out[l]
        x = layer_norm(ALPHA * x + (1 + g1) * mix, ln1_g[l], ln1_b[l])
        u = x * (1 + sc2) + sh2
        h = jnp.square(jax.nn.relu(u @ w_up[l])) @ w_down[l]
        x = layer_norm(ALPHA * x + (1 + g2) * h, ln2_g[l], ln2_b[l])
    return x
```

```python
import math
import os
CUT = int(os.environ.get('KCUT', '99'))
from contextlib import ExitStack
import numpy as np
import concourse.bass as bass
import concourse.mybir as mybir
from concourse.bass_utils import run_bass_kernel_spmd

F32 = mybir.dt.float32
BF16 = mybir.dt.bfloat16
ALU = mybir.AluOpType
AF = mybir.ActivationFunctionType
AX = mybir.AxisListType

LN_EPS = 1e-5
LNX_EPS = 64e-5
ALPHA = 2.0 ** 0.25
C0 = math.exp(-0.5)


def make_cfg(D=4096, S=4096, GQA=8):
    c = dict(D=D, S=S, GQA=GQA)
    c["D_A"] = D // 2
    c["H_A"] = c["D_A"] // 64
    c["H_Q"] = c["H_A"]
    c["H_KV"] = c["H_Q"] // GQA
    assert c["H_KV"] == 4
    c["HPC"] = c["H_A"] // 4
    c["NHP"] = c["HPC"] // 2
    c["QPC"] = c["H_Q"] // 4
    c["LW"] = max(32, int(round(c["D_A"] ** 0.5 * 1.8 / 32)) * 32)
    c["LA"] = c["LW"]
    c["LG"] = max(32, int(round(c["D_A"] ** 0.6 * 0.8 / 32)) * 32)
    c["DFF"] = 4 * D
    c["NDC"] = D // 128
    c["HP_SUB"] = min(2, c["NHP"])
    c["NSUB"] = c["NHP"] // c["HP_SUB"]
    c["NB"] = 3 * c["HP_SUB"] + 3
    c["NAB"] = c["QPC"] // 2 + 2
    c["YR"] = D // 4
    c["NT"] = S // 4
    c["TC"] = min(512, S)
    c["OFF_K"] = c["D_A"]
    c["OFF_V"] = 2 * c["D_A"]
    c["OFF_W"] = 3 * c["D_A"]
    c["OFF_A"] = c["OFF_W"] + c["LW"]
    c["OFF_G"] = c["OFF_A"] + c["LA"]
    c["RWKV_COLS"] = c["OFF_G"] + c["LG"]
    c["OFF_Q"] = c["RWKV_COLS"]
    c["OFF_KB"] = c["OFF_Q"] + c["D_A"]
    c["OFF_VB"] = c["OFF_KB"] + 4 * 64
    c["N_IN"] = c["OFF_VB"] + 4 * 64
    return c


class Vw:
    def __init__(s, ap, k):
        s.ap = ap
        s.k = k


class Buf:
    def __init__(s, t, k):
        s.t = t
        s.k = k

    def __getitem__(s, idx):
        return Vw(s.t[idx], s.k)

    def sk(s, suf):
        return Buf(s.t, s.k + "/" + str(suf))


class Phase:
    ENG = ("tensor", "vector", "scalar", "gpsimd", "sync")

    def __init__(s, nc, name):
        s.nc = nc
        s.name = name
        s.ops = {e: [] for e in s.ENG}
        s.cnt = {e: 0 for e in s.ENG}
        s.dcnt = {}
        s.lastw = {}
        s.rd = {}
        s.mem = ExitStack()
        s.nbuf = 0

    def sb(s, shape, dt, key=None):
        s.nbuf += 1
        nm = f"{s.name}_{key or 'b'}{s.nbuf}"
        return Buf(s.mem.enter_context(s.nc.sbuf_tensor(nm, list(shape), dt)), nm)

    def ps(s, shape, dt, key=None):
        s.nbuf += 1
        nm = f"{s.name}_{key or 'p'}{s.nbuf}"
        return Buf(s.mem.enter_context(s.nc.psum_tensor(nm, list(shape), dt)), nm)

    def _deps(s, reads, writes):
        d = {}

        def add(tok):
            if tok is not None and d.get(tok[0], 0) < tok[1]:
                d[tok[0]] = tok[1]
        for b in reads:
            add(s.lastw.get(b))
        for b in writes:
            add(s.lastw.get(b))
            for k, v in s.rd.get(b, {}).items():
                add((k, v))
        return d

    def _commit(s, tok, reads, writes):
        for b in writes:
            s.lastw[b] = tok
            s.rd[b] = {}
        for b in reads:
            r = s.rd.setdefault(b, {})
            if r.get(tok[0], 0) < tok[1]:
                r[tok[0]] = tok[1]

    def op(s, eng, fn, reads=(), writes=()):
        if getattr(s, "dead", False):
            return
        d = s._deps(reads, writes)
        s.cnt[eng] += 1
        tok = (("E", eng), s.cnt[eng])
        s._commit(tok, reads, writes)
        s.ops[eng].append((fn, d, tok, False))

    def dma(s, eng, key, fn, reads=(), writes=(), n=1):
        if getattr(s, "dead", False):
            return
        d = s._deps(reads, writes)
        c = s.dcnt.get(key, 0) + 16 * n
        s.dcnt[key] = c
        tok = (("D", key), c)
        s._commit(tok, reads, writes)
        s.ops[eng].append((fn, d, tok, True))

    def emit(s):
        nc = s.nc
        sems = {}
        handles = []
        for e in s.ENG:
            h = nc.alloc_semaphore(name=f"{s.name}_e_{e}")
            sems[("E", e)] = h
            handles.append(h)
        for i, k in enumerate(s.dcnt):
            h = nc.alloc_semaphore(name=f"{s.name}_d{i}")
            sems[("D", k)] = h
            handles.append(h)
        with nc.Block() as block:
            for e in s.ENG:
                if not s.ops[e] and e != "sync":
                    continue

                def body(eng, e=e):
                    waited = {}
                    for fn, d, tok, isdma in s.ops[e]:
                        for k, v in d.items():
                            if e == "tensor" and k == ("E", "tensor"):
                                continue
                            if waited.get(k, 0) < v:
                                eng.wait_ge(sems[k], v)
                                waited[k] = v
                        if isdma:
                            fn(eng, sems[tok[0]])
                        else:
                            fn(eng).then_inc(sems[tok[0]], 1)
                    if e == "sync":
                        for k, v in s.dcnt.items():
                            if waited.get(("D", k), 0) < v:
                                eng.wait_ge(sems[("D", k)], v)
                getattr(block, e)(body)
        nc.clear_and_free_semaphores(handles)
        nc.all_engine_barrier()
        s.mem.close()

    def mm(s, out, lhsT, rhs, st=True, sp=True):
        s.op("tensor", lambda e: e.matmul(out.ap, lhsT.ap, rhs.ap, start=st, stop=sp),
             [lhsT.k, rhs.k], [out.k])

    def tr(s, out, in_, ident):
        s.op("tensor", lambda e: e.transpose(out.ap, in_.ap, ident.ap), [in_.k, ident.k], [out.k])

    def act(s, out, in_, func, bias=None, scale=None, extra=()):
        rk = [in_.k] + list(extra)
        kw = {}
        if bias is not None:
            if isinstance(bias, Vw):
                kw["bias"] = bias.ap
                rk.append(bias.k)
            else:
                kw["bias"] = bias
        if scale is not None:
            if isinstance(scale, Vw):
                kw["scale"] = scale.ap
                rk.append(scale.k)
            else:
                kw["scale"] = scale
        s.op("scalar", lambda e: e.activation(out.ap, in_.ap, func, **kw), rk, [out.k])

    def tt(s, eng, out, in0, in1, op):
        s.op(eng, lambda e: e.tensor_tensor(out.ap, in0.ap, in1.ap, op), [in0.k, in1.k], [out.k])

    def ts(s, eng, out, in0, s1, op0, s2=None, op1=None):
        rk = [in0.k]
        a1 = s1
        a2 = s2
        if isinstance(s1, Vw):
            a1 = s1.ap
            rk.append(s1.k)
        if isinstance(s2, Vw):
            a2 = s2.ap
            rk.append(s2.k)
        if op1 is None:
            s.op(eng, lambda e: e.tensor_scalar(out.ap, in0.ap, a1, None, op0), rk, [out.k])
        else:
            s.op(eng, lambda e: e.tensor_scalar(out.ap, in0.ap, a1, a2, op0, op1), rk, [out.k])

    def rsqrt(s, out, in_, addc):
        s.op("scalar", lambda e: e.activation(out.ap, in_.ap, AF.Sqrt, bias=addc), [in_.k], [out.k])
        s.op("vector", lambda e: e.reciprocal(out.ap, out.ap), [out.k], [out.k])

    def stt(s, out, in0, sc, in1, op0, op1):
        rk = [in0.k, in1.k]
        a = sc
        if isinstance(sc, Vw):
            a = sc.ap
            rk.append(sc.k)
        s.op("vector", lambda e: e.scalar_tensor_tensor(out.ap, in0.ap, a, in1.ap, op0, op1), rk, [out.k])

    def cp(s, eng, out, in_):
        if eng == "scalar":
            s.op("scalar", lambda e: e.activation(out.ap, in_.ap, AF.Copy), [in_.k], [out.k])
        else:
            s.op(eng, lambda e: e.tensor_copy(out.ap, in_.ap), [in_.k], [out.k])

    def ms(s, eng, out, val):
        s.op(eng, lambda e: e.memset(out.ap, val), [], [out.k])

    def ld(s, eng, key, out, in_ap, reads=(), **kw):
        s.dma(eng, key, lambda e, sem: e.dma_start(out=out.ap, in_=in_ap, **kw).then_inc(sem, 16),
              list(reads), [out.k])

    def ldm(s, eng, key, out_key, pairs, reads=()):
        def fn(e, sem):
            for o, i in pairs:
                e.dma_start(out=o, in_=i).then_inc(sem, 16)
        s.dma(eng, key, fn, list(reads), [out_key], n=len(pairs))

    def stq(s, eng, key, out_ap, in_, writes=(), **kw):
        s.dma(eng, key, lambda e, sem: e.dma_start(out=out_ap, in_=in_.ap, **kw).then_inc(sem, 16),
              [in_.k], list(writes))


def load_cols(ph, rows_ap, nrows, identf, pst, key):
    out = ph.sb([128, nrows], F32, key)
    r0 = 0
    i = 0
    while r0 < nrows:
        n = min(128, nrows - r0)
        tmp = ph.sb([128, 128], F32, key + "t")
        ph.ld("sync", f"{key}ld{i}", tmp[0:n, :], rows_ap[r0:r0 + n, :])
        ph.tr(pst[:, 0:n], tmp[0:n, :], identf[0:n, 0:n])
        ph.cp("vector", out[:, r0:r0 + n], pst[:, 0:n])
        r0 += n
        i += 1
    return out


def ln_stats(ph, xt, D, stats, mv, rstd, nmr, eps):
    nch = max(1, D // 512)
    w = D // nch
    for i in range(nch):
        ph.op("vector", (lambda o, a: (lambda e: e.bn_stats(o.ap, a.ap)))(stats[:, i * 6:(i + 1) * 6], xt[:, i * w:(i + 1) * w]),
              [xt.k], [stats.k])
    ph.op("vector", (lambda o, a: (lambda e: e.bn_aggr(o.ap, a.ap)))(mv[:, :], stats[:, :]), [stats.k], [mv.k])
    rv = rstd if isinstance(rstd, Vw) else rstd[:, :]
    nv = nmr if isinstance(nmr, Vw) else nmr[:, :]
    ph.rsqrt(rv, mv[:, 1:2], eps)
    ph.stt(nv, mv[:, 0:1], -1.0, rv, ALU.mult, ALU.mult)


def build(cfg, debug=False, upto=99):
    D, S = cfg["D"], cfg["S"]
    NDC, NB, NAB, NHP, HP_SUB, NSUB = cfg["NDC"], cfg["NB"], cfg["NAB"], cfg["NHP"], cfg["HP_SUB"], cfg["NSUB"]
    QPC, LW, LA, LG, DFF, YR, NT, TC = cfg["QPC"], cfg["LW"], cfg["LA"], cfg["LG"], cfg["DFF"], cfg["YR"], cfg["NT"], cfg["TC"]
    NCH = TC // 128
    NTC = S // TC
    NQB = QPC // 2
    nc = bass.Bass("TRN2", target_bir_lowering=False)

    def din(name, shape, dt=F32):
        return nc.dram_tensor(name, list(shape), dt, kind="ExternalInput").ap()

    x_in = din("x", [S, D])
    xq_in = din("xq", [NT, D])
    ccol_in = din("ccol", [128, NDC])
    wmod_in = din("wmod", [D, 6 * D // 4])
    bmod_in = din("bmod", [1, 6 * D // 4])
    lnv_in = din("lnv", [6 * NDC, 128])
    lnr_in = din("lnr", [6, D])
    wrw_in = din("wrw", [NSUB, D, NB * 128])
    wat_in = din("wat", [D, NAB * 128])
    mu_in = din("mu", [128, NSUB * NB])
    chp_in = din("chp", [128, 7 * NHP])
    wdec_in = din("wdec", [128, NHP * 128])
    wicl_in = din("wicl", [128, NHP * 128])
    wgat_in = din("wgat", [128, NHP * 128])
    bias_in = din("biasat", [128, QPC * 256])
    sink_in = din("sinks", [128, QPC])
    cst_in = din("cst", [128, 8 * 128])
    wout_in = din("wout", [D, D])
    wup_in = din("wup", [D, DFF])
    wdn_in = din("wdn", [DFF, D])
    out_t = nc.dram_tensor("out", [NT, D], F32, kind="ExternalOutput").ap()

    modraw_t = nc.dram_tensor("modraw", [6 * NDC, 128], F32)
    modraw = modraw_t.ap()
    modpart_t = nc.dram_tensor("modpart", [6 * NDC // 4, 128], F32)
    modpart = modpart_t.ap()
    uTd = nc.dram_tensor("uTd", [NDC, 128, S], BF16).ap()
    NRB = YR // 128
    ybuf_t = [nc.dram_tensor(f"ybuf{j}", [128, S], BF16) for j in range(NRB)]
    yall_t = [nc.dram_tensor(f"yall{j}", [4 * 128, S], BF16) for j in range(NRB)]
    ydbg = nc.dram_tensor("ydbg", [YR, S], BF16, kind="ExternalOutput").ap() if debug else None
    zbuf = nc.dram_tensor("zbuf", [NT, D], F32, **(dict(kind="ExternalOutput") if debug else dict())).ap()
    x1buf = nc.dram_tensor("x1buf", [NT, D], F32, **(dict(kind="ExternalOutput") if debug else dict())).ap()
    hbuf = nc.dram_tensor("hbuf", [DFF // 128, 128, NT], BF16, **(dict(kind="ExternalOutput") if debug else dict())).ap()
    z2buf = nc.dram_tensor("z2buf", [NT, D], F32, **(dict(kind="ExternalOutput") if debug else dict())).ap()
    if debug:
        d7h = nc.dram_tensor("d7h", [128, (1024 // 128), NT], BF16, kind="ExternalOutput").ap()
        d7w = nc.dram_tensor("d7w", [128, (1024 // 128), D // 2], BF16, kind="ExternalOutput").ap()
        d7p = nc.dram_tensor("d7p", [128, min(512, D // 2)], F32, kind="ExternalOutput").ap()
        d7acc = nc.dram_tensor("d7acc", [128, D // 2], F32, kind="ExternalOutput").ap()
    u2Td = nc.dram_tensor("u2Td", [NDC, 128, NT], BF16, **(dict(kind="ExternalOutput") if debug else dict())).ap()

    def dbgcopy():
        if debug and not _dbgdone:
            _dbgdone.append(1)
            dbg_sem0 = nc.alloc_semaphore(name="dbg_sem0")
            with nc.Block() as block:
                @block.gpsimd
                def _(g):
                    for j in range(NRB):
                        g.dma_start(out=ydbg[j * 128:(j + 1) * 128, :], in_=ybuf_t[j].ap()).then_inc(dbg_sem0, 16)
                    g.wait_ge(dbg_sem0, 16 * NRB)
            nc.clear_and_free_semaphores([dbg_sem0])
            nc.all_engine_barrier()
    _dbgdone = []

    def consts(ph):
        c = ph.sb([128, 8 * 128], F32, "cst")
        ph.ld("sync", "cst", c[:, :], cst_in[:, :])
        names = ["ident", "mus", "mui", "mls", "bo", "ones", "amp", "amc"]
        d = {n: c.t[:, i * 128:(i + 1) * 128] for i, n in enumerate(names)}
        return c, d

    def cv(c, d, n, sl=None):
        ap = d[n]
        if sl is not None:
            ap = ap[sl]
        return Vw(ap, c.k)

    ph = Phase(nc, "p0")
    ccol = ph.sb([128, NDC], F32)
    cond = ph.sb([128, NDC], BF16)
    ph.ld("sync", "ccol", ccol[:, :], ccol_in[:, :])
    ph.act(cond[:, :], ccol[:, :], AF.Silu)
    MQ = 6 * D // 4
    GW = 2048 if MQ % 2048 == 0 else 1536
    assert MQ % GW == 0 and GW % 512 == 0
    NG = MQ // GW
    NQ_ = GW // 512
    wms = [ph.sb([128, GW], BF16, "wm") for _ in range(4)]
    psb = [[ph.ps([128, 512], F32, "pm") for _ in range(NQ_)] for _ in range(2)]
    bms = [ph.sb([1, GW], F32, "bm") for _ in range(2)]
    mds = [ph.sb([1, GW], F32, "md") for _ in range(2)]
    it = 0
    for g in range(NG):
        pb = psb[g % 2]
        ph.ld("sync", f"bm{g % 2}", bms[g % 2][:, :], bmod_in[0:1, g * GW:(g + 1) * GW])
        for k in range(NDC):
            wm = wms[it % 4]
            ph.ld("gpsimd", f"wm{it % 4}", wm[:, :], wmod_in[k * 128:(k + 1) * 128, g * GW:(g + 1) * GW])
            it += 1
            for q in range(NQ_):
                ph.mm(pb[q][0:1, :], cond[:, k:k + 1], wm[:, q * 512:(q + 1) * 512], st=(k == 0), sp=(k == NDC - 1))
        for q in range(NQ_):
            ph.tt("vector", mds[g % 2][0:1, q * 512:(q + 1) * 512], pb[q][0:1, :],
                  bms[g % 2][0:1, q * 512:(q + 1) * 512], ALU.add)
        ph.stq("sync", f"mdst{g % 2}",
               modpart[g * (GW // 128):(g + 1) * (GW // 128), :].rearrange("(o a) b -> o (a b)", o=1),
               mds[g % 2][0:1, :], writes=["modpart"])
    ph.emit()
    cc0 = nc.alloc_semaphore(name="cc0_sem")
    with nc.Block() as block:
        @block.gpsimd
        def _(g):
            g.collective_compute("AllGather", ALU.bypass, replica_groups=[[0, 1, 2, 3], [4, 5, 6, 7]],
                                 ins=[modpart_t.ap().opt()], outs=[modraw_t.ap().opt()]).then_inc(cc0)
            g.wait_ge(cc0, 1)
    nc.clear_and_free_semaphores([cc0])
    nc.all_engine_barrier()

    if upto < 1:
        return nc
    ph = Phase(nc, "p1")
    cst, cd = consts(ph)
    identb = ph.sb([128, 128], BF16, "idb")
    ph.cp("vector", identb[:, :], cv(cst, cd, "ident"))
    pstf = ph.ps([128, 128], F32, "pstf")
    identF = ph.sb([128, 128], F32, "idf")
    ph.cp("vector", identF[:, :], cv(cst, cd, "ident"))
    modc = load_cols(ph, modraw, 6 * NDC, identF, pstf, "modc")
    lnc = load_cols(ph, lnv_in, 6 * NDC, identF, pstf, "lnc")
    A1 = ph.sb([128, NDC], F32, "A1")
    B1 = ph.sb([128, NDC], F32, "B1")
    t1 = ph.sb([128, NDC], F32, "t1")
    ph.ts("vector", t1[:, :], modc[:, NDC:2 * NDC], 1.0, ALU.add)
    ph.tt("vector", A1[:, :], t1[:, :], lnc[:, 0:NDC], ALU.mult)
    ph.tt("vector", B1[:, :], t1[:, :], lnc[:, NDC:2 * NDC], ALU.mult)
    ph.tt("vector", B1[:, :], B1[:, :], modc[:, 0:NDC], ALU.add)
    xts = [ph.sb([128, D], F32, "xt") for _ in range(2)]
    xhs = [ph.sb([128, D], BF16, "xh") for _ in range(2)]
    uTt = [ph.sb([128, NDC, TC], BF16, "uTt") for _ in range(2)]
    nst = max(1, D // 512)
    stats = ph.sb([128, nst * 6], F32, "stats")
    mv = ph.sb([128, 2], F32, "mv")
    rstd = ph.sb([128, 1], F32, "rstd")
    nmr = ph.sb([128, 1], F32, "nmr")
    pstb = [ph.ps([128, 128], BF16, "pstb") for _ in range(4)]
    lnt = [(ph.sb([128, nst * 6], F32, "stats"), ph.sb([128, 2], F32, "mv"), ph.sb([128, 1], F32, "rstd"),
            ph.sb([128, 1], F32, "nmr")) for _ in range(2)]
    ti = 0
    for tcI in range(NTC):
        ub = uTt[tcI % 2]
        for c in range(NCH):
            i = tcI * NCH + c
            xt = xts[i % 2]
            xh = xhs[i % 2]
            ph.ld("sync", f"xt{i % 2}", xt[:, :], x_in[i * 128:(i + 1) * 128, :])
            stats, mv, rstd, nmr = lnt[i % 2]
            ln_stats(ph, xt, D, stats, mv, rstd, nmr, LN_EPS)
            ph.act(xh[:, :], xt[:, :], AF.Identity, bias=nmr[:, :], scale=rstd[:, :])
            for k in range(NDC):
                pt = pstb[ti % 4]
                ti += 1
                ph.tr(pt[:, :], xh[:, k * 128:(k + 1) * 128], identb[:, :])
                if k % 2 == 0:
                    ph.ts("vector", ub[:, k, c * 128:(c + 1) * 128], pt[:, :], A1[:, k:k + 1], ALU.mult,
                          B1[:, k:k + 1], ALU.add)
                else:
                    ph.act(ub[:, k, c * 128:(c + 1) * 128], pt[:, :], AF.Identity, bias=B1[:, k:k + 1],
                           scale=A1[:, k:k + 1])
        ph.stq("gpsimd", f"uTst{tcI % 2}", uTd[:, :, tcI * TC:(tcI + 1) * TC].rearrange("k p t -> p k t"),
               ub[:, :, :])
    ph.emit()

    if upto < 2:
        dbgcopy()
        return nc
    TC_save = TC
    TC = min(256, S)
    NCH = TC // 128
    NTC = S // TC
    for sp in range(NSUB):
        ph = Phase(nc, f"p2{sp}")
        cst, cd = consts(ph)
        IDN = cv(cst, cd, "ident")
        MUS = cv(cst, cd, "mus")
        MUI = cv(cst, cd, "mui")
        MLS = cv(cst, cd, "mls")
        BO = cv(cst, cd, "bo")
        ONES = cv(cst, cd, "ones")
        W = ph.sb([128, NDC, NB * 128], BF16, "W")
        ph.ldm("gpsimd", "Wld", W.k, [(W.t[:, k, :], wrw_in[sp, k * 128:(k + 1) * 128, :]) for k in range(NDC)])
        mu = ph.sb([128, NB], F32, "mu")
        omu = ph.sb([128, NB], F32, "omu")
        ph.ld("sync", "mu", mu[:, :], mu_in[:, sp * NB:(sp + 1) * NB])
        ph.ts("vector", omu[:, :], mu[:, :], -1.0, ALU.mult, 1.0, ALU.add)
        chp = ph.sb([128, 7 * NHP], F32, "chp")
        ph.ld("sync", "chp", chp[:, :], chp_in[:, :])
        omka = ph.sb([128, NHP], F32, "omka")
        ph.ts("vector", omka[:, :], chp[:, 3 * NHP:4 * NHP], -1.0, ALU.mult, 1.0, ALU.add)
        g8 = ph.sb([128, NHP], F32, "g8")
        ph.ts("vector", g8[:, :], chp[:, 5 * NHP:6 * NHP], 8.0, ALU.mult)
        wdec = ph.sb([128, NHP * 128], F32, "wdec")
        wicl = ph.sb([128, NHP * 128], F32, "wicl")
        wgat = ph.sb([128, NHP * 128], F32, "wgat")
        ph.ld("sync", "wdec", wdec[:, :], wdec_in[:, :])
        ph.ld("sync", "wicl", wicl[:, :], wicl_in[:, :])
        ph.ld("sync", "wgat", wgat[:, :], wgat_in[:, :])
        uTs = [ph.sb([128, NDC, TC], BF16, "uT") for _ in range(2)]
        praw = [ph.sb([128, TC + 1], F32, "praw") for _ in range(2)]
        carry = ph.sb([128, NB], F32, "carry")
        ph.ms("vector", carry[:, :], 0.0)
        tmpa = [ph.sb([128, TC], F32, "tmpa") for _ in range(2)]
        sh = [ph.sb([128, TC], F32, f"sh{j}") for j in range(NB)]
        psw = [ph.ps([128, 512], F32, "psw") for _ in range(3)]
        psq_b = [ph.ps([128, 512], F32, "psq") for _ in range(5)]
        NPQ = int(os.environ.get('NPQ', '5'))
        psq = [psq_b[i % 5].sk(i // 5)[:, (i // 5) * 128:(i // 5 + 1) * 128] for i in range(NPQ)]
        qn = [0]
        wn = [0]

        def nq():
            qn[0] += 1
            return psq[qn[0] % NPQ]

        def nw():
            wn[0] += 1
            return psw[wn[0] % 3]

        def F(key):
            return ph.sb([128, TC], F32, key)

        sg, et, gt, kk, kkn, kp, bv, cs, Wt, Wi, Wp, rt, kt, bt, at_, bon, tq1, yraw, yc = [
            F(n) for n in ["sg", "et", "gt", "kk", "kkn", "kp", "bv", "cs", "Wt", "Wi", "Wp", "rt", "kt", "bt",
                           "at", "bon", "tq1", "yraw", "yc"]]
        yo = [ph.sb([128, TC], BF16, "yo") for _ in range(2)]
        btm = [F("btm"), F("btm")]
        ktm = [F("ktm"), F("ktm")]
        atm = [F("atm"), F("atm")]
        ST = [ph.sb([128, 128], F32, f"ST{h}") for h in range(HP_SUB)]
        for h in range(HP_SUB):
            ph.ms("vector", ST[h][:, :], 0.0)

        def M(key):
            return ph.sb([128, 128], F32, key)

        mats = {}
        for c2 in range(2):
            for h in range(2):
                mats[(c2, h)] = dict(P=[M("P"), M("P")], PT=[M("PT"), M("PT")], T=M("T"), AKA=M("AKA"),
                                     ABR=M("ABR"), AKR=M("AKR"))
        pairm = []
        for c2 in range(2):
            d = dict(VTp=M("VTp"), VTe=M("VTe"), VTo=M("VTo"), BTT=M("BTT"), KTT=M("KTT"), RHS=M("RHS"),
                     SAp=M("SAp"), SAe=M("SAe"), SAo=M("SAo"))
            for n in ["VTe", "VTo", "SAe", "SAo"]:
                ph.ms("gpsimd", d[n][:, :], 0.0)
            pairm.append(d)
        NLEV = 6

        for tcI in range(NTC):
            uT = uTs[tcI % 2]
            ph.ld("sync", f"uT{tcI % 2}", uT[:, :, :],
                  uTd[:, :, tcI * TC:(tcI + 1) * TC].rearrange("k p t -> p k t"), reads=["uTd"])
            for j in range(NB):
                bank = nw()
                for k in range(NDC):
                    ph.mm(bank[:, 0:TC], W[:, k, j * 128:(j + 1) * 128], uT[:, k, :], st=(k == 0), sp=(k == NDC - 1))
                pr = praw[j % 2]
                ph.cp("gpsimd", pr[:, 0:1], carry[:, j:j + 1])
                ph.cp("scalar", pr[:, 1:TC + 1], bank[:, 0:TC])
                ph.cp("gpsimd", carry[:, j:j + 1], pr[:, TC:TC + 1])
                tm = tmpa[j % 2]
                ph.ts("vector", tm[:, :], pr[:, 0:TC], mu[:, j:j + 1], ALU.mult)
                ph.stt(sh[j][:, :], pr[:, 1:TC + 1], omu[:, j:j + 1], tm[:, :], ALU.mult, ALU.add)
            if CUT == 1:
                ph.dead = True
            xw, xa, xg = sh[NB - 3], sh[NB - 2], sh[NB - 1]
            ph.act(xw[:, :], xw[:, :], AF.Tanh)
            ph.act(xg[:, :], xg[:, :], AF.Sigmoid)
            for hl in range(HP_SUB):
                hp = sp * HP_SUB + hl
                r_, k_, v_ = sh[hl], sh[HP_SUB + hl], sh[2 * HP_SUB + hl]

                def P_(i):
                    return chp[:, i * NHP + hp:i * NHP + hp + 1]
                w0c, a0c, kkc, kac, rkc, lgc, lbc = [P_(i) for i in range(7)]
                hs = slice(hp * 128, (hp + 1) * 128)
                b = nw()
                ph.mm(b[:, 0:TC], wdec[0:LW, hs], xw[0:LW, :])
                ph.act(sg[:, :], b[:, 0:TC], AF.Sigmoid, bias=w0c)
                b = nw()
                ph.mm(b[:, 0:TC], wicl[0:LA, hs], xa[0:LA, :])
                ph.act(et[:, :], b[:, 0:TC], AF.Sigmoid, bias=a0c)
                b = nw()
                ph.mm(b[:, 0:TC], wgat[0:LG, hs], xg[0:LG, :])
                ph.cp("vector", gt[:, :], b[:, 0:TC])
                ph.act(kk[:, :], k_[:, :], AF.Copy, scale=kkc)
                ph.tt("gpsimd", tq1[:, :], kk[:, :], kk[:, :], ALU.mult)
                b = nw()
                ph.mm(b[:, 0:TC], BO, tq1[:, :])
                ph.rsqrt(tq1[:, :], b[:, 0:TC], 1e-24)
                ph.tt("gpsimd", kkn[:, :], kk[:, :], tq1[:, :], ALU.mult)
                ph.ts("vector", tq1[:, :], et[:, :], kac, ALU.mult, omka[:, hp:hp + 1], ALU.add)
                ph.tt("gpsimd", kp[:, :], k_[:, :], tq1[:, :], ALU.mult)
                ph.tt("gpsimd", bv[:, :], kkn[:, :], et[:, :], ALU.mult)
                for c in range(NCH):
                    sl = slice(c * 128, (c + 1) * 128)
                    ph.op("vector", (lambda o, d0, d1: (lambda e: e.tensor_tensor_scan(o.ap, d0.ap, d1.ap, 0.0, ALU.mult, ALU.add)))(
                        cs[:, sl], ONES, sg[:, sl]), [ONES.k, sg.k], [cs.k])
                ph.act(Wt[:, :], cs[:, :], AF.Exp, scale=-C0)
                ph.act(Wi[:, :], cs[:, :], AF.Exp, scale=C0)
                ph.tt("gpsimd", tq1[:, :], cs[:, :], sg[:, :], ALU.subtract)
                ph.act(Wp[:, :], tq1[:, :], AF.Exp, scale=-C0)
                ph.tt("gpsimd", rt[:, :], r_[:, :], Wt[:, :], ALU.mult)
                ph.tt("gpsimd", kt[:, :], kp[:, :], Wi[:, :], ALU.mult)
                ph.tt("gpsimd", bt[:, :], bv[:, :], Wi[:, :], ALU.mult)
                ph.stt(at_[:, :], kkn[:, :], -1.0, Wp[:, :], ALU.mult, ALU.mult)
                for h in range(2):
                    mcol = Vw(cd["bo"][:, 127 * h:127 * h + 1], cst.k)
                    ph.act(btm[h][:, :], bt[:, :], AF.Copy, scale=mcol)
                    ph.ts("vector", ktm[h][:, :], kt[:, :], mcol, ALU.mult)
                    ph.act(atm[h][:, :], at_[:, :], AF.Copy, scale=mcol)
                ph.tt("gpsimd", tq1[:, :], r_[:, :], kp[:, :], ALU.mult)
                ph.act(tq1[:, :], tq1[:, :], AF.Copy, scale=rkc)
                b = nw()
                ph.mm(b[:, 0:TC], BO, tq1[:, :])
                ph.tt("vector", bon[:, :], b[:, 0:TC], v_[:, :], ALU.mult)
                if CUT == 2:
                    ph.dead = True
                for c in range(NCH):
                    sl = slice(c * 128, (c + 1) * 128)
                    pm = pairm[c % 2]
                    for h in range(2):
                        hb = slice(64 * h, 64 * h + 64)
                        m = mats[(c % 2, h)]
                        if CUT == 31:
                            ph.dead = True
                        q = nq()
                        ph.mm(q, btm[h][:, sl], at_[:, sl])
                        ph.tt("vector", m["P"][0][:, :], q, MUS, ALU.mult)
                        if CUT == 20:
                            ph.dead = True
                        if CUT == 32:
                            ph.dead = True
                        q = nq()
                        ph.mm(q, atm[h][:, sl], bt[:, sl])
                        ph.tt("vector", m["PT"][0][:, :], q, MLS, ALU.mult)
                        if CUT == 33:
                            ph.dead = True
                        q = nq()
                        ph.mm(q, ktm[h][:, sl], at_[:, sl])
                        ph.tt("vector", m["AKA"][:, :], q, MUS, ALU.mult)
                        if CUT == 34:
                            ph.dead = True
                        q = nq()
                        ph.mm(q, btm[h][:, sl], rt[:, sl])
                        ph.tt("vector", m["ABR"][:, :], q, MUI, ALU.mult)
                        if CUT == 35:
                            ph.dead = True
                        q = nq()
                        ph.mm(q, ktm[h][:, sl], rt[:, sl])
                        ph.tt("vector", m["AKR"][:, :], q, MUI, ALU.mult)
                        if CUT == 36:
                            ph.dead = True
                        ph.tt("gpsimd", m["T"][:, :], m["P"][0][:, :], IDN, ALU.add)
                    if CUT == 21:
                        ph.dead = True
                    q = nq()
                    ph.tr(q, v_[:, sl], IDN)
                    ph.cp("scalar", pm["VTp"][:, :], q)
                    ph.cp("gpsimd", pm["VTe"].sk("a")[:, 0:64], pm["VTp"][:, 0:64])
                    ph.cp("gpsimd", pm["VTo"].sk("a")[:, 64:128], pm["VTp"][:, 64:128])
                    q = nq()
                    ph.tr(q, bt[:, sl], IDN)
                    ph.cp("scalar", pm["BTT"][:, :], q)
                    q = nq()
                    ph.tr(q, kt[:, sl], IDN)
                    ph.cp("scalar", pm["KTT"][:, :], q)
                if CUT == 3:
                    ph.dead = True
                assert NCH <= 2
                for lev in range(1, NLEV + 1):
                    a, bnew = (lev - 1) % 2, lev % 2
                    for c in range(NCH):
                        for h in range(2):
                            m = mats[(c % 2, h)]
                            if lev < NLEV:
                                q = nq()
                                ph.mm(q, m["PT"][a][:, :], m["P"][a][:, :])
                                ph.cp("scalar", m["P"][bnew][:, :], q)
                            q = nq()
                            ph.mm(q, m["P"][a][:, :], m["PT"][a][:, :])
                            ph.cp("scalar" if h == 0 else "vector", m["PT"][bnew][:, :], q)
                    for c in range(NCH):
                        for h in range(2):
                            m = mats[(c % 2, h)]
                            q = nq()
                            ph.mm(q, m["PT"][bnew][:, :], m["T"][:, :])
                            ph.tt("vector", m["T"][:, :], m["T"][:, :], q, ALU.add)
                for c in range(NCH):
                    sl = slice(c * 128, (c + 1) * 128)
                    pm = pairm[c % 2]
                    if CUT == 4:
                        ph.dead = True
                    me, mo = mats[(c % 2, 0)], mats[(c % 2, 1)]
                    STh = ST[hl]
                    q = nq()
                    ph.mm(q, at_[:, sl], STh[:, :], st=True, sp=False)
                    ph.mm(q, me["AKA"][:, :], pm["VTe"].sk("a")[:, :], st=False, sp=False)
                    ph.mm(q, mo["AKA"][:, :], pm["VTo"].sk("a")[:, :], st=False, sp=True)
                    ph.cp("scalar", pm["RHS"][:, :], q)
                    q = nq()
                    qa = Vw(q.ap[:, 0:64], q.k)
                    qb = Vw(q.ap[:, 64:128], q.k)
                    ph.mm(qa, me["T"][:, :], pm["RHS"][:, 0:64])
                    ph.mm(qb, mo["T"][:, :], pm["RHS"][:, 64:128])
                    ph.cp("scalar", pm["SAp"][:, :], q)
                    ph.cp("gpsimd", pm["SAe"].sk("a")[:, 0:64], pm["SAp"][:, 0:64])
                    ph.cp("gpsimd", pm["SAo"].sk("a")[:, 64:128], pm["SAp"][:, 64:128])
                    q = nq()
                    ph.mm(q, STh[:, :], rt[:, sl], st=True, sp=False)
                    ph.mm(q, pm["SAe"].sk("a")[:, :], me["ABR"][:, :], st=False, sp=False)
                    ph.mm(q, pm["SAo"].sk("a")[:, :], mo["ABR"][:, :], st=False, sp=False)
                    ph.mm(q, pm["VTe"].sk("a")[:, :], me["AKR"][:, :], st=False, sp=False)
                    ph.mm(q, pm["VTo"].sk("a")[:, :], mo["AKR"][:, :], st=False, sp=True)
                    ph.cp("scalar", yraw[:, sl], q)
                    q = nq()
                    ph.mm(q, IDN, STh[:, :], st=True, sp=False)
                    ph.mm(q, pm["BTT"][:, :], pm["SAp"][:, :], st=False, sp=False)
                    ph.mm(q, pm["KTT"][:, :], pm["VTp"][:, :], st=False, sp=True)
                    ph.stt(STh[:, :], q, Wt[:, c * 128 + 127:c * 128 + 128], BO, ALU.mult, ALU.mult)
                if CUT == 5:
                    ph.dead = True
                b = nw()
                ph.mm(b[:, 0:TC], BO, yraw[:, :])
                ph.stt(yc[:, :], b[:, 0:TC], -1.0 / 64, yraw[:, :], ALU.mult, ALU.add)
                ph.tt("gpsimd", tq1[:, :], yc[:, :], yc[:, :], ALU.mult)
                b = nw()
                ph.mm(b[:, 0:TC], BO, tq1[:, :])
                ph.rsqrt(tq1[:, :], b[:, 0:TC], 64 * LNX_EPS)
                ph.tt("gpsimd", yc[:, :], yc[:, :], tq1[:, :], ALU.mult)
                ph.ts("vector", yc[:, :], yc[:, :], g8[:, hp:hp + 1], ALU.mult, lbc, ALU.add)
                ph.tt("gpsimd", yc[:, :], yc[:, :], bon[:, :], ALU.add)
                yob = yo[(tcI * HP_SUB + hl) % 2]
                ph.tt("vector", yob[:, :], yc[:, :], gt[:, :], ALU.mult)
                ph.stq("sync", f"yo{(tcI * HP_SUB + hl) % 2}", ybuf_t[hp].ap()[:, tcI * TC:(tcI + 1) * TC],
                       yob[:, :], writes=["ybuf"])
        ph.emit()

    if upto < 3:
        dbgcopy()
        return nc
    TC = TC_save
    NCH = TC // 128
    NTC = S // TC
    ph = Phase(nc, "p3")
    cst, cd = consts(ph)
    IDN = cv(cst, cd, "ident")
    identb = ph.sb([128, 128], BF16, "idb")
    ph.cp("vector", identb[:, :], IDN)
    W = ph.sb([128, NDC, NAB * 128], BF16, "W")
    ph.ldm("gpsimd", "Wld", W.k, [(W.t[:, k, :], wat_in[k * 128:(k + 1) * 128, :]) for k in range(NDC)])
    biasm = ph.sb([128, QPC * 256], F32, "biasm")
    ph.ld("sync", "bias", biasm[:, :], bias_in[:, :])
    for h in range(QPC):
        for hf in range(2):
            ph.tt("vector", biasm[:, h * 256 + hf * 128:h * 256 + hf * 128 + 128],
                  biasm[:, h * 256 + hf * 128:h * 256 + hf * 128 + 128],
                  cv(cst, cd, "amc") if hf == 1 else cv(cst, cd, "amp"), ALU.add)
    sinkc = ph.sb([128, QPC], F32, "sink")
    ph.ld("sync", "sink", sinkc[:, :], sink_in[:, :])
    uTs = [ph.sb([128, NDC, TC], BF16, "uT") for _ in range(2)]
    qTm = [[ph.sb([128, TC], BF16, f"qT{i}_{h}") for h in range(2)] for i in range(NQB)]
    kT = ph.sb([128, TC + 128], BF16, "kT")
    vT = ph.sb([128, TC], BF16, "vT")
    VTe = ph.sb([128, NCH + 1, 128], BF16, "VTe")
    VTo = ph.sb([128, NCH + 1, 128], BF16, "VTo")
    ph.ms("vector", VTe[:, :, :], 0.0)
    ph.ms("vector", VTo[:, :, :], 0.0)
    ph.ms("vector", kT[:, :], 0.0)
    psw = [ph.ps([128, 512], F32, "psw") for _ in range(1)]
    class _Sub:
        def __init__(s_, bank, c0, n, key):
            s_.b, s_.c0, s_.n, s_.k = bank, c0, n, bank.k + "/" + key

        def __getitem__(s_, idx):
            rows, cols = idx
            a = s_.c0 + (cols.start or 0)
            bb = s_.c0 + (cols.stop if cols.stop is not None else s_.n)
            return Vw(s_.b.t[rows, a:bb], s_.k)
    pss = [ph.ps([128, 512], F32, "pss") for _ in range(3)]
    pstb = [ph.ps([128, 1024], BF16, "pstb") for _ in range(2)]
    pso = [ph.ps([128, 512], F32, "pso") for _ in range(2)]
    NS = 3
    ssb = [ph.sb([128, 512], F32, "ssb") for _ in range(NS)]
    esb = [ph.sb([128, 512], F32, "esb") for _ in range(NS)]
    pb = [ph.sb([128, 512], BF16, "pb") for _ in range(NS)]
    pTs = [ph.sb([128, 512], BF16, "pTs") for _ in range(NS)]
    sm = [ph.sb([128, 16], F32, "sm") for _ in range(NS)]
    biasm0 = ph.sb([128, QPC * 256], F32, "biasm0")
    ph.cp("vector", biasm0[:, :], biasm[:, :])
    for h in range(QPC):
        ph.ms("vector", biasm0[:, h * 256:h * 256 + 128], -30000.0)
    yb = [ph.sb([128, TC], BF16, "yb") for _ in range(2 * NQB)]
    hn = 0
    wn = 0
    for tcI in range(NTC):
        uT = uTs[tcI % 2]
        ph.ld("sync", f"uT{tcI % 2}", uT[:, :, :],
              uTd[:, :, tcI * TC:(tcI + 1) * TC].rearrange("k p t -> p k t"), reads=["uTd"])
        if tcI > 0:
            ph.cp("gpsimd", kT[:, 0:128], kT[:, TC:TC + 128])
            ph.cp("gpsimd", VTe[:, 0, :], VTe[:, NCH, :])
            ph.cp("gpsimd", VTo[:, 0, :], VTo[:, NCH, :])
        for j in range(NAB):
            bank = psw[0]
            wn += 1
            for k in range(NDC):
                ph.mm(bank[:, 0:TC], W[:, k, j * 128:(j + 1) * 128], uT[:, k, :], st=(k == 0), sp=(k == NDC - 1))
            if j < NQB:
                for h2 in range(2):
                    ph.ts("vector", qTm[j][h2][:, :], bank[:, 0:TC], Vw(cd["bo"][:, 127 * h2:127 * h2 + 1], cst.k), ALU.mult)
            elif j == NQB:
                ph.cp("scalar", kT[:, 128:128 + TC], bank[:, 0:TC])
            else:
                ph.cp("scalar", vT[:, :], bank[:, 0:TC])
        for c in range(NCH):
            pt = pstb[c % 2]
            ph.tr(pt[:, 0:128], vT[:, c * 128:(c + 1) * 128], identb[:, :])
            ph.cp("vector", VTe[:, c + 1, 0:64], pt[:, 0:64])
            ph.cp("vector", VTo[:, c + 1, 64:128], pt[:, 64:128])
        for c in range(NCH):
            first = (tcI == 0 and c == 0)
            bm = biasm0 if first else biasm
            for qb in range(NQB):
                po = pso[hn % 2]
                ps_ = pss[hn % 3]
                ptb = pstb[hn % 2]
                s_, e_, p_, pT_, m_ = ssb[hn % NS], esb[hn % NS], pb[hn % NS], pTs[hn % NS], sm[hn % NS]
                hn += 1
                for h2 in range(2):
                    ph.mm(ps_[:, h2 * 256:(h2 + 1) * 256], qTm[qb][h2][:, c * 128:(c + 1) * 128], kT[:, c * 128:c * 128 + 256])
                ph.stt(s_[:, :], ps_[:, :], 0.125, bm[:, 2 * qb * 256:(2 * qb + 2) * 256], ALU.mult, ALU.add)
                s3 = Vw(s_.t[:, :].rearrange("p (h k) -> p h k", h=2), s_.k)
                e3 = Vw(e_.t[:, :].rearrange("p (h k) -> p h k", h=2), e_.k)
                ph.op("vector", (lambda o, a: (lambda e: e.reduce_max(o.ap, a.ap, AX.X)))(m_[:, 0:2], s3), [s_.k], [m_.k])
                ph.tt("vector", m_[:, 2:4], m_[:, 0:2], sinkc[:, 2 * qb:2 * qb + 2], ALU.max)
                ph.ts("vector", m_[:, 4:6], m_[:, 2:4], -1.0, ALU.mult)
                for h2 in range(2):
                    ph.act(e_[:, h2 * 256:(h2 + 1) * 256], s_[:, h2 * 256:(h2 + 1) * 256], AF.Exp, bias=m_[:, 4 + h2:5 + h2])
                ph.tt("vector", m_[:, 6:8], sinkc[:, 2 * qb:2 * qb + 2], m_[:, 4:6], ALU.add)
                ph.act(m_[:, 8:10], m_[:, 6:8], AF.Exp)
                ph.op("vector", (lambda o, a: (lambda e: e.reduce_sum(o.ap, a.ap, AX.X)))(m_[:, 10:12], e3), [e_.k], [m_.k])
                ph.tt("vector", m_[:, 12:14], m_[:, 10:12], m_[:, 8:10], ALU.add)
                ph.op("vector", (lambda o, a: (lambda e: e.reciprocal(o.ap, a.ap)))(m_[:, 14:16], m_[:, 12:14]), [m_.k], [m_.k])
                for h2 in range(2):
                    ph.act(p_[:, h2 * 256:(h2 + 1) * 256], e_[:, h2 * 256:(h2 + 1) * 256], AF.Copy, scale=m_[:, 14 + h2:15 + h2])
                for i4 in range(4):
                    ph.tr(ptb[:, i4 * 128:(i4 + 1) * 128], p_[:, i4 * 128:(i4 + 1) * 128], identb[:, :])
                ph.cp("vector" if hn % 2 else "scalar", pT_[:, :], ptb[:, 0:512])
                for h2 in range(2):
                    VT = VTe if h2 == 0 else VTo
                    ph.mm(po[:, 0:128], VT[:, c, :], pT_[:, (2 * h2) * 128:(2 * h2 + 1) * 128], st=(h2 == 0), sp=False)
                    ph.mm(po[:, 0:128], VT[:, c + 1, :], pT_[:, (2 * h2 + 1) * 128:(2 * h2 + 2) * 128], st=False, sp=(h2 == 1))
                ybt = yb[(tcI % 2) * NQB + qb]
                ph.cp("scalar", ybt[:, c * 128:(c + 1) * 128], po[:, 0:128])
        for qb in range(NQB):
            ybt = yb[(tcI % 2) * NQB + qb]
            ph.stq("sync", f"yb{(tcI % 2) * NQB + qb}",
                   ybuf_t[NHP + qb].ap()[:, tcI * TC:(tcI + 1) * TC], ybt[:, :],
                   writes=["ybuf"])
    ph.emit()

    dbgcopy()
    if upto < 4:
        return nc
    cc_sem = nc.alloc_semaphore(name="cc_sem")
    with nc.Block() as block:
        @block.gpsimd
        def _(g):
            for j in range(NRB):
                g.collective_compute("AllGather", ALU.bypass, replica_groups=[[0, 1, 2, 3], [4, 5, 6, 7]],
                                     ins=[ybuf_t[j].ap().opt()], outs=[yall_t[j].ap().opt()]).then_inc(cc_sem)
            g.wait_ge(cc_sem, NRB)
    nc.clear_and_free_semaphores([cc_sem])
    nc.all_engine_barrier()

    if upto < 5:
        return nc
    ph = Phase(nc, "p5")
    cst, cd = consts(ph)
    KC = 4 * YR // 128
    cT = ph.sb([128, KC, NT], BF16, "cT")
    rk_per = YR // 128

    def ldc(e, sem):
        pid = e.partition_id()
        tq = pid % 4
        for j in range(NRB):
            yv = yall_t[j].ap().rearrange("(r p) t -> p r t", p=128)
            e.dma_start(out=cT.t[:, j * 4:(j + 1) * 4, :], in_=yv[:, :, bass.ds(tq * NT, NT)]).then_inc(sem, 16)
    ph.dma("gpsimd", "cTld", ldc, [], [cT.k], n=NRB)
    g1row = modraw[2 * NDC:3 * NDC, :].rearrange("(o a) b -> o (a b)", o=1)
    NTT = NT // 128
    xt = ph.sb([128, D], F32, "xt")
    nst = max(1, D // 512)
    stats = ph.sb([128, nst * 6], F32, "stats")
    mv = ph.sb([128, 2], F32, "mv")
    rs_all = ph.sb([128, NTT], F32, "rs")
    nm_all = ph.sb([128, NTT], F32, "nm")
    for tt_ in range(NTT):
        ph.ld("sync", "xt", xt[:, :], xq_in[tt_ * 128:(tt_ + 1) * 128, :])
        ln_stats(ph, xt, D, stats, mv, rs_all.sk(tt_)[:, tt_:tt_ + 1], nm_all.sk(tt_)[:, tt_:tt_ + 1], LN_EPS)
    DB = min(512, D)
    NDB = D // DB
    Wo = [ph.sb([128, KC, DB], BF16, "Wo") for _ in range(2)]
    psm = [ph.ps([128, DB], F32, "psm") for _ in range(4)]
    xp = [ph.sb([128, DB], F32, "xp") for _ in range(2)]
    zp = [ph.sb([128, DB], F32, "zp") for _ in range(2)]
    mg = [ph.sb([128, DB], F32, "mg") for _ in range(2)]
    gEs = [ph.sb([128, DB], F32, "gE") for _ in range(2)]
    bEs = [ph.sb([128, DB], F32, "bE") for _ in range(2)]
    G1s = [ph.sb([128, DB], F32, "G1") for _ in range(2)]
    D_A = cfg["D_A"]
    HW_ = cfg["HPC"] * 64
    it = 0

    def p5_loads(db):
        Wb = Wo[db % 2]
        gE, bE, G1 = gEs[db % 2], bEs[db % 2], G1s[db % 2]
        dsl = slice(db * DB, (db + 1) * DB)
        ph.ld("sync", f"gE{db % 2}", gE[:, :], lnr_in[0, dsl].partition_broadcast(128))
        ph.ld("sync", f"bE{db % 2}", bE[:, :], lnr_in[1, dsl].partition_broadcast(128))
        ph.ld("sync", f"G1{db % 2}", G1[:, :], g1row[0, dsl].partition_broadcast(128))
        ph.ts("vector", G1[:, :], G1[:, :], 1.0, ALU.add)
        pairs = []
        for kc in range(KC):
            j = kc // 4
            r = kc % 4
            row = (r * HW_ + j * 128) if j * 128 < HW_ else (D_A + r * (cfg["QPC"] * 64) + (j * 128 - HW_))
            pairs.append((Wb.t[:, kc, :], wout_in[row:row + 128, dsl]))
        ph.ldm("gpsimd", f"Wo{db % 2}", Wb.k, pairs)

    p5_loads(0)
    for db in range(NDB):
        if db + 1 < NDB:
            p5_loads(db + 1)
        Wb = Wo[db % 2]
        gE, bE, G1 = gEs[db % 2], bEs[db % 2], G1s[db % 2]
        dsl = slice(db * DB, (db + 1) * DB)
        for tt_ in range(NTT):
            pm_ = psm[it % 4]
            x_ = xp[it % 2]
            z_ = zp[it % 2]
            g_ = mg[it % 2]
            for kc in range(KC):
                ph.mm(pm_[:, :], cT[:, kc, tt_ * 128:(tt_ + 1) * 128], Wb[:, kc, :], st=(kc == 0), sp=(kc == KC - 1))
            ph.ld("sync", f"xp{it % 2}", x_[:, :], xq_in[tt_ * 128:(tt_ + 1) * 128, dsl])
            ph.act(x_[:, :], x_[:, :], AF.Identity, bias=nm_all.sk(tt_)[:, tt_:tt_ + 1], scale=rs_all.sk(tt_)[:, tt_:tt_ + 1])
            ph.tt("vector", x_[:, :], x_[:, :], gE[:, :], ALU.mult)
            ph.tt("vector", x_[:, :], x_[:, :], bE[:, :], ALU.add)
            ph.tt("vector", g_[:, :], pm_[:, :], G1[:, :], ALU.mult)
            ph.stt(z_[:, :], x_[:, :], ALPHA, g_[:, :], ALU.mult, ALU.add)
            ph.stq("sync", f"zst{it % 2}", zbuf[tt_ * 128:(tt_ + 1) * 128, dsl], z_[:, :], writes=["zbuf"])
            it += 1
    ph.emit()

    if upto < 6:
        return nc
    ph = Phase(nc, "p6a")
    cst, cd = consts(ph)
    identb = ph.sb([128, 128], BF16, "idb")
    ph.cp("vector", identb[:, :], cv(cst, cd, "ident"))
    identF = ph.sb([128, 128], F32, "idf")
    ph.cp("vector", identF[:, :], cv(cst, cd, "ident"))
    pstf = ph.ps([128, 128], F32, "pstf")
    modc = load_cols(ph, modraw, 6 * NDC, identF, pstf, "modc")
    lnc = load_cols(ph, lnv_in, 6 * NDC, identF, pstf, "lnc")
    A2 = ph.sb([128, NDC], F32, "A2")
    B2 = ph.sb([128, NDC], F32, "B2")
    t1 = ph.sb([128, NDC], F32, "t1")
    ph.ts("vector", t1[:, :], modc[:, 4 * NDC:5 * NDC], 1.0, ALU.add)
    ph.tt("vector", A2[:, :], t1[:, :], lnc[:, 2 * NDC:3 * NDC], ALU.mult)
    ph.tt("vector", B2[:, :], t1[:, :], lnc[:, 3 * NDC:4 * NDC], ALU.mult)
    ph.tt("vector", B2[:, :], B2[:, :], modc[:, 3 * NDC:4 * NDC], ALU.add)
    g1b = ph.sb([128, D], F32, "g1b")
    b1b = ph.sb([128, D], F32, "b1b")
    ph.ld("sync", "g1b", g1b[:, :], lnr_in[2, :].partition_broadcast(128))
    ph.ld("sync", "b1b", b1b[:, :], lnr_in[3, :].partition_broadcast(128))
    u2s = [ph.sb([128, NDC, 128], BF16, "u2s") for _ in range(2)]
    zt = ph.sb([128, D], F32, "zt")
    zh = ph.sb([128, D], BF16, "zh")
    x1t = ph.sb([128, D], F32, "x1t")
    stats = ph.sb([128, nst * 6], F32, "stats")
    mv = ph.sb([128, 2], F32, "mv")
    rstd = ph.sb([128, 1], F32, "rstd")
    nmr = ph.sb([128, 1], F32, "nmr")
    pstb = [ph.ps([128, 128], BF16, "pstb") for _ in range(4)]
    ti = 0
    for tt_ in range(NTT):
        u2 = u2s[tt_ % 2]
        ph.ld("sync", "zt", zt[:, :], zbuf[tt_ * 128:(tt_ + 1) * 128, :])
        ln_stats(ph, zt, D, stats, mv, rstd, nmr, LN_EPS)
        ph.act(zh[:, :], zt[:, :], AF.Identity, bias=nmr[:, :], scale=rstd[:, :])
        ph.act(x1t[:, :], zt[:, :], AF.Identity, bias=nmr[:, :], scale=rstd[:, :])
        ph.tt("gpsimd", x1t[:, :], x1t[:, :], g1b[:, :], ALU.mult)
        ph.tt("gpsimd", x1t[:, :], x1t[:, :], b1b[:, :], ALU.add)
        ph.stq("sync", "x1st", x1buf[tt_ * 128:(tt_ + 1) * 128, :], x1t[:, :])
        for k in range(NDC):
            pt = pstb[ti % 4]
            ti += 1
            ph.tr(pt[:, :], zh[:, k * 128:(k + 1) * 128], identb[:, :])
            ph.ts("vector", u2[:, k, :], pt[:, :], A2[:, k:k + 1], ALU.mult, B2[:, k:k + 1], ALU.add)
        ph.stq("sync", f"u2st{tt_ % 2}", u2Td[:, :, tt_ * 128:(tt_ + 1) * 128].rearrange("k p t -> p k t"), u2[:, :, :])
    ph.emit()

    ph = Phase(nc, "p6b")
    u2T = ph.sb([128, NDC, NT], BF16, "u2T")
    ph.ld("sync", "u2T", u2T[:, :, :], u2Td[:, :, :].rearrange("k p t -> p k t"))
    FB = 512
    NFB = DFF // FB
    TH = min(512, NT)
    NTH = NT // TH
    Wu = [ph.sb([128, NDC, FB], BF16, "Wu") for _ in range(2)]
    psu = [ph.ps([128, 512], F32, "psu") for _ in range(4)]
    hr = [ph.sb([128, TH], F32, "hr") for _ in range(2)]
    hT = [ph.sb([128, FB // 128, NT], BF16, "hT") for _ in range(2)]
    it = 0

    def p6_loads(fb):
        Wb = Wu[fb % 2]
        ph.ldm("gpsimd", f"Wu{fb % 2}", Wb.k,
               [(Wb.t[:, k, :], wup_in[k * 128:(k + 1) * 128, fb * FB:(fb + 1) * FB]) for k in range(NDC)])

    p6_loads(0)
    for fb in range(NFB):
        if fb + 1 < NFB:
            p6_loads(fb + 1)
        Wb = Wu[fb % 2]
        hb_ = hT[fb % 2]
        for c in range(FB // 128):
            for th in range(NTH):
                pu = psu[it % 4]
                h_ = hr[it % 2]
                it += 1
                for k in range(NDC):
                    ph.mm(pu[:, 0:TH], Wb[:, k, c * 128:(c + 1) * 128], u2T[:, k, th * TH:(th + 1) * TH],
                          st=(k == 0), sp=(k == NDC - 1))
                ph.act(h_[:, :], pu[:, 0:TH], AF.Relu)
                ph.tt("vector", hb_[:, c, th * TH:(th + 1) * TH], h_[:, :], h_[:, :], ALU.mult)
        ph.stq("sync", f"hst{fb % 2}", hbuf[fb * (FB // 128):(fb + 1) * (FB // 128), :, :].rearrange("c p t -> p c t"),
               hb_[:, :, :])
    ph.emit()

    if upto < 7:
        return nc
    ph = Phase(nc, "p7")
    DH = D // 2
    DQ = min(512, DH)
    NDQ = DH // DQ
    FBD = 1024
    NFD = DFF // FBD
    KB = FBD // 128
    acc = ph.sb([128, NTT, DH], F32, "acc")
    Wd = [ph.sb([128, KB, DH], BF16, "Wd") for _ in range(2)]
    hTb = [ph.sb([128, KB, NT], BF16, "hTb") for _ in range(2)]
    G2 = ph.sb([128, DH], F32, "G2")
    psd = [ph.ps([128, DQ], F32, "psd") for _ in range(4)]
    x1p = [ph.sb([128, DQ], F32, "x1p") for _ in range(2)]
    tmpds = [ph.sb([128, DQ], F32, "tmpd") for _ in range(2)]
    it = 0
    blocks = [(dh, fd) for dh in range(2) for fd in range(NFD)]

    def p7_loads(bi):
        dh, fd = blocks[bi]
        Wb = Wd[bi % 2]
        hb_ = hTb[bi % 2]
        ph.ldm("gpsimd", f"Wd{bi % 2}", Wb.k,
               [(Wb.t[:, kb, :], wdn_in[fd * FBD + kb * 128:fd * FBD + (kb + 1) * 128, dh * DH:(dh + 1) * DH])
                for kb in range(KB)])
        ph.ldm("sync", f"hTb{bi % 2}", hb_.k, [(hb_.t[:, kb, :], hbuf[fd * KB + kb, :, :]) for kb in range(KB)])

    p7_loads(0)
    for bi, (dh, fd) in enumerate(blocks):
        if bi + 1 < len(blocks):
            p7_loads(bi + 1)
        Wb = Wd[bi % 2]
        hb_ = hTb[bi % 2]
        if fd == 0:
            ph.ld("sync", "G2", G2[:, :],
                  modraw[5 * NDC:6 * NDC, :].rearrange("(o a) b -> o (a b)", o=1)[0, dh * DH:(dh + 1) * DH].partition_broadcast(128))
            ph.ts("vector", G2[:, :], G2[:, :], 1.0, ALU.add)
        for tt_ in range(NTT):
            for dq in range(NDQ):
                pd = psd[it % 4]
                it += 1
                for kb in range(KB):
                    ph.mm(pd[:, :], hb_[:, kb, tt_ * 128:(tt_ + 1) * 128], Wb[:, kb, dq * DQ:(dq + 1) * DQ],
                          st=(kb == 0), sp=(kb == KB - 1))
                a_ = acc.sk(f"{tt_}_{dq}")[:, tt_, dq * DQ:(dq + 1) * DQ]
                if fd == 0:
                    ph.cp("scalar", a_, pd[:, :])
                else:
                    ph.tt("vector", a_, pd[:, :], a_, ALU.add)
        if fd == NFD - 1:
            for tt_ in range(NTT):
                for dq in range(NDQ):
                    a_ = acc.sk(f"{tt_}_{dq}")[:, tt_, dq * DQ:(dq + 1) * DQ]
                    x_ = x1p[(tt_ * NDQ + dq) % 2]
                    c0 = dh * DH + dq * DQ
                    ph.ld("sync", f"x1p{(tt_ * NDQ + dq) % 2}", x_[:, :], x1buf[tt_ * 128:(tt_ + 1) * 128, c0:c0 + DQ])
                    ph.tt("vector", a_, a_, G2[:, dq * DQ:(dq + 1) * DQ], ALU.mult)
                    ph.stt(x_[:, :], x_[:, :], ALPHA, a_, ALU.mult, ALU.add)
                    ph.stq("sync", f"z2st{(tt_ * NDQ + dq) % 2}", z2buf[tt_ * 128:(tt_ + 1) * 128, c0:c0 + DQ], x_[:, :])
    ph.emit()

    ph = Phase(nc, "p8")
    g2b = ph.sb([128, D], F32, "g2b")
    b2b = ph.sb([128, D], F32, "b2b")
    ph.ld("sync", "g2b", g2b[:, :], lnr_in[4, :].partition_broadcast(128))
    ph.ld("sync", "b2b", b2b[:, :], lnr_in[5, :].partition_broadcast(128))
    zts = [ph.sb([128, D], F32, "zt") for _ in range(2)]
    ots = [ph.sb([128, D], F32, "ot") for _ in range(2)]
    stats = ph.sb([128, nst * 6], F32, "stats")
    mv = ph.sb([128, 2], F32, "mv")
    rstd = ph.sb([128, 1], F32, "rstd")
    nmr = ph.sb([128, 1], F32, "nmr")
    for tt_ in range(NTT):
        zt = zts[tt_ % 2]
        ot = ots[tt_ % 2]
        ph.ld("sync", f"zt{tt_ % 2}", zt[:, :], z2buf[tt_ * 128:(tt_ + 1) * 128, :], reads=["z2buf"])
        ln_stats(ph, zt, D, stats, mv, rstd, nmr, LN_EPS)
        ph.act(ot[:, :], zt[:, :], AF.Identity, bias=nmr[:, :], scale=rstd[:, :])
        ph.tt("gpsimd", ot[:, :], ot[:, :], g2b[:, :], ALU.mult)
        ph.tt("vector", ot[:, :], ot[:, :], b2b[:, :], ALU.add)
        ph.stq("sync", f"ost{tt_ % 2}", out_t[tt_ * 128:(tt_ + 1) * 128, :], ot[:, :], writes=["out"])
    ph.op("vector", lambda e: e.memset(nmr.t[:, :], 0.0), ["out"], [nmr.k])
    ph.emit()
    return nc


def t5_bucket(dist):
    n = np.maximum(dist, 0)
    nf = np.maximum(n, 1).astype(np.float32)
    large = 16 + (np.log(nf / 16) / math.log(128 / 16) * (32 - 16)).astype(np.int32)
    large = np.minimum(large, 31)
    return np.where(n < 16, n, large)


def host_inputs(cfg, inp):
    D, S, NDC = cfg["D"], cfg["S"], cfg["NDC"]
    HPC, NHP, QPC, NSUB, HP_SUB, NB, NAB = cfg["HPC"], cfg["NHP"], cfg["QPC"], cfg["NSUB"], cfg["HP_SUB"], cfg["NB"], cfg["NAB"]
    LW, LA, LG, NT = cfg["LW"], cfg["LA"], cfg["LG"], cfg["NT"]
    f = lambda a: np.ascontiguousarray(np.asarray(a, dtype=np.float32))
    x = f(inp["x"])
    c = f(inp["c"])
    w_in = f(inp["w_in"])[0]
    mu_shift = f(inp["mu_shift"])[0]
    lnrows = np.stack([f(inp["ln_emb_g"]), f(inp["ln_emb_b"]), f(inp["ln1_g"])[0], f(inp["ln1_b"])[0],
                       f(inp["ln2_g"])[0], f(inp["ln2_b"])[0]], 0)
    j = np.arange(128)[:, None]
    s_ = np.arange(128)[None, :]
    ident = np.eye(128, dtype=np.float32)
    mus = (j < s_).astype(np.float32)
    mui = (j <= s_).astype(np.float32)
    mls = (j > s_).astype(np.float32)
    bo = np.kron(np.eye(2, dtype=np.float32), np.ones((64, 64), np.float32))
    ones = np.ones((128, 128), np.float32)
    NEG = -30000.0
    amask_prev = np.where(s_ > j, 0.0, NEG).astype(np.float32)
    amask_cur = np.where(s_ <= j, 0.0, NEG).astype(np.float32)
    cst = np.concatenate([ident, mus, mui, mls, bo, ones, amask_prev, amask_cur], 1)
    qi = np.arange(128)[:, None]
    kj = np.arange(256)[None, :]
    bucket = t5_bucket(qi + 128 - kj)
    rpb = f(inp["rpb_table"])
    zero128 = np.zeros((D, 1), np.float32)
    maps = []
    for core in range(8):
        b, hg = core // 4, core % 4
        m = {}
        m["x"] = x[b]
        m["xq"] = np.ascontiguousarray(x[b, hg * NT:(hg + 1) * NT])
        m["ccol"] = np.ascontiguousarray(c[b].reshape(NDC, 128).T)
        MQ = 6 * D // 4
        m["wmod"] = np.ascontiguousarray(f(inp["w_mod"])[0][:, hg * MQ:(hg + 1) * MQ])
        m["bmod"] = np.ascontiguousarray(f(inp["b_mod"])[0][None, hg * MQ:(hg + 1) * MQ])
        m["lnv"] = np.ascontiguousarray(lnrows.reshape(6 * NDC, 128))
        m["lnr"] = lnrows
        wrw = np.zeros((NSUB, D, NB * 128), np.float32)
        mu = np.zeros((128, NSUB * NB), np.float32)
        for sp in range(NSUB):
            blocks = []
            for off in (0, cfg["OFF_K"], cfg["OFF_V"]):
                for hl in range(HP_SUB):
                    hp = sp * HP_SUB + hl
                    c0 = off + (hg * HPC + 2 * hp) * 64
                    blocks.append((c0, 128))
            blocks += [(cfg["OFF_W"], LW), (cfg["OFF_A"], LA), (cfg["OFF_G"], LG)]
            for jb, (c0, n) in enumerate(blocks):
                wrw[sp, :, jb * 128:jb * 128 + n] = w_in[:, c0:c0 + n]
                mu[0:n, sp * NB + jb] = mu_shift[c0:c0 + n]
        m["wrw"] = wrw
        m["mu"] = mu
        wat = np.zeros((D, NAB * 128), np.float32)
        for i in range(QPC // 2):
            c0 = cfg["OFF_Q"] + (hg * QPC + 2 * i) * 64
            wat[:, i * 128:(i + 1) * 128] = w_in[:, c0:c0 + 128]
        kc0 = cfg["OFF_KB"] + hg * 64
        vc0 = cfg["OFF_VB"] + hg * 64
        jb = QPC // 2
        wat[:, jb * 128:jb * 128 + 64] = w_in[:, kc0:kc0 + 64]
        wat[:, jb * 128 + 64:jb * 128 + 128] = w_in[:, kc0:kc0 + 64]
        wat[:, (jb + 1) * 128:(jb + 1) * 128 + 64] = w_in[:, vc0:vc0 + 64]
        wat[:, (jb + 1) * 128 + 64:(jb + 2) * 128] = w_in[:, vc0:vc0 + 64]
        m["wat"] = wat
        chp = np.zeros((128, 7 * NHP), np.float32)
        vecs = [f(inp["w0"])[0], f(inp["a0"])[0], f(inp["k_k"])[0], f(inp["k_a"])[0], f(inp["r_k"])[0].reshape(-1),
                f(inp["lnx_g"])[0], f(inp["lnx_b"])[0]]
        wdec = np.zeros((128, NHP * 128), np.float32)
        wicl = np.zeros((128, NHP * 128), np.float32)
        wgat = np.zeros((128, NHP * 128), np.float32)
        for hp in range(NHP):
            a0_ = (hg * HPC + 2 * hp) * 64
            for i, v in enumerate(vecs):
                chp[:, i * NHP + hp] = v[a0_:a0_ + 128]
            wdec[0:LW, hp * 128:(hp + 1) * 128] = f(inp["w_decay_up"])[0][:, a0_:a0_ + 128]
            wicl[0:LA, hp * 128:(hp + 1) * 128] = f(inp["w_iclr_up"])[0][:, a0_:a0_ + 128]
            wgat[0:LG, hp * 128:(hp + 1) * 128] = f(inp["w_gate_up"])[0][:, a0_:a0_ + 128]
        m["chp"], m["wdec"], m["wicl"], m["wgat"] = chp, wdec, wicl, wgat
        bias = np.zeros((128, QPC * 256), np.float32)
        sinks = np.zeros((128, QPC), np.float32)
        for ql in range(QPC):
            hq = hg * QPC + ql
            bias[:, ql * 256:(ql + 1) * 256] = rpb[bucket, hq]
            sinks[:, ql] = f(inp["attn_sinks"])[0][hq]
        m["biasat"] = bias
        m["sinks"] = sinks
        m["cst"] = cst
        m["wout"] = f(inp["w_out"])[0]
        m["wup"] = f(inp["w_up"])[0]
        m["wdn"] = f(inp["w_down"])[0]
        maps.append(m)
    return maps


_NC_CACHE = {}


def kernel(**inputs):
    cfg = make_cfg()
    if "nc" not in _NC_CACHE:
        _NC_CACHE["nc"] = build(cfg)
    nc = _NC_CACHE["nc"]
    maps = host_inputs(cfg, inputs)
    res = run_bass_kernel_spmd(nc, maps, core_ids=list(range(8)))
    NT = cfg["NT"]
    out = np.zeros((2, cfg["S"], cfg["D"]), np.float32)
    for core in range(8):
        b, q = core // 4, core % 4
        out[b, q * NT:(q + 1) * NT] = res.results[core]["out"]
    return out
```

```python
import math
import os
CUT = int(os.environ.get('KCUT', '99'))
from contextlib import ExitStack
import numpy as np
import concourse.bass as bass
import concourse.mybir as mybir
from concourse.bass_utils import run_bass_kernel_spmd

F32 = mybir.dt.float32
BF16 = mybir.dt.bfloat16
ALU = mybir.AluOpType
AF = mybir.ActivationFunctionType
AX = mybir.AxisListType

LN_EPS = 1e-5
LNX_EPS = 64e-5
ALPHA = 2.0 ** 0.25
C0 = math.exp(-0.5)


def make_cfg(D=4096, S=4096, GQA=8):
    c = dict(D=D, S=S, GQA=GQA)
    c["D_A"] = D // 2
    c["H_A"] = c["D_A"] // 64
    c["H_Q"] = c["H_A"]
    c["H_KV"] = c["H_Q"] // GQA
    assert c["H_KV"] == 4
    c["HPC"] = c["H_A"] // 4
    c["NHP"] = c["HPC"] // 2
    c["QPC"] = c["H_Q"] // 4
    c["LW"] = max(32, int(round(c["D_A"] ** 0.5 * 1.8 / 32)) * 32)
    c["LA"] = c["LW"]
    c["LG"] = max(32, int(round(c["D_A"] ** 0.6 * 0.8 / 32)) * 32)
    c["DFF"] = 4 * D
    c["NDC"] = D // 128
    c["HP_SUB"] = min(2, c["NHP"])
    c["NSUB"] = c["NHP"] // c["HP_SUB"]
    c["NB"] = 3 * c["HP_SUB"] + 3
    c["NAB"] = c["QPC"] // 2 + 2
    c["YR"] = D // 4
    c["NT"] = S // 4
    c["TC"] = min(512, S)
    c["OFF_K"] = c["D_A"]
    c["OFF_V"] = 2 * c["D_A"]
    c["OFF_W"] = 3 * c["D_A"]
    c["OFF_A"] = c["OFF_W"] + c["LW"]
    c["OFF_G"] = c["OFF_A"] + c["LA"]
    c["RWKV_COLS"] = c["OFF_G"] + c["LG"]
    c["OFF_Q"] = c["RWKV_COLS"]
    c["OFF_KB"] = c["OFF_Q"] + c["D_A"]
    c["OFF_VB"] = c["OFF_KB"] + 4 * 64
    c["N_IN"] = c["OFF_VB"] + 4 * 64
    return c


class Vw:
    def __init__(s, ap, k):
        s.ap = ap
        s.k = k


class Buf:
    def __init__(s, t, k):
        s.t = t
        s.k = k

    def __getitem__(s, idx):
        return Vw(s.t[idx], s.k)

    def sk(s, suf):
        return Buf(s.t, s.k + "/" + str(suf))


class Phase:
    ENG = ("tensor", "vector", "scalar", "gpsimd", "sync")

    def __init__(s, nc, name):
        s.nc = nc
        s.name = name
        s.ops = {e: [] for e in s.ENG}
        s.cnt = {e: 0 for e in s.ENG}
        s.dcnt = {}
        s.lastw = {}
        s.rd = {}
        s.mem = ExitStack()
        s.nbuf = 0

    def sb(s, shape, dt, key=None):
        s.nbuf += 1
        nm = f"{s.name}_{key or 'b'}{s.nbuf}"
        return Buf(s.mem.enter_context(s.nc.sbuf_tensor(nm, list(shape), dt)), nm)

    def ps(s, shape, dt, key=None):
        s.nbuf += 1
        nm = f"{s.name}_{key or 'p'}{s.nbuf}"
        return Buf(s.mem.enter_context(s.nc.psum_tensor(nm, list(shape), dt)), nm)

    def _deps(s, reads, writes):
        d = {}

        def add(tok):
            if tok is not None and d.get(tok[0], 0) < tok[1]:
                d[tok[0]] = tok[1]
        for b in reads:
            add(s.lastw.get(b))
        for b in writes:
            add(s.lastw.get(b))
            for k, v in s.rd.get(b, {}).items():
                add((k, v))
        return d

    def _commit(s, tok, reads, writes):
        for b in writes:
            s.lastw[b] = tok
            s.rd[b] = {}
        for b in reads:
            r = s.rd.setdefault(b, {})
            if r.get(tok[0], 0) < tok[1]:
                r[tok[0]] = tok[1]

    def op(s, eng, fn, reads=(), writes=()):
        if getattr(s, "dead", False):
            return
        d = s._deps(reads, writes)
        s.cnt[eng] += 1
        tok = (("E", eng), s.cnt[eng])
        s._commit(tok, reads, writes)
        s.ops[eng].append((fn, d, tok, False))

    def dma(s, eng, key, fn, reads=(), writes=(), n=1):
        if getattr(s, "dead", False):
            return
        d = s._deps(reads, writes)
        c = s.dcnt.get(key, 0) + 16 * n
        s.dcnt[key] = c
        tok = (("D", key), c)
        s._commit(tok, reads, writes)
        s.ops[eng].append((fn, d, tok, True))

    def emit(s):
        nc = s.nc
        sems = {}
        handles = []
        for e in s.ENG:
            h = nc.alloc_semaphore(name=f"{s.name}_e_{e}")
            sems[("E", e)] = h
            handles.append(h)
        for i, k in enumerate(s.dcnt):
            h = nc.alloc_semaphore(name=f"{s.name}_d{i}")
            sems[("D", k)] = h
            handles.append(h)
        with nc.Block() as block:
            for e in s.ENG:
                if not s.ops[e] and e != "sync":
                    continue

                def body(eng, e=e):
                    waited = {}
                    for fn, d, tok, isdma in s.ops[e]:
                        for k, v in d.items():
                            if e == "tensor" and k == ("E", "tensor"):
                                continue
                            if waited.get(k, 0) < v:
                                eng.wait_ge(sems[k], v)
                                waited[k] = v
                        if isdma:
                            fn(eng, sems[tok[0]])
                        else:
                            fn(eng).then_inc(sems[tok[0]], 1)
                    if e == "sync":
                        for k, v in s.dcnt.items():
                            if waited.get(("D", k), 0) < v:
                                eng.wait_ge(sems[("D", k)], v)
                getattr(block, e)(body)
        nc.clear_and_free_semaphores(handles)
        nc.all_engine_barrier()
        s.mem.close()

    def mm(s, out, lhsT, rhs, st=True, sp=True):
        s.op("tensor", lambda e: e.matmul(out.ap, lhsT.ap, rhs.ap, start=st, stop=sp),
             [lhsT.k, rhs.k], [out.k])

    def tr(s, out, in_, ident):
        s.op("tensor", lambda e: e.transpose(out.ap, in_.ap, ident.ap), [in_.k, ident.k], [out.k])

    def act(s, out, in_, func, bias=None, scale=None, extra=()):
        rk = [in_.k] + list(extra)
        kw = {}
        if bias is not None:
            if isinstance(bias, Vw):
                kw["bias"] = bias.ap
                rk.append(bias.k)
            else:
                kw["bias"] = bias
        if scale is not None:
            if isinstance(scale, Vw):
                kw["scale"] = scale.ap
                rk.append(scale.k)
            else:
                kw["scale"] = scale
        s.op("scalar", lambda e: e.activation(out.ap, in_.ap, func, **kw), rk, [out.k])

    def tt(s, eng, out, in0, in1, op):
        s.op(eng, lambda e: e.tensor_tensor(out.ap, in0.ap, in1.ap, op), [in0.k, in1.k], [out.k])

    def ts(s, eng, out, in0, s1, op0, s2=None, op1=None):
        rk = [in0.k]
        a1 = s1
        a2 = s2
        if isinstance(s1, Vw):
            a1 = s1.ap
            rk.append(s1.k)
        if isinstance(s2, Vw):
            a2 = s2.ap
            rk.append(s2.k)
        if op1 is None:
            s.op(eng, lambda e: e.tensor_scalar(out.ap, in0.ap, a1, None, op0), rk, [out.k])
        else:
            s.op(eng, lambda e: e.tensor_scalar(out.ap, in0.ap, a1, a2, op0, op1), rk, [out.k])

    def rsqrt(s, out, in_, addc):
        s.op("scalar", lambda e: e.activation(out.ap, in_.ap, AF.Sqrt, bias=addc), [in_.k], [out.k])
        s.op("vector", lambda e: e.reciprocal(out.ap, out.ap), [out.k], [out.k])

    def stt(s, out, in0, sc, in1, op0, op1):
        rk = [in0.k, in1.k]
        a = sc
        if isinstance(sc, Vw):
            a = sc.ap
            rk.append(sc.k)
        s.op("vector", lambda e: e.scalar_tensor_tensor(out.ap, in0.ap, a, in1.ap, op0, op1), rk, [out.k])

    def cp(s, eng, out, in_):
        if eng == "scalar":
            s.op("scalar", lambda e: e.activation(out.ap, in_.ap, AF.Copy), [in_.k], [out.k])
        else:
            s.op(eng, lambda e: e.tensor_copy(out.ap, in_.ap), [in_.k], [out.k])

    def ms(s, eng, out, val):
        s.op(eng, lambda e: e.memset(out.ap, val), [], [out.k])

    def ld(s, eng, key, out, in_ap, reads=(), **kw):
        s.dma(eng, key, lambda e, sem: e.dma_start(out=out.ap, in_=in_ap, **kw).then_inc(sem, 16),
              list(reads), [out.k])

    def ldm(s, eng, key, out_key, pairs, reads=()):
        def fn(e, sem):
            for o, i in pairs:
                e.dma_start(out=o, in_=i).then_inc(sem, 16)
        s.dma(eng, key, fn, list(reads), [out_key], n=len(pairs))

    def stq(s, eng, key, out_ap, in_, writes=(), **kw):
        s.dma(eng, key, lambda e, sem: e.dma_start(out=out_ap, in_=in_.ap, **kw).then_inc(sem, 16),
              [in_.k], list(writes))


def load_cols(ph, rows_ap, nrows, identf, pst, key):
    out = ph.sb([128, nrows], F32, key)
    r0 = 0
    i = 0
    while r0 < nrows:
        n = min(128, nrows - r0)
        tmp = ph.sb([128, 128], F32, key + "t")
        ph.ld("sync", f"{key}ld{i}", tmp[0:n, :], rows_ap[r0:r0 + n, :])
        ph.tr(pst[:, 0:n], tmp[0:n, :], identf[0:n, 0:n])
        ph.cp("vector", out[:, r0:r0 + n], pst[:, 0:n])
        r0 += n
        i += 1
    return out


def ln_stats(ph, xt, D, stats, mv, rstd, nmr, eps):
    nch = max(1, D // 512)
    w = D // nch
    for i in range(nch):
        ph.op("vector", (lambda o, a: (lambda e: e.bn_stats(o.ap, a.ap)))(stats[:, i * 6:(i + 1) * 6], xt[:, i * w:(i + 1) * w]),
              [xt.k], [stats.k])
    ph.op("vector", (lambda o, a: (lambda e: e.bn_aggr(o.ap, a.ap)))(mv[:, :], stats[:, :]), [stats.k], [mv.k])
    rv = rstd if isinstance(rstd, Vw) else rstd[:, :]
    nv = nmr if isinstance(nmr, Vw) else nmr[:, :]
    ph.rsqrt(rv, mv[:, 1:2], eps)
    ph.stt(nv, mv[:, 0:1], -1.0, rv, ALU.mult, ALU.mult)


def build(cfg, debug=False, upto=99):
    D, S = cfg["D"], cfg["S"]
    NDC, NB, NAB, NHP, HP_SUB, NSUB = cfg["NDC"], cfg["NB"], cfg["NAB"], cfg["NHP"], cfg["HP_SUB"], cfg["NSUB"]
    QPC, LW, LA, LG, DFF, YR, NT, TC = cfg["QPC"], cfg["LW"], cfg["LA"], cfg["LG"], cfg["DFF"], cfg["YR"], cfg["NT"], cfg["TC"]
    NCH = TC // 128
    NTC = S // TC
    NQB = QPC // 2
    nc = bass.Bass("TRN2", target_bir_lowering=False)

    def din(name, shape, dt=F32):
        return nc.dram_tensor(name, list(shape), dt, kind="ExternalInput").ap()

    x_in = din("x", [S, D])
    xq_in = din("xq", [NT, D])
    ccol_in = din("ccol", [128, NDC])
    wmod_in = din("wmod", [D, 6 * D // 4])
    bmod_in = din("bmod", [1, 6 * D // 4])
    lnv_in = din("lnv", [6 * NDC, 128])
    lnr_in = din("lnr", [6, D])
    wrw_in = din("wrw", [NSUB, D, NB * 128])
    wat_in = din("wat", [D, NAB * 128])
    mu_in = din("mu", [128, NSUB * NB])
    chp_in = din("chp", [128, 7 * NHP])
    wdec_in = din("wdec", [128, NHP * 128])
    wicl_in = din("wicl", [128, NHP * 128])
    wgat_in = din("wgat", [128, NHP * 128])
    bias_in = din("biasat", [128, QPC * 256])
    sink_in = din("sinks", [128, QPC])
    cst_in = din("cst", [128, 8 * 128])
    wout_in = din("wout", [D, D])
    wup_in = din("wup", [D, DFF])
    wdn_in = din("wdn", [DFF, D])
    out_t = nc.dram_tensor("out", [NT, D], F32, kind="ExternalOutput").ap()

    modraw_t = nc.dram_tensor("modraw", [6 * NDC, 128], F32)
    modraw = modraw_t.ap()
    modpart_t = nc.dram_tensor("modpart", [6 * NDC // 4, 128], F32)
    modpart = modpart_t.ap()
    uTd = nc.dram_tensor("uTd", [NDC, 128, S], BF16).ap()
    NRB = YR // 128
    ybuf_t = [nc.dram_tensor(f"ybuf{j}", [128, S], BF16) for j in range(NRB)]
    yall_t = [nc.dram_tensor(f"yall{j}", [4 * 128, S], BF16) for j in range(NRB)]
    ydbg = nc.dram_tensor("ydbg", [YR, S], BF16, kind="ExternalOutput").ap() if debug else None
    zbuf = nc.dram_tensor("zbuf", [NT, D], F32, **(dict(kind="ExternalOutput") if debug else dict())).ap()
    x1buf = nc.dram_tensor("x1buf", [NT, D], F32, **(dict(kind="ExternalOutput") if debug else dict())).ap()
    hbuf = nc.dram_tensor("hbuf", [DFF // 128, 128, NT], BF16, **(dict(kind="ExternalOutput") if debug else dict())).ap()
    z2buf = nc.dram_tensor("z2buf", [NT, D], F32, **(dict(kind="ExternalOutput") if debug else dict())).ap()
    if debug:
        d7h = nc.dram_tensor("d7h", [128, (1024 // 128), NT], BF16, kind="ExternalOutput").ap()
        d7w = nc.dram_tensor("d7w", [128, (1024 // 128), D // 2], BF16, kind="ExternalOutput").ap()
        d7p = nc.dram_tensor("d7p", [128, min(512, D // 2)], F32, kind="ExternalOutput").ap()
        d7acc = nc.dram_tensor("d7acc", [128, D // 2], F32, kind="ExternalOutput").ap()
    u2Td = nc.dram_tensor("u2Td", [NDC, 128, NT], BF16, **(dict(kind="ExternalOutput") if debug else dict())).ap()

    def dbgcopy():
        if debug and not _dbgdone:
            _dbgdone.append(1)
            dbg_sem0 = nc.alloc_semaphore(name="dbg_sem0")
            with nc.Block() as block:
                @block.gpsimd
                def _(g):
                    for j in range(NRB):
                        g.dma_start(out=ydbg[j * 128:(j + 1) * 128, :], in_=ybuf_t[j].ap()).then_inc(dbg_sem0, 16)
                    g.wait_ge(dbg_sem0, 16 * NRB)
            nc.clear_and_free_semaphores([dbg_sem0])
            nc.all_engine_barrier()
    _dbgdone = []

    def consts(ph):
        c = ph.sb([128, 8 * 128], F32, "cst")
        ph.ld("sync", "cst", c[:, :], cst_in[:, :])
        names = ["ident", "mus", "mui", "mls", "bo", "ones", "amp", "amc"]
        d = {n: c.t[:, i * 128:(i + 1) * 128] for i, n in enumerate(names)}
        return c, d

    def cv(c, d, n, sl=None):
        ap = d[n]
        if sl is not None:
            ap = ap[sl]
        return Vw(ap, c.k)

    ph = Phase(nc, "p0")
    ccol = ph.sb([128, NDC], F32)
    cond = ph.sb([128, NDC], BF16)
    ph.ld("sync", "ccol", ccol[:, :], ccol_in[:, :])
    ph.act(cond[:, :], ccol[:, :], AF.Silu)
    MQ = 6 * D // 4
    GW = 2048 if MQ % 2048 == 0 else 1536
    assert MQ % GW == 0 and GW % 512 == 0
    NG = MQ // GW
    NQ_ = GW // 512
    wms = [ph.sb([128, GW], BF16, "wm") for _ in range(4)]
    psb = [[ph.ps([128, 512], F32, "pm") for _ in range(NQ_)] for _ in range(2)]
    bms = [ph.sb([1, GW], F32, "bm") for _ in range(2)]
    mds = [ph.sb([1, GW], F32, "md") for _ in range(2)]
    it = 0
    for g in range(NG):
        pb = psb[g % 2]
        ph.ld("sync", f"bm{g % 2}", bms[g % 2][:, :], bmod_in[0:1, g * GW:(g + 1) * GW])
        for k in range(NDC):
            wm = wms[it % 4]
            ph.ld("gpsimd", f"wm{it % 4}", wm[:, :], wmod_in[k * 128:(k + 1) * 128, g * GW:(g + 1) * GW])
            it += 1
            for q in range(NQ_):
                ph.mm(pb[q][0:1, :], cond[:, k:k + 1], wm[:, q * 512:(q + 1) * 512], st=(k == 0), sp=(k == NDC - 1))
        for q in range(NQ_):
            ph.tt("vector", mds[g % 2][0:1, q * 512:(q + 1) * 512], pb[q][0:1, :],
                  bms[g % 2][0:1, q * 512:(q + 1) * 512], ALU.add)
        ph.stq("sync", f"mdst{g % 2}",
               modpart[g * (GW // 128):(g + 1) * (GW // 128), :].rearrange("(o a) b -> o (a b)", o=1),
               mds[g % 2][0:1, :], writes=["modpart"])
    ph.emit()
    cc0 = nc.alloc_semaphore(name="cc0_sem")
    with nc.Block() as block:
        @block.gpsimd
        def _(g):
            g.collective_compute("AllGather", ALU.bypass, replica_groups=[[0, 1, 2, 3], [4, 5, 6, 7]],
                                 ins=[modpart_t.ap().opt()], outs=[modraw_t.ap().opt()]).then_inc(cc0)
            g.wait_ge(cc0, 1)
    nc.clear_and_free_semaphores([cc0])
    nc.all_engine_barrier()

    if upto < 1:
        return nc
    ph = Phase(nc, "p1")
    cst, cd = consts(ph)
    identb = ph.sb([128, 128], BF16, "idb")
    ph.cp("vector", identb[:, :], cv(cst, cd, "ident"))
    pstf = ph.ps([128, 128], F32, "pstf")
    identF = ph.sb([128, 128], F32, "idf")
    ph.cp("vector", identF[:, :], cv(cst, cd, "ident"))
    modc = load_cols(ph, modraw, 6 * NDC, identF, pstf, "modc")
    lnc = load_cols(ph, lnv_in, 6 * NDC, identF, pstf, "lnc")
    A1 = ph.sb([128, NDC], F32, "A1")
    B1 = ph.sb([128, NDC], F32, "B1")
    t1 = ph.sb([128, NDC], F32, "t1")
    ph.ts("vector", t1[:, :], modc[:, NDC:2 * NDC], 1.0, ALU.add)
    ph.tt("vector", A1[:, :], t1[:, :], lnc[:, 0:NDC], ALU.mult)
    ph.tt("vector", B1[:, :], t1[:, :], lnc[:, NDC:2 * NDC], ALU.mult)
    ph.tt("vector", B1[:, :], B1[:, :], modc[:, 0:NDC], ALU.add)
    xts = [ph.sb([128, D], F32, "xt") for _ in range(2)]
    xhs = [ph.sb([128, D], BF16, "xh") for _ in range(2)]
    uTt = [ph.sb([128, NDC, TC], BF16, "uTt") for _ in range(2)]
    nst = max(1, D // 512)
    stats = ph.sb([128, nst * 6], F32, "stats")
    mv = ph.sb([128, 2], F32, "mv")
    rstd = ph.sb([128, 1], F32, "rstd")
    nmr = ph.sb([128, 1], F32, "nmr")
    pstb = [ph.ps([128, 128], BF16, "pstb") for _ in range(4)]
    lnt = [(ph.sb([128, nst * 6], F32, "stats"), ph.sb([128, 2], F32, "mv"), ph.sb([128, 1], F32, "rstd"),
            ph.sb([128, 1], F32, "nmr")) for _ in range(2)]
    ti = 0
    for tcI in range(NTC):
        ub = uTt[tcI % 2]
        for c in range(NCH):
            i = tcI * NCH + c
            xt = xts[i % 2]
            xh = xhs[i % 2]
            ph.ld("sync", f"xt{i % 2}", xt[:, :], x_in[i * 128:(i + 1) * 128, :])
            stats, mv, rstd, nmr = lnt[i % 2]
            ln_stats(ph, xt, D, stats, mv, rstd, nmr, LN_EPS)
            ph.act(xh[:, :], xt[:, :], AF.Identity, bias=nmr[:, :], scale=rstd[:, :])
            for k in range(NDC):
                pt = pstb[ti % 4]
                ti += 1
                ph.tr(pt[:, :], xh[:, k * 128:(k + 1) * 128], identb[:, :])
                if k % 2 == 0:
                    ph.ts("vector", ub[:, k, c * 128:(c + 1) * 128], pt[:, :], A1[:, k:k + 1], ALU.mult,
                          B1[:, k:k + 1], ALU.add)
                else:
                    ph.act(ub[:, k, c * 128:(c + 1) * 128], pt[:, :], AF.Identity, bias=B1[:, k:k + 1],
                           scale=A1[:, k:k + 1])
        ph.stq("gpsimd", f"uTst{tcI % 2}", uTd[:, :, tcI * TC:(tcI + 1) * TC].rearrange("k p t -> p k t"),
               ub[:, :, :])
    ph.emit()

    if upto < 2:
        dbgcopy()
        return nc
    TC_save = TC
    TC = min(256, S)
    NCH = TC // 128
    NTC = S // TC
    for sp in range(NSUB):
        ph = Phase(nc, f"p2{sp}")
        cst, cd = consts(ph)
        IDN = cv(cst, cd, "ident")
        MUS = cv(cst, cd, "mus")
        MUI = cv(cst, cd, "mui")
        MLS = cv(cst, cd, "mls")
        BO = cv(cst, cd, "bo")
        ONES = cv(cst, cd, "ones")
        W = ph.sb([128, NDC, NB * 128], BF16, "W")
        ph.ldm("gpsimd", "Wld", W.k, [(W.t[:, k, :], wrw_in[sp, k * 128:(k + 1) * 128, :]) for k in range(NDC)])
        mu = ph.sb([128, NB], F32, "mu")
        omu = ph.sb([128, NB], F32, "omu")
        ph.ld("sync", "mu", mu[:, :], mu_in[:, sp * NB:(sp + 1) * NB])
        ph.ts("vector", omu[:, :], mu[:, :], -1.0, ALU.mult, 1.0, ALU.add)
        chp = ph.sb([128, 7 * NHP], F32, "chp")
        ph.ld("sync", "chp", chp[:, :], chp_in[:, :])
        omka = ph.sb([128, NHP], F32, "omka")
        ph.ts("vector", omka[:, :], chp[:, 3 * NHP:4 * NHP], -1.0, ALU.mult, 1.0, ALU.add)
        g8 = ph.sb([128, NHP], F32, "g8")
        ph.ts("vector", g8[:, :], chp[:, 5 * NHP:6 * NHP], 8.0, ALU.mult)
        wdec = ph.sb([128, NHP * 128], F32, "wdec")
        wicl = ph.sb([128, NHP * 128], F32, "wicl")
        wgat = ph.sb([128, NHP * 128], F32, "wgat")
        ph.ld("sync", "wdec", wdec[:, :], wdec_in[:, :])
        ph.ld("sync", "wicl", wicl[:, :], wicl_in[:, :])
        ph.ld("sync", "wgat", wgat[:, :], wgat_in[:, :])
        uTs = [ph.sb([128, NDC, TC], BF16, "uT") for _ in range(2)]
        praw = [ph.sb([128, TC + 1], F32, "praw") for _ in range(2)]
        carry = ph.sb([128, NB], F32, "carry")
        ph.ms("vector", carry[:, :], 0.0)
        tmpa = [ph.sb([128, TC], F32, "tmpa") for _ in range(2)]
        sh = [ph.sb([128, TC], F32, f"sh{j}") for j in range(NB)]
        psw = [ph.ps([128, 512], F32, "psw") for _ in range(3)]
        psq_b = [ph.ps([128, 512], F32, "psq") for _ in range(5)]
        NPQ = int(os.environ.get('NPQ', '5'))
        psq = [psq_b[i % 5].sk(i // 5)[:, (i // 5) * 128:(i // 5 + 1) * 128] for i in range(NPQ)]
        qn = [0]
        wn = [0]

        def nq():
            qn[0] += 1
            return psq[qn[0] % NPQ]

        def nw():
            wn[0] += 1
            return psw[wn[0] % 3]

        def F(key):
            return ph.sb([128, TC], F32, key)

        sg, et, gt, kk, kkn, kp, bv, cs, Wt, Wi, Wp, rt, kt, bt, at_, bon, tq1, yraw, yc = [
            F(n) for n in ["sg", "et", "gt", "kk", "kkn", "kp", "bv", "cs", "Wt", "Wi", "Wp", "rt", "kt", "bt",
                           "at", "bon", "tq1", "yraw", "yc"]]
        yo = [ph.sb([128, TC], BF16, "yo") for _ in range(2)]
        btm = [F("btm"), F("btm")]
        ktm = [F("ktm"), F("ktm")]
        atm = [F("atm"), F("atm")]
        ST = [ph.sb([128, 128], F32, f"ST{h}") for h in range(HP_SUB)]
        for h in range(HP_SUB):
            ph.ms("vector", ST[h][:, :], 0.0)

        def M(key):
            return ph.sb([128, 128], F32, key)

        mats = {}
        for c2 in range(2):
            for h in range(2):
                mats[(c2, h)] = dict(P=[M("P"), M("P")], PT=[M("PT"), M("PT")], T=M("T"), AKA=M("AKA"),
                                     ABR=M("ABR"), AKR=M("AKR"))
        pairm = []
        for c2 in range(2):
            d = dict(VTp=M("VTp"), VTe=M("VTe"), VTo=M("VTo"), BTT=M("BTT"), KTT=M("KTT"), RHS=M("RHS"),
                     SAp=M("SAp"), SAe=M("SAe"), SAo=M("SAo"))
            for n in ["VTe", "VTo", "SAe", "SAo"]:
                ph.ms("gpsimd", d[n][:, :], 0.0)
            pairm.append(d)
        NLEV = 6

        blk_lora = [NB - 3, NB - 2, NB - 1]

        def hp_blocks(hl_):
            return [hl_, HP_SUB + hl_, 2 * HP_SUB + hl_]
        xw, xa, xg = sh[NB - 3], sh[NB - 2], sh[NB - 1]

        def load_uT(t):
            ph.ld("sync", f"uT{t % 2}", uTs[t % 2][:, :, :],
                  uTd[:, :, t * TC:(t + 1) * TC].rearrange("k p t -> p k t"))

        def inproj(t, blocks):
            uT = uTs[t % 2]
            for j in blocks:
                bank = nw()
                for k in range(NDC):
                    ph.mm(bank[:, 0:TC], W[:, k, j * 128:(j + 1) * 128], uT[:, k, :], st=(k == 0), sp=(k == NDC - 1))
                pr = praw[j % 2]
                ph.cp("gpsimd", pr[:, 0:1], carry[:, j:j + 1])
                ph.cp("scalar", pr[:, 1:TC + 1], bank[:, 0:TC])
                ph.cp("gpsimd", carry[:, j:j + 1], pr[:, TC:TC + 1])
                tm = tmpa[j % 2]
                ph.ts("vector", tm[:, :], pr[:, 0:TC], mu[:, j:j + 1], ALU.mult)
                ph.stt(sh[j][:, :], pr[:, 1:TC + 1], omu[:, j:j + 1], tm[:, :], ALU.mult, ALU.add)
                if j == NB - 3:
                    ph.act(xw[:, :], xw[:, :], AF.Tanh)
                if j == NB - 1:
                    ph.act(xg[:, :], xg[:, :], AF.Sigmoid)

        load_uT(0)
        inproj(0, blk_lora + hp_blocks(0))
        for tcI in range(NTC):
            if tcI + 1 < NTC:
                load_uT(tcI + 1)
            for hl in range(HP_SUB):
                hp = sp * HP_SUB + hl
                r_, k_, v_ = sh[hl], sh[HP_SUB + hl], sh[2 * HP_SUB + hl]

                def P_(i):
                    return chp[:, i * NHP + hp:i * NHP + hp + 1]
                w0c, a0c, kkc, kac, rkc, lgc, lbc = [P_(i) for i in range(7)]
                hs = slice(hp * 128, (hp + 1) * 128)
                b = nw()
                ph.mm(b[:, 0:TC], wdec[0:LW, hs], xw[0:LW, :])
                ph.act(sg[:, :], b[:, 0:TC], AF.Sigmoid, bias=w0c)
                b = nw()
                ph.mm(b[:, 0:TC], wicl[0:LA, hs], xa[0:LA, :])
                ph.act(et[:, :], b[:, 0:TC], AF.Sigmoid, bias=a0c)
                b = nw()
                ph.mm(b[:, 0:TC], wgat[0:LG, hs], xg[0:LG, :])
                ph.cp("vector", gt[:, :], b[:, 0:TC])
                if hl + 1 < HP_SUB:
                    inproj(tcI, hp_blocks(hl + 1))
                elif tcI + 1 < NTC and HP_SUB > 1:
                    inproj(tcI + 1, blk_lora + hp_blocks(0))
                ph.act(kk[:, :], k_[:, :], AF.Copy, scale=kkc)
                ph.tt("gpsimd", tq1[:, :], kk[:, :], kk[:, :], ALU.mult)
                b = nw()
                ph.mm(b[:, 0:TC], BO, tq1[:, :])
                ph.rsqrt(tq1[:, :], b[:, 0:TC], 1e-24)
                ph.tt("gpsimd", kkn[:, :], kk[:, :], tq1[:, :], ALU.mult)
                ph.ts("vector", tq1[:, :], et[:, :], kac, ALU.mult, omka[:, hp:hp + 1], ALU.add)
                ph.tt("gpsimd", kp[:, :], k_[:, :], tq1[:, :], ALU.mult)
                ph.tt("gpsimd", bv[:, :], kkn[:, :], et[:, :], ALU.mult)
                for c in range(NCH):
                    sl = slice(c * 128, (c + 1) * 128)
                    ph.op("vector", (lambda o, d0, d1: (lambda e: e.tensor_tensor_scan(o.ap, d0.ap, d1.ap, 0.0, ALU.mult, ALU.add)))(
                        cs[:, sl], ONES, sg[:, sl]), [ONES.k, sg.k], [cs.k])
                ph.act(Wt[:, :], cs[:, :], AF.Exp, scale=-C0)
                ph.act(Wi[:, :], cs[:, :], AF.Exp, scale=C0)
                ph.tt("gpsimd", tq1[:, :], cs[:, :], sg[:, :], ALU.subtract)
                ph.act(Wp[:, :], tq1[:, :], AF.Exp, scale=-C0)
                ph.tt("gpsimd", rt[:, :], r_[:, :], Wt[:, :], ALU.mult)
                ph.tt("gpsimd", kt[:, :], kp[:, :], Wi[:, :], ALU.mult)
                ph.tt("gpsimd", bt[:, :], bv[:, :], Wi[:, :], ALU.mult)
                ph.stt(at_[:, :], kkn[:, :], -1.0, Wp[:, :], ALU.mult, ALU.mult)
                for h in range(2):
                    mcol = Vw(cd["bo"][:, 127 * h:127 * h + 1], cst.k)
                    ph.act(btm[h][:, :], bt[:, :], AF.Copy, scale=mcol)
                    ph.ts("vector", ktm[h][:, :], kt[:, :], mcol, ALU.mult)
                    ph.act(atm[h][:, :], at_[:, :], AF.Copy, scale=mcol)
                ph.tt("gpsimd", tq1[:, :], r_[:, :], kp[:, :], ALU.mult)
                ph.act(tq1[:, :], tq1[:, :], AF.Copy, scale=rkc)
                b = nw()
                ph.mm(b[:, 0:TC], BO, tq1[:, :])
                ph.tt("vector", bon[:, :], b[:, 0:TC], v_[:, :], ALU.mult)
                if CUT == 2:
                    ph.dead = True
                for c in range(NCH):
                    sl = slice(c * 128, (c + 1) * 128)
                    pm = pairm[c % 2]
                    for h in range(2):
                        hb = slice(64 * h, 64 * h + 64)
                        m = mats[(c % 2, h)]
                        if CUT == 31:
                            ph.dead = True
                        q = nq()
                        ph.mm(q, btm[h][:, sl], at_[:, sl])
                        ph.tt("vector", m["P"][0][:, :], q, MUS, ALU.mult)
                        if CUT == 20:
                            ph.dead = True
                        if CUT == 32:
                            ph.dead = True
                        q = nq()
                        ph.mm(q, atm[h][:, sl], bt[:, sl])
                        ph.tt("vector", m["PT"][0][:, :], q, MLS, ALU.mult)
                        if CUT == 33:
                            ph.dead = True
                        q = nq()
                        ph.mm(q, ktm[h][:, sl], at_[:, sl])
                        ph.tt("vector", m["AKA"][:, :], q, MUS, ALU.mult)
                        if CUT == 34:
                            ph.dead = True
                        q = nq()
                        ph.mm(q, btm[h][:, sl], rt[:, sl])
                        ph.tt("vector", m["ABR"][:, :], q, MUI, ALU.mult)
                        if CUT == 35:
                            ph.dead = True
                        q = nq()
                        ph.mm(q, ktm[h][:, sl], rt[:, sl])
                        ph.tt("vector", m["AKR"][:, :], q, MUI, ALU.mult)
                        if CUT == 36:
                            ph.dead = True
                        ph.tt("gpsimd", m["T"][:, :], m["P"][0][:, :], IDN, ALU.add)
                    if CUT == 21:
                        ph.dead = True
                    q = nq()
                    ph.tr(q, v_[:, sl], IDN)
                    ph.cp("scalar", pm["VTp"][:, :], q)
                    ph.cp("gpsimd", pm["VTe"].sk("a")[:, 0:64], pm["VTp"][:, 0:64])
                    ph.cp("gpsimd", pm["VTo"].sk("a")[:, 64:128], pm["VTp"][:, 64:128])
                    q = nq()
                    ph.tr(q, bt[:, sl], IDN)
                    ph.cp("scalar", pm["BTT"][:, :], q)
                    q = nq()
                    ph.tr(q, kt[:, sl], IDN)
                    ph.cp("scalar", pm["KTT"][:, :], q)
                if CUT == 3:
                    ph.dead = True
                assert NCH <= 2
                for lev in range(1, NLEV + 1):
                    a, bnew = (lev - 1) % 2, lev % 2
                    for c in range(NCH):
                        for h in range(2):
                            m = mats[(c % 2, h)]
                            if lev < NLEV:
                                q = nq()
                                ph.mm(q, m["PT"][a][:, :], m["P"][a][:, :])
                                ph.cp("scalar", m["P"][bnew][:, :], q)
                            q = nq()
                            ph.mm(q, m["P"][a][:, :], m["PT"][a][:, :])
                            ph.cp("scalar" if h == 0 else "vector", m["PT"][bnew][:, :], q)
                    for c in range(NCH):
                        for h in range(2):
                            m = mats[(c % 2, h)]
                            q = nq()
                            ph.mm(q, m["PT"][bnew][:, :], m["T"][:, :])
                            ph.tt("vector", m["T"][:, :], m["T"][:, :], q, ALU.add)
                for c in range(NCH):
                    sl = slice(c * 128, (c + 1) * 128)
                    pm = pairm[c % 2]
                    if CUT == 4:
                        ph.dead = True
                    me, mo = mats[(c % 2, 0)], mats[(c % 2, 1)]
                    STh = ST[hl]
                    q = nq()
                    ph.mm(q, at_[:, sl], STh[:, :], st=True, sp=False)
                    ph.mm(q, me["AKA"][:, :], pm["VTe"].sk("a")[:, :], st=False, sp=False)
                    ph.mm(q, mo["AKA"][:, :], pm["VTo"].sk("a")[:, :], st=False, sp=True)
                    ph.cp("scalar", pm["RHS"][:, :], q)
                    q = nq()
                    qa = Vw(q.ap[:, 0:64], q.k)
                    qb = Vw(q.ap[:, 64:128], q.k)
                    ph.mm(qa, me["T"][:, :], pm["RHS"][:, 0:64])
                    ph.mm(qb, mo["T"][:, :], pm["RHS"][:, 64:128])
                    ph.cp("scalar", pm["SAp"][:, :], q)
                    ph.cp("gpsimd", pm["SAe"].sk("a")[:, 0:64], pm["SAp"][:, 0:64])
                    ph.cp("gpsimd", pm["SAo"].sk("a")[:, 64:128], pm["SAp"][:, 64:128])
                    q = nq()
                    ph.mm(q, STh[:, :], rt[:, sl], st=True, sp=False)
                    ph.mm(q, pm["SAe"].sk("a")[:, :], me["ABR"][:, :], st=False, sp=False)
                    ph.mm(q, pm["SAo"].sk("a")[:, :], mo["ABR"][:, :], st=False, sp=False)
                    ph.mm(q, pm["VTe"].sk("a")[:, :], me["AKR"][:, :], st=False, sp=False)
                    ph.mm(q, pm["VTo"].sk("a")[:, :], mo["AKR"][:, :], st=False, sp=True)
                    ph.cp("scalar", yraw[:, sl], q)
                    q = nq()
                    ph.mm(q, IDN, STh[:, :], st=True, sp=False)
                    ph.mm(q, pm["BTT"][:, :], pm["SAp"][:, :], st=False, sp=False)
                    ph.mm(q, pm["KTT"][:, :], pm["VTp"][:, :], st=False, sp=True)
                    ph.stt(STh[:, :], q, Wt[:, c * 128 + 127:c * 128 + 128], BO, ALU.mult, ALU.mult)
                if CUT == 5:
                    ph.dead = True
                b = nw()
                ph.mm(b[:, 0:TC], BO, yraw[:, :])
                ph.stt(yc[:, :], b[:, 0:TC], -1.0 / 64, yraw[:, :], ALU.mult, ALU.add)
                ph.tt("gpsimd", tq1[:, :], yc[:, :], yc[:, :], ALU.mult)
                b = nw()
                ph.mm(b[:, 0:TC], BO, tq1[:, :])
                ph.rsqrt(tq1[:, :], b[:, 0:TC], 64 * LNX_EPS)
                ph.tt("gpsimd", yc[:, :], yc[:, :], tq1[:, :], ALU.mult)
                ph.ts("vector", yc[:, :], yc[:, :], g8[:, hp:hp + 1], ALU.mult, lbc, ALU.add)
                ph.tt("gpsimd", yc[:, :], yc[:, :], bon[:, :], ALU.add)
                yob = yo[(tcI * HP_SUB + hl) % 2]
                ph.tt("vector", yob[:, :], yc[:, :], gt[:, :], ALU.mult)
                ph.stq("sync", f"yo{(tcI * HP_SUB + hl) % 2}", ybuf_t[hp].ap()[:, tcI * TC:(tcI + 1) * TC],
                       yob[:, :], writes=["ybuf"])
            if HP_SUB == 1 and tcI + 1 < NTC:
                inproj(tcI + 1, blk_lora + hp_blocks(0))
        ph.emit()

    if upto < 3:
        dbgcopy()
        return nc
    TC = TC_save
    NCH = TC // 128
    NTC = S // TC
    ph = Phase(nc, "p3")
    cst, cd = consts(ph)
    IDN = cv(cst, cd, "ident")
    identb = ph.sb([128, 128], BF16, "idb")
    ph.cp("vector", identb[:, :], IDN)
    W = ph.sb([128, NDC, NAB * 128], BF16, "W")
    ph.ldm("gpsimd", "Wld", W.k, [(W.t[:, k, :], wat_in[k * 128:(k + 1) * 128, :]) for k in range(NDC)])
    biasm = ph.sb([128, QPC * 256], F32, "biasm")
    ph.ld("sync", "bias", biasm[:, :], bias_in[:, :])
    for h in range(QPC):
        for hf in range(2):
            ph.tt("vector", biasm[:, h * 256 + hf * 128:h * 256 + hf * 128 + 128],
                  biasm[:, h * 256 + hf * 128:h * 256 + hf * 128 + 128],
                  cv(cst, cd, "amc") if hf == 1 else cv(cst, cd, "amp"), ALU.add)
    sinkc = ph.sb([128, QPC], F32, "sink")
    ph.ld("sync", "sink", sinkc[:, :], sink_in[:, :])
    uTs = [ph.sb([128, NDC, TC], BF16, "uT") for _ in range(2)]
    qTm = [[ph.sb([128, TC], BF16, f"qT{i}_{h}") for h in range(2)] for i in range(NQB)]
    kT = ph.sb([128, TC + 128], BF16, "kT")
    vT = ph.sb([128, TC], BF16, "vT")
    VTe = ph.sb([128, NCH + 1, 128], BF16, "VTe")
    VTo = ph.sb([128, NCH + 1, 128], BF16, "VTo")
    ph.ms("vector", VTe[:, :, :], 0.0)
    ph.ms("vector", VTo[:, :, :], 0.0)
    ph.ms("vector", kT[:, :], 0.0)
    psw = [ph.ps([128, 512], F32, "psw") for _ in range(1)]
    class _Sub:
        def __init__(s_, bank, c0, n, key):
            s_.b, s_.c0, s_.n, s_.k = bank, c0, n, bank.k + "/" + key

        def __getitem__(s_, idx):
            rows, cols = idx
            a = s_.c0 + (cols.start or 0)
            bb = s_.c0 + (cols.stop if cols.stop is not None else s_.n)
            return Vw(s_.b.t[rows, a:bb], s_.k)
    pss = [ph.ps([128, 512], F32, "pss") for _ in range(3)]
    pstb = [ph.ps([128, 1024], BF16, "pstb") for _ in range(2)]
    pso = [ph.ps([128, 512], F32, "pso") for _ in range(2)]
    NS = 3
    ssb = [ph.sb([128, 512], F32, "ssb") for _ in range(NS)]
    esb = [ph.sb([128, 512], F32, "esb") for _ in range(NS)]
    pb = [ph.sb([128, 512], BF16, "pb") for _ in range(NS)]
    pTs = [ph.sb([128, 512], BF16, "pTs") for _ in range(NS)]
    sm = [ph.sb([128, 16], F32, "sm") for _ in range(NS)]
    biasm0 = ph.sb([128, QPC * 256], F32, "biasm0")
    ph.cp("vector", biasm0[:, :], biasm[:, :])
    for h in range(QPC):
        ph.ms("vector", biasm0[:, h * 256:h * 256 + 128], -30000.0)
    yb = [ph.sb([128, TC], BF16, "yb") for _ in range(2 * NQB)]
    hn = 0
    wn = 0
    for tcI in range(NTC):
        uT = uTs[tcI % 2]
        ph.ld("sync", f"uT{tcI % 2}", uT[:, :, :],
              uTd[:, :, tcI * TC:(tcI + 1) * TC].rearrange("k p t -> p k t"), reads=["uTd"])
        if tcI > 0:
            ph.cp("gpsimd", kT[:, 0:128], kT[:, TC:TC + 128])
            ph.cp("gpsimd", VTe[:, 0, :], VTe[:, NCH, :])
            ph.cp("gpsimd", VTo[:, 0, :], VTo[:, NCH, :])
        for j in range(NAB):
            bank = psw[0]
            wn += 1
            for k in range(NDC):
                ph.mm(bank[:, 0:TC], W[:, k, j * 128:(j + 1) * 128], uT[:, k, :], st=(k == 0), sp=(k == NDC - 1))
            if j < NQB:
                for h2 in range(2):
                    ph.ts("vector", qTm[j][h2][:, :], bank[:, 0:TC], Vw(cd["bo"][:, 127 * h2:127 * h2 + 1], cst.k), ALU.mult)
            elif j == NQB:
                ph.cp("scalar", kT[:, 128:128 + TC], bank[:, 0:TC])
            else:
                ph.cp("scalar", vT[:, :], bank[:, 0:TC])
        for c in range(NCH):
            pt = pstb[c % 2]
            ph.tr(pt[:, 0:128], vT[:, c * 128:(c + 1) * 128], identb[:, :])
            ph.cp("vector", VTe[:, c + 1, 0:64], pt[:, 0:64])
            ph.cp("vector", VTo[:, c + 1, 64:128], pt[:, 64:128])
        for c in range(NCH):
            first = (tcI == 0 and c == 0)
            bm = biasm0 if first else biasm
            for qb in range(NQB):
                po = pso[hn % 2]
                ps_ = pss[hn % 3]
                ptb = pstb[hn % 2]
                s_, e_, p_, pT_, m_ = ssb[hn % NS], esb[hn % NS], pb[hn % NS], pTs[hn % NS], sm[hn % NS]
                hn += 1
                for h2 in range(2):
                    ph.mm(ps_[:, h2 * 256:(h2 + 1) * 256], qTm[qb][h2][:, c * 128:(c + 1) * 128], kT[:, c * 128:c * 128 + 256])
                ph.stt(s_[:, :], ps_[:, :], 0.125, bm[:, 2 * qb * 256:(2 * qb + 2) * 256], ALU.mult, ALU.add)
                s3 = Vw(s_.t[:, :].rearrange("p (h k) -> p h k", h=2), s_.k)
                e3 = Vw(e_.t[:, :].rearrange("p (h k) -> p h k", h=2), e_.k)
                ph.op("vector", (lambda o, a: (lambda e: e.reduce_max(o.ap, a.ap, AX.X)))(m_[:, 0:2], s3), [s_.k], [m_.k])
                ph.tt("vector", m_[:, 2:4], m_[:, 0:2], sinkc[:, 2 * qb:2 * qb + 2], ALU.max)
                ph.ts("vector", m_[:, 4:6], m_[:, 2:4], -1.0, ALU.mult)
                for h2 in range(2):
                    ph.act(e_[:, h2 * 256:(h2 + 1) * 256], s_[:, h2 * 256:(h2 + 1) * 256], AF.Exp, bias=m_[:, 4 + h2:5 + h2])
                ph.tt("vector", m_[:, 6:8], sinkc[:, 2 * qb:2 * qb + 2], m_[:, 4:6], ALU.add)
                ph.act(m_[:, 8:10], m_[:, 6:8], AF.Exp)
                ph.op("vector", (lambda o, a: (lambda e: e.reduce_sum(o.ap, a.ap, AX.X)))(m_[:, 10:12], e3), [e_.k], [m_.k])
                ph.tt("vector", m_[:, 12:14], m_[:, 10:12], m_[:, 8:10], ALU.add)
                ph.op("vector", (lambda o, a: (lambda e: e.reciprocal(o.ap, a.ap)))(m_[:, 14:16], m_[:, 12:14]), [m_.k], [m_.k])
                for h2 in range(2):
                    ph.act(p_[:, h2 * 256:(h2 + 1) * 256], e_[:, h2 * 256:(h2 + 1) * 256], AF.Copy, scale=m_[:, 14 + h2:15 + h2])
                for i4 in range(4):
                    ph.tr(ptb[:, i4 * 128:(i4 + 1) * 128], p_[:, i4 * 128:(i4 + 1) * 128], identb[:, :])
                ph.cp("vector" if hn % 2 else "scalar", pT_[:, :], ptb[:, 0:512])
                for h2 in range(2):
                    VT = VTe if h2 == 0 else VTo
                    ph.mm(po[:, 0:128], VT[:, c, :], pT_[:, (2 * h2) * 128:(2 * h2 + 1) * 128], st=(h2 == 0), sp=False)
                    ph.mm(po[:, 0:128], VT[:, c + 1, :], pT_[:, (2 * h2 + 1) * 128:(2 * h2 + 2) * 128], st=False, sp=(h2 == 1))
                ybt = yb[(tcI % 2) * NQB + qb]
                ph.cp("scalar", ybt[:, c * 128:(c + 1) * 128], po[:, 0:128])
        for qb in range(NQB):
            ybt = yb[(tcI % 2) * NQB + qb]
            ph.stq("sync", f"yb{(tcI % 2) * NQB + qb}",
                   ybuf_t[NHP + qb].ap()[:, tcI * TC:(tcI + 1) * TC], ybt[:, :],
                   writes=["ybuf"])
    ph.emit()

    dbgcopy()
    if upto < 4:
        return nc
    cc_sem = nc.alloc_semaphore(name="cc_sem")
    with nc.Block() as block:
        @block.gpsimd
        def _(g):
            for j in range(NRB):
                g.collective_compute("AllGather", ALU.bypass, replica_groups=[[0, 1, 2, 3], [4, 5, 6, 7]],
                                     ins=[ybuf_t[j].ap().opt()], outs=[yall_t[j].ap().opt()]).then_inc(cc_sem)
            g.wait_ge(cc_sem, NRB)
    nc.clear_and_free_semaphores([cc_sem])
    nc.all_engine_barrier()

    if upto < 5:
        return nc
    ph = Phase(nc, "p5")
    cst, cd = consts(ph)
    KC = 4 * YR // 128
    cT = ph.sb([128, KC, NT], BF16, "cT")
    rk_per = YR // 128

    def ldc(e, sem):
        pid = e.partition_id()
        tq = pid % 4
        for j in range(NRB):
            yv = yall_t[j].ap().rearrange("(r p) t -> p r t", p=128)
            e.dma_start(out=cT.t[:, j * 4:(j + 1) * 4, :], in_=yv[:, :, bass.ds(tq * NT, NT)]).then_inc(sem, 16)
    ph.dma("gpsimd", "cTld", ldc, [], [cT.k], n=NRB)
    g1row = modraw[2 * NDC:3 * NDC, :].rearrange("(o a) b -> o (a b)", o=1)
    NTT = NT // 128
    xt = ph.sb([128, D], F32, "xt")
    nst = max(1, D // 512)
    stats = ph.sb([128, nst * 6], F32, "stats")
    mv = ph.sb([128, 2], F32, "mv")
    rs_all = ph.sb([128, NTT], F32, "rs")
    nm_all = ph.sb([128, NTT], F32, "nm")
    for tt_ in range(NTT):
        ph.ld("sync", "xt", xt[:, :], xq_in[tt_ * 128:(tt_ + 1) * 128, :])
        ln_stats(ph, xt, D, stats, mv, rs_all.sk(tt_)[:, tt_:tt_ + 1], nm_all.sk(tt_)[:, tt_:tt_ + 1], LN_EPS)
    DB = min(512, D)
    NDB = D // DB
    Wo = [ph.sb([128, KC, DB], BF16, "Wo") for _ in range(2)]
    psm = [ph.ps([128, DB], F32, "psm") for _ in range(4)]
    xp = [ph.sb([128, DB], F32, "xp") for _ in range(2)]
    zp = [ph.sb([128, DB], F32, "zp") for _ in range(2)]
    mg = [ph.sb([128, DB], F32, "mg") for _ in range(2)]
    gEs = [ph.sb([128, DB], F32, "gE") for _ in range(2)]
    bEs = [ph.sb([128, DB], F32, "bE") for _ in range(2)]
    G1s = [ph.sb([128, DB], F32, "G1") for _ in range(2)]
    D_A = cfg["D_A"]
    HW_ = cfg["HPC"] * 64
    it = 0

    def p5_loads(db):
        Wb = Wo[db % 2]
        gE, bE, G1 = gEs[db % 2], bEs[db % 2], G1s[db % 2]
        dsl = slice(db * DB, (db + 1) * DB)
        ph.ld("sync", f"gE{db % 2}", gE[:, :], lnr_in[0, dsl].partition_broadcast(128))
        ph.ld("sync", f"bE{db % 2}", bE[:, :], lnr_in[1, dsl].partition_broadcast(128))
        ph.ld("sync", f"G1{db % 2}", G1[:, :], g1row[0, dsl].partition_broadcast(128))
        ph.ts("vector", G1[:, :], G1[:, :], 1.0, ALU.add)
        pairs = []
        for kc in range(KC):
            j = kc // 4
            r = kc % 4
            row = (r * HW_ + j * 128) if j * 128 < HW_ else (D_A + r * (cfg["QPC"] * 64) + (j * 128 - HW_))
            pairs.append((Wb.t[:, kc, :], wout_in[row:row + 128, dsl]))
        ph.ldm("gpsimd", f"Wo{db % 2}", Wb.k, pairs)

    p5_loads(0)
    for db in range(NDB):
        if db + 1 < NDB:
            p5_loads(db + 1)
        Wb = Wo[db % 2]
        gE, bE, G1 = gEs[db % 2], bEs[db % 2], G1s[db % 2]
        dsl = slice(db * DB, (db + 1) * DB)
        for tt_ in range(NTT):
            pm_ = psm[it % 4]
            x_ = xp[it % 2]
            z_ = zp[it % 2]
            g_ = mg[it % 2]
            for kc in range(KC):
                ph.mm(pm_[:, :], cT[:, kc, tt_ * 128:(tt_ + 1) * 128], Wb[:, kc, :], st=(kc == 0), sp=(kc == KC - 1))
            ph.ld("sync", f"xp{it % 2}", x_[:, :], xq_in[tt_ * 128:(tt_ + 1) * 128, dsl])
            ph.act(x_[:, :], x_[:, :], AF.Identity, bias=nm_all.sk(tt_)[:, tt_:tt_ + 1], scale=rs_all.sk(tt_)[:, tt_:tt_ + 1])
            ph.tt("vector", x_[:, :], x_[:, :], gE[:, :], ALU.mult)
            ph.tt("vector", x_[:, :], x_[:, :], bE[:, :], ALU.add)
            ph.tt("vector", g_[:, :], pm_[:, :], G1[:, :], ALU.mult)
            ph.stt(z_[:, :], x_[:, :], ALPHA, g_[:, :], ALU.mult, ALU.add)
            ph.stq("sync", f"zst{it % 2}", zbuf[tt_ * 128:(tt_ + 1) * 128, dsl], z_[:, :], writes=["zbuf"])
            it += 1
    ph.emit()

    if upto < 6:
        return nc
    ph = Phase(nc, "p6a")
    cst, cd = consts(ph)
    identb = ph.sb([128, 128], BF16, "idb")
    ph.cp("vector", identb[:, :], cv(cst, cd, "ident"))
    identF = ph.sb([128, 128], F32, "idf")
    ph.cp("vector", identF[:, :], cv(cst, cd, "ident"))
    pstf = ph.ps([128, 128], F32, "pstf")
    modc = load_cols(ph, modraw, 6 * NDC, identF, pstf, "modc")
    lnc = load_cols(ph, lnv_in, 6 * NDC, identF, pstf, "lnc")
    A2 = ph.sb([128, NDC], F32, "A2")
    B2 = ph.sb([128, NDC], F32, "B2")
    t1 = ph.sb([128, NDC], F32, "t1")
    ph.ts("vector", t1[:, :], modc[:, 4 * NDC:5 * NDC], 1.0, ALU.add)
    ph.tt("vector", A2[:, :], t1[:, :], lnc[:, 2 * NDC:3 * NDC], ALU.mult)
    ph.tt("vector", B2[:, :], t1[:, :], lnc[:, 3 * NDC:4 * NDC], ALU.mult)
    ph.tt("vector", B2[:, :], B2[:, :], modc[:, 3 * NDC:4 * NDC], ALU.add)
    g1b = ph.sb([128, D], F32, "g1b")
    b1b = ph.sb([128, D], F32, "b1b")
    ph.ld("sync", "g1b", g1b[:, :], lnr_in[2, :].partition_broadcast(128))
    ph.ld("sync", "b1b", b1b[:, :], lnr_in[3, :].partition_broadcast(128))
    u2s = [ph.sb([128, NDC, 128], BF16, "u2s") for _ in range(2)]
    zt = ph.sb([128, D], F32, "zt")
    zh = ph.sb([128, D], BF16, "zh")
    x1t = ph.sb([128, D], F32, "x1t")
    stats = ph.sb([128, nst * 6], F32, "stats")
    mv = ph.sb([128, 2], F32, "mv")
    rstd = ph.sb([128, 1], F32, "rstd")
    nmr = ph.sb([128, 1], F32, "nmr")
    pstb = [ph.ps([128, 128], BF16, "pstb") for _ in range(4)]
    ti = 0
    for tt_ in range(NTT):
        u2 = u2s[tt_ % 2]
        ph.ld("sync", "zt", zt[:, :], zbuf[tt_ * 128:(tt_ + 1) * 128, :])
        ln_stats(ph, zt, D, stats, mv, rstd, nmr, LN_EPS)
        ph.act(zh[:, :], zt[:, :], AF.Identity, bias=nmr[:, :], scale=rstd[:, :])
        ph.act(x1t[:, :], zt[:, :], AF.Identity, bias=nmr[:, :], scale=rstd[:, :])
        ph.tt("gpsimd", x1t[:, :], x1t[:, :], g1b[:, :], ALU.mult)
        ph.tt("gpsimd", x1t[:, :], x1t[:, :], b1b[:, :], ALU.add)
        ph.stq("sync", "x1st", x1buf[tt_ * 128:(tt_ + 1) * 128, :], x1t[:, :])
        for k in range(NDC):
            pt = pstb[ti % 4]
            ti += 1
            ph.tr(pt[:, :], zh[:, k * 128:(k + 1) * 128], identb[:, :])
            ph.ts("vector", u2[:, k, :], pt[:, :], A2[:, k:k + 1], ALU.mult, B2[:, k:k + 1], ALU.add)
        ph.stq("sync", f"u2st{tt_ % 2}", u2Td[:, :, tt_ * 128:(tt_ + 1) * 128].rearrange("k p t -> p k t"), u2[:, :, :])
    ph.emit()

    ph = Phase(nc, "p6b")
    u2T = ph.sb([128, NDC, NT], BF16, "u2T")
    ph.ld("sync", "u2T", u2T[:, :, :], u2Td[:, :, :].rearrange("k p t -> p k t"))
    FB = 512
    NFB = DFF // FB
    TH = min(512, NT)
    NTH = NT // TH
    Wu = [ph.sb([128, NDC, FB], BF16, "Wu") for _ in range(2)]
    psu = [ph.ps([128, 512], F32, "psu") for _ in range(4)]
    hr = [ph.sb([128, TH], F32, "hr") for _ in range(2)]
    hT = [ph.sb([128, FB // 128, NT], BF16, "hT") for _ in range(2)]
    it = 0

    def p6_loads(fb):
        Wb = Wu[fb % 2]
        ph.ldm("gpsimd", f"Wu{fb % 2}", Wb.k,
               [(Wb.t[:, k, :], wup_in[k * 128:(k + 1) * 128, fb * FB:(fb + 1) * FB]) for k in range(NDC)])

    p6_loads(0)
    for fb in range(NFB):
        if fb + 1 < NFB:
            p6_loads(fb + 1)
        Wb = Wu[fb % 2]
        hb_ = hT[fb % 2]
        for c in range(FB // 128):
            for th in range(NTH):
                pu = psu[it % 4]
                h_ = hr[it % 2]
                it += 1
                for k in range(NDC):
                    ph.mm(pu[:, 0:TH], Wb[:, k, c * 128:(c + 1) * 128], u2T[:, k, th * TH:(th + 1) * TH],
                          st=(k == 0), sp=(k == NDC - 1))
                ph.act(h_[:, :], pu[:, 0:TH], AF.Relu)
                ph.tt("vector", hb_[:, c, th * TH:(th + 1) * TH], h_[:, :], h_[:, :], ALU.mult)
        ph.stq("sync", f"hst{fb % 2}", hbuf[fb * (FB // 128):(fb + 1) * (FB // 128), :, :].rearrange("c p t -> p c t"),
               hb_[:, :, :])
    ph.emit()

    if upto < 7:
        return nc
    ph = Phase(nc, "p7")
    DH = D // 2
    DQ = min(512, DH)
    NDQ = DH // DQ
    FBD = 1024
    NFD = DFF // FBD
    KB = FBD // 128
    acc = ph.sb([128, NTT, DH], F32, "acc")
    Wd = [ph.sb([128, KB, DH], BF16, "Wd") for _ in range(2)]
    hTb = [ph.sb([128, KB, NT], BF16, "hTb") for _ in range(2)]
    G2 = ph.sb([128, DH], F32, "G2")
    psd = [ph.ps([128, DQ], F32, "psd") for _ in range(4)]
    x1p = [ph.sb([128, DQ], F32, "x1p") for _ in range(2)]
    tmpds = [ph.sb([128, DQ], F32, "tmpd") for _ in range(2)]
    it = 0
    blocks = [(dh, fd) for dh in range(2) for fd in range(NFD)]

    def p7_loads(bi):
        dh, fd = blocks[bi]
        Wb = Wd[bi % 2]
        hb_ = hTb[bi % 2]
        ph.ldm("gpsimd", f"Wd{bi % 2}", Wb.k,
               [(Wb.t[:, kb, :], wdn_in[fd * FBD + kb * 128:fd * FBD + (kb + 1) * 128, dh * DH:(dh + 1) * DH])
                for kb in range(KB)])
        ph.ldm("sync", f"hTb{bi % 2}", hb_.k, [(hb_.t[:, kb, :], hbuf[fd * KB + kb, :, :]) for kb in range(KB)])

    p7_loads(0)
    for bi, (dh, fd) in enumerate(blocks):
        if bi + 1 < len(blocks):
            p7_loads(bi + 1)
        Wb = Wd[bi % 2]
        hb_ = hTb[bi % 2]
        if fd == 0:
            ph.ld("sync", "G2", G2[:, :],
                  modraw[5 * NDC:6 * NDC, :].rearrange("(o a) b -> o (a b)", o=1)[0, dh * DH:(dh + 1) * DH].partition_broadcast(128))
            ph.ts("vector", G2[:, :], G2[:, :], 1.0, ALU.add)
        for tt_ in range(NTT):
            for dq in range(NDQ):
                pd = psd[it % 4]
                it += 1
                for kb in range(KB):
                    ph.mm(pd[:, :], hb_[:, kb, tt_ * 128:(tt_ + 1) * 128], Wb[:, kb, dq * DQ:(dq + 1) * DQ],
                          st=(kb == 0), sp=(kb == KB - 1))
                a_ = acc.sk(f"{tt_}_{dq}")[:, tt_, dq * DQ:(dq + 1) * DQ]
                if fd == 0:
                    ph.cp("scalar", a_, pd[:, :])
                else:
                    ph.tt("vector", a_, pd[:, :], a_, ALU.add)
        if fd == NFD - 1:
            for tt_ in range(NTT):
                for dq in range(NDQ):
                    a_ = acc.sk(f"{tt_}_{dq}")[:, tt_, dq * DQ:(dq + 1) * DQ]
                    x_ = x1p[(tt_ * NDQ + dq) % 2]
                    c0 = dh * DH + dq * DQ
                    ph.ld("sync", f"x1p{(tt_ * NDQ + dq) % 2}", x_[:, :], x1buf[tt_ * 128:(tt_ + 1) * 128, c0:c0 + DQ])
                    ph.tt("vector", a_, a_, G2[:, dq * DQ:(dq + 1) * DQ], ALU.mult)
                    ph.stt(x_[:, :], x_[:, :], ALPHA, a_, ALU.mult, ALU.add)
                    ph.stq("sync", f"z2st{(tt_ * NDQ + dq) % 2}", z2buf[tt_ * 128:(tt_ + 1) * 128, c0:c0 + DQ], x_[:, :])
    ph.emit()

    ph = Phase(nc, "p8")
    g2b = ph.sb([128, D], F32, "g2b")
    b2b = ph.sb([128, D], F32, "b2b")
    ph.ld("sync", "g2b", g2b[:, :], lnr_in[4, :].partition_broadcast(128))
    ph.ld("sync", "b2b", b2b[:, :], lnr_in[5, :].partition_broadcast(128))
    zts = [ph.sb([128, D], F32, "zt") for _ in range(2)]
    ots = [ph.sb([128, D], F32, "ot") for _ in range(2)]
    stats = ph.sb([128, nst * 6], F32, "stats")
    mv = ph.sb([128, 2], F32, "mv")
    rstd = ph.sb([128, 1], F32, "rstd")
    nmr = ph.sb([128, 1], F32, "nmr")
    for tt_ in range(NTT):
        zt = zts[tt_ % 2]
        ot = ots[tt_ % 2]
        ph.ld("sync", f"zt{tt_ % 2}", zt[:, :], z2buf[tt_ * 128:(tt_ + 1) * 128, :], reads=["z2buf"])
        ln_stats(ph, zt, D, stats, mv, rstd, nmr, LN_EPS)
        ph.act(ot[:, :], zt[:, :], AF.Identity, bias=nmr[:, :], scale=rstd[:, :])
        ph.tt("gpsimd", ot[:, :], ot[:, :], g2b[:, :], ALU.mult)
        ph.tt("vector", ot[:, :], ot[:, :], b2b[:, :], ALU.add)
        ph.stq("sync", f"ost{tt_ % 2}", out_t[tt_ * 128:(tt_ + 1) * 128, :], ot[:, :], writes=["out"])
    ph.op("vector", lambda e: e.memset(nmr.t[:, :], 0.0), ["out"], [nmr.k])
    ph.emit()
    return nc


def t5_bucket(dist):
    n = np.maximum(dist, 0)
    nf = np.maximum(n, 1).astype(np.float32)
    large = 16 + (np.log(nf / 16) / math.log(128 / 16) * (32 - 16)).astype(np.int32)
    large = np.minimum(large, 31)
    return np.where(n < 16, n, large)


def host_inputs(cfg, inp):
    D, S, NDC = cfg["D"], cfg["S"], cfg["NDC"]
    HPC, NHP, QPC, NSUB, HP_SUB, NB, NAB = cfg["HPC"], cfg["NHP"], cfg["QPC"], cfg["NSUB"], cfg["HP_SUB"], cfg["NB"], cfg["NAB"]
    LW, LA, LG, NT = cfg["LW"], cfg["LA"], cfg["LG"], cfg["NT"]
    f = lambda a: np.ascontiguousarray(np.asarray(a, dtype=np.float32))
    x = f(inp["x"])
    c = f(inp["c"])
    w_in = f(inp["w_in"])[0]
    mu_shift = f(inp["mu_shift"])[0]
    lnrows = np.stack([f(inp["ln_emb_g"]), f(inp["ln_emb_b"]), f(inp["ln1_g"])[0], f(inp["ln1_b"])[0],
                       f(inp["ln2_g"])[0], f(inp["ln2_b"])[0]], 0)
    j = np.arange(128)[:, None]
    s_ = np.arange(128)[None, :]
    ident = np.eye(128, dtype=np.float32)
    mus = (j < s_).astype(np.float32)
    mui = (j <= s_).astype(np.float32)
    mls = (j > s_).astype(np.float32)
    bo = np.kron(np.eye(2, dtype=np.float32), np.ones((64, 64), np.float32))
    ones = np.ones((128, 128), np.float32)
    NEG = -30000.0
    amask_prev = np.where(s_ > j, 0.0, NEG).astype(np.float32)
    amask_cur = np.where(s_ <= j, 0.0, NEG).astype(np.float32)
    cst = np.concatenate([ident, mus, mui, mls, bo, ones, amask_prev, amask_cur], 1)
    qi = np.arange(128)[:, None]
    kj = np.arange(256)[None, :]
    bucket = t5_bucket(qi + 128 - kj)
    rpb = f(inp["rpb_table"])
    zero128 = np.zeros((D, 1), np.float32)
    maps = []
    for core in range(8):
        b, hg = core // 4, core % 4
        m = {}
        m["x"] = x[b]
        m["xq"] = np.ascontiguousarray(x[b, hg * NT:(hg + 1) * NT])
        m["ccol"] = np.ascontiguousarray(c[b].reshape(NDC, 128).T)
        MQ = 6 * D // 4
        m["wmod"] = np.ascontiguousarray(f(inp["w_mod"])[0][:, hg * MQ:(hg + 1) * MQ])
        m["bmod"] = np.ascontiguousarray(f(inp["b_mod"])[0][None, hg * MQ:(hg + 1) * MQ])
        m["lnv"] = np.ascontiguousarray(lnrows.reshape(6 * NDC, 128))
        m["lnr"] = lnrows
        wrw = np.zeros((NSUB, D, NB * 128), np.float32)
        mu = np.zeros((128, NSUB * NB), np.float32)
        for sp in range(NSUB):
            blocks = []
            for off in (0, cfg["OFF_K"], cfg["OFF_V"]):
                for hl in range(HP_SUB):
                    hp = sp * HP_SUB + hl
                    c0 = off + (hg * HPC + 2 * hp) * 64
                    blocks.append((c0, 128))
            blocks += [(cfg["OFF_W"], LW), (cfg["OFF_A"], LA), (cfg["OFF_G"], LG)]
            for jb, (c0, n) in enumerate(blocks):
                wrw[sp, :, jb * 128:jb * 128 + n] = w_in[:, c0:c0 + n]
                mu[0:n, sp * NB + jb] = mu_shift[c0:c0 + n]
        m["wrw"] = wrw
        m["mu"] = mu
        wat = np.zeros((D, NAB * 128), np.float32)
        for i in range(QPC // 2):
            c0 = cfg["OFF_Q"] + (hg * QPC + 2 * i) * 64
            wat[:, i * 128:(i + 1) * 128] = w_in[:, c0:c0 + 128]
        kc0 = cfg["OFF_KB"] + hg * 64
        vc0 = cfg["OFF_VB"] + hg * 64
        jb = QPC // 2
        wat[:, jb * 128:jb * 128 + 64] = w_in[:, kc0:kc0 + 64]
        wat[:, jb * 128 + 64:jb * 128 + 128] = w_in[:, kc0:kc0 + 64]
        wat[:, (jb + 1) * 128:(jb + 1) * 128 + 64] = w_in[:, vc0:vc0 + 64]
        wat[:, (jb + 1) * 128 + 64:(jb + 2) * 128] = w_in[:, vc0:vc0 + 64]
        m["wat"] = wat
        chp = np.zeros((128, 7 * NHP), np.float32)
        vecs = [f(inp["w0"])[0], f(inp["a0"])[0], f(inp["k_k"])[0], f(inp["k_a"])[0], f(inp["r_k"])[0].reshape(-1),
                f(inp["lnx_g"])[0], f(inp["lnx_b"])[0]]
        wdec = np.zeros((128, NHP * 128), np.float32)
        wicl = np.zeros((128, NHP * 128), np.float32)
        wgat = np.zeros((128, NHP * 128), np.float32)
        for hp in range(NHP):
            a0_ = (hg * HPC + 2 * hp) * 64
            for i, v in enumerate(vecs):
                chp[:, i * NHP + hp] = v[a0_:a0_ + 128]
            wdec[0:LW, hp * 128:(hp + 1) * 128] = f(inp["w_decay_up"])[0][:, a0_:a0_ + 128]
            wicl[0:LA, hp * 128:(hp + 1) * 128] = f(inp["w_iclr_up"])[0][:, a0_:a0_ + 128]
            wgat[0:LG, hp * 128:(hp + 1) * 128] = f(inp["w_gate_up"])[0][:, a0_:a0_ + 128]
        m["chp"], m["wdec"], m["wicl"], m["wgat"] = chp, wdec, wicl, wgat
        bias = np.zeros((128, QPC * 256), np.float32)
        sinks = np.zeros((128, QPC), np.float32)
        for ql in range(QPC):
            hq = hg * QPC + ql
            bias[:, ql * 256:(ql + 1) * 256] = rpb[bucket, hq]
            sinks[:, ql] = f(inp["attn_sinks"])[0][hq]
        m["biasat"] = bias
        m["sinks"] = sinks
        m["cst"] = cst
        m["wout"] = f(inp["w_out"])[0]
        m["wup"] = f(inp["w_up"])[0]
        m["wdn"] = f(inp["w_down"])[0]
        maps.append(m)
    return maps


_NC_CACHE = {}


def kernel(**inputs):
    cfg = make_cfg()
    if "nc" not in _NC_CACHE:
        _NC_CACHE["nc"] = build(cfg)
    nc = _NC_CACHE["nc"]
    maps = host_inputs(cfg, inputs)
    res = run_bass_kernel_spmd(nc, maps, core_ids=list(range(8)))
    NT = cfg["NT"]
    out = np.zeros((2, cfg["S"], cfg["D"]), np.float32)
    for core in range(8):
        b, q = core // 4, core % 4
        out[b, q * NT:(q + 1) * NT] = res.results[core]["out"]
    return out
```

```python
import math
import os
CUT = int(os.environ.get('KCUT', '99'))
from contextlib import ExitStack
import numpy as np
import concourse.bass as bass
import concourse.mybir as mybir
from concourse.bass_utils import run_bass_kernel_spmd

F32 = mybir.dt.float32
BF16 = mybir.dt.bfloat16
ALU = mybir.AluOpType
AF = mybir.ActivationFunctionType
AX = mybir.AxisListType

LN_EPS = 1e-5
LNX_EPS = 64e-5
ALPHA = 2.0 ** 0.25
C0 = math.exp(-0.5)


def make_cfg(D=4096, S=4096, GQA=8):
    c = dict(D=D, S=S, GQA=GQA)
    c["D_A"] = D // 2
    c["H_A"] = c["D_A"] // 64
    c["H_Q"] = c["H_A"]
    c["H_KV"] = c["H_Q"] // GQA
    assert c["H_KV"] == 4
    c["HPC"] = c["H_A"] // 4
    c["NHP"] = c["HPC"] // 2
    c["QPC"] = c["H_Q"] // 4
    c["LW"] = max(32, int(round(c["D_A"] ** 0.5 * 1.8 / 32)) * 32)
    c["LA"] = c["LW"]
    c["LG"] = max(32, int(round(c["D_A"] ** 0.6 * 0.8 / 32)) * 32)
    c["DFF"] = 4 * D
    c["NDC"] = D // 128
    c["HP_SUB"] = min(2, c["NHP"])
    c["NSUB"] = c["NHP"] // c["HP_SUB"]
    c["NB"] = 3 * c["HP_SUB"] + 3
    c["NAB"] = c["QPC"] // 2 + 2
    c["YR"] = D // 4
    c["NT"] = S // 4
    c["TC"] = min(512, S)
    c["OFF_K"] = c["D_A"]
    c["OFF_V"] = 2 * c["D_A"]
    c["OFF_W"] = 3 * c["D_A"]
    c["OFF_A"] = c["OFF_W"] + c["LW"]
    c["OFF_G"] = c["OFF_A"] + c["LA"]
    c["RWKV_COLS"] = c["OFF_G"] + c["LG"]
    c["OFF_Q"] = c["RWKV_COLS"]
    c["OFF_KB"] = c["OFF_Q"] + c["D_A"]
    c["OFF_VB"] = c["OFF_KB"] + 4 * 64
    c["N_IN"] = c["OFF_VB"] + 4 * 64
    return c


class Vw:
    def __init__(s, ap, k):
        s.ap = ap
        s.k = k


class Buf:
    def __init__(s, t, k):
        s.t = t
        s.k = k

    def __getitem__(s, idx):
        return Vw(s.t[idx], s.k)

    def sk(s, suf):
        return Buf(s.t, s.k + "/" + str(suf))


class Phase:
    ENG = ("tensor", "vector", "scalar", "gpsimd", "sync")

    def __init__(s, nc, name):
        s.nc = nc
        s.name = name
        s.ops = {e: [] for e in s.ENG}
        s.cnt = {e: 0 for e in s.ENG}
        s.dcnt = {}
        s.lastw = {}
        s.rd = {}
        s.mem = ExitStack()
        s.nbuf = 0

    def sb(s, shape, dt, key=None):
        s.nbuf += 1
        nm = f"{s.name}_{key or 'b'}{s.nbuf}"
        return Buf(s.mem.enter_context(s.nc.sbuf_tensor(nm, list(shape), dt)), nm)

    def ps(s, shape, dt, key=None):
        s.nbuf += 1
        nm = f"{s.name}_{key or 'p'}{s.nbuf}"
        return Buf(s.mem.enter_context(s.nc.psum_tensor(nm, list(shape), dt)), nm)

    def _deps(s, reads, writes):
        d = {}

        def add(tok):
            if tok is not None and d.get(tok[0], 0) < tok[1]:
                d[tok[0]] = tok[1]
        for b in reads:
            add(s.lastw.get(b))
        for b in writes:
            add(s.lastw.get(b))
            for k, v in s.rd.get(b, {}).items():
                add((k, v))
        return d

    def _commit(s, tok, reads, writes):
        for b in writes:
            s.lastw[b] = tok
            s.rd[b] = {}
        for b in reads:
            r = s.rd.setdefault(b, {})
            if r.get(tok[0], 0) < tok[1]:
                r[tok[0]] = tok[1]

    def op(s, eng, fn, reads=(), writes=()):
        if getattr(s, "dead", False):
            return
        d = s._deps(reads, writes)
        s.cnt[eng] += 1
        tok = (("E", eng), s.cnt[eng])
        s._commit(tok, reads, writes)
        s.ops[eng].append((fn, d, tok, False))

    def dma(s, eng, key, fn, reads=(), writes=(), n=1):
        if getattr(s, "dead", False):
            return
        d = s._deps(reads, writes)
        c = s.dcnt.get(key, 0) + 16 * n
        s.dcnt[key] = c
        tok = (("D", key), c)
        s._commit(tok, reads, writes)
        s.ops[eng].append((fn, d, tok, True))

    def emit(s):
        nc = s.nc
        sems = {}
        handles = []
        for e in s.ENG:
            h = nc.alloc_semaphore(name=f"{s.name}_e_{e}")
            sems[("E", e)] = h
            handles.append(h)
        for i, k in enumerate(s.dcnt):
            h = nc.alloc_semaphore(name=f"{s.name}_d{i}")
            sems[("D", k)] = h
            handles.append(h)
        with nc.Block() as block:
            for e in s.ENG:
                if not s.ops[e] and e != "sync":
                    continue

                def body(eng, e=e):
                    waited = {}
                    for fn, d, tok, isdma in s.ops[e]:
                        for k, v in d.items():
                            if e == "tensor" and k == ("E", "tensor"):
                                continue
                            if waited.get(k, 0) < v:
                                eng.wait_ge(sems[k], v)
                                waited[k] = v
                        if isdma:
                            fn(eng, sems[tok[0]])
                        else:
                            fn(eng).then_inc(sems[tok[0]], 1)
                    if e == "sync":
                        for k, v in s.dcnt.items():
                            if waited.get(("D", k), 0) < v:
                                eng.wait_ge(sems[("D", k)], v)
                getattr(block, e)(body)
        nc.clear_and_free_semaphores(handles)
        nc.all_engine_barrier()
        s.mem.close()

    def mm(s, out, lhsT, rhs, st=True, sp=True):
        s.op("tensor", lambda e: e.matmul(out.ap, lhsT.ap, rhs.ap, start=st, stop=sp),
             [lhsT.k, rhs.k], [out.k])

    def tr(s, out, in_, ident):
        s.op("tensor", lambda e: e.transpose(out.ap, in_.ap, ident.ap), [in_.k, ident.k], [out.k])

    def act(s, out, in_, func, bias=None, scale=None, extra=()):
        rk = [in_.k] + list(extra)
        kw = {}
        if bias is not None:
            if isinstance(bias, Vw):
                kw["bias"] = bias.ap
                rk.append(bias.k)
            else:
                kw["bias"] = bias
        if scale is not None:
            if isinstance(scale, Vw):
                kw["scale"] = scale.ap
                rk.append(scale.k)
            else:
                kw["scale"] = scale
        s.op("scalar", lambda e: e.activation(out.ap, in_.ap, func, **kw), rk, [out.k])

    def tt(s, eng, out, in0, in1, op):
        s.op(eng, lambda e: e.tensor_tensor(out.ap, in0.ap, in1.ap, op), [in0.k, in1.k], [out.k])

    def ts(s, eng, out, in0, s1, op0, s2=None, op1=None):
        rk = [in0.k]
        a1 = s1
        a2 = s2
        if isinstance(s1, Vw):
            a1 = s1.ap
            rk.append(s1.k)
        if isinstance(s2, Vw):
            a2 = s2.ap
            rk.append(s2.k)
        if op1 is None:
            s.op(eng, lambda e: e.tensor_scalar(out.ap, in0.ap, a1, None, op0), rk, [out.k])
        else:
            s.op(eng, lambda e: e.tensor_scalar(out.ap, in0.ap, a1, a2, op0, op1), rk, [out.k])

    def rsqrt(s, out, in_, addc):
        s.op("scalar", lambda e: e.activation(out.ap, in_.ap, AF.Sqrt, bias=addc), [in_.k], [out.k])
        s.op("vector", lambda e: e.reciprocal(out.ap, out.ap), [out.k], [out.k])

    def stt(s, out, in0, sc, in1, op0, op1):
        rk = [in0.k, in1.k]
        a = sc
        if isinstance(sc, Vw):
            a = sc.ap
            rk.append(sc.k)
        s.op("vector", lambda e: e.scalar_tensor_tensor(out.ap, in0.ap, a, in1.ap, op0, op1), rk, [out.k])

    def cp(s, eng, out, in_):
        if eng == "scalar":
            s.op("scalar", lambda e: e.activation(out.ap, in_.ap, AF.Copy), [in_.k], [out.k])
        else:
            s.op(eng, lambda e: e.tensor_copy(out.ap, in_.ap), [in_.k], [out.k])

    def ms(s, eng, out, val):
        s.op(eng, lambda e: e.memset(out.ap, val), [], [out.k])

    def ld(s, eng, key, out, in_ap, reads=(), **kw):
        s.dma(eng, key, lambda e, sem: e.dma_start(out=out.ap, in_=in_ap, **kw).then_inc(sem, 16),
              list(reads), [out.k])

    def ldm(s, eng, key, out_key, pairs, reads=()):
        def fn(e, sem):
            for o, i in pairs:
                e.dma_start(out=o, in_=i).then_inc(sem, 16)
        s.dma(eng, key, fn, list(reads), [out_key], n=len(pairs))

    def stq(s, eng, key, out_ap, in_, writes=(), **kw):
        s.dma(eng, key, lambda e, sem: e.dma_start(out=out_ap, in_=in_.ap, **kw).then_inc(sem, 16),
              [in_.k], list(writes))


def load_cols(ph, rows_ap, nrows, identf, pst, key):
    out = ph.sb([128, nrows], F32, key)
    r0 = 0
    i = 0
    while r0 < nrows:
        n = min(128, nrows - r0)
        tmp = ph.sb([128, 128], F32, key + "t")
        ph.ld("sync", f"{key}ld{i}", tmp[0:n, :], rows_ap[r0:r0 + n, :])
        ph.tr(pst[:, 0:n], tmp[0:n, :], identf[0:n, 0:n])
        ph.cp("vector", out[:, r0:r0 + n], pst[:, 0:n])
        r0 += n
        i += 1
    return out


def ln_stats(ph, xt, D, stats, mv, rstd, nmr, eps):
    nch = max(1, D // 512)
    w = D // nch
    for i in range(nch):
        ph.op("vector", (lambda o, a: (lambda e: e.bn_stats(o.ap, a.ap)))(stats[:, i * 6:(i + 1) * 6], xt[:, i * w:(i + 1) * w]),
              [xt.k], [stats.k])
    ph.op("vector", (lambda o, a: (lambda e: e.bn_aggr(o.ap, a.ap)))(mv[:, :], stats[:, :]), [stats.k], [mv.k])
    rv = rstd if isinstance(rstd, Vw) else rstd[:, :]
    nv = nmr if isinstance(nmr, Vw) else nmr[:, :]
    ph.rsqrt(rv, mv[:, 1:2], eps)
    ph.stt(nv, mv[:, 0:1], -1.0, rv, ALU.mult, ALU.mult)


def build(cfg, debug=False, upto=99):
    D, S = cfg["D"], cfg["S"]
    NDC, NB, NAB, NHP, HP_SUB, NSUB = cfg["NDC"], cfg["NB"], cfg["NAB"], cfg["NHP"], cfg["HP_SUB"], cfg["NSUB"]
    QPC, LW, LA, LG, DFF, YR, NT, TC = cfg["QPC"], cfg["LW"], cfg["LA"], cfg["LG"], cfg["DFF"], cfg["YR"], cfg["NT"], cfg["TC"]
    NCH = TC // 128
    NTC = S // TC
    NQB = QPC // 2
    nc = bass.Bass("TRN2", target_bir_lowering=False)

    def din(name, shape, dt=F32):
        return nc.dram_tensor(name, list(shape), dt, kind="ExternalInput").ap()

    x_in = din("x", [S, D])
    xq_in = din("xq", [NT, D])
    ccol_in = din("ccol", [128, NDC])
    wmod_in = din("wmod", [D, 6 * D // 4])
    bmod_in = din("bmod", [1, 6 * D // 4])
    lnv_in = din("lnv", [6 * NDC, 128])
    lnr_in = din("lnr", [6, D])
    wrw_in = din("wrw", [NSUB, D, NB * 128])
    wat_in = din("wat", [D, NAB * 128])
    mu_in = din("mu", [128, NSUB * NB])
    chp_in = din("chp", [128, 7 * NHP])
    wdec_in = din("wdec", [128, NHP * 128])
    wicl_in = din("wicl", [128, NHP * 128])
    wgat_in = din("wgat", [128, NHP * 128])
    bias_in = din("biasat", [128, QPC * 256])
    sink_in = din("sinks", [128, QPC])
    cst_in = din("cst", [128, 8 * 128])
    wout_in = din("wout", [D, D])
    wup_in = din("wup", [D, DFF])
    wdn_in = din("wdn", [DFF, D])
    out_t = nc.dram_tensor("out", [NT, D], F32, kind="ExternalOutput").ap()

    modraw_t = nc.dram_tensor("modraw", [6 * NDC, 128], F32)
    modraw = modraw_t.ap()
    modpart_t = nc.dram_tensor("modpart", [6 * NDC // 4, 128], F32)
    modpart = modpart_t.ap()
    uTd = nc.dram_tensor("uTd", [NDC, 128, S], BF16).ap()
    NRB = YR // 128
    ybuf_t = [nc.dram_tensor(f"ybuf{j}", [128, S], BF16) for j in range(NRB)]
    yall_t = [nc.dram_tensor(f"yall{j}", [4 * 128, S], BF16) for j in range(NRB)]
    ydbg = nc.dram_tensor("ydbg", [YR, S], BF16, kind="ExternalOutput").ap() if debug else None
    zbuf = nc.dram_tensor("zbuf", [NT, D], F32, **(dict(kind="ExternalOutput") if debug else dict())).ap()
    x1buf = nc.dram_tensor("x1buf", [NT, D], F32, **(dict(kind="ExternalOutput") if debug else dict())).ap()
    hbuf = nc.dram_tensor("hbuf", [DFF // 128, 128, NT], BF16, **(dict(kind="ExternalOutput") if debug else dict())).ap()
    z2buf = nc.dram_tensor("z2buf", [NT, D], F32, **(dict(kind="ExternalOutput") if debug else dict())).ap()
    if debug:
        d7h = nc.dram_tensor("d7h", [128, (1024 // 128), NT], BF16, kind="ExternalOutput").ap()
        d7w = nc.dram_tensor("d7w", [128, (1024 // 128), D // 2], BF16, kind="ExternalOutput").ap()
        d7p = nc.dram_tensor("d7p", [128, min(512, D // 2)], F32, kind="ExternalOutput").ap()
        d7acc = nc.dram_tensor("d7acc", [128, D // 2], F32, kind="ExternalOutput").ap()
    u2Td = nc.dram_tensor("u2Td", [NDC, 128, NT], BF16, **(dict(kind="ExternalOutput") if debug else dict())).ap()

    def dbgcopy():
        if debug and not _dbgdone:
            _dbgdone.append(1)
            dbg_sem0 = nc.alloc_semaphore(name="dbg_sem0")
            with nc.Block() as block:
                @block.gpsimd
                def _(g):
                    for j in range(NRB):
                        g.dma_start(out=ydbg[j * 128:(j + 1) * 128, :], in_=ybuf_t[j].ap()).then_inc(dbg_sem0, 16)
                    g.wait_ge(dbg_sem0, 16 * NRB)
            nc.clear_and_free_semaphores([dbg_sem0])
            nc.all_engine_barrier()
    _dbgdone = []

    def consts(ph):
        c = ph.sb([128, 8 * 128], F32, "cst")
        ph.ld("sync", "cst", c[:, :], cst_in[:, :])
        names = ["ident", "mus", "mui", "mls", "bo", "ones", "amp", "amc"]
        d = {n: c.t[:, i * 128:(i + 1) * 128] for i, n in enumerate(names)}
        return c, d

    def cv(c, d, n, sl=None):
        ap = d[n]
        if sl is not None:
            ap = ap[sl]
        return Vw(ap, c.k)

    ph = Phase(nc, "p0")
    ccol = ph.sb([128, NDC], F32)
    cond = ph.sb([128, NDC], BF16)
    ph.ld("sync", "ccol", ccol[:, :], ccol_in[:, :])
    ph.act(cond[:, :], ccol[:, :], AF.Silu)
    MQ = 6 * D // 4
    GW = 2048 if MQ % 2048 == 0 else 1536
    assert MQ % GW == 0 and GW % 512 == 0
    NG = MQ // GW
    NQ_ = GW // 512
    wms = [ph.sb([128, GW], BF16, "wm") for _ in range(4)]
    psb = [[ph.ps([128, 512], F32, "pm") for _ in range(NQ_)] for _ in range(2)]
    bms = [ph.sb([1, GW], F32, "bm") for _ in range(2)]
    mds = [ph.sb([1, GW], F32, "md") for _ in range(2)]
    it = 0
    for g in range(NG):
        pb = psb[g % 2]
        ph.ld("sync", f"bm{g % 2}", bms[g % 2][:, :], bmod_in[0:1, g * GW:(g + 1) * GW])
        for k in range(NDC):
            wm = wms[it % 4]
            ph.ld("gpsimd", f"wm{it % 4}", wm[:, :], wmod_in[k * 128:(k + 1) * 128, g * GW:(g + 1) * GW])
            it += 1
            for q in range(NQ_):
                ph.mm(pb[q][0:1, :], cond[:, k:k + 1], wm[:, q * 512:(q + 1) * 512], st=(k == 0), sp=(k == NDC - 1))
        for q in range(NQ_):
            ph.tt("vector", mds[g % 2][0:1, q * 512:(q + 1) * 512], pb[q][0:1, :],
                  bms[g % 2][0:1, q * 512:(q + 1) * 512], ALU.add)
        ph.stq("sync", f"mdst{g % 2}",
               modpart[g * (GW // 128):(g + 1) * (GW // 128), :].rearrange("(o a) b -> o (a b)", o=1),
               mds[g % 2][0:1, :], writes=["modpart"])
    ph.emit()
    cc0 = nc.alloc_semaphore(name="cc0_sem")
    with nc.Block() as block:
        @block.gpsimd
        def _(g):
            g.collective_compute("AllGather", ALU.bypass, replica_groups=[[0, 1, 2, 3], [4, 5, 6, 7]],
                                 ins=[modpart_t.ap().opt()], outs=[modraw_t.ap().opt()]).then_inc(cc0)
            g.wait_ge(cc0, 1)
    nc.clear_and_free_semaphores([cc0])
    nc.all_engine_barrier()

    if upto < 1:
        return nc
    ph = Phase(nc, "p1")
    cst, cd = consts(ph)
    identb = ph.sb([128, 128], BF16, "idb")
    ph.cp("vector", identb[:, :], cv(cst, cd, "ident"))
    pstf = ph.ps([128, 128], F32, "pstf")
    identF = ph.sb([128, 128], F32, "idf")
    ph.cp("vector", identF[:, :], cv(cst, cd, "ident"))
    modc = load_cols(ph, modraw, 6 * NDC, identF, pstf, "modc")
    lnc = load_cols(ph, lnv_in, 6 * NDC, identF, pstf, "lnc")
    A1 = ph.sb([128, NDC], F32, "A1")
    B1 = ph.sb([128, NDC], F32, "B1")
    t1 = ph.sb([128, NDC], F32, "t1")
    ph.ts("vector", t1[:, :], modc[:, NDC:2 * NDC], 1.0, ALU.add)
    ph.tt("vector", A1[:, :], t1[:, :], lnc[:, 0:NDC], ALU.mult)
    ph.tt("vector", B1[:, :], t1[:, :], lnc[:, NDC:2 * NDC], ALU.mult)
    ph.tt("vector", B1[:, :], B1[:, :], modc[:, 0:NDC], ALU.add)
    xts = [ph.sb([128, D], F32, "xt") for _ in range(2)]
    xhs = [ph.sb([128, D], BF16, "xh") for _ in range(2)]
    uTt = [ph.sb([128, NDC, TC], BF16, "uTt") for _ in range(2)]
    nst = max(1, D // 512)
    stats = ph.sb([128, nst * 6], F32, "stats")
    mv = ph.sb([128, 2], F32, "mv")
    rstd = ph.sb([128, 1], F32, "rstd")
    nmr = ph.sb([128, 1], F32, "nmr")
    pstb = [ph.ps([128, 128], BF16, "pstb") for _ in range(4)]
    lnt = [(ph.sb([128, nst * 6], F32, "stats"), ph.sb([128, 2], F32, "mv"), ph.sb([128, 1], F32, "rstd"),
            ph.sb([128, 1], F32, "nmr")) for _ in range(2)]
    ti = [0]
    NTL = NTC * NCH

    def p1_A(i):
        xt = xts[i % 2]
        xh = xhs[i % 2]
        ph.ld("sync", f"xt{i % 2}", xt[:, :], x_in[i * 128:(i + 1) * 128, :])
        stats, mv, rstd, nmr = lnt[i % 2]
        ln_stats(ph, xt, D, stats, mv, rstd, nmr, LN_EPS)
        ph.act(xh[:, :], xt[:, :], AF.Identity, bias=nmr[:, :], scale=rstd[:, :])

    def p1_B(i):
        tcI, c = i // NCH, i % NCH
        ub = uTt[tcI % 2]
        xh = xhs[i % 2]
        for k in range(NDC):
            pt = pstb[ti[0] % 4]
            ti[0] += 1
            ph.tr(pt[:, :], xh[:, k * 128:(k + 1) * 128], identb[:, :])
            if k % 2 == 0:
                ph.ts("vector", ub[:, k, c * 128:(c + 1) * 128], pt[:, :], A1[:, k:k + 1], ALU.mult,
                      B1[:, k:k + 1], ALU.add)
            else:
                ph.act(ub[:, k, c * 128:(c + 1) * 128], pt[:, :], AF.Identity, bias=B1[:, k:k + 1],
                       scale=A1[:, k:k + 1])
        if c == NCH - 1:
            ph.stq("gpsimd", f"uTst{tcI % 2}", uTd[:, :, tcI * TC:(tcI + 1) * TC].rearrange("k p t -> p k t"),
                   ub[:, :, :])

    p1_A(0)
    for i in range(NTL):
        if i + 1 < NTL:
            p1_A(i + 1)
        p1_B(i)
    ph.emit()

    if upto < 2:
        dbgcopy()
        return nc
    TC_save = TC
    TC = min(256, S)
    NCH = TC // 128
    NTC = S // TC
    for sp in range(NSUB):
        ph = Phase(nc, f"p2{sp}")
        cst, cd = consts(ph)
        IDN = cv(cst, cd, "ident")
        MUS = cv(cst, cd, "mus")
        MUI = cv(cst, cd, "mui")
        MLS = cv(cst, cd, "mls")
        BO = cv(cst, cd, "bo")
        ONES = cv(cst, cd, "ones")
        W = ph.sb([128, NDC, NB * 128], BF16, "W")
        ph.ldm("gpsimd", "Wld", W.k, [(W.t[:, k, :], wrw_in[sp, k * 128:(k + 1) * 128, :]) for k in range(NDC)])
        mu = ph.sb([128, NB], F32, "mu")
        omu = ph.sb([128, NB], F32, "omu")
        ph.ld("sync", "mu", mu[:, :], mu_in[:, sp * NB:(sp + 1) * NB])
        ph.ts("vector", omu[:, :], mu[:, :], -1.0, ALU.mult, 1.0, ALU.add)
        chp = ph.sb([128, 7 * NHP], F32, "chp")
        ph.ld("sync", "chp", chp[:, :], chp_in[:, :])
        omka = ph.sb([128, NHP], F32, "omka")
        ph.ts("vector", omka[:, :], chp[:, 3 * NHP:4 * NHP], -1.0, ALU.mult, 1.0, ALU.add)
        g8 = ph.sb([128, NHP], F32, "g8")
        ph.ts("vector", g8[:, :], chp[:, 5 * NHP:6 * NHP], 8.0, ALU.mult)
        wdec = ph.sb([128, NHP * 128], F32, "wdec")
        wicl = ph.sb([128, NHP * 128], F32, "wicl")
        wgat = ph.sb([128, NHP * 128], F32, "wgat")
        ph.ld("sync", "wdec", wdec[:, :], wdec_in[:, :])
        ph.ld("sync", "wicl", wicl[:, :], wicl_in[:, :])
        ph.ld("sync", "wgat", wgat[:, :], wgat_in[:, :])
        uTs = [ph.sb([128, NDC, TC], BF16, "uT") for _ in range(2)]
        praw = [ph.sb([128, TC + 1], F32, "praw") for _ in range(2)]
        carry = ph.sb([128, NB], F32, "carry")
        ph.ms("vector", carry[:, :], 0.0)
        tmpa = [ph.sb([128, TC], F32, "tmpa") for _ in range(2)]
        sh = [ph.sb([128, TC], F32, f"sh{j}") for j in range(NB)]
        psw = [ph.ps([128, 512], F32, "psw") for _ in range(3)]
        psq_b = [ph.ps([128, 512], F32, "psq") for _ in range(5)]
        NPQ = int(os.environ.get('NPQ', '5'))
        psq = [psq_b[i % 5].sk(i // 5)[:, (i // 5) * 128:(i // 5 + 1) * 128] for i in range(NPQ)]
        qn = [0]
        wn = [0]

        def nq():
            qn[0] += 1
            return psq[qn[0] % NPQ]

        def nw():
            wn[0] += 1
            return psw[wn[0] % 3]

        def F(key):
            return ph.sb([128, TC], F32, key)

        sg, et, gt, kk, kkn, kp, bv, cs, Wt, Wi, Wp, rt, kt, bt, at_, bon, tq1, yraw, yc = [
            F(n) for n in ["sg", "et", "gt", "kk", "kkn", "kp", "bv", "cs", "Wt", "Wi", "Wp", "rt", "kt", "bt",
                           "at", "bon", "tq1", "yraw", "yc"]]
        yo = [ph.sb([128, TC], BF16, "yo") for _ in range(2)]
        btm = [F("btm"), F("btm")]
        ktm = [F("ktm"), F("ktm")]
        atm = [F("atm"), F("atm")]
        ST = [ph.sb([128, 128], F32, f"ST{h}") for h in range(HP_SUB)]
        for h in range(HP_SUB):
            ph.ms("vector", ST[h][:, :], 0.0)

        def M(key):
            return ph.sb([128, 128], F32, key)

        mats = {}
        for c2 in range(2):
            for h in range(2):
                mats[(c2, h)] = dict(P=[M("P"), M("P")], PT=[M("PT"), M("PT")], T=M("T"), AKA=M("AKA"),
                                     ABR=M("ABR"), AKR=M("AKR"))
        pairm = []
        for c2 in range(2):
            d = dict(VTp=M("VTp"), VTe=M("VTe"), VTo=M("VTo"), BTT=M("BTT"), KTT=M("KTT"), RHS=M("RHS"),
                     SAp=M("SAp"), SAe=M("SAe"), SAo=M("SAo"))
            for n in ["VTe", "VTo", "SAe", "SAo"]:
                ph.ms("gpsimd", d[n][:, :], 0.0)
            pairm.append(d)
        NLEV = 6

        blk_lora = [NB - 3, NB - 2, NB - 1]

        def hp_blocks(hl_):
            return [hl_, HP_SUB + hl_, 2 * HP_SUB + hl_]
        xw, xa, xg = sh[NB - 3], sh[NB - 2], sh[NB - 1]

        def load_uT(t):
            ph.ld("sync", f"uT{t % 2}", uTs[t % 2][:, :, :],
                  uTd[:, :, t * TC:(t + 1) * TC].rearrange("k p t -> p k t"))

        def inproj(t, blocks):
            uT = uTs[t % 2]
            for j in blocks:
                bank = nw()
                for k in range(NDC):
                    ph.mm(bank[:, 0:TC], W[:, k, j * 128:(j + 1) * 128], uT[:, k, :], st=(k == 0), sp=(k == NDC - 1))
                pr = praw[j % 2]
                ph.cp("gpsimd", pr[:, 0:1], carry[:, j:j + 1])
                ph.cp("scalar", pr[:, 1:TC + 1], bank[:, 0:TC])
                ph.cp("gpsimd", carry[:, j:j + 1], pr[:, TC:TC + 1])
                tm = tmpa[j % 2]
                ph.ts("vector", tm[:, :], pr[:, 0:TC], mu[:, j:j + 1], ALU.mult)
                ph.stt(sh[j][:, :], pr[:, 1:TC + 1], omu[:, j:j + 1], tm[:, :], ALU.mult, ALU.add)
                if j == NB - 3:
                    ph.act(xw[:, :], xw[:, :], AF.Tanh)
                if j == NB - 1:
                    ph.act(xg[:, :], xg[:, :], AF.Sigmoid)

        load_uT(0)
        inproj(0, blk_lora + hp_blocks(0))
        for tcI in range(NTC):
            if tcI + 1 < NTC:
                load_uT(tcI + 1)
            for hl in range(HP_SUB):
                hp = sp * HP_SUB + hl
                r_, k_, v_ = sh[hl], sh[HP_SUB + hl], sh[2 * HP_SUB + hl]

                def P_(i):
                    return chp[:, i * NHP + hp:i * NHP + hp + 1]
                w0c, a0c, kkc, kac, rkc, lgc, lbc = [P_(i) for i in range(7)]
                hs = slice(hp * 128, (hp + 1) * 128)
                b = nw()
                ph.mm(b[:, 0:TC], wdec[0:LW, hs], xw[0:LW, :])
                ph.act(sg[:, :], b[:, 0:TC], AF.Sigmoid, bias=w0c)
                b = nw()
                ph.mm(b[:, 0:TC], wicl[0:LA, hs], xa[0:LA, :])
                ph.act(et[:, :], b[:, 0:TC], AF.Sigmoid, bias=a0c)
                b = nw()
                ph.mm(b[:, 0:TC], wgat[0:LG, hs], xg[0:LG, :])
                ph.cp("vector", gt[:, :], b[:, 0:TC])
                if hl + 1 < HP_SUB:
                    inproj(tcI, hp_blocks(hl + 1))
                elif tcI + 1 < NTC and HP_SUB > 1:
                    inproj(tcI + 1, blk_lora + hp_blocks(0))
                ph.act(kk[:, :], k_[:, :], AF.Copy, scale=kkc)
                ph.tt("gpsimd", tq1[:, :], kk[:, :], kk[:, :], ALU.mult)
                b = nw()
                ph.mm(b[:, 0:TC], BO, tq1[:, :])
                ph.rsqrt(tq1[:, :], b[:, 0:TC], 1e-24)
                ph.tt("gpsimd", kkn[:, :], kk[:, :], tq1[:, :], ALU.mult)
                ph.ts("vector", tq1[:, :], et[:, :], kac, ALU.mult, omka[:, hp:hp + 1], ALU.add)
                ph.tt("gpsimd", kp[:, :], k_[:, :], tq1[:, :], ALU.mult)
                ph.tt("gpsimd", bv[:, :], kkn[:, :], et[:, :], ALU.mult)
                for c in range(NCH):
                    sl = slice(c * 128, (c + 1) * 128)
                    ph.op("vector", (lambda o, d0, d1: (lambda e: e.tensor_tensor_scan(o.ap, d0.ap, d1.ap, 0.0, ALU.mult, ALU.add)))(
                        cs[:, sl], ONES, sg[:, sl]), [ONES.k, sg.k], [cs.k])
                ph.act(Wt[:, :], cs[:, :], AF.Exp, scale=-C0)
                ph.act(Wi[:, :], cs[:, :], AF.Exp, scale=C0)
                ph.tt("gpsimd", tq1[:, :], cs[:, :], sg[:, :], ALU.subtract)
                ph.act(Wp[:, :], tq1[:, :], AF.Exp, scale=-C0)
                ph.tt("gpsimd", rt[:, :], r_[:, :], Wt[:, :], ALU.mult)
                ph.tt("gpsimd", kt[:, :], kp[:, :], Wi[:, :], ALU.mult)
                ph.tt("gpsimd", bt[:, :], bv[:, :], Wi[:, :], ALU.mult)
                ph.stt(at_[:, :], kkn[:, :], -1.0, Wp[:, :], ALU.mult, ALU.mult)
                for h in range(2):
                    mcol = Vw(cd["bo"][:, 127 * h:127 * h + 1], cst.k)
                    ph.act(btm[h][:, :], bt[:, :], AF.Copy, scale=mcol)
                    ph.ts("vector", ktm[h][:, :], kt[:, :], mcol, ALU.mult)
                    ph.act(atm[h][:, :], at_[:, :], AF.Copy, scale=mcol)
                ph.tt("gpsimd", tq1[:, :], r_[:, :], kp[:, :], ALU.mult)
                ph.act(tq1[:, :], tq1[:, :], AF.Copy, scale=rkc)
                b = nw()
                ph.mm(b[:, 0:TC], BO, tq1[:, :])
                ph.tt("vector", bon[:, :], b[:, 0:TC], v_[:, :], ALU.mult)
                if CUT == 2:
                    ph.dead = True
                for c in range(NCH):
                    sl = slice(c * 128, (c + 1) * 128)
                    pm = pairm[c % 2]
                    for h in range(2):
                        hb = slice(64 * h, 64 * h + 64)
                        m = mats[(c % 2, h)]
                        if CUT == 31:
                            ph.dead = True
                        q = nq()
                        ph.mm(q, btm[h][:, sl], at_[:, sl])
                        ph.tt("vector", m["P"][0][:, :], q, MUS, ALU.mult)
                        if CUT == 20:
                            ph.dead = True
                        if CUT == 32:
                            ph.dead = True
                        q = nq()
                        ph.mm(q, atm[h][:, sl], bt[:, sl])
                        ph.tt("vector", m["PT"][0][:, :], q, MLS, ALU.mult)
                        if CUT == 33:
                            ph.dead = True
                        q = nq()
                        ph.mm(q, ktm[h][:, sl], at_[:, sl])
                        ph.tt("vector", m["AKA"][:, :], q, MUS, ALU.mult)
                        if CUT == 34:
                            ph.dead = True
                        q = nq()
                        ph.mm(q, btm[h][:, sl], rt[:, sl])
                        ph.tt("vector", m["ABR"][:, :], q, MUI, ALU.mult)
                        if CUT == 35:
                            ph.dead = True
                        q = nq()
                        ph.mm(q, ktm[h][:, sl], rt[:, sl])
                        ph.tt("vector", m["AKR"][:, :], q, MUI, ALU.mult)
                        if CUT == 36:
                            ph.dead = True
                        ph.tt("gpsimd", m["T"][:, :], m["P"][0][:, :], IDN, ALU.add)
                    if CUT == 21:
                        ph.dead = True
                    q = nq()
                    ph.tr(q, v_[:, sl], IDN)
                    ph.cp("scalar", pm["VTp"][:, :], q)
                    ph.cp("gpsimd", pm["VTe"].sk("a")[:, 0:64], pm["VTp"][:, 0:64])
                    ph.cp("gpsimd", pm["VTo"].sk("a")[:, 64:128], pm["VTp"][:, 64:128])
                    q = nq()
                    ph.tr(q, bt[:, sl], IDN)
                    ph.cp("scalar", pm["BTT"][:, :], q)
                    q = nq()
                    ph.tr(q, kt[:, sl], IDN)
                    ph.cp("scalar", pm["KTT"][:, :], q)
                if CUT == 3:
                    ph.dead = True
                assert NCH <= 2
                for lev in range(1, NLEV + 1):
                    a, bnew = (lev - 1) % 2, lev % 2
                    for c in range(NCH):
                        for h in range(2):
                            m = mats[(c % 2, h)]
                            if lev < NLEV:
                                q = nq()
                                ph.mm(q, m["PT"][a][:, :], m["P"][a][:, :])
                                ph.cp("scalar", m["P"][bnew][:, :], q)
                            q = nq()
                            ph.mm(q, m["P"][a][:, :], m["PT"][a][:, :])
                            ph.cp("scalar" if h == 0 else "vector", m["PT"][bnew][:, :], q)
                    for c in range(NCH):
                        for h in range(2):
                            m = mats[(c % 2, h)]
                            q = nq()
                            ph.mm(q, m["PT"][bnew][:, :], m["T"][:, :])
                            ph.tt("vector", m["T"][:, :], m["T"][:, :], q, ALU.add)
                for c in range(NCH):
                    sl = slice(c * 128, (c + 1) * 128)
                    pm = pairm[c % 2]
                    if CUT == 4:
                        ph.dead = True
                    me, mo = mats[(c % 2, 0)], mats[(c % 2, 1)]
                    STh = ST[hl]
                    q = nq()
                    ph.mm(q, at_[:, sl], STh[:, :], st=True, sp=False)
                    ph.mm(q, me["AKA"][:, :], pm["VTe"].sk("a")[:, :], st=False, sp=False)
                    ph.mm(q, mo["AKA"][:, :], pm["VTo"].sk("a")[:, :], st=False, sp=True)
                    ph.cp("scalar", pm["RHS"][:, :], q)
                    q = nq()
                    qa = Vw(q.ap[:, 0:64], q.k)
                    qb = Vw(q.ap[:, 64:128], q.k)
                    ph.mm(qa, me["T"][:, :], pm["RHS"][:, 0:64])
                    ph.mm(qb, mo["T"][:, :], pm["RHS"][:, 64:128])
                    ph.cp("scalar", pm["SAp"][:, :], q)
                    ph.cp("gpsimd", pm["SAe"].sk("a")[:, 0:64], pm["SAp"][:, 0:64])
                    ph.cp("gpsimd", pm["SAo"].sk("a")[:, 64:128], pm["SAp"][:, 64:128])
                    q = nq()
                    ph.mm(q, STh[:, :], rt[:, sl], st=True, sp=False)
                    ph.mm(q, pm["SAe"].sk("a")[:, :], me["ABR"][:, :], st=False, sp=False)
                    ph.mm(q, pm["SAo"].sk("a")[:, :], mo["ABR"][:, :], st=False, sp=False)
                    ph.mm(q, pm["VTe"].sk("a")[:, :], me["AKR"][:, :], st=False, sp=False)
                    ph.mm(q, pm["VTo"].sk("a")[:, :], mo["AKR"][:, :], st=False, sp=True)
                    ph.cp("scalar", yraw[:, sl], q)
                    q = nq()
                    ph.mm(q, IDN, STh[:, :], st=True, sp=False)
                    ph.mm(q, pm["BTT"][:, :], pm["SAp"][:, :], st=False, sp=False)
                    ph.mm(q, pm["KTT"][:, :], pm["VTp"][:, :], st=False, sp=True)
                    ph.stt(STh[:, :], q, Wt[:, c * 128 + 127:c * 128 + 128], BO, ALU.mult, ALU.mult)
                if CUT == 5:
                    ph.dead = True
                b = nw()
                ph.mm(b[:, 0:TC], BO, yraw[:, :])
                ph.stt(yc[:, :], b[:, 0:TC], -1.0 / 64, yraw[:, :], ALU.mult, ALU.add)
                ph.tt("gpsimd", tq1[:, :], yc[:, :], yc[:, :], ALU.mult)
                b = nw()
                ph.mm(b[:, 0:TC], BO, tq1[:, :])
                ph.rsqrt(tq1[:, :], b[:, 0:TC], 64 * LNX_EPS)
                ph.tt("gpsimd", yc[:, :], yc[:, :], tq1[:, :], ALU.mult)
                ph.ts("vector", yc[:, :], yc[:, :], g8[:, hp:hp + 1], ALU.mult, lbc, ALU.add)
                ph.tt("gpsimd", yc[:, :], yc[:, :], bon[:, :], ALU.add)
                yob = yo[(tcI * HP_SUB + hl) % 2]
                ph.tt("vector", yob[:, :], yc[:, :], gt[:, :], ALU.mult)
                ph.stq("sync", f"yo{(tcI * HP_SUB + hl) % 2}", ybuf_t[hp].ap()[:, tcI * TC:(tcI + 1) * TC],
                       yob[:, :], writes=["ybuf"])
            if HP_SUB == 1 and tcI + 1 < NTC:
                inproj(tcI + 1, blk_lora + hp_blocks(0))
        ph.emit()

    if upto < 3:
        dbgcopy()
        return nc
    TC = TC_save
    NCH = TC // 128
    NTC = S // TC
    ph = Phase(nc, "p3")
    cst, cd = consts(ph)
    IDN = cv(cst, cd, "ident")
    identb = ph.sb([128, 128], BF16, "idb")
    ph.cp("vector", identb[:, :], IDN)
    W = ph.sb([128, NDC, NAB * 128], BF16, "W")
    ph.ldm("gpsimd", "Wld", W.k, [(W.t[:, k, :], wat_in[k * 128:(k + 1) * 128, :]) for k in range(NDC)])
    biasm = ph.sb([128, QPC * 256], F32, "biasm")
    ph.ld("sync", "bias", biasm[:, :], bias_in[:, :])
    for h in range(QPC):
        for hf in range(2):
            ph.tt("vector", biasm[:, h * 256 + hf * 128:h * 256 + hf * 128 + 128],
                  biasm[:, h * 256 + hf * 128:h * 256 + hf * 128 + 128],
                  cv(cst, cd, "amc") if hf == 1 else cv(cst, cd, "amp"), ALU.add)
    sinkc = ph.sb([128, QPC], F32, "sink")
    ph.ld("sync", "sink", sinkc[:, :], sink_in[:, :])
    uTs = [ph.sb([128, NDC, TC], BF16, "uT") for _ in range(2)]
    qTm = [[ph.sb([128, TC], BF16, f"qT{i}_{h}") for h in range(2)] for i in range(NQB)]
    kT = ph.sb([128, TC + 128], BF16, "kT")
    vT = ph.sb([128, TC], BF16, "vT")
    VTe = ph.sb([128, NCH + 1, 128], BF16, "VTe")
    VTo = ph.sb([128, NCH + 1, 128], BF16, "VTo")
    ph.ms("vector", VTe[:, :, :], 0.0)
    ph.ms("vector", VTo[:, :, :], 0.0)
    ph.ms("vector", kT[:, :], 0.0)
    psw = [ph.ps([128, 512], F32, "psw") for _ in range(1)]
    class _Sub:
        def __init__(s_, bank, c0, n, key):
            s_.b, s_.c0, s_.n, s_.k = bank, c0, n, bank.k + "/" + key

        def __getitem__(s_, idx):
            rows, cols = idx
            a = s_.c0 + (cols.start or 0)
            bb = s_.c0 + (cols.stop if cols.stop is not None else s_.n)
            return Vw(s_.b.t[rows, a:bb], s_.k)
    pss = [ph.ps([128, 512], F32, "pss") for _ in range(3)]
    pstb = [ph.ps([128, 1024], BF16, "pstb") for _ in range(2)]
    pso = [ph.ps([128, 512], F32, "pso") for _ in range(2)]
    NS = 3
    ssb = [ph.sb([128, 512], F32, "ssb") for _ in range(NS)]
    esb = [ph.sb([128, 512], F32, "esb") for _ in range(NS)]
    pb = [ph.sb([128, 512], BF16, "pb") for _ in range(NS)]
    pTs = [ph.sb([128, 512], BF16, "pTs") for _ in range(NS)]
    sm = [ph.sb([128, 16], F32, "sm") for _ in range(NS)]
    biasm0 = ph.sb([128, QPC * 256], F32, "biasm0")
    ph.cp("vector", biasm0[:, :], biasm[:, :])
    for h in range(QPC):
        ph.ms("vector", biasm0[:, h * 256:h * 256 + 128], -30000.0)
    yb = [ph.sb([128, TC], BF16, "yb") for _ in range(2 * NQB)]
    hn = 0
    wn = 0
    for tcI in range(NTC):
        uT = uTs[tcI % 2]
        ph.ld("sync", f"uT{tcI % 2}", uT[:, :, :],
              uTd[:, :, tcI * TC:(tcI + 1) * TC].rearrange("k p t -> p k t"), reads=["uTd"])
        if tcI > 0:
            ph.cp("gpsimd", kT[:, 0:128], kT[:, TC:TC + 128])
            ph.cp("gpsimd", VTe[:, 0, :], VTe[:, NCH, :])
            ph.cp("gpsimd", VTo[:, 0, :], VTo[:, NCH, :])
        for j in range(NAB):
            bank = psw[0]
            wn += 1
            for k in range(NDC):
                ph.mm(bank[:, 0:TC], W[:, k, j * 128:(j + 1) * 128], uT[:, k, :], st=(k == 0), sp=(k == NDC - 1))
            if j < NQB:
                for h2 in range(2):
                    ph.ts("vector", qTm[j][h2][:, :], bank[:, 0:TC], Vw(cd["bo"][:, 127 * h2:127 * h2 + 1], cst.k), ALU.mult)
            elif j == NQB:
                ph.cp("scalar", kT[:, 128:128 + TC], bank[:, 0:TC])
            else:
                ph.cp("scalar", vT[:, :], bank[:, 0:TC])
        for c in range(NCH):
            pt = pstb[c % 2]
            ph.tr(pt[:, 0:128], vT[:, c * 128:(c + 1) * 128], identb[:, :])
            ph.cp("vector", VTe[:, c + 1, 0:64], pt[:, 0:64])
            ph.cp("vector", VTo[:, c + 1, 64:128], pt[:, 64:128])
        for c in range(NCH):
            first = (tcI == 0 and c == 0)
            bm = biasm0 if first else biasm
            for qb in range(NQB):
                po = pso[hn % 2]
                ps_ = pss[hn % 3]
                ptb = pstb[hn % 2]
                s_, e_, p_, pT_, m_ = ssb[hn % NS], esb[hn % NS], pb[hn % NS], pTs[hn % NS], sm[hn % NS]
                hn += 1
                for h2 in range(2):
                    ph.mm(ps_[:, h2 * 256:(h2 + 1) * 256], qTm[qb][h2][:, c * 128:(c + 1) * 128], kT[:, c * 128:c * 128 + 256])
                ph.stt(s_[:, :], ps_[:, :], 0.125, bm[:, 2 * qb * 256:(2 * qb + 2) * 256], ALU.mult, ALU.add)
                s3 = Vw(s_.t[:, :].rearrange("p (h k) -> p h k", h=2), s_.k)
                e3 = Vw(e_.t[:, :].rearrange("p (h k) -> p h k", h=2), e_.k)
                ph.op("vector", (lambda o, a: (lambda e: e.reduce_max(o.ap, a.ap, AX.X)))(m_[:, 0:2], s3), [s_.k], [m_.k])
                ph.tt("vector", m_[:, 2:4], m_[:, 0:2], sinkc[:, 2 * qb:2 * qb + 2], ALU.max)
                ph.ts("vector", m_[:, 4:6], m_[:, 2:4], -1.0, ALU.mult)
                for h2 in range(2):
                    ph.act(e_[:, h2 * 256:(h2 + 1) * 256], s_[:, h2 * 256:(h2 + 1) * 256], AF.Exp, bias=m_[:, 4 + h2:5 + h2])
                ph.tt("vector", m_[:, 6:8], sinkc[:, 2 * qb:2 * qb + 2], m_[:, 4:6], ALU.add)
                ph.act(m_[:, 8:10], m_[:, 6:8], AF.Exp)
                ph.op("vector", (lambda o, a: (lambda e: e.reduce_sum(o.ap, a.ap, AX.X)))(m_[:, 10:12], e3), [e_.k], [m_.k])
                ph.tt("vector", m_[:, 12:14], m_[:, 10:12], m_[:, 8:10], ALU.add)
                ph.op("vector", (lambda o, a: (lambda e: e.reciprocal(o.ap, a.ap)))(m_[:, 14:16], m_[:, 12:14]), [m_.k], [m_.k])
                for h2 in range(2):
                    ph.act(p_[:, h2 * 256:(h2 + 1) * 256], e_[:, h2 * 256:(h2 + 1) * 256], AF.Copy, scale=m_[:, 14 + h2:15 + h2])
                for i4 in range(4):
                    ph.tr(ptb[:, i4 * 128:(i4 + 1) * 128], p_[:, i4 * 128:(i4 + 1) * 128], identb[:, :])
                ph.cp("vector" if hn % 2 else "scalar", pT_[:, :], ptb[:, 0:512])
                for h2 in range(2):
                    VT = VTe if h2 == 0 else VTo
                    ph.mm(po[:, 0:128], VT[:, c, :], pT_[:, (2 * h2) * 128:(2 * h2 + 1) * 128], st=(h2 == 0), sp=False)
                    ph.mm(po[:, 0:128], VT[:, c + 1, :], pT_[:, (2 * h2 + 1) * 128:(2 * h2 + 2) * 128], st=False, sp=(h2 == 1))
                ybt = yb[(tcI % 2) * NQB + qb]
                ph.cp("scalar", ybt[:, c * 128:(c + 1) * 128], po[:, 0:128])
        for qb in range(NQB):
            ybt = yb[(tcI % 2) * NQB + qb]
            ph.stq("sync", f"yb{(tcI % 2) * NQB + qb}",
                   ybuf_t[NHP + qb].ap()[:, tcI * TC:(tcI + 1) * TC], ybt[:, :],
                   writes=["ybuf"])
    ph.emit()

    dbgcopy()
    if upto < 4:
        return nc
    cc_sem = nc.alloc_semaphore(name="cc_sem")
    with nc.Block() as block:
        @block.gpsimd
        def _(g):
            for j in range(NRB):
                g.collective_compute("AllGather", ALU.bypass, replica_groups=[[0, 1, 2, 3], [4, 5, 6, 7]],
                                     ins=[ybuf_t[j].ap().opt()], outs=[yall_t[j].ap().opt()]).then_inc(cc_sem)
            g.wait_ge(cc_sem, NRB)
    nc.clear_and_free_semaphores([cc_sem])
    nc.all_engine_barrier()

    if upto < 5:
        return nc
    ph = Phase(nc, "p5")
    cst, cd = consts(ph)
    KC = 4 * YR // 128
    cT = ph.sb([128, KC, NT], BF16, "cT")
    rk_per = YR // 128

    def ldc(e, sem):
        pid = e.partition_id()
        tq = pid % 4
        for j in range(NRB):
            yv = yall_t[j].ap().rearrange("(r p) t -> p r t", p=128)
            e.dma_start(out=cT.t[:, j * 4:(j + 1) * 4, :], in_=yv[:, :, bass.ds(tq * NT, NT)]).then_inc(sem, 16)
    ph.dma("gpsimd", "cTld", ldc, [], [cT.k], n=NRB)
    g1row = modraw[2 * NDC:3 * NDC, :].rearrange("(o a) b -> o (a b)", o=1)
    NTT = NT // 128
    xt = ph.sb([128, D], F32, "xt")
    nst = max(1, D // 512)
    stats = ph.sb([128, nst * 6], F32, "stats")
    mv = ph.sb([128, 2], F32, "mv")
    rs_all = ph.sb([128, NTT], F32, "rs")
    nm_all = ph.sb([128, NTT], F32, "nm")
    for tt_ in range(NTT):
        ph.ld("sync", "xt", xt[:, :], xq_in[tt_ * 128:(tt_ + 1) * 128, :])
        ln_stats(ph, xt, D, stats, mv, rs_all.sk(tt_)[:, tt_:tt_ + 1], nm_all.sk(tt_)[:, tt_:tt_ + 1], LN_EPS)
    DB = min(512, D)
    NDB = D // DB
    Wo = [ph.sb([128, KC, DB], BF16, "Wo") for _ in range(2)]
    psm = [ph.ps([128, DB], F32, "psm") for _ in range(4)]
    xp = [ph.sb([128, DB], F32, "xp") for _ in range(2)]
    zp = [ph.sb([128, DB], F32, "zp") for _ in range(2)]
    mg = [ph.sb([128, DB], F32, "mg") for _ in range(2)]
    gEs = [ph.sb([128, DB], F32, "gE") for _ in range(2)]
    bEs = [ph.sb([128, DB], F32, "bE") for _ in range(2)]
    G1s = [ph.sb([128, DB], F32, "G1") for _ in range(2)]
    D_A = cfg["D_A"]
    HW_ = cfg["HPC"] * 64
    it = 0

    def p5_loads(db):
        Wb = Wo[db % 2]
        gE, bE, G1 = gEs[db % 2], bEs[db % 2], G1s[db % 2]
        dsl = slice(db * DB, (db + 1) * DB)
        ph.ld("sync", f"gE{db % 2}", gE[:, :], lnr_in[0, dsl].partition_broadcast(128))
        ph.ld("sync", f"bE{db % 2}", bE[:, :], lnr_in[1, dsl].partition_broadcast(128))
        ph.ld("sync", f"G1{db % 2}", G1[:, :], g1row[0, dsl].partition_broadcast(128))
        ph.ts("vector", G1[:, :], G1[:, :], 1.0, ALU.add)
        pairs = []
        for kc in range(KC):
            j = kc // 4
            r = kc % 4
            row = (r * HW_ + j * 128) if j * 128 < HW_ else (D_A + r * (cfg["QPC"] * 64) + (j * 128 - HW_))
            pairs.append((Wb.t[:, kc, :], wout_in[row:row + 128, dsl]))
        ph.ldm("gpsimd", f"Wo{db % 2}", Wb.k, pairs)

    p5_loads(0)
    for db in range(NDB):
        if db + 1 < NDB:
            p5_loads(db + 1)
        Wb = Wo[db % 2]
        gE, bE, G1 = gEs[db % 2], bEs[db % 2], G1s[db % 2]
        dsl = slice(db * DB, (db + 1) * DB)
        for tt_ in range(NTT):
            pm_ = psm[it % 4]
            x_ = xp[it % 2]
            z_ = zp[it % 2]
            g_ = mg[it % 2]
            for kc in range(KC):
                ph.mm(pm_[:, :], cT[:, kc, tt_ * 128:(tt_ + 1) * 128], Wb[:, kc, :], st=(kc == 0), sp=(kc == KC - 1))
            ph.ld("sync", f"xp{it % 2}", x_[:, :], xq_in[tt_ * 128:(tt_ + 1) * 128, dsl])
            ph.act(x_[:, :], x_[:, :], AF.Identity, bias=nm_all.sk(tt_)[:, tt_:tt_ + 1], scale=rs_all.sk(tt_)[:, tt_:tt_ + 1])
            ph.tt("vector", x_[:, :], x_[:, :], gE[:, :], ALU.mult)
            ph.tt("vector", x_[:, :], x_[:, :], bE[:, :], ALU.add)
            ph.tt("vector", g_[:, :], pm_[:, :], G1[:, :], ALU.mult)
            ph.stt(z_[:, :], x_[:, :], ALPHA, g_[:, :], ALU.mult, ALU.add)
            ph.stq("sync", f"zst{it % 2}", zbuf[tt_ * 128:(tt_ + 1) * 128, dsl], z_[:, :], writes=["zbuf"])
            it += 1
    ph.emit()

    if upto < 6:
        return nc
    ph = Phase(nc, "p6a")
    cst, cd = consts(ph)
    identb = ph.sb([128, 128], BF16, "idb")
    ph.cp("vector", identb[:, :], cv(cst, cd, "ident"))
    identF = ph.sb([128, 128], F32, "idf")
    ph.cp("vector", identF[:, :], cv(cst, cd, "ident"))
    pstf = ph.ps([128, 128], F32, "pstf")
    modc = load_cols(ph, modraw, 6 * NDC, identF, pstf, "modc")
    lnc = load_cols(ph, lnv_in, 6 * NDC, identF, pstf, "lnc")
    A2 = ph.sb([128, NDC], F32, "A2")
    B2 = ph.sb([128, NDC], F32, "B2")
    t1 = ph.sb([128, NDC], F32, "t1")
    ph.ts("vector", t1[:, :], modc[:, 4 * NDC:5 * NDC], 1.0, ALU.add)
    ph.tt("vector", A2[:, :], t1[:, :], lnc[:, 2 * NDC:3 * NDC], ALU.mult)
    ph.tt("vector", B2[:, :], t1[:, :], lnc[:, 3 * NDC:4 * NDC], ALU.mult)
    ph.tt("vector", B2[:, :], B2[:, :], modc[:, 3 * NDC:4 * NDC], ALU.add)
    g1b = ph.sb([128, D], F32, "g1b")
    b1b = ph.sb([128, D], F32, "b1b")
    ph.ld("sync", "g1b", g1b[:, :], lnr_in[2, :].partition_broadcast(128))
    ph.ld("sync", "b1b", b1b[:, :], lnr_in[3, :].partition_broadcast(128))
    u2s = [ph.sb([128, NDC, 128], BF16, "u2s") for _ in range(2)]
    zt = ph.sb([128, D], F32, "zt")
    zh = ph.sb([128, D], BF16, "zh")
    x1t = ph.sb([128, D], F32, "x1t")
    stats = ph.sb([128, nst * 6], F32, "stats")
    mv = ph.sb([128, 2], F32, "mv")
    rstd = ph.sb([128, 1], F32, "rstd")
    nmr = ph.sb([128, 1], F32, "nmr")
    pstb = [ph.ps([128, 128], BF16, "pstb") for _ in range(4)]
    ti = 0
    for tt_ in range(NTT):
        u2 = u2s[tt_ % 2]
        ph.ld("sync", "zt", zt[:, :], zbuf[tt_ * 128:(tt_ + 1) * 128, :])
        ln_stats(ph, zt, D, stats, mv, rstd, nmr, LN_EPS)
        ph.act(zh[:, :], zt[:, :], AF.Identity, bias=nmr[:, :], scale=rstd[:, :])
        ph.act(x1t[:, :], zt[:, :], AF.Identity, bias=nmr[:, :], scale=rstd[:, :])
        ph.tt("vector", x1t[:, :], x1t[:, :], g1b[:, :], ALU.mult)
        ph.tt("vector", x1t[:, :], x1t[:, :], b1b[:, :], ALU.add)
        ph.stq("sync", "x1st", x1buf[tt_ * 128:(tt_ + 1) * 128, :], x1t[:, :])
        for k in range(NDC):
            pt = pstb[ti % 4]
            ti += 1
            ph.tr(pt[:, :], zh[:, k * 128:(k + 1) * 128], identb[:, :])
            ph.ts("vector", u2[:, k, :], pt[:, :], A2[:, k:k + 1], ALU.mult, B2[:, k:k + 1], ALU.add)
        ph.stq("sync", f"u2st{tt_ % 2}", u2Td[:, :, tt_ * 128:(tt_ + 1) * 128].rearrange("k p t -> p k t"), u2[:, :, :])
    ph.emit()

    ph = Phase(nc, "p6b")
    u2T = ph.sb([128, NDC, NT], BF16, "u2T")
    ph.ld("sync", "u2T", u2T[:, :, :], u2Td[:, :, :].rearrange("k p t -> p k t"))
    FB = 512
    NFB = DFF // FB
    TH = min(512, NT)
    NTH = NT // TH
    Wu = [ph.sb([128, NDC, FB], BF16, "Wu") for _ in range(2)]
    psu = [ph.ps([128, 512], F32, "psu") for _ in range(4)]
    hr = [ph.sb([128, TH], F32, "hr") for _ in range(2)]
    hT = [ph.sb([128, FB // 128, NT], BF16, "hT") for _ in range(2)]
    it = 0

    def p6_loads(fb):
        Wb = Wu[fb % 2]
        ph.ldm("gpsimd", f"Wu{fb % 2}", Wb.k,
               [(Wb.t[:, k, :], wup_in[k * 128:(k + 1) * 128, fb * FB:(fb + 1) * FB]) for k in range(NDC)])

    p6_loads(0)
    for fb in range(NFB):
        if fb + 1 < NFB:
            p6_loads(fb + 1)
        Wb = Wu[fb % 2]
        hb_ = hT[fb % 2]
        for c in range(FB // 128):
            for th in range(NTH):
                pu = psu[it % 4]
                h_ = hr[it % 2]
                it += 1
                for k in range(NDC):
                    ph.mm(pu[:, 0:TH], Wb[:, k, c * 128:(c + 1) * 128], u2T[:, k, th * TH:(th + 1) * TH],
                          st=(k == 0), sp=(k == NDC - 1))
                ph.act(h_[:, :], pu[:, 0:TH], AF.Relu)
                ph.tt("vector", hb_[:, c, th * TH:(th + 1) * TH], h_[:, :], h_[:, :], ALU.mult)
        ph.stq("sync", f"hst{fb % 2}", hbuf[fb * (FB // 128):(fb + 1) * (FB // 128), :, :].rearrange("c p t -> p c t"),
               hb_[:, :, :])
    ph.emit()

    if upto < 7:
        return nc
    ph = Phase(nc, "p7")
    DH = D // 2
    DQ = min(512, DH)
    NDQ = DH // DQ
    FBD = 1024
    NFD = DFF // FBD
    KB = FBD // 128
    acc = ph.sb([128, NTT, DH], F32, "acc")
    Wd = [ph.sb([128, KB, DH], BF16, "Wd") for _ in range(2)]
    hTb = [ph.sb([128, KB, NT], BF16, "hTb") for _ in range(2)]
    G2 = ph.sb([128, DH], F32, "G2")
    psd = [ph.ps([128, DQ], F32, "psd") for _ in range(4)]
    x1p = [ph.sb([128, DQ], F32, "x1p") for _ in range(2)]
    tmpds = [ph.sb([128, DQ], F32, "tmpd") for _ in range(2)]
    it = 0
    blocks = [(dh, fd) for dh in range(2) for fd in range(NFD)]

    def p7_loads(bi):
        dh, fd = blocks[bi]
        Wb = Wd[bi % 2]
        hb_ = hTb[bi % 2]
        ph.ldm("gpsimd", f"Wd{bi % 2}", Wb.k,
               [(Wb.t[:, kb, :], wdn_in[fd * FBD + kb * 128:fd * FBD + (kb + 1) * 128, dh * DH:(dh + 1) * DH])
                for kb in range(KB)])
        ph.ldm("sync", f"hTb{bi % 2}", hb_.k, [(hb_.t[:, kb, :], hbuf[fd * KB + kb, :, :]) for kb in range(KB)])

    p7_loads(0)
    for bi, (dh, fd) in enumerate(blocks):
        if bi + 1 < len(blocks):
            p7_loads(bi + 1)
        Wb = Wd[bi % 2]
        hb_ = hTb[bi % 2]
        if fd == 0:
            ph.ld("sync", "G2", G2[:, :],
                  modraw[5 * NDC:6 * NDC, :].rearrange("(o a) b -> o (a b)", o=1)[0, dh * DH:(dh + 1) * DH].partition_broadcast(128))
            ph.ts("vector", G2[:, :], G2[:, :], 1.0, ALU.add)
        for tt_ in range(NTT):
            for dq in range(NDQ):
                pd = psd[it % 4]
                it += 1
                for kb in range(KB):
                    ph.mm(pd[:, :], hb_[:, kb, tt_ * 128:(tt_ + 1) * 128], Wb[:, kb, dq * DQ:(dq + 1) * DQ],
                          st=(kb == 0), sp=(kb == KB - 1))
                a_ = acc.sk(f"{tt_}_{dq}")[:, tt_, dq * DQ:(dq + 1) * DQ]
                if fd == 0:
                    ph.cp("scalar", a_, pd[:, :])
                else:
                    ph.tt("vector", a_, pd[:, :], a_, ALU.add)
        if fd == NFD - 1:
            for tt_ in range(NTT):
                for dq in range(NDQ):
                    a_ = acc.sk(f"{tt_}_{dq}")[:, tt_, dq * DQ:(dq + 1) * DQ]
                    x_ = x1p[(tt_ * NDQ + dq) % 2]
                    c0 = dh * DH + dq * DQ
                    ph.ld("sync", f"x1p{(tt_ * NDQ + dq) % 2}", x_[:, :], x1buf[tt_ * 128:(tt_ + 1) * 128, c0:c0 + DQ])
                    ph.tt("vector", a_, a_, G2[:, dq * DQ:(dq + 1) * DQ], ALU.mult)
                    ph.stt(x_[:, :], x_[:, :], ALPHA, a_, ALU.mult, ALU.add)
                    ph.stq("sync", f"z2st{(tt_ * NDQ + dq) % 2}", z2buf[tt_ * 128:(tt_ + 1) * 128, c0:c0 + DQ], x_[:, :])
    ph.emit()

    ph = Phase(nc, "p8")
    g2b = ph.sb([128, D], F32, "g2b")
    b2b = ph.sb([128, D], F32, "b2b")
    ph.ld("sync", "g2b", g2b[:, :], lnr_in[4, :].partition_broadcast(128))
    ph.ld("sync", "b2b", b2b[:, :], lnr_in[5, :].partition_broadcast(128))
    zts = [ph.sb([128, D], F32, "zt") for _ in range(2)]
    ots = [ph.sb([128, D], F32, "ot") for _ in range(2)]
    stats = ph.sb([128, nst * 6], F32, "stats")
    mv = ph.sb([128, 2], F32, "mv")
    rstd = ph.sb([128, 1], F32, "rstd")
    nmr = ph.sb([128, 1], F32, "nmr")
    for tt_ in range(NTT):
        zt = zts[tt_ % 2]
        ot = ots[tt_ % 2]
        ph.ld("sync", f"zt{tt_ % 2}", zt[:, :], z2buf[tt_ * 128:(tt_ + 1) * 128, :], reads=["z2buf"])
        ln_stats(ph, zt, D, stats, mv, rstd, nmr, LN_EPS)
        ph.act(ot[:, :], zt[:, :], AF.Identity, bias=nmr[:, :], scale=rstd[:, :])
        ph.tt("vector", ot[:, :], ot[:, :], g2b[:, :], ALU.mult)
        ph.tt("vector", ot[:, :], ot[:, :], b2b[:, :], ALU.add)
        ph.stq("sync", f"ost{tt_ % 2}", out_t[tt_ * 128:(tt_ + 1) * 128, :], ot[:, :], writes=["out"])
    ph.op("vector", lambda e: e.memset(nmr.t[:, :], 0.0), ["out"], [nmr.k])
    ph.emit()
    return nc


def t5_bucket(dist):
    n = np.maximum(dist, 0)
    nf = np.maximum(n, 1).astype(np.float32)
    large = 16 + (np.log(nf / 16) / math.log(128 / 16) * (32 - 16)).astype(np.int32)
    large = np.minimum(large, 31)
    return np.where(n < 16, n, large)


def host_inputs(cfg, inp):
    D, S, NDC = cfg["D"], cfg["S"], cfg["NDC"]
    HPC, NHP, QPC, NSUB, HP_SUB, NB, NAB = cfg["HPC"], cfg["NHP"], cfg["QPC"], cfg["NSUB"], cfg["HP_SUB"], cfg["NB"], cfg["NAB"]
    LW, LA, LG, NT = cfg["LW"], cfg["LA"], cfg["LG"], cfg["NT"]
    f = lambda a: np.ascontiguousarray(np.asarray(a, dtype=np.float32))
    x = f(inp["x"])
    c = f(inp["c"])
    w_in = f(inp["w_in"])[0]
    mu_shift = f(inp["mu_shift"])[0]
    lnrows = np.stack([f(inp["ln_emb_g"]), f(inp["ln_emb_b"]), f(inp["ln1_g"])[0], f(inp["ln1_b"])[0],
                       f(inp["ln2_g"])[0], f(inp["ln2_b"])[0]], 0)
    j = np.arange(128)[:, None]
    s_ = np.arange(128)[None, :]
    ident = np.eye(128, dtype=np.float32)
    mus = (j < s_).astype(np.float32)
    mui = (j <= s_).astype(np.float32)
    mls = (j > s_).astype(np.float32)
    bo = np.kron(np.eye(2, dtype=np.float32), np.ones((64, 64), np.float32))
    ones = np.ones((128, 128), np.float32)
    NEG = -30000.0
    amask_prev = np.where(s_ > j, 0.0, NEG).astype(np.float32)
    amask_cur = np.where(s_ <= j, 0.0, NEG).astype(np.float32)
    cst = np.concatenate([ident, mus, mui, mls, bo, ones, amask_prev, amask_cur], 1)
    qi = np.arange(128)[:, None]
    kj = np.arange(256)[None, :]
    bucket = t5_bucket(qi + 128 - kj)
    rpb = f(inp["rpb_table"])
    zero128 = np.zeros((D, 1), np.float32)
    maps = []
    for core in range(8):
        b, hg = core // 4, core % 4
        m = {}
        m["x"] = x[b]
        m["xq"] = np.ascontiguousarray(x[b, hg * NT:(hg + 1) * NT])
        m["ccol"] = np.ascontiguousarray(c[b].reshape(NDC, 128).T)
        MQ = 6 * D // 4
        m["wmod"] = np.ascontiguousarray(f(inp["w_mod"])[0][:, hg * MQ:(hg + 1) * MQ])
        m["bmod"] = np.ascontiguousarray(f(inp["b_mod"])[0][None, hg * MQ:(hg + 1) * MQ])
        m["lnv"] = np.ascontiguousarray(lnrows.reshape(6 * NDC, 128))
        m["lnr"] = lnrows
        wrw = np.zeros((NSUB, D, NB * 128), np.float32)
        mu = np.zeros((128, NSUB * NB), np.float32)
        for sp in range(NSUB):
            blocks = []
            for off in (0, cfg["OFF_K"], cfg["OFF_V"]):
                for hl in range(HP_SUB):
                    hp = sp * HP_SUB + hl
                    c0 = off + (hg * HPC + 2 * hp) * 64
                    blocks.append((c0, 128))
            blocks += [(cfg["OFF_W"], LW), (cfg["OFF_A"], LA), (cfg["OFF_G"], LG)]
            for jb, (c0, n) in enumerate(blocks):
                wrw[sp, :, jb * 128:jb * 128 + n] = w_in[:, c0:c0 + n]
                mu[0:n, sp * NB + jb] = mu_shift[c0:c0 + n]
        m["wrw"] = wrw
        m["mu"] = mu
        wat = np.zeros((D, NAB * 128), np.float32)
        for i in range(QPC // 2):
            c0 = cfg["OFF_Q"] + (hg * QPC + 2 * i) * 64
            wat[:, i * 128:(i + 1) * 128] = w_in[:, c0:c0 + 128]
        kc0 = cfg["OFF_KB"] + hg * 64
        vc0 = cfg["OFF_VB"] + hg * 64
        jb = QPC // 2
        wat[:, jb * 128:jb * 128 + 64] = w_in[:, kc0:kc0 + 64]
        wat[:, jb * 128 + 64:jb * 128 + 128] = w_in[:, kc0:kc0 + 64]
        wat[:, (jb + 1) * 128:(jb + 1) * 128 + 64] = w_in[:, vc0:vc0 + 64]
        wat[:, (jb + 1) * 128 + 64:(jb + 2) * 128] = w_in[:, vc0:vc0 + 64]
        m["wat"] = wat
        chp = np.zeros((128, 7 * NHP), np.float32)
        vecs = [f(inp["w0"])[0], f(inp["a0"])[0], f(inp["k_k"])[0], f(inp["k_a"])[0], f(inp["r_k"])[0].reshape(-1),
                f(inp["lnx_g"])[0], f(inp["lnx_b"])[0]]
        wdec = np.zeros((128, NHP * 128), np.float32)
        wicl = np.zeros((128, NHP * 128), np.float32)
        wgat = np.zeros((128, NHP * 128), np.float32)
        for hp in range(NHP):
            a0_ = (hg * HPC + 2 * hp) * 64
            for i, v in enumerate(vecs):
                chp[:, i * NHP + hp] = v[a0_:a0_ + 128]
            wdec[0:LW, hp * 128:(hp + 1) * 128] = f(inp["w_decay_up"])[0][:, a0_:a0_ + 128]
            wicl[0:LA, hp * 128:(hp + 1) * 128] = f(inp["w_iclr_up"])[0][:, a0_:a0_ + 128]
            wgat[0:LG, hp * 128:(hp + 1) * 128] = f(inp["w_gate_up"])[0][:, a0_:a0_ + 128]
        m["chp"], m["wdec"], m["wicl"], m["wgat"] = chp, wdec, wicl, wgat
        bias = np.zeros((128, QPC * 256), np.float32)
        sinks = np.zeros((128, QPC), np.float32)
        for ql in range(QPC):
            hq = hg * QPC + ql
            bias[:, ql * 256:(ql + 1) * 256] = rpb[bucket, hq]
            sinks[:, ql] = f(inp["attn_sinks"])[0][hq]
        m["biasat"] = bias
        m["sinks"] = sinks
        m["cst"] = cst
        m["wout"] = f(inp["w_out"])[0]
        m["wup"] = f(inp["w_up"])[0]
        m["wdn"] = f(inp["w_down"])[0]
        maps.append(m)
    return maps


_NC_CACHE = {}


def kernel(**inputs):
    cfg = make_cfg()
    if "nc" not in _NC_CACHE:
        _NC_CACHE["nc"] = build(cfg)
    nc = _NC_CACHE["nc"]
    maps = host_inputs(cfg, inputs)
    res = run_bass_kernel_spmd(nc, maps, core_ids=list(range(8)))
    NT = cfg["NT"]
    out = np.zeros((2, cfg["S"], cfg["D"]), np.float32)
    for core in range(8):
        b, q = core // 4, core % 4
        out[b, q * NT:(q + 1) * NT] = res.results[core]["out"]
    return out
```

```python
import math
import os
CUT = int(os.environ.get('KCUT', '99'))
from contextlib import ExitStack
import numpy as np
import concourse.bass as bass
import concourse.mybir as mybir
from concourse.bass_utils import run_bass_kernel_spmd

F32 = mybir.dt.float32
BF16 = mybir.dt.bfloat16
ALU = mybir.AluOpType
AF = mybir.ActivationFunctionType
AX = mybir.AxisListType

LN_EPS = 1e-5
LNX_EPS = 64e-5
ALPHA = 2.0 ** 0.25
C0 = math.exp(-0.5)


def make_cfg(D=4096, S=4096, GQA=8):
    c = dict(D=D, S=S, GQA=GQA)
    c["D_A"] = D // 2
    c["H_A"] = c["D_A"] // 64
    c["H_Q"] = c["H_A"]
    c["H_KV"] = c["H_Q"] // GQA
    assert c["H_KV"] == 4
    c["HPC"] = c["H_A"] // 4
    c["NHP"] = c["HPC"] // 2
    c["QPC"] = c["H_Q"] // 4
    c["LW"] = max(32, int(round(c["D_A"] ** 0.5 * 1.8 / 32)) * 32)
    c["LA"] = c["LW"]
    c["LG"] = max(32, int(round(c["D_A"] ** 0.6 * 0.8 / 32)) * 32)
    c["DFF"] = 4 * D
    c["NDC"] = D // 128
    c["HP_SUB"] = min(2, c["NHP"])
    c["NSUB"] = c["NHP"] // c["HP_SUB"]
    c["NB"] = 3 * c["HP_SUB"] + 3
    c["NAB"] = c["QPC"] // 2 + 2
    c["YR"] = D // 4
    c["NT"] = S // 4
    c["TC"] = min(512, S)
    c["OFF_K"] = c["D_A"]
    c["OFF_V"] = 2 * c["D_A"]
    c["OFF_W"] = 3 * c["D_A"]
    c["OFF_A"] = c["OFF_W"] + c["LW"]
    c["OFF_G"] = c["OFF_A"] + c["LA"]
    c["RWKV_COLS"] = c["OFF_G"] + c["LG"]
    c["OFF_Q"] = c["RWKV_COLS"]
    c["OFF_KB"] = c["OFF_Q"] + c["D_A"]
    c["OFF_VB"] = c["OFF_KB"] + 4 * 64
    c["N_IN"] = c["OFF_VB"] + 4 * 64
    return c


class Vw:
    def __init__(s, ap, k):
        s.ap = ap
        s.k = k


class Buf:
    def __init__(s, t, k):
        s.t = t
        s.k = k

    def __getitem__(s, idx):
        return Vw(s.t[idx], s.k)

    def sk(s, suf):
        return Buf(s.t, s.k + "/" + str(suf))


class Phase:
    ENG = ("tensor", "vector", "scalar", "gpsimd", "sync")

    def __init__(s, nc, name):
        s.nc = nc
        s.name = name
        s.ops = {e: [] for e in s.ENG}
        s.cnt = {e: 0 for e in s.ENG}
        s.dcnt = {}
        s.lastw = {}
        s.rd = {}
        s.mem = ExitStack()
        s.nbuf = 0

    def sb(s, shape, dt, key=None):
        s.nbuf += 1
        nm = f"{s.name}_{key or 'b'}{s.nbuf}"
        return Buf(s.mem.enter_context(s.nc.sbuf_tensor(nm, list(shape), dt)), nm)

    def ps(s, shape, dt, key=None):
        s.nbuf += 1
        nm = f"{s.name}_{key or 'p'}{s.nbuf}"
        return Buf(s.mem.enter_context(s.nc.psum_tensor(nm, list(shape), dt)), nm)

    def _deps(s, reads, writes):
        d = {}

        def add(tok):
            if tok is not None and d.get(tok[0], 0) < tok[1]:
                d[tok[0]] = tok[1]
        for b in reads:
            add(s.lastw.get(b))
        for b in writes:
            add(s.lastw.get(b))
            for k, v in s.rd.get(b, {}).items():
                add((k, v))
        return d

    def _commit(s, tok, reads, writes):
        for b in writes:
            s.lastw[b] = tok
            s.rd[b] = {}
        for b in reads:
            r = s.rd.setdefault(b, {})
            if r.get(tok[0], 0) < tok[1]:
                r[tok[0]] = tok[1]

    def op(s, eng, fn, reads=(), writes=()):
        if getattr(s, "dead", False):
            return
        d = s._deps(reads, writes)
        s.cnt[eng] += 1
        tok = (("E", eng), s.cnt[eng])
        s._commit(tok, reads, writes)
        s.ops[eng].append((fn, d, tok, False))

    def dma(s, eng, key, fn, reads=(), writes=(), n=1):
        if getattr(s, "dead", False):
            return
        d = s._deps(reads, writes)
        c = s.dcnt.get(key, 0) + 16 * n
        s.dcnt[key] = c
        tok = (("D", key), c)
        s._commit(tok, reads, writes)
        s.ops[eng].append((fn, d, tok, True))

    def emit(s):
        nc = s.nc
        sems = {}
        handles = []
        for e in s.ENG:
            h = nc.alloc_semaphore(name=f"{s.name}_e_{e}")
            sems[("E", e)] = h
            handles.append(h)
        for i, k in enumerate(s.dcnt):
            h = nc.alloc_semaphore(name=f"{s.name}_d{i}")
            sems[("D", k)] = h
            handles.append(h)
        with nc.Block() as block:
            for e in s.ENG:
                if not s.ops[e] and e != "sync":
                    continue

                def body(eng, e=e):
                    waited = {}
                    for fn, d, tok, isdma in s.ops[e]:
                        for k, v in d.items():
                            if e == "tensor" and k == ("E", "tensor"):
                                continue
                            if waited.get(k, 0) < v:
                                eng.wait_ge(sems[k], v)
                                waited[k] = v
                        if isdma:
                            fn(eng, sems[tok[0]])
                        else:
                            fn(eng).then_inc(sems[tok[0]], 1)
                    if e == "sync":
                        for k, v in s.dcnt.items():
                            if waited.get(("D", k), 0) < v:
                                eng.wait_ge(sems[("D", k)], v)
                getattr(block, e)(body)
        nc.clear_and_free_semaphores(handles)
        nc.all_engine_barrier()
        s.mem.close()

    def mm(s, out, lhsT, rhs, st=True, sp=True):
        s.op("tensor", lambda e: e.matmul(out.ap, lhsT.ap, rhs.ap, start=st, stop=sp),
             [lhsT.k, rhs.k], [out.k])

    def tr(s, out, in_, ident):
        s.op("tensor", lambda e: e.transpose(out.ap, in_.ap, ident.ap), [in_.k, ident.k], [out.k])

    def act(s, out, in_, func, bias=None, scale=None, extra=()):
        rk = [in_.k] + list(extra)
        kw = {}
        if bias is not None:
            if isinstance(bias, Vw):
                kw["bias"] = bias.ap
                rk.append(bias.k)
            else:
                kw["bias"] = bias
        if scale is not None:
            if isinstance(scale, Vw):
                kw["scale"] = scale.ap
                rk.append(scale.k)
            else:
                kw["scale"] = scale
        s.op("scalar", lambda e: e.activation(out.ap, in_.ap, func, **kw), rk, [out.k])

    def tt(s, eng, out, in0, in1, op):
        s.op(eng, lambda e: e.tensor_tensor(out.ap, in0.ap, in1.ap, op), [in0.k, in1.k], [out.k])

    def ts(s, eng, out, in0, s1, op0, s2=None, op1=None):
        rk = [in0.k]
        a1 = s1
        a2 = s2
        if isinstance(s1, Vw):
            a1 = s1.ap
            rk.append(s1.k)
        if isinstance(s2, Vw):
            a2 = s2.ap
            rk.append(s2.k)
        if op1 is None:
            s.op(eng, lambda e: e.tensor_scalar(out.ap, in0.ap, a1, None, op0), rk, [out.k])
        else:
            s.op(eng, lambda e: e.tensor_scalar(out.ap, in0.ap, a1, a2, op0, op1), rk, [out.k])

    def rsqrt(s, out, in_, addc):
        s.op("scalar", lambda e: e.activation(out.ap, in_.ap, AF.Sqrt, bias=addc), [in_.k], [out.k])
        s.op("vector", lambda e: e.reciprocal(out.ap, out.ap), [out.k], [out.k])

    def stt(s, out, in0, sc, in1, op0, op1):
        rk = [in0.k, in1.k]
        a = sc
        if isinstance(sc, Vw):
            a = sc.ap
            rk.append(sc.k)
        s.op("vector", lambda e: e.scalar_tensor_tensor(out.ap, in0.ap, a, in1.ap, op0, op1), rk, [out.k])

    def cp(s, eng, out, in_):
        if eng == "scalar":
            s.op("scalar", lambda e: e.activation(out.ap, in_.ap, AF.Copy), [in_.k], [out.k])
        else:
            s.op(eng, lambda e: e.tensor_copy(out.ap, in_.ap), [in_.k], [out.k])

    def ms(s, eng, out, val):
        s.op(eng, lambda e: e.memset(out.ap, val), [], [out.k])

    def ld(s, eng, key, out, in_ap, reads=(), **kw):
        s.dma(eng, key, lambda e, sem: e.dma_start(out=out.ap, in_=in_ap, **kw).then_inc(sem, 16),
              list(reads), [out.k])

    def ldm(s, eng, key, out_key, pairs, reads=()):
        def fn(e, sem):
            for o, i in pairs:
                e.dma_start(out=o, in_=i).then_inc(sem, 16)
        s.dma(eng, key, fn, list(reads), [out_key], n=len(pairs))

    def stq(s, eng, key, out_ap, in_, writes=(), **kw):
        s.dma(eng, key, lambda e, sem: e.dma_start(out=out_ap, in_=in_.ap, **kw).then_inc(sem, 16),
              [in_.k], list(writes))


def load_cols(ph, rows_ap, nrows, identf, pst, key):
    out = ph.sb([128, nrows], F32, key)
    r0 = 0
    i = 0
    while r0 < nrows:
        n = min(128, nrows - r0)
        tmp = ph.sb([128, 128], F32, key + "t")
        ph.ld("sync", f"{key}ld{i}", tmp[0:n, :], rows_ap[r0:r0 + n, :])
        ph.tr(pst[:, 0:n], tmp[0:n, :], identf[0:n, 0:n])
        ph.cp("vector", out[:, r0:r0 + n], pst[:, 0:n])
        r0 += n
        i += 1
    return out


def ln_stats(ph, xt, D, stats, mv, rstd, nmr, eps):
    nch = max(1, D // 512)
    w = D // nch
    for i in range(nch):
        ph.op("vector", (lambda o, a: (lambda e: e.bn_stats(o.ap, a.ap)))(stats[:, i * 6:(i + 1) * 6], xt[:, i * w:(i + 1) * w]),
              [xt.k], [stats.k])
    ph.op("vector", (lambda o, a: (lambda e: e.bn_aggr(o.ap, a.ap)))(mv[:, :], stats[:, :]), [stats.k], [mv.k])
    rv = rstd if isinstance(rstd, Vw) else rstd[:, :]
    nv = nmr if isinstance(nmr, Vw) else nmr[:, :]
    ph.rsqrt(rv, mv[:, 1:2], eps)
    ph.stt(nv, mv[:, 0:1], -1.0, rv, ALU.mult, ALU.mult)


def build(cfg, debug=False, upto=99):
    D, S = cfg["D"], cfg["S"]
    NDC, NB, NAB, NHP, HP_SUB, NSUB = cfg["NDC"], cfg["NB"], cfg["NAB"], cfg["NHP"], cfg["HP_SUB"], cfg["NSUB"]
    QPC, LW, LA, LG, DFF, YR, NT, TC = cfg["QPC"], cfg["LW"], cfg["LA"], cfg["LG"], cfg["DFF"], cfg["YR"], cfg["NT"], cfg["TC"]
    NCH = TC // 128
    NTC = S // TC
    NQB = QPC // 2
    nc = bass.Bass("TRN2", target_bir_lowering=False)

    def din(name, shape, dt=F32):
        return nc.dram_tensor(name, list(shape), dt, kind="ExternalInput").ap()

    x_in = din("x", [S, D])
    xq_in = din("xq", [NT, D])
    ccol_in = din("ccol", [128, NDC])
    wmod_in = din("wmod", [D, 6 * D // 4])
    bmod_in = din("bmod", [1, 6 * D // 4])
    lnv_in = din("lnv", [6 * NDC, 128])
    lnr_in = din("lnr", [6, D])
    wrw_in = din("wrw", [NSUB, D, NB * 128])
    wat_in = din("wat", [D, NAB * 128])
    mu_in = din("mu", [128, NSUB * NB])
    chp_in = din("chp", [128, 7 * NHP])
    wdec_in = din("wdec", [128, NHP * 128])
    wicl_in = din("wicl", [128, NHP * 128])
    wgat_in = din("wgat", [128, NHP * 128])
    bias_in = din("biasat", [128, QPC * 256])
    sink_in = din("sinks", [128, QPC])
    cst_in = din("cst", [128, 8 * 128])
    wout_in = din("wout", [D, D])
    wup_in = din("wup", [D, DFF])
    wdn_in = din("wdn", [DFF, D])
    out_t = nc.dram_tensor("out", [NT, D], F32, kind="ExternalOutput").ap()

    modraw_t = nc.dram_tensor("modraw", [6 * NDC, 128], F32)
    modraw = modraw_t.ap()
    modpart_t = nc.dram_tensor("modpart", [6 * NDC // 4, 128], F32)
    modpart = modpart_t.ap()
    uTd = nc.dram_tensor("uTd", [NDC, 128, S], BF16).ap()
    NRB = YR // 128
    ybuf_t = [nc.dram_tensor(f"ybuf{j}", [128, S], BF16) for j in range(NRB)]
    yall_t = [nc.dram_tensor(f"yall{j}", [4 * 128, S], BF16) for j in range(NRB)]
    ydbg = nc.dram_tensor("ydbg", [YR, S], BF16, kind="ExternalOutput").ap() if debug else None
    zbuf = nc.dram_tensor("zbuf", [NT, D], F32, **(dict(kind="ExternalOutput") if debug else dict())).ap()
    x1buf = nc.dram_tensor("x1buf", [NT, D], F32, **(dict(kind="ExternalOutput") if debug else dict())).ap()
    hbuf = nc.dram_tensor("hbuf", [DFF // 128, 128, NT], BF16, **(dict(kind="ExternalOutput") if debug else dict())).ap()
    z2buf = nc.dram_tensor("z2buf", [NT, D], F32, **(dict(kind="ExternalOutput") if debug else dict())).ap()
    if debug:
        d7h = nc.dram_tensor("d7h", [128, (1024 // 128), NT], BF16, kind="ExternalOutput").ap()
        d7w = nc.dram_tensor("d7w", [128, (1024 // 128), D // 2], BF16, kind="ExternalOutput").ap()
        d7p = nc.dram_tensor("d7p", [128, min(512, D // 2)], F32, kind="ExternalOutput").ap()
        d7acc = nc.dram_tensor("d7acc", [128, D // 2], F32, kind="ExternalOutput").ap()
    u2Td = nc.dram_tensor("u2Td", [NDC, 128, NT], BF16, **(dict(kind="ExternalOutput") if debug else dict())).ap()

    def dbgcopy():
        if debug and not _dbgdone:
            _dbgdone.append(1)
            dbg_sem0 = nc.alloc_semaphore(name="dbg_sem0")
            with nc.Block() as block:
                @block.gpsimd
                def _(g):
                    for j in range(NRB):
                        g.dma_start(out=ydbg[j * 128:(j + 1) * 128, :], in_=ybuf_t[j].ap()).then_inc(dbg_sem0, 16)
                    g.wait_ge(dbg_sem0, 16 * NRB)
            nc.clear_and_free_semaphores([dbg_sem0])
            nc.all_engine_barrier()
    _dbgdone = []

    def consts(ph):
        c = ph.sb([128, 8 * 128], F32, "cst")
        ph.ld("sync", "cst", c[:, :], cst_in[:, :])
        names = ["ident", "mus", "mui", "mls", "bo", "ones", "amp", "amc"]
        d = {n: c.t[:, i * 128:(i + 1) * 128] for i, n in enumerate(names)}
        return c, d

    def cv(c, d, n, sl=None):
        ap = d[n]
        if sl is not None:
            ap = ap[sl]
        return Vw(ap, c.k)

    ph = Phase(nc, "p0")
    ccol = ph.sb([128, NDC], F32)
    cond = ph.sb([128, NDC], BF16)
    ph.ld("sync", "ccol", ccol[:, :], ccol_in[:, :])
    ph.act(cond[:, :], ccol[:, :], AF.Silu)
    MQ = 6 * D // 4
    GW = 2048 if MQ % 2048 == 0 else 1536
    assert MQ % GW == 0 and GW % 512 == 0
    NG = MQ // GW
    NQ_ = GW // 512
    wms = [ph.sb([128, GW], BF16, "wm") for _ in range(4)]
    psb = [[ph.ps([128, 512], F32, "pm") for _ in range(NQ_)] for _ in range(2)]
    bms = [ph.sb([1, GW], F32, "bm") for _ in range(2)]
    mds = [ph.sb([1, GW], F32, "md") for _ in range(2)]
    it = 0
    for g in range(NG):
        pb = psb[g % 2]
        ph.ld("sync", f"bm{g % 2}", bms[g % 2][:, :], bmod_in[0:1, g * GW:(g + 1) * GW])
        for k in range(NDC):
            wm = wms[it % 4]
            ph.ld("gpsimd", f"wm{it % 4}", wm[:, :], wmod_in[k * 128:(k + 1) * 128, g * GW:(g + 1) * GW])
            it += 1
            for q in range(NQ_):
                ph.mm(pb[q][0:1, :], cond[:, k:k + 1], wm[:, q * 512:(q + 1) * 512], st=(k == 0), sp=(k == NDC - 1))
        for q in range(NQ_):
            ph.tt("vector", mds[g % 2][0:1, q * 512:(q + 1) * 512], pb[q][0:1, :],
                  bms[g % 2][0:1, q * 512:(q + 1) * 512], ALU.add)
        ph.stq("sync", f"mdst{g % 2}",
               modpart[g * (GW // 128):(g + 1) * (GW // 128), :].rearrange("(o a) b -> o (a b)", o=1),
               mds[g % 2][0:1, :], writes=["modpart"])
    ph.emit()
    cc0 = nc.alloc_semaphore(name="cc0_sem")
    with nc.Block() as block:
        @block.gpsimd
        def _(g):
            g.collective_compute("AllGather", ALU.bypass, replica_groups=[[0, 1, 2, 3], [4, 5, 6, 7]],
                                 ins=[modpart_t.ap().opt()], outs=[modraw_t.ap().opt()]).then_inc(cc0)
            g.wait_ge(cc0, 1)
    nc.clear_and_free_semaphores([cc0])
    nc.all_engine_barrier()

    if upto < 1:
        return nc
    ph = Phase(nc, "p1")
    cst, cd = consts(ph)
    identb = ph.sb([128, 128], BF16, "idb")
    ph.cp("vector", identb[:, :], cv(cst, cd, "ident"))
    pstf = ph.ps([128, 128], F32, "pstf")
    identF = ph.sb([128, 128], F32, "idf")
    ph.cp("vector", identF[:, :], cv(cst, cd, "ident"))
    modc = load_cols(ph, modraw, 6 * NDC, identF, pstf, "modc")
    lnc = load_cols(ph, lnv_in, 6 * NDC, identF, pstf, "lnc")
    A1 = ph.sb([128, NDC], F32, "A1")
    B1 = ph.sb([128, NDC], F32, "B1")
    t1 = ph.sb([128, NDC], F32, "t1")
    ph.ts("vector", t1[:, :], modc[:, NDC:2 * NDC], 1.0, ALU.add)
    ph.tt("vector", A1[:, :], t1[:, :], lnc[:, 0:NDC], ALU.mult)
    ph.tt("vector", B1[:, :], t1[:, :], lnc[:, NDC:2 * NDC], ALU.mult)
    ph.tt("vector", B1[:, :], B1[:, :], modc[:, 0:NDC], ALU.add)
    xts = [ph.sb([128, D], F32, "xt") for _ in range(2)]
    xhs = [ph.sb([128, D], BF16, "xh") for _ in range(2)]
    uTt = [ph.sb([128, NDC, TC], BF16, "uTt") for _ in range(2)]
    nst = max(1, D // 512)
    stats = ph.sb([128, nst * 6], F32, "stats")
    mv = ph.sb([128, 2], F32, "mv")
    rstd = ph.sb([128, 1], F32, "rstd")
    nmr = ph.sb([128, 1], F32, "nmr")
    pstb = [ph.ps([128, 128], BF16, "pstb") for _ in range(7)]
    lnt = [(ph.sb([128, nst * 6], F32, "stats"), ph.sb([128, 2], F32, "mv"), ph.sb([128, 1], F32, "rstd"),
            ph.sb([128, 1], F32, "nmr")) for _ in range(2)]
    ti = [0]
    NTL = NTC * NCH

    def p1_A(i):
        xt = xts[i % 2]
        xh = xhs[i % 2]
        ph.ld("sync", f"xt{i % 2}", xt[:, :], x_in[i * 128:(i + 1) * 128, :])
        stats, mv, rstd, nmr = lnt[i % 2]
        ln_stats(ph, xt, D, stats, mv, rstd, nmr, LN_EPS)
        ph.act(xh[:, :], xt[:, :], AF.Identity, bias=nmr[:, :], scale=rstd[:, :])

    def p1_B(i):
        tcI, c = i // NCH, i % NCH
        ub = uTt[tcI % 2]
        xh = xhs[i % 2]
        for k in range(NDC):
            pt = pstb[ti[0] % 7]
            ti[0] += 1
            ph.tr(pt[:, :], xh[:, k * 128:(k + 1) * 128], identb[:, :])
            if k % 2 == 0:
                ph.ts("vector", ub[:, k, c * 128:(c + 1) * 128], pt[:, :], A1[:, k:k + 1], ALU.mult,
                      B1[:, k:k + 1], ALU.add)
            else:
                ph.act(ub[:, k, c * 128:(c + 1) * 128], pt[:, :], AF.Identity, bias=B1[:, k:k + 1],
                       scale=A1[:, k:k + 1])
        if c == NCH - 1:
            ph.stq("gpsimd", f"uTst{tcI % 2}", uTd[:, :, tcI * TC:(tcI + 1) * TC].rearrange("k p t -> p k t"),
                   ub[:, :, :])

    p1_A(0)
    for i in range(NTL):
        if i + 1 < NTL:
            p1_A(i + 1)
        p1_B(i)
    ph.emit()

    if upto < 2:
        dbgcopy()
        return nc
    TC_save = TC
    TC = min(256, S)
    NCH = TC // 128
    NTC = S // TC
    for sp in range(NSUB):
        ph = Phase(nc, f"p2{sp}")
        cst, cd = consts(ph)
        IDN = cv(cst, cd, "ident")
        MUS = cv(cst, cd, "mus")
        MUI = cv(cst, cd, "mui")
        MLS = cv(cst, cd, "mls")
        BO = cv(cst, cd, "bo")
        ONES = cv(cst, cd, "ones")
        W = ph.sb([128, NDC, NB * 128], BF16, "W")
        ph.ldm("gpsimd", "Wld", W.k, [(W.t[:, k, :], wrw_in[sp, k * 128:(k + 1) * 128, :]) for k in range(NDC)])
        mu = ph.sb([128, NB], F32, "mu")
        omu = ph.sb([128, NB], F32, "omu")
        ph.ld("sync", "mu", mu[:, :], mu_in[:, sp * NB:(sp + 1) * NB])
        ph.ts("vector", omu[:, :], mu[:, :], -1.0, ALU.mult, 1.0, ALU.add)
        chp = ph.sb([128, 7 * NHP], F32, "chp")
        ph.ld("sync", "chp", chp[:, :], chp_in[:, :])
        omka = ph.sb([128, NHP], F32, "omka")
        ph.ts("vector", omka[:, :], chp[:, 3 * NHP:4 * NHP], -1.0, ALU.mult, 1.0, ALU.add)
        g8 = ph.sb([128, NHP], F32, "g8")
        ph.ts("vector", g8[:, :], chp[:, 5 * NHP:6 * NHP], 8.0, ALU.mult)
        wdec = ph.sb([128, NHP * 128], F32, "wdec")
        wicl = ph.sb([128, NHP * 128], F32, "wicl")
        wgat = ph.sb([128, NHP * 128], F32, "wgat")
        ph.ld("sync", "wdec", wdec[:, :], wdec_in[:, :])
        ph.ld("sync", "wicl", wicl[:, :], wicl_in[:, :])
        ph.ld("sync", "wgat", wgat[:, :], wgat_in[:, :])
        uTs = [ph.sb([128, NDC, TC], BF16, "uT") for _ in range(2)]
        praw = [ph.sb([128, TC + 1], F32, "praw") for _ in range(2)]
        carry = ph.sb([128, NB], F32, "carry")
        ph.ms("vector", carry[:, :], 0.0)
        tmpa = [ph.sb([128, TC], F32, "tmpa") for _ in range(2)]
        sh = [ph.sb([128, TC], F32, f"sh{j}") for j in range(NB)]
        psw = [ph.ps([128, 512], F32, "psw") for _ in range(3)]
        psq_b = [ph.ps([128, 512], F32, "psq") for _ in range(5)]
        NPQ = int(os.environ.get('NPQ', '5'))
        psq = [psq_b[i % 5].sk(i // 5)[:, (i // 5) * 128:(i // 5 + 1) * 128] for i in range(NPQ)]
        qn = [0]
        wn = [0]

        def nq():
            qn[0] += 1
            return psq[qn[0] % NPQ]

        def nw():
            wn[0] += 1
            return psw[wn[0] % 3]

        def F(key):
            return ph.sb([128, TC], F32, key)

        sg, et, gt, kk, kkn, kp, bv, cs, Wt, Wi, Wp, rt, kt, bt, at_, bon, tq1, yraw, yc = [
            F(n) for n in ["sg", "et", "gt", "kk", "kkn", "kp", "bv", "cs", "Wt", "Wi", "Wp", "rt", "kt", "bt",
                           "at", "bon", "tq1", "yraw", "yc"]]
        yo = [ph.sb([128, TC], BF16, "yo") for _ in range(2)]
        btm = [F("btm"), F("btm")]
        ktm = [F("ktm"), F("ktm")]
        atm = [F("atm"), F("atm")]
        ST = [ph.sb([128, 128], F32, f"ST{h}") for h in range(HP_SUB)]
        for h in range(HP_SUB):
            ph.ms("vector", ST[h][:, :], 0.0)

        def M(key):
            return ph.sb([128, 128], F32, key)

        mats = {}
        for c2 in range(2):
            for h in range(2):
                mats[(c2, h)] = dict(P=[M("P"), M("P")], PT=[M("PT"), M("PT")], T=M("T"), AKA=M("AKA"),
                                     ABR=M("ABR"), AKR=M("AKR"))
        pairm = []
        for c2 in range(2):
            d = dict(VTp=M("VTp"), VTe=M("VTe"), VTo=M("VTo"), BTT=M("BTT"), KTT=M("KTT"), RHS=M("RHS"),
                     SAp=M("SAp"), SAe=M("SAe"), SAo=M("SAo"))
            for n in ["VTe", "VTo", "SAe", "SAo"]:
                ph.ms("gpsimd", d[n][:, :], 0.0)
            pairm.append(d)
        NLEV = 6

        blk_lora = [NB - 3, NB - 2, NB - 1]

        def hp_blocks(hl_):
            return [hl_, HP_SUB + hl_, 2 * HP_SUB + hl_]
        xw, xa, xg = sh[NB - 3], sh[NB - 2], sh[NB - 1]

        def load_uT(t):
            ph.ld("sync", f"uT{t % 2}", uTs[t % 2][:, :, :],
                  uTd[:, :, t * TC:(t + 1) * TC].rearrange("k p t -> p k t"))

        def inproj(t, blocks):
            uT = uTs[t % 2]
            for j in blocks:
                bank = nw()
                for k in range(NDC):
                    ph.mm(bank[:, 0:TC], W[:, k, j * 128:(j + 1) * 128], uT[:, k, :], st=(k == 0), sp=(k == NDC - 1))
                pr = praw[j % 2]
                ph.cp("gpsimd", pr[:, 0:1], carry[:, j:j + 1])
                ph.cp("scalar", pr[:, 1:TC + 1], bank[:, 0:TC])
                ph.cp("gpsimd", carry[:, j:j + 1], pr[:, TC:TC + 1])
                tm = tmpa[j % 2]
                ph.ts("vector", tm[:, :], pr[:, 0:TC], mu[:, j:j + 1], ALU.mult)
                ph.stt(sh[j][:, :], pr[:, 1:TC + 1], omu[:, j:j + 1], tm[:, :], ALU.mult, ALU.add)
                if j == NB - 3:
                    ph.act(xw[:, :], xw[:, :], AF.Tanh)
                if j == NB - 1:
                    ph.act(xg[:, :], xg[:, :], AF.Sigmoid)

        load_uT(0)
        inproj(0, blk_lora + hp_blocks(0))
        for tcI in range(NTC):
            if tcI + 1 < NTC:
                load_uT(tcI + 1)
            for hl in range(HP_SUB):
                hp = sp * HP_SUB + hl
                r_, k_, v_ = sh[hl], sh[HP_SUB + hl], sh[2 * HP_SUB + hl]

                def P_(i):
                    return chp[:, i * NHP + hp:i * NHP + hp + 1]
                w0c, a0c, kkc, kac, rkc, lgc, lbc = [P_(i) for i in range(7)]
                hs = slice(hp * 128, (hp + 1) * 128)
                b = nw()
                ph.mm(b[:, 0:TC], wdec[0:LW, hs], xw[0:LW, :])
                ph.act(sg[:, :], b[:, 0:TC], AF.Sigmoid, bias=w0c)
                b = nw()
                ph.mm(b[:, 0:TC], wicl[0:LA, hs], xa[0:LA, :])
                ph.act(et[:, :], b[:, 0:TC], AF.Sigmoid, bias=a0c)
                b = nw()
                ph.mm(b[:, 0:TC], wgat[0:LG, hs], xg[0:LG, :])
                ph.cp("vector", gt[:, :], b[:, 0:TC])
                if hl + 1 < HP_SUB:
                    inproj(tcI, hp_blocks(hl + 1))
                elif tcI + 1 < NTC and HP_SUB > 1:
                    inproj(tcI + 1, blk_lora + hp_blocks(0))
                ph.act(kk[:, :], k_[:, :], AF.Copy, scale=kkc)
                ph.tt("gpsimd", tq1[:, :], kk[:, :], kk[:, :], ALU.mult)
                b = nw()
                ph.mm(b[:, 0:TC], BO, tq1[:, :])
                ph.rsqrt(tq1[:, :], b[:, 0:TC], 1e-24)
                ph.tt("gpsimd", kkn[:, :], kk[:, :], tq1[:, :], ALU.mult)
                ph.ts("vector", tq1[:, :], et[:, :], kac, ALU.mult, omka[:, hp:hp + 1], ALU.add)
                ph.tt("gpsimd", kp[:, :], k_[:, :], tq1[:, :], ALU.mult)
                ph.tt("gpsimd", bv[:, :], kkn[:, :], et[:, :], ALU.mult)
                for c in range(NCH):
                    sl = slice(c * 128, (c + 1) * 128)
                    ph.op("vector", (lambda o, d0, d1: (lambda e: e.tensor_tensor_scan(o.ap, d0.ap, d1.ap, 0.0, ALU.mult, ALU.add)))(
                        cs[:, sl], ONES, sg[:, sl]), [ONES.k, sg.k], [cs.k])
                ph.act(Wt[:, :], cs[:, :], AF.Exp, scale=-C0)
                ph.act(Wi[:, :], cs[:, :], AF.Exp, scale=C0)
                ph.tt("gpsimd", tq1[:, :], cs[:, :], sg[:, :], ALU.subtract)
                ph.act(Wp[:, :], tq1[:, :], AF.Exp, scale=-C0)
                ph.tt("gpsimd", rt[:, :], r_[:, :], Wt[:, :], ALU.mult)
                ph.tt("gpsimd", kt[:, :], kp[:, :], Wi[:, :], ALU.mult)
                ph.tt("gpsimd", bt[:, :], bv[:, :], Wi[:, :], ALU.mult)
                ph.stt(at_[:, :], kkn[:, :], -1.0, Wp[:, :], ALU.mult, ALU.mult)
                for h in range(2):
                    mcol = Vw(cd["bo"][:, 127 * h:127 * h + 1], cst.k)
                    ph.act(btm[h][:, :], bt[:, :], AF.Copy, scale=mcol)
                    ph.ts("vector", ktm[h][:, :], kt[:, :], mcol, ALU.mult)
                    ph.act(atm[h][:, :], at_[:, :], AF.Copy, scale=mcol)
                ph.tt("gpsimd", tq1[:, :], r_[:, :], kp[:, :], ALU.mult)
                ph.act(tq1[:, :], tq1[:, :], AF.Copy, scale=rkc)
                b = nw()
                ph.mm(b[:, 0:TC], BO, tq1[:, :])
                ph.tt("vector", bon[:, :], b[:, 0:TC], v_[:, :], ALU.mult)
                if CUT == 2:
                    ph.dead = True
                for c in range(NCH):
                    sl = slice(c * 128, (c + 1) * 128)
                    pm = pairm[c % 2]
                    for h in range(2):
                        hb = slice(64 * h, 64 * h + 64)
                        m = mats[(c % 2, h)]
                        if CUT == 31:
                            ph.dead = True
                        q = nq()
                        ph.mm(q, btm[h][:, sl], at_[:, sl])
                        ph.tt("vector", m["P"][0][:, :], q, MUS, ALU.mult)
                        if CUT == 20:
                            ph.dead = True
                        if CUT == 32:
                            ph.dead = True
                        q = nq()
                        ph.mm(q, atm[h][:, sl], bt[:, sl])
                        ph.tt("vector", m["PT"][0][:, :], q, MLS, ALU.mult)
                        if CUT == 33:
                            ph.dead = True
                        q = nq()
                        ph.mm(q, ktm[h][:, sl], at_[:, sl])
                        ph.tt("vector", m["AKA"][:, :], q, MUS, ALU.mult)
                        if CUT == 34:
                            ph.dead = True
                        q = nq()
                        ph.mm(q, btm[h][:, sl], rt[:, sl])
                        ph.tt("vector", m["ABR"][:, :], q, MUI, ALU.mult)
                        if CUT == 35:
                            ph.dead = True
                        q = nq()
                        ph.mm(q, ktm[h][:, sl], rt[:, sl])
                        ph.tt("vector", m["AKR"][:, :], q, MUI, ALU.mult)
                        if CUT == 36:
                            ph.dead = True
                        ph.tt("gpsimd", m["T"][:, :], m["P"][0][:, :], IDN, ALU.add)
                    if CUT == 21:
                        ph.dead = True
                    q = nq()
                    ph.tr(q, v_[:, sl], IDN)
                    ph.cp("scalar", pm["VTp"][:, :], q)
                    ph.cp("gpsimd", pm["VTe"].sk("a")[:, 0:64], pm["VTp"][:, 0:64])
                    ph.cp("gpsimd", pm["VTo"].sk("a")[:, 64:128], pm["VTp"][:, 64:128])
                    q = nq()
                    ph.tr(q, bt[:, sl], IDN)
                    ph.cp("scalar", pm["BTT"][:, :], q)
                    q = nq()
                    ph.tr(q, kt[:, sl], IDN)
                    ph.cp("scalar", pm["KTT"][:, :], q)
                if CUT == 3:
                    ph.dead = True
                assert NCH <= 2
                for lev in range(1, NLEV + 1):
                    a, bnew = (lev - 1) % 2, lev % 2
                    for c in range(NCH):
                        for h in range(2):
                            m = mats[(c % 2, h)]
                            if lev < NLEV:
                                q = nq()
                                ph.mm(q, m["PT"][a][:, :], m["P"][a][:, :])
                                ph.cp("scalar", m["P"][bnew][:, :], q)
                            q = nq()
                            ph.mm(q, m["P"][a][:, :], m["PT"][a][:, :])
                            ph.cp("scalar" if h == 0 else "vector", m["PT"][bnew][:, :], q)
                    for c in range(NCH):
                        for h in range(2):
                            m = mats[(c % 2, h)]
                            q = nq()
                            ph.mm(q, m["PT"][bnew][:, :], m["T"][:, :])
                            ph.tt("vector", m["T"][:, :], m["T"][:, :], q, ALU.add)
                for c in range(NCH):
                    sl = slice(c * 128, (c + 1) * 128)
                    pm = pairm[c % 2]
                    if CUT == 4:
                        ph.dead = True
                    me, mo = mats[(c % 2, 0)], mats[(c % 2, 1)]
                    STh = ST[hl]
                    q = nq()
                    ph.mm(q, at_[:, sl], STh[:, :], st=True, sp=False)
                    ph.mm(q, me["AKA"][:, :], pm["VTe"].sk("a")[:, :], st=False, sp=False)
                    ph.mm(q, mo["AKA"][:, :], pm["VTo"].sk("a")[:, :], st=False, sp=True)
                    ph.cp("scalar", pm["RHS"][:, :], q)
                    q = nq()
                    qa = Vw(q.ap[:, 0:64], q.k)
                    qb = Vw(q.ap[:, 64:128], q.k)
                    ph.mm(qa, me["T"][:, :], pm["RHS"][:, 0:64])
                    ph.mm(qb, mo["T"][:, :], pm["RHS"][:, 64:128])
                    ph.cp("scalar", pm["SAp"][:, :], q)
                    ph.cp("gpsimd", pm["SAe"].sk("a")[:, 0:64], pm["SAp"][:, 0:64])
                    ph.cp("gpsimd", pm["SAo"].sk("a")[:, 64:128], pm["SAp"][:, 64:128])
                    q = nq()
                    ph.mm(q, STh[:, :], rt[:, sl], st=True, sp=False)
                    ph.mm(q, pm["SAe"].sk("a")[:, :], me["ABR"][:, :], st=False, sp=False)
                    ph.mm(q, pm["SAo"].sk("a")[:, :], mo["ABR"][:, :], st=False, sp=False)
                    ph.mm(q, pm["VTe"].sk("a")[:, :], me["AKR"][:, :], st=False, sp=False)
                    ph.mm(q, pm["VTo"].sk("a")[:, :], mo["AKR"][:, :], st=False, sp=True)
                    ph.cp("scalar", yraw[:, sl], q)
                    q = nq()
                    ph.mm(q, IDN, STh[:, :], st=True, sp=False)
                    ph.mm(q, pm["BTT"][:, :], pm["SAp"][:, :], st=False, sp=False)
                    ph.mm(q, pm["KTT"][:, :], pm["VTp"][:, :], st=False, sp=True)
                    ph.stt(STh[:, :], q, Wt[:, c * 128 + 127:c * 128 + 128], BO, ALU.mult, ALU.mult)
                if CUT == 5:
                    ph.dead = True
                b = nw()
                ph.mm(b[:, 0:TC], BO, yraw[:, :])
                ph.stt(yc[:, :], b[:, 0:TC], -1.0 / 64, yraw[:, :], ALU.mult, ALU.add)
                ph.tt("gpsimd", tq1[:, :], yc[:, :], yc[:, :], ALU.mult)
                b = nw()
                ph.mm(b[:, 0:TC], BO, tq1[:, :])
                ph.rsqrt(tq1[:, :], b[:, 0:TC], 64 * LNX_EPS)
                ph.tt("gpsimd", yc[:, :], yc[:, :], tq1[:, :], ALU.mult)
                ph.ts("vector", yc[:, :], yc[:, :], g8[:, hp:hp + 1], ALU.mult, lbc, ALU.add)
                ph.tt("gpsimd", yc[:, :], yc[:, :], bon[:, :], ALU.add)
                yob = yo[(tcI * HP_SUB + hl) % 2]
                ph.tt("vector", yob[:, :], yc[:, :], gt[:, :], ALU.mult)
                ph.stq("sync", f"yo{(tcI * HP_SUB + hl) % 2}", ybuf_t[hp].ap()[:, tcI * TC:(tcI + 1) * TC],
                       yob[:, :], writes=["ybuf"])
            if HP_SUB == 1 and tcI + 1 < NTC:
                inproj(tcI + 1, blk_lora + hp_blocks(0))
        ph.emit()

    if upto < 3:
        dbgcopy()
        return nc
    TC = TC_save
    NCH = TC // 128
    NTC = S // TC
    ph = Phase(nc, "p3")
    cst, cd = consts(ph)
    IDN = cv(cst, cd, "ident")
    identb = ph.sb([128, 128], BF16, "idb")
    ph.cp("vector", identb[:, :], IDN)
    W = ph.sb([128, NDC, NAB * 128], BF16, "W")
    ph.ldm("gpsimd", "Wld", W.k, [(W.t[:, k, :], wat_in[k * 128:(k + 1) * 128, :]) for k in range(NDC)])
    biasm = ph.sb([128, QPC * 256], F32, "biasm")
    ph.ld("sync", "bias", biasm[:, :], bias_in[:, :])
    for h in range(QPC):
        for hf in range(2):
            ph.tt("vector", biasm[:, h * 256 + hf * 128:h * 256 + hf * 128 + 128],
                  biasm[:, h * 256 + hf * 128:h * 256 + hf * 128 + 128],
                  cv(cst, cd, "amc") if hf == 1 else cv(cst, cd, "amp"), ALU.add)
    sinkc = ph.sb([128, QPC], F32, "sink")
    ph.ld("sync", "sink", sinkc[:, :], sink_in[:, :])
    uTs = [ph.sb([128, NDC, TC], BF16, "uT") for _ in range(2)]
    qTm = [[ph.sb([128, TC], BF16, f"qT{i}_{h}") for h in range(2)] for i in range(NQB)]
    kT = ph.sb([128, TC + 128], BF16, "kT")
    vT = ph.sb([128, TC], BF16, "vT")
    VTe = ph.sb([128, NCH + 1, 128], BF16, "VTe")
    VTo = ph.sb([128, NCH + 1, 128], BF16, "VTo")
    ph.ms("vector", VTe[:, :, :], 0.0)
    ph.ms("vector", VTo[:, :, :], 0.0)
    ph.ms("vector", kT[:, :], 0.0)
    psw = [ph.ps([128, 512], F32, "psw") for _ in range(1)]
    class _Sub:
        def __init__(s_, bank, c0, n, key):
            s_.b, s_.c0, s_.n, s_.k = bank, c0, n, bank.k + "/" + key

        def __getitem__(s_, idx):
            rows, cols = idx
            a = s_.c0 + (cols.start or 0)
            bb = s_.c0 + (cols.stop if cols.stop is not None else s_.n)
            return Vw(s_.b.t[rows, a:bb], s_.k)
    pss = [ph.ps([128, 512], F32, "pss") for _ in range(3)]
    pstb = [ph.ps([128, 1024], BF16, "pstb") for _ in range(2)]
    pso = [ph.ps([128, 512], F32, "pso") for _ in range(2)]
    NS = 4
    ssb = [ph.sb([128, 512], F32, "ssb") for _ in range(NS)]
    esb = [ph.sb([128, 512], F32, "esb") for _ in range(NS)]
    pb = [ph.sb([128, 512], BF16, "pb") for _ in range(NS)]
    pTs = [ph.sb([128, 512], BF16, "pTs") for _ in range(NS)]
    sm = [ph.sb([128, 16], F32, "sm") for _ in range(NS)]
    biasm0 = ph.sb([128, QPC * 256], F32, "biasm0")
    ph.cp("vector", biasm0[:, :], biasm[:, :])
    for h in range(QPC):
        ph.ms("vector", biasm0[:, h * 256:h * 256 + 128], -30000.0)
    yb = [ph.sb([128, TC], BF16, "yb") for _ in range(2 * NQB)]
    hn = 0
    wn = 0
    for tcI in range(NTC):
        uT = uTs[tcI % 2]
        ph.ld("sync", f"uT{tcI % 2}", uT[:, :, :],
              uTd[:, :, tcI * TC:(tcI + 1) * TC].rearrange("k p t -> p k t"), reads=["uTd"])
        if tcI > 0:
            ph.cp("gpsimd", kT[:, 0:128], kT[:, TC:TC + 128])
            ph.cp("gpsimd", VTe[:, 0, :], VTe[:, NCH, :])
            ph.cp("gpsimd", VTo[:, 0, :], VTo[:, NCH, :])
        for j in range(NAB):
            bank = psw[0]
            wn += 1
            for k in range(NDC):
                ph.mm(bank[:, 0:TC], W[:, k, j * 128:(j + 1) * 128], uT[:, k, :], st=(k == 0), sp=(k == NDC - 1))
            if j < NQB:
                for h2 in range(2):
                    ph.ts("vector", qTm[j][h2][:, :], bank[:, 0:TC], Vw(cd["bo"][:, 127 * h2:127 * h2 + 1], cst.k), ALU.mult)
            elif j == NQB:
                ph.cp("scalar", kT[:, 128:128 + TC], bank[:, 0:TC])
            else:
                ph.cp("scalar", vT[:, :], bank[:, 0:TC])
        for c in range(NCH):
            pt = pstb[c % 2]
            ph.tr(pt[:, 0:128], vT[:, c * 128:(c + 1) * 128], identb[:, :])
            ph.cp("vector", VTe[:, c + 1, 0:64], pt[:, 0:64])
            ph.cp("vector", VTo[:, c + 1, 64:128], pt[:, 64:128])
        def pair_stages(c, qb, g, bm):
            po, ps_, ptb = pso[g], pss[g], pstb[g]
            sl_ = qb % NS
            s_, e_, p_, pT_, m_ = ssb[sl_], esb[sl_], pb[sl_], pTs[sl_], sm[sl_]
            s3 = Vw(s_.t[:, :].rearrange("p (h k) -> p h k", h=2), s_.k)
            e3 = Vw(e_.t[:, :].rearrange("p (h k) -> p h k", h=2), e_.k)
            sk2 = sinkc[:, 2 * qb:2 * qb + 2]
            ybt = yb[(tcI % 2) * NQB + qb]
            st = []

            def f0():
                for h2 in range(2):
                    ph.mm(ps_[:, h2 * 256:(h2 + 1) * 256], qTm[qb][h2][:, c * 128:(c + 1) * 128], kT[:, c * 128:c * 128 + 256])
            st.append(f0)
            st.append(lambda: ph.stt(s_[:, :], ps_[:, :], 0.125, bm[:, 2 * qb * 256:(2 * qb + 2) * 256], ALU.mult, ALU.add))
            st.append(lambda: ph.op("vector", (lambda o, a: (lambda e: e.reduce_max(o.ap, a.ap, AX.X)))(m_[:, 0:2], s3), [s_.k], [m_.k]))
            st.append(lambda: ph.tt("vector", m_[:, 2:4], m_[:, 0:2], sk2, ALU.max))
            st.append(lambda: ph.ts("vector", m_[:, 4:6], m_[:, 2:4], -1.0, ALU.mult))

            def f5():
                for h2 in range(2):
                    ph.act(e_[:, h2 * 256:(h2 + 1) * 256], s_[:, h2 * 256:(h2 + 1) * 256], AF.Exp, bias=m_[:, 4 + h2:5 + h2])
            st.append(f5)
            st.append(lambda: ph.tt("vector", m_[:, 6:8], sk2, m_[:, 4:6], ALU.add))
            st.append(lambda: ph.act(m_[:, 8:10], m_[:, 6:8], AF.Exp))
            st.append(lambda: ph.op("vector", (lambda o, a: (lambda e: e.reduce_sum(o.ap, a.ap, AX.X)))(m_[:, 10:12], e3), [e_.k], [m_.k]))
            st.append(lambda: ph.tt("vector", m_[:, 12:14], m_[:, 10:12], m_[:, 8:10], ALU.add))
            st.append(lambda: ph.op("vector", (lambda o, a: (lambda e: e.reciprocal(o.ap, a.ap)))(m_[:, 14:16], m_[:, 12:14]), [m_.k], [m_.k]))

            def f11():
                for h2 in range(2):
                    ph.act(p_[:, h2 * 256:(h2 + 1) * 256], e_[:, h2 * 256:(h2 + 1) * 256], AF.Copy, scale=m_[:, 14 + h2:15 + h2])
            st.append(f11)

            def f12():
                for i4 in range(4):
                    ph.tr(ptb[:, i4 * 128:(i4 + 1) * 128], p_[:, i4 * 128:(i4 + 1) * 128], identb[:, :])
            st.append(f12)
            st.append(lambda: ph.cp("vector" if g else "scalar", pT_[:, :], ptb[:, 0:512]))

            def f14():
                for h2 in range(2):
                    VT = VTe if h2 == 0 else VTo
                    ph.mm(po[:, 0:128], VT[:, c, :], pT_[:, (2 * h2) * 128:(2 * h2 + 1) * 128], st=(h2 == 0), sp=False)
                    ph.mm(po[:, 0:128], VT[:, c + 1, :], pT_[:, (2 * h2 + 1) * 128:(2 * h2 + 2) * 128], st=False, sp=(h2 == 1))
            st.append(f14)
            st.append(lambda: ph.cp("scalar", ybt[:, c * 128:(c + 1) * 128], po[:, 0:128]))
            return st

        for c in range(NCH):
            first = (tcI == 0 and c == 0)
            bm = biasm0 if first else biasm
            for q0 in range(0, NQB, 2):
                grp = [pair_stages(c, qb, qb - q0, bm) for qb in range(q0, min(q0 + 2, NQB))]
                for si in range(len(grp[0])):
                    for stl in grp:
                        stl[si]()
        for qb in range(NQB):
            ybt = yb[(tcI % 2) * NQB + qb]
            ph.stq("sync", f"yb{(tcI % 2) * NQB + qb}",
                   ybuf_t[NHP + qb].ap()[:, tcI * TC:(tcI + 1) * TC], ybt[:, :],
                   writes=["ybuf"])
    ph.emit()

    dbgcopy()
    if upto < 4:
        return nc
    cc_sem = nc.alloc_semaphore(name="cc_sem")
    with nc.Block() as block:
        @block.gpsimd
        def _(g):
            for j in range(NRB):
                g.collective_compute("AllGather", ALU.bypass, replica_groups=[[0, 1, 2, 3], [4, 5, 6, 7]],
                                     ins=[ybuf_t[j].ap().opt()], outs=[yall_t[j].ap().opt()]).then_inc(cc_sem)
            g.wait_ge(cc_sem, NRB)
    nc.clear_and_free_semaphores([cc_sem])
    nc.all_engine_barrier()

    if upto < 5:
        return nc
    ph = Phase(nc, "p5")
    cst, cd = consts(ph)
    KC = 4 * YR // 128
    cT = ph.sb([128, KC, NT], BF16, "cT")
    rk_per = YR // 128

    def ldc(e, sem):
        pid = e.partition_id()
        tq = pid % 4
        for j in range(NRB):
            yv = yall_t[j].ap().rearrange("(r p) t -> p r t", p=128)
            e.dma_start(out=cT.t[:, j * 4:(j + 1) * 4, :], in_=yv[:, :, bass.ds(tq * NT, NT)]).then_inc(sem, 16)
    ph.dma("gpsimd", "cTld", ldc, [], [cT.k], n=NRB)
    g1row = modraw[2 * NDC:3 * NDC, :].rearrange("(o a) b -> o (a b)", o=1)
    NTT = NT // 128
    xt = ph.sb([128, D], F32, "xt")
    nst = max(1, D // 512)
    stats = ph.sb([128, nst * 6], F32, "stats")
    mv = ph.sb([128, 2], F32, "mv")
    rs_all = ph.sb([128, NTT], F32, "rs")
    nm_all = ph.sb([128, NTT], F32, "nm")
    for tt_ in range(NTT):
        ph.ld("sync", "xt", xt[:, :], xq_in[tt_ * 128:(tt_ + 1) * 128, :])
        ln_stats(ph, xt, D, stats, mv, rs_all.sk(tt_)[:, tt_:tt_ + 1], nm_all.sk(tt_)[:, tt_:tt_ + 1], LN_EPS)
    DB = min(512, D)
    NDB = D // DB
    Wo = [ph.sb([128, KC, DB], BF16, "Wo") for _ in range(2)]
    psm = [ph.ps([128, DB], F32, "psm") for _ in range(4)]
    xp = [ph.sb([128, DB], F32, "xp") for _ in range(2)]
    zp = [ph.sb([128, DB], F32, "zp") for _ in range(2)]
    mg = [ph.sb([128, DB], F32, "mg") for _ in range(2)]
    gEs = [ph.sb([128, DB], F32, "gE") for _ in range(2)]
    bEs = [ph.sb([128, DB], F32, "bE") for _ in range(2)]
    G1s = [ph.sb([128, DB], F32, "G1") for _ in range(2)]
    D_A = cfg["D_A"]
    HW_ = cfg["HPC"] * 64
    it = 0

    def p5_loads(db):
        Wb = Wo[db % 2]
        gE, bE, G1 = gEs[db % 2], bEs[db % 2], G1s[db % 2]
        dsl = slice(db * DB, (db + 1) * DB)
        ph.ld("sync", f"gE{db % 2}", gE[:, :], lnr_in[0, dsl].partition_broadcast(128))
        ph.ld("sync", f"bE{db % 2}", bE[:, :], lnr_in[1, dsl].partition_broadcast(128))
        ph.ld("sync", f"G1{db % 2}", G1[:, :], g1row[0, dsl].partition_broadcast(128))
        ph.ts("vector", G1[:, :], G1[:, :], 1.0, ALU.add)
        pairs = []
        for kc in range(KC):
            j = kc // 4
            r = kc % 4
            row = (r * HW_ + j * 128) if j * 128 < HW_ else (D_A + r * (cfg["QPC"] * 64) + (j * 128 - HW_))
            pairs.append((Wb.t[:, kc, :], wout_in[row:row + 128, dsl]))
        ph.ldm("gpsimd", f"Wo{db % 2}", Wb.k, pairs)

    p5_loads(0)
    for db in range(NDB):
        if db + 1 < NDB:
            p5_loads(db + 1)
        Wb = Wo[db % 2]
        gE, bE, G1 = gEs[db % 2], bEs[db % 2], G1s[db % 2]
        dsl = slice(db * DB, (db + 1) * DB)
        for tt_ in range(NTT):
            pm_ = psm[it % 4]
            x_ = xp[it % 2]
            z_ = zp[it % 2]
            g_ = mg[it % 2]
            for kc in range(KC):
                ph.mm(pm_[:, :], cT[:, kc, tt_ * 128:(tt_ + 1) * 128], Wb[:, kc, :], st=(kc == 0), sp=(kc == KC - 1))
            ph.ld("sync", f"xp{it % 2}", x_[:, :], xq_in[tt_ * 128:(tt_ + 1) * 128, dsl])
            ph.act(x_[:, :], x_[:, :], AF.Identity, bias=nm_all.sk(tt_)[:, tt_:tt_ + 1], scale=rs_all.sk(tt_)[:, tt_:tt_ + 1])
            ph.tt("vector", x_[:, :], x_[:, :], gE[:, :], ALU.mult)
            ph.tt("vector", x_[:, :], x_[:, :], bE[:, :], ALU.add)
            ph.tt("vector", g_[:, :], pm_[:, :], G1[:, :], ALU.mult)
            ph.stt(z_[:, :], x_[:, :], ALPHA, g_[:, :], ALU.mult, ALU.add)
            ph.stq("sync", f"zst{it % 2}", zbuf[tt_ * 128:(tt_ + 1) * 128, dsl], z_[:, :], writes=["zbuf"])
            it += 1
    ph.emit()

    if upto < 6:
        return nc
    ph = Phase(nc, "p6a")
    cst, cd = consts(ph)
    identb = ph.sb([128, 128], BF16, "idb")
    ph.cp("vector", identb[:, :], cv(cst, cd, "ident"))
    identF = ph.sb([128, 128], F32, "idf")
    ph.cp("vector", identF[:, :], cv(cst, cd, "ident"))
    pstf = ph.ps([128, 128], F32, "pstf")
    modc = load_cols(ph, modraw, 6 * NDC, identF, pstf, "modc")
    lnc = load_cols(ph, lnv_in, 6 * NDC, identF, pstf, "lnc")
    A2 = ph.sb([128, NDC], F32, "A2")
    B2 = ph.sb([128, NDC], F32, "B2")
    t1 = ph.sb([128, NDC], F32, "t1")
    ph.ts("vector", t1[:, :], modc[:, 4 * NDC:5 * NDC], 1.0, ALU.add)
    ph.tt("vector", A2[:, :], t1[:, :], lnc[:, 2 * NDC:3 * NDC], ALU.mult)
    ph.tt("vector", B2[:, :], t1[:, :], lnc[:, 3 * NDC:4 * NDC], ALU.mult)
    ph.tt("vector", B2[:, :], B2[:, :], modc[:, 3 * NDC:4 * NDC], ALU.add)
    g1b = ph.sb([128, D], F32, "g1b")
    b1b = ph.sb([128, D], F32, "b1b")
    ph.ld("sync", "g1b", g1b[:, :], lnr_in[2, :].partition_broadcast(128))
    ph.ld("sync", "b1b", b1b[:, :], lnr_in[3, :].partition_broadcast(128))
    u2s = [ph.sb([128, NDC, 128], BF16, "u2s") for _ in range(2)]
    zt = ph.sb([128, D], F32, "zt")
    zh = ph.sb([128, D], BF16, "zh")
    x1t = ph.sb([128, D], F32, "x1t")
    stats = ph.sb([128, nst * 6], F32, "stats")
    mv = ph.sb([128, 2], F32, "mv")
    rstd = ph.sb([128, 1], F32, "rstd")
    nmr = ph.sb([128, 1], F32, "nmr")
    pstb = [ph.ps([128, 128], BF16, "pstb") for _ in range(7)]
    ti = 0
    for tt_ in range(NTT):
        u2 = u2s[tt_ % 2]
        ph.ld("sync", "zt", zt[:, :], zbuf[tt_ * 128:(tt_ + 1) * 128, :])
        ln_stats(ph, zt, D, stats, mv, rstd, nmr, LN_EPS)
        ph.act(zh[:, :], zt[:, :], AF.Identity, bias=nmr[:, :], scale=rstd[:, :])
        ph.act(x1t[:, :], zt[:, :], AF.Identity, bias=nmr[:, :], scale=rstd[:, :])
        ph.tt("vector", x1t[:, :], x1t[:, :], g1b[:, :], ALU.mult)
        ph.tt("vector", x1t[:, :], x1t[:, :], b1b[:, :], ALU.add)
        ph.stq("sync", "x1st", x1buf[tt_ * 128:(tt_ + 1) * 128, :], x1t[:, :])
        for k in range(NDC):
            pt = pstb[ti % 7]
            ti += 1
            ph.tr(pt[:, :], zh[:, k * 128:(k + 1) * 128], identb[:, :])
            ph.ts("vector", u2[:, k, :], pt[:, :], A2[:, k:k + 1], ALU.mult, B2[:, k:k + 1], ALU.add)
        ph.stq("sync", f"u2st{tt_ % 2}", u2Td[:, :, tt_ * 128:(tt_ + 1) * 128].rearrange("k p t -> p k t"), u2[:, :, :])
    ph.emit()

    ph = Phase(nc, "p6b")
    u2T = ph.sb([128, NDC, NT], BF16, "u2T")
    ph.ld("sync", "u2T", u2T[:, :, :], u2Td[:, :, :].rearrange("k p t -> p k t"))
    FB = 512
    NFB = DFF // FB
    TH = min(512, NT)
    NTH = NT // TH
    Wu = [ph.sb([128, NDC, FB], BF16, "Wu") for _ in range(2)]
    psu = [ph.ps([128, 512], F32, "psu") for _ in range(4)]
    hr = [ph.sb([128, TH], F32, "hr") for _ in range(2)]
    hT = [ph.sb([128, FB // 128, NT], BF16, "hT") for _ in range(2)]
    it = 0

    def p6_loads(fb):
        Wb = Wu[fb % 2]
        ph.ldm("gpsimd", f"Wu{fb % 2}", Wb.k,
               [(Wb.t[:, k, :], wup_in[k * 128:(k + 1) * 128, fb * FB:(fb + 1) * FB]) for k in range(NDC)])

    p6_loads(0)
    for fb in range(NFB):
        if fb + 1 < NFB:
            p6_loads(fb + 1)
        Wb = Wu[fb % 2]
        hb_ = hT[fb % 2]
        for c in range(FB // 128):
            for th in range(NTH):
                pu = psu[it % 4]
                h_ = hr[it % 2]
                it += 1
                for k in range(NDC):
                    ph.mm(pu[:, 0:TH], Wb[:, k, c * 128:(c + 1) * 128], u2T[:, k, th * TH:(th + 1) * TH],
                          st=(k == 0), sp=(k == NDC - 1))
                ph.act(h_[:, :], pu[:, 0:TH], AF.Relu)
                ph.tt("vector", hb_[:, c, th * TH:(th + 1) * TH], h_[:, :], h_[:, :], ALU.mult)
        ph.stq("sync", f"hst{fb % 2}", hbuf[fb * (FB // 128):(fb + 1) * (FB // 128), :, :].rearrange("c p t -> p c t"),
               hb_[:, :, :])
    ph.emit()

    if upto < 7:
        return nc
    ph = Phase(nc, "p7")
    DH = D // 2
    DQ = min(512, DH)
    NDQ = DH // DQ
    FBD = 1024
    NFD = DFF // FBD
    KB = FBD // 128
    acc = ph.sb([128, NTT, DH], F32, "acc")
    Wd = [ph.sb([128, KB, DH], BF16, "Wd") for _ in range(2)]
    hTb = [ph.sb([128, KB, NT], BF16, "hTb") for _ in range(2)]
    G2 = ph.sb([128, DH], F32, "G2")
    psd = [ph.ps([128, DQ], F32, "psd") for _ in range(4)]
    x1p = [ph.sb([128, DQ], F32, "x1p") for _ in range(2)]
    tmpds = [ph.sb([128, DQ], F32, "tmpd") for _ in range(2)]
    it = 0
    blocks = [(dh, fd) for dh in range(2) for fd in range(NFD)]

    def p7_loads(bi):
        dh, fd = blocks[bi]
        Wb = Wd[bi % 2]
        hb_ = hTb[bi % 2]
        ph.ldm("gpsimd", f"Wd{bi % 2}", Wb.k,
               [(Wb.t[:, kb, :], wdn_in[fd * FBD + kb * 128:fd * FBD + (kb + 1) * 128, dh * DH:(dh + 1) * DH])
                for kb in range(KB)])
        ph.ldm("sync", f"hTb{bi % 2}", hb_.k, [(hb_.t[:, kb, :], hbuf[fd * KB + kb, :, :]) for kb in range(KB)])

    p7_loads(0)
    for bi, (dh, fd) in enumerate(blocks):
        if bi + 1 < len(blocks):
            p7_loads(bi + 1)
        Wb = Wd[bi % 2]
        hb_ = hTb[bi % 2]
        if fd == 0:
            ph.ld("sync", "G2", G2[:, :],
                  modraw[5 * NDC:6 * NDC, :].rearrange("(o a) b -> o (a b)", o=1)[0, dh * DH:(dh + 1) * DH].partition_broadcast(128))
            ph.ts("vector", G2[:, :], G2[:, :], 1.0, ALU.add)
        for tt_ in range(NTT):
            for dq in range(NDQ):
                pd = psd[it % 4]
                it += 1
                for kb in range(KB):
                    ph.mm(pd[:, :], hb_[:, kb, tt_ * 128:(tt_ + 1) * 128], Wb[:, kb, dq * DQ:(dq + 1) * DQ],
                          st=(kb == 0), sp=(kb == KB - 1))
                a_ = acc.sk(f"{tt_}_{dq}")[:, tt_, dq * DQ:(dq + 1) * DQ]
                if fd == 0:
                    ph.cp("scalar", a_, pd[:, :])
                else:
                    ph.tt("vector", a_, pd[:, :], a_, ALU.add)
        if fd == NFD - 1:
            for tt_ in range(NTT):
                for dq in range(NDQ):
                    a_ = acc.sk(f"{tt_}_{dq}")[:, tt_, dq * DQ:(dq + 1) * DQ]
                    x_ = x1p[(tt_ * NDQ + dq) % 2]
                    c0 = dh * DH + dq * DQ
                    ph.ld("sync", f"x1p{(tt_ * NDQ + dq) % 2}", x_[:, :], x1buf[tt_ * 128:(tt_ + 1) * 128, c0:c0 + DQ])
                    ph.tt("vector", a_, a_, G2[:, dq * DQ:(dq + 1) * DQ], ALU.mult)
                    ph.stt(x_[:, :], x_[:, :], ALPHA, a_, ALU.mult, ALU.add)
                    ph.stq("sync", f"z2st{(tt_ * NDQ + dq) % 2}", z2buf[tt_ * 128:(tt_ + 1) * 128, c0:c0 + DQ], x_[:, :])
    ph.emit()

    ph = Phase(nc, "p8")
    g2b = ph.sb([128, D], F32, "g2b")
    b2b = ph.sb([128, D], F32, "b2b")
    ph.ld("sync", "g2b", g2b[:, :], lnr_in[4, :].partition_broadcast(128))
    ph.ld("sync", "b2b", b2b[:, :], lnr_in[5, :].partition_broadcast(128))
    zts = [ph.sb([128, D], F32, "zt") for _ in range(2)]
    ots = [ph.sb([128, D], F32, "ot") for _ in range(2)]
    stats = ph.sb([128, nst * 6], F32, "stats")
    mv = ph.sb([128, 2], F32, "mv")
    rstd = ph.sb([128, 1], F32, "rstd")
    nmr = ph.sb([128, 1], F32, "nmr")
    for tt_ in range(NTT):
        zt = zts[tt_ % 2]
        ot = ots[tt_ % 2]
        ph.ld("sync", f"zt{tt_ % 2}", zt[:, :], z2buf[tt_ * 128:(tt_ + 1) * 128, :], reads=["z2buf"])
        ln_stats(ph, zt, D, stats, mv, rstd, nmr, LN_EPS)
        ph.act(ot[:, :], zt[:, :], AF.Identity, bias=nmr[:, :], scale=rstd[:, :])
        ph.tt("vector", ot[:, :], ot[:, :], g2b[:, :], ALU.mult)
        ph.tt("vector", ot[:, :], ot[:, :], b2b[:, :], ALU.add)
        ph.stq("sync", f"ost{tt_ % 2}", out_t[tt_ * 128:(tt_ + 1) * 128, :], ot[:, :], writes=["out"])
    ph.op("vector", lambda e: e.memset(nmr.t[:, :], 0.0), ["out"], [nmr.k])
    ph.emit()
    return nc


def t5_bucket(dist):
    n = np.maximum(dist, 0)
    nf = np.maximum(n, 1).astype(np.float32)
    large = 16 + (np.log(nf / 16) / math.log(128 / 16) * (32 - 16)).astype(np.int32)
    large = np.minimum(large, 31)
    return np.where(n < 16, n, large)


def host_inputs(cfg, inp):
    D, S, NDC = cfg["D"], cfg["S"], cfg["NDC"]
    HPC, NHP, QPC, NSUB, HP_SUB, NB, NAB = cfg["HPC"], cfg["NHP"], cfg["QPC"], cfg["NSUB"], cfg["HP_SUB"], cfg["NB"], cfg["NAB"]
    LW, LA, LG, NT = cfg["LW"], cfg["LA"], cfg["LG"], cfg["NT"]
    f = lambda a: np.ascontiguousarray(np.asarray(a, dtype=np.float32))
    x = f(inp["x"])
    c = f(inp["c"])
    w_in = f(inp["w_in"])[0]
    mu_shift = f(inp["mu_shift"])[0]
    lnrows = np.stack([f(inp["ln_emb_g"]), f(inp["ln_emb_b"]), f(inp["ln1_g"])[0], f(inp["ln1_b"])[0],
                       f(inp["ln2_g"])[0], f(inp["ln2_b"])[0]], 0)
    j = np.arange(128)[:, None]
    s_ = np.arange(128)[None, :]
    ident = np.eye(128, dtype=np.float32)
    mus = (j < s_).astype(np.float32)
    mui = (j <= s_).astype(np.float32)
    mls = (j > s_).astype(np.float32)
    bo = np.kron(np.eye(2, dtype=np.float32), np.ones((64, 64), np.float32))
    ones = np.ones((128, 128), np.float32)
    NEG = -30000.0
    amask_prev = np.where(s_ > j, 0.0, NEG).astype(np.float32)
    amask_cur = np.where(s_ <= j, 0.0, NEG).astype(np.float32)
    cst = np.concatenate([ident, mus, mui, mls, bo, ones, amask_prev, amask_cur], 1)
    qi = np.arange(128)[:, None]
    kj = np.arange(256)[None, :]
    bucket = t5_bucket(qi + 128 - kj)
    rpb = f(inp["rpb_table"])
    zero128 = np.zeros((D, 1), np.float32)
    maps = []
    for core in range(8):
        b, hg = core // 4, core % 4
        m = {}
        m["x"] = x[b]
        m["xq"] = np.ascontiguousarray(x[b, hg * NT:(hg + 1) * NT])
        m["ccol"] = np.ascontiguousarray(c[b].reshape(NDC, 128).T)
        MQ = 6 * D // 4
        m["wmod"] = np.ascontiguousarray(f(inp["w_mod"])[0][:, hg * MQ:(hg + 1) * MQ])
        m["bmod"] = np.ascontiguousarray(f(inp["b_mod"])[0][None, hg * MQ:(hg + 1) * MQ])
        m["lnv"] = np.ascontiguousarray(lnrows.reshape(6 * NDC, 128))
        m["lnr"] = lnrows
        wrw = np.zeros((NSUB, D, NB * 128), np.float32)
        mu = np.zeros((128, NSUB * NB), np.float32)
        for sp in range(NSUB):
            blocks = []
            for off in (0, cfg["OFF_K"], cfg["OFF_V"]):
                for hl in range(HP_SUB):
                    hp = sp * HP_SUB + hl
                    c0 = off + (hg * HPC + 2 * hp) * 64
                    blocks.append((c0, 128))
            blocks += [(cfg["OFF_W"], LW), (cfg["OFF_A"], LA), (cfg["OFF_G"], LG)]
            for jb, (c0, n) in enumerate(blocks):
                wrw[sp, :, jb * 128:jb * 128 + n] = w_in[:, c0:c0 + n]
                mu[0:n, sp * NB + jb] = mu_shift[c0:c0 + n]
        m["wrw"] = wrw
        m["mu"] = mu
        wat = np.zeros((D, NAB * 128), np.float32)
        for i in range(QPC // 2):
            c0 = cfg["OFF_Q"] + (hg * QPC + 2 * i) * 64
            wat[:, i * 128:(i + 1) * 128] = w_in[:, c0:c0 + 128]
        kc0 = cfg["OFF_KB"] + hg * 64
        vc0 = cfg["OFF_VB"] + hg * 64
        jb = QPC // 2
        wat[:, jb * 128:jb * 128 + 64] = w_in[:, kc0:kc0 + 64]
        wat[:, jb * 128 + 64:jb * 128 + 128] = w_in[:, kc0:kc0 + 64]
        wat[:, (jb + 1) * 128:(jb + 1) * 128 + 64] = w_in[:, vc0:vc0 + 64]
        wat[:, (jb + 1) * 128 + 64:(jb + 2) * 128] = w_in[:, vc0:vc0 + 64]
        m["wat"] = wat
        chp = np.zeros((128, 7 * NHP), np.float32)
        vecs = [f(inp["w0"])[0], f(inp["a0"])[0], f(inp["k_k"])[0], f(inp["k_a"])[0], f(inp["r_k"])[0].reshape(-1),
                f(inp["lnx_g"])[0], f(inp["lnx_b"])[0]]
        wdec = np.zeros((128, NHP * 128), np.float32)
        wicl = np.zeros((128, NHP * 128), np.float32)
        wgat = np.zeros((128, NHP * 128), np.float32)
        for hp in range(NHP):
            a0_ = (hg * HPC + 2 * hp) * 64
            for i, v in enumerate(vecs):
                chp[:, i * NHP + hp] = v[a0_:a0_ + 128]
            wdec[0:LW, hp * 128:(hp + 1) * 128] = f(inp["w_decay_up"])[0][:, a0_:a0_ + 128]
            wicl[0:LA, hp * 128:(hp + 1) * 128] = f(inp["w_iclr_up"])[0][:, a0_:a0_ + 128]
            wgat[0:LG, hp * 128:(hp + 1) * 128] = f(inp["w_gate_up"])[0][:, a0_:a0_ + 128]
        m["chp"], m["wdec"], m["wicl"], m["wgat"] = chp, wdec, wicl, wgat
        bias = np.zeros((128, QPC * 256), np.float32)
        sinks = np.zeros((128, QPC), np.float32)
        for ql in range(QPC):
            hq = hg * QPC + ql
            bias[:, ql * 256:(ql + 1) * 256] = rpb[bucket, hq]
            sinks[:, ql] = f(inp["attn_sinks"])[0][hq]
        m["biasat"] = bias
        m["sinks"] = sinks
        m["cst"] = cst
        m["wout"] = f(inp["w_out"])[0]
        m["wup"] = f(inp["w_up"])[0]
        m["wdn"] = f(inp["w_down"])[0]
        maps.append(m)
    return maps


_NC_CACHE = {}


def kernel(**inputs):
    cfg = make_cfg()
    if "nc" not in _NC_CACHE:
        _NC_CACHE["nc"] = build(cfg)
    nc = _NC_CACHE["nc"]
    maps = host_inputs(cfg, inputs)
    res = run_bass_kernel_spmd(nc, maps, core_ids=list(range(8)))
    NT = cfg["NT"]
    out = np.zeros((2, cfg["S"], cfg["D"]), np.float32)
    for core in range(8):
        b, q = core // 4, core % 4
        out[b, q * NT:(q + 1) * NT] = res.results[core]["out"]
    return out
```

```python
import math
import os
CUT = int(os.environ.get('KCUT', '99'))
from contextlib import ExitStack
import numpy as np
import concourse.bass as bass
import concourse.mybir as mybir
from concourse.bass_utils import run_bass_kernel_spmd

F32 = mybir.dt.float32
BF16 = mybir.dt.bfloat16
ALU = mybir.AluOpType
AF = mybir.ActivationFunctionType
AX = mybir.AxisListType

LN_EPS = 1e-5
LNX_EPS = 64e-5
ALPHA = 2.0 ** 0.25
C0 = math.exp(-0.5)


def make_cfg(D=4096, S=4096, GQA=8):
    c = dict(D=D, S=S, GQA=GQA)
    c["D_A"] = D // 2
    c["H_A"] = c["D_A"] // 64
    c["H_Q"] = c["H_A"]
    c["H_KV"] = c["H_Q"] // GQA
    assert c["H_KV"] == 4
    c["HPC"] = c["H_A"] // 4
    c["NHP"] = c["HPC"] // 2
    c["QPC"] = c["H_Q"] // 4
    c["LW"] = max(32, int(round(c["D_A"] ** 0.5 * 1.8 / 32)) * 32)
    c["LA"] = c["LW"]
    c["LG"] = max(32, int(round(c["D_A"] ** 0.6 * 0.8 / 32)) * 32)
    c["DFF"] = 4 * D
    c["NDC"] = D // 128
    c["HP_SUB"] = min(2, c["NHP"])
    c["NSUB"] = c["NHP"] // c["HP_SUB"]
    c["NB"] = 3 * c["HP_SUB"] + 3
    c["NAB"] = c["QPC"] // 2 + 2
    c["YR"] = D // 4
    c["NT"] = S // 4
    c["TC"] = min(512, S)
    c["OFF_K"] = c["D_A"]
    c["OFF_V"] = 2 * c["D_A"]
    c["OFF_W"] = 3 * c["D_A"]
    c["OFF_A"] = c["OFF_W"] + c["LW"]
    c["OFF_G"] = c["OFF_A"] + c["LA"]
    c["RWKV_COLS"] = c["OFF_G"] + c["LG"]
    c["OFF_Q"] = c["RWKV_COLS"]
    c["OFF_KB"] = c["OFF_Q"] + c["D_A"]
    c["OFF_VB"] = c["OFF_KB"] + 4 * 64
    c["N_IN"] = c["OFF_VB"] + 4 * 64
    return c


class Vw:
    def __init__(s, ap, k):
        s.ap = ap
        s.k = k


class Buf:
    def __init__(s, t, k):
        s.t = t
        s.k = k

    def __getitem__(s, idx):
        return Vw(s.t[idx], s.k)

    def sk(s, suf):
        return Buf(s.t, s.k + "/" + str(suf))


class Phase:
    ENG = ("tensor", "vector", "scalar", "gpsimd", "sync")

    def __init__(s, nc, name):
        s.nc = nc
        s.name = name
        s.ops = {e: [] for e in s.ENG}
        s.cnt = {e: 0 for e in s.ENG}
        s.dcnt = {}
        s.lastw = {}
        s.rd = {}
        s.mem = ExitStack()
        s.nbuf = 0

    def sb(s, shape, dt, key=None):
        s.nbuf += 1
        nm = f"{s.name}_{key or 'b'}{s.nbuf}"
        return Buf(s.mem.enter_context(s.nc.sbuf_tensor(nm, list(shape), dt)), nm)

    def ps(s, shape, dt, key=None):
        s.nbuf += 1
        nm = f"{s.name}_{key or 'p'}{s.nbuf}"
        return Buf(s.mem.enter_context(s.nc.psum_tensor(nm, list(shape), dt)), nm)

    def _deps(s, reads, writes):
        d = {}

        def add(tok):
            if tok is not None and d.get(tok[0], 0) < tok[1]:
                d[tok[0]] = tok[1]
        for b in reads:
            add(s.lastw.get(b))
        for b in writes:
            add(s.lastw.get(b))
            for k, v in s.rd.get(b, {}).items():
                add((k, v))
        return d

    def _commit(s, tok, reads, writes):
        for b in writes:
            s.lastw[b] = tok
            s.rd[b] = {}
        for b in reads:
            r = s.rd.setdefault(b, {})
            if r.get(tok[0], 0) < tok[1]:
                r[tok[0]] = tok[1]

    def op(s, eng, fn, reads=(), writes=()):
        if getattr(s, "dead", False):
            return
        d = s._deps(reads, writes)
        s.cnt[eng] += 1
        tok = (("E", eng), s.cnt[eng])
        s._commit(tok, reads, writes)
        s.ops[eng].append((fn, d, tok, False))

    def dma(s, eng, key, fn, reads=(), writes=(), n=1):
        if getattr(s, "dead", False):
            return
        d = s._deps(reads, writes)
        c = s.dcnt.get(key, 0) + 16 * n
        s.dcnt[key] = c
        tok = (("D", key), c)
        s._commit(tok, reads, writes)
        s.ops[eng].append((fn, d, tok, True))

    def emit(s):
        nc = s.nc
        sems = {}
        handles = []
        for e in s.ENG:
            h = nc.alloc_semaphore(name=f"{s.name}_e_{e}")
            sems[("E", e)] = h
            handles.append(h)
        for i, k in enumerate(s.dcnt):
            h = nc.alloc_semaphore(name=f"{s.name}_d{i}")
            sems[("D", k)] = h
            handles.append(h)
        with nc.Block() as block:
            for e in s.ENG:
                if not s.ops[e] and e != "sync":
                    continue

                def body(eng, e=e):
                    waited = {}
                    for fn, d, tok, isdma in s.ops[e]:
                        for k, v in d.items():
                            if e == "tensor" and k == ("E", "tensor"):
                                continue
                            if waited.get(k, 0) < v:
                                eng.wait_ge(sems[k], v)
                                waited[k] = v
                        if isdma:
                            fn(eng, sems[tok[0]])
                        else:
                            fn(eng).then_inc(sems[tok[0]], 1)
                    if e == "sync":
                        for k, v in s.dcnt.items():
                            if waited.get(("D", k), 0) < v:
                                eng.wait_ge(sems[("D", k)], v)
                getattr(block, e)(body)
        nc.clear_and_free_semaphores(handles)
        nc.all_engine_barrier()
        s.mem.close()

    def mm(s, out, lhsT, rhs, st=True, sp=True):
        s.op("tensor", lambda e: e.matmul(out.ap, lhsT.ap, rhs.ap, start=st, stop=sp),
             [lhsT.k, rhs.k], [out.k])

    def tr(s, out, in_, ident):
        s.op("tensor", lambda e: e.transpose(out.ap, in_.ap, ident.ap), [in_.k, ident.k], [out.k])

    def act(s, out, in_, func, bias=None, scale=None, extra=()):
        rk = [in_.k] + list(extra)
        kw = {}
        if bias is not None:
            if isinstance(bias, Vw):
                kw["bias"] = bias.ap
                rk.append(bias.k)
            else:
                kw["bias"] = bias
        if scale is not None:
            if isinstance(scale, Vw):
                kw["scale"] = scale.ap
                rk.append(scale.k)
            else:
                kw["scale"] = scale
        s.op("scalar", lambda e: e.activation(out.ap, in_.ap, func, **kw), rk, [out.k])

    def tt(s, eng, out, in0, in1, op):
        s.op(eng, lambda e: e.tensor_tensor(out.ap, in0.ap, in1.ap, op), [in0.k, in1.k], [out.k])

    def ts(s, eng, out, in0, s1, op0, s2=None, op1=None):
        rk = [in0.k]
        a1 = s1
        a2 = s2
        if isinstance(s1, Vw):
            a1 = s1.ap
            rk.append(s1.k)
        if isinstance(s2, Vw):
            a2 = s2.ap
            rk.append(s2.k)
        if op1 is None:
            s.op(eng, lambda e: e.tensor_scalar(out.ap, in0.ap, a1, None, op0), rk, [out.k])
        else:
            s.op(eng, lambda e: e.tensor_scalar(out.ap, in0.ap, a1, a2, op0, op1), rk, [out.k])

    def rsqrt(s, out, in_, addc):
        s.op("scalar", lambda e: e.activation(out.ap, in_.ap, AF.Sqrt, bias=addc), [in_.k], [out.k])
        s.op("vector", lambda e: e.reciprocal(out.ap, out.ap), [out.k], [out.k])

    def stt(s, out, in0, sc, in1, op0, op1):
        rk = [in0.k, in1.k]
        a = sc
        if isinstance(sc, Vw):
            a = sc.ap
            rk.append(sc.k)
        s.op("vector", lambda e: e.scalar_tensor_tensor(out.ap, in0.ap, a, in1.ap, op0, op1), rk, [out.k])

    def cp(s, eng, out, in_):
        if eng == "scalar":
            s.op("scalar", lambda e: e.activation(out.ap, in_.ap, AF.Copy), [in_.k], [out.k])
        else:
            s.op(eng, lambda e: e.tensor_copy(out.ap, in_.ap), [in_.k], [out.k])

    def ms(s, eng, out, val):
        s.op(eng, lambda e: e.memset(out.ap, val), [], [out.k])

    def ld(s, eng, key, out, in_ap, reads=(), **kw):
        s.dma(eng, key, lambda e, sem: e.dma_start(out=out.ap, in_=in_ap, **kw).then_inc(sem, 16),
              list(reads), [out.k])

    def ldm(s, eng, key, out_key, pairs, reads=()):
        def fn(e, sem):
            for o, i in pairs:
                e.dma_start(out=o, in_=i).then_inc(sem, 16)
        s.dma(eng, key, fn, list(reads), [out_key], n=len(pairs))

    def stq(s, eng, key, out_ap, in_, writes=(), **kw):
        s.dma(eng, key, lambda e, sem: e.dma_start(out=out_ap, in_=in_.ap, **kw).then_inc(sem, 16),
              [in_.k], list(writes))


def load_cols(ph, rows_ap, nrows, identf, pst, key):
    out = ph.sb([128, nrows], F32, key)
    r0 = 0
    i = 0
    while r0 < nrows:
        n = min(128, nrows - r0)
        tmp = ph.sb([128, 128], F32, key + "t")
        ph.ld("sync", f"{key}ld{i}", tmp[0:n, :], rows_ap[r0:r0 + n, :])
        ph.tr(pst[:, 0:n], tmp[0:n, :], identf[0:n, 0:n])
        ph.cp("vector", out[:, r0:r0 + n], pst[:, 0:n])
        r0 += n
        i += 1
    return out


def ln_stats(ph, xt, D, stats, mv, rstd, nmr, eps):
    nch = max(1, D // 512)
    w = D // nch
    for i in range(nch):
        ph.op("vector", (lambda o, a: (lambda e: e.bn_stats(o.ap, a.ap)))(stats[:, i * 6:(i + 1) * 6], xt[:, i * w:(i + 1) * w]),
              [xt.k], [stats.k + f"/{i}"])
    ph.op("vector", (lambda o, a: (lambda e: e.bn_aggr(o.ap, a.ap)))(mv[:, :], stats[:, :]),
          [stats.k + f"/{i}" for i in range(nch)], [mv.k])
    rv = rstd if isinstance(rstd, Vw) else rstd[:, :]
    nv = nmr if isinstance(nmr, Vw) else nmr[:, :]
    ph.rsqrt(rv, mv[:, 1:2], eps)
    ph.stt(nv, mv[:, 0:1], -1.0, rv, ALU.mult, ALU.mult)


def build(cfg, debug=False, upto=99):
    D, S = cfg["D"], cfg["S"]
    NDC, NB, NAB, NHP, HP_SUB, NSUB = cfg["NDC"], cfg["NB"], cfg["NAB"], cfg["NHP"], cfg["HP_SUB"], cfg["NSUB"]
    QPC, LW, LA, LG, DFF, YR, NT, TC = cfg["QPC"], cfg["LW"], cfg["LA"], cfg["LG"], cfg["DFF"], cfg["YR"], cfg["NT"], cfg["TC"]
    NCH = TC // 128
    NTC = S // TC
    NQB = QPC // 2
    nc = bass.Bass("TRN2", target_bir_lowering=False)

    def din(name, shape, dt=F32):
        return nc.dram_tensor(name, list(shape), dt, kind="ExternalInput").ap()

    x_in = din("x", [S, D])
    xq_in = din("xq", [NT, D])
    ccol_in = din("ccol", [128, NDC])
    wmod_in = din("wmod", [D, 6 * D // 4])
    bmod_in = din("bmod", [1, 6 * D // 4])
    lnv_in = din("lnv", [6 * NDC, 128])
    lnr_in = din("lnr", [6, D])
    wrw_in = din("wrw", [NSUB, D, NB * 128])
    wat_in = din("wat", [D, NAB * 128])
    mu_in = din("mu", [128, NSUB * NB])
    chp_in = din("chp", [128, 7 * NHP])
    wdec_in = din("wdec", [128, NHP * 128])
    wicl_in = din("wicl", [128, NHP * 128])
    wgat_in = din("wgat", [128, NHP * 128])
    bias_in = din("biasat", [128, QPC * 256])
    sink_in = din("sinks", [128, QPC])
    cst_in = din("cst", [128, 8 * 128])
    wout_in = din("wout", [D, D])
    wup_in = din("wup", [D, DFF])
    wdn_in = din("wdn", [DFF, D])
    out_t = nc.dram_tensor("out", [NT, D], F32, kind="ExternalOutput").ap()

    modraw_t = nc.dram_tensor("modraw", [6 * NDC, 128], F32)
    modraw = modraw_t.ap()
    modpart_t = nc.dram_tensor("modpart", [6 * NDC // 4, 128], F32)
    modpart = modpart_t.ap()
    uTd = nc.dram_tensor("uTd", [NDC, 128, S], BF16).ap()
    NRB = YR // 128
    ybuf_t = [nc.dram_tensor(f"ybuf{j}", [128, S], BF16) for j in range(NRB)]
    yall_t = [nc.dram_tensor(f"yall{j}", [4 * 128, S], BF16) for j in range(NRB)]
    ydbg = nc.dram_tensor("ydbg", [YR, S], BF16, kind="ExternalOutput").ap() if debug else None
    zbuf = nc.dram_tensor("zbuf", [NT, D], F32, **(dict(kind="ExternalOutput") if debug else dict())).ap()
    x1buf = nc.dram_tensor("x1buf", [NT, D], F32, **(dict(kind="ExternalOutput") if debug else dict())).ap()
    hbuf = nc.dram_tensor("hbuf", [DFF // 128, 128, NT], BF16, **(dict(kind="ExternalOutput") if debug else dict())).ap()
    z2buf = nc.dram_tensor("z2buf", [NT, D], F32, **(dict(kind="ExternalOutput") if debug else dict())).ap()
    if debug:
        d7h = nc.dram_tensor("d7h", [128, (1024 // 128), NT], BF16, kind="ExternalOutput").ap()
        d7w = nc.dram_tensor("d7w", [128, (1024 // 128), D // 2], BF16, kind="ExternalOutput").ap()
        d7p = nc.dram_tensor("d7p", [128, min(512, D // 2)], F32, kind="ExternalOutput").ap()
        d7acc = nc.dram_tensor("d7acc", [128, D // 2], F32, kind="ExternalOutput").ap()
    u2Td = nc.dram_tensor("u2Td", [NDC, 128, NT], BF16, **(dict(kind="ExternalOutput") if debug else dict())).ap()

    def dbgcopy():
        if debug and not _dbgdone:
            _dbgdone.append(1)
            dbg_sem0 = nc.alloc_semaphore(name="dbg_sem0")
            with nc.Block() as block:
                @block.gpsimd
                def _(g):
                    for j in range(NRB):
                        g.dma_start(out=ydbg[j * 128:(j + 1) * 128, :], in_=ybuf_t[j].ap()).then_inc(dbg_sem0, 16)
                    g.wait_ge(dbg_sem0, 16 * NRB)
            nc.clear_and_free_semaphores([dbg_sem0])
            nc.all_engine_barrier()
    _dbgdone = []

    def consts(ph):
        c = ph.sb([128, 8 * 128], F32, "cst")
        ph.ld("sync", "cst", c[:, :], cst_in[:, :])
        names = ["ident", "mus", "mui", "mls", "bo", "ones", "amp", "amc"]
        d = {n: c.t[:, i * 128:(i + 1) * 128] for i, n in enumerate(names)}
        return c, d

    def cv(c, d, n, sl=None):
        ap = d[n]
        if sl is not None:
            ap = ap[sl]
        return Vw(ap, c.k)

    ph = Phase(nc, "p0")
    ccol = ph.sb([128, NDC], F32)
    cond = ph.sb([128, NDC], BF16)
    ph.ld("sync", "ccol", ccol[:, :], ccol_in[:, :])
    ph.act(cond[:, :], ccol[:, :], AF.Silu)
    MQ = 6 * D // 4
    GW = 2048 if MQ % 2048 == 0 else 1536
    assert MQ % GW == 0 and GW % 512 == 0
    NG = MQ // GW
    NQ_ = GW // 512
    wms = [ph.sb([128, GW], BF16, "wm") for _ in range(4)]
    psb = [[ph.ps([128, 512], F32, "pm") for _ in range(NQ_)] for _ in range(2)]
    bms = [ph.sb([1, GW], F32, "bm") for _ in range(2)]
    mds = [ph.sb([1, GW], F32, "md") for _ in range(2)]
    it = 0
    for g in range(NG):
        pb = psb[g % 2]
        ph.ld("sync", f"bm{g % 2}", bms[g % 2][:, :], bmod_in[0:1, g * GW:(g + 1) * GW])
        for k in range(NDC):
            wm = wms[it % 4]
            ph.ld("gpsimd", f"wm{it % 4}", wm[:, :], wmod_in[k * 128:(k + 1) * 128, g * GW:(g + 1) * GW])
            it += 1
            for q in range(NQ_):
                ph.mm(pb[q][0:1, :], cond[:, k:k + 1], wm[:, q * 512:(q + 1) * 512], st=(k == 0), sp=(k == NDC - 1))
        for q in range(NQ_):
            ph.tt("vector", mds[g % 2][0:1, q * 512:(q + 1) * 512], pb[q][0:1, :],
                  bms[g % 2][0:1, q * 512:(q + 1) * 512], ALU.add)
        ph.stq("sync", f"mdst{g % 2}",
               modpart[g * (GW // 128):(g + 1) * (GW // 128), :].rearrange("(o a) b -> o (a b)", o=1),
               mds[g % 2][0:1, :], writes=["modpart"])
    ph.emit()
    cc0 = nc.alloc_semaphore(name="cc0_sem")
    with nc.Block() as block:
        @block.gpsimd
        def _(g):
            g.collective_compute("AllGather", ALU.bypass, replica_groups=[[0, 1, 2, 3], [4, 5, 6, 7]],
                                 ins=[modpart_t.ap().opt()], outs=[modraw_t.ap().opt()]).then_inc(cc0)
            g.wait_ge(cc0, 1)
    nc.clear_and_free_semaphores([cc0])
    nc.all_engine_barrier()

    if upto < 1:
        return nc
    ph = Phase(nc, "p1")
    cst, cd = consts(ph)
    identb = ph.sb([128, 128], BF16, "idb")
    ph.cp("vector", identb[:, :], cv(cst, cd, "ident"))
    pstf = ph.ps([128, 128], F32, "pstf")
    identF = ph.sb([128, 128], F32, "idf")
    ph.cp("vector", identF[:, :], cv(cst, cd, "ident"))
    modc = load_cols(ph, modraw, 6 * NDC, identF, pstf, "modc")
    lnc = load_cols(ph, lnv_in, 6 * NDC, identF, pstf, "lnc")
    A1 = ph.sb([128, NDC], F32, "A1")
    B1 = ph.sb([128, NDC], F32, "B1")
    t1 = ph.sb([128, NDC], F32, "t1")
    ph.ts("vector", t1[:, :], modc[:, NDC:2 * NDC], 1.0, ALU.add)
    ph.tt("vector", A1[:, :], t1[:, :], lnc[:, 0:NDC], ALU.mult)
    ph.tt("vector", B1[:, :], t1[:, :], lnc[:, NDC:2 * NDC], ALU.mult)
    ph.tt("vector", B1[:, :], B1[:, :], modc[:, 0:NDC], ALU.add)
    xts = [ph.sb([128, D], F32, "xt") for _ in range(2)]
    xhs = [ph.sb([128, D], BF16, "xh") for _ in range(2)]
    uTt = [ph.sb([128, NDC, TC], BF16, "uTt") for _ in range(2)]
    nst = max(1, D // 512)
    stats = ph.sb([128, nst * 6], F32, "stats")
    mv = ph.sb([128, 2], F32, "mv")
    rstd = ph.sb([128, 1], F32, "rstd")
    nmr = ph.sb([128, 1], F32, "nmr")
    pstb = [ph.ps([128, 128], BF16, "pstb") for _ in range(7)]
    lnt = [(ph.sb([128, nst * 6], F32, "stats"), ph.sb([128, 2], F32, "mv"), ph.sb([128, 1], F32, "rstd"),
            ph.sb([128, 1], F32, "nmr")) for _ in range(2)]
    ti = [0]
    NTL = NTC * NCH

    def p1_A(i):
        xt = xts[i % 2]
        xh = xhs[i % 2]
        ph.ld("sync", f"xt{i % 2}", xt[:, :], x_in[i * 128:(i + 1) * 128, :])
        stats, mv, rstd, nmr = lnt[i % 2]
        ln_stats(ph, xt, D, stats, mv, rstd, nmr, LN_EPS)
        ph.act(xh[:, :], xt[:, :], AF.Identity, bias=nmr[:, :], scale=rstd[:, :])

    def p1_B(i):
        tcI, c = i // NCH, i % NCH
        ub = uTt[tcI % 2]
        xh = xhs[i % 2]
        for k in range(NDC):
            pt = pstb[ti[0] % 7]
            ti[0] += 1
            ph.tr(pt[:, :], xh[:, k * 128:(k + 1) * 128], identb[:, :])
            ubk = ub.sk(k % 8)
            if k % 2 == 0:
                ph.ts("vector", ubk[:, k, c * 128:(c + 1) * 128], pt[:, :], A1[:, k:k + 1], ALU.mult,
                      B1[:, k:k + 1], ALU.add)
            else:
                ph.act(ubk[:, k, c * 128:(c + 1) * 128], pt[:, :], AF.Identity, bias=B1[:, k:k + 1],
                       scale=A1[:, k:k + 1])
        if c == NCH - 1:
            ph.dma("gpsimd", f"uTst{tcI % 2}",
                   (lambda o, i_: (lambda e, sem: e.dma_start(out=o, in_=i_).then_inc(sem, 16)))(
                       uTd[:, :, tcI * TC:(tcI + 1) * TC].rearrange("k p t -> p k t"), ub.t[:, :, :]),
                   [ub.k + f"/{q}" for q in range(8)], [])

    p1_A(0)
    for i in range(NTL):
        if i + 1 < NTL:
            p1_A(i + 1)
        p1_B(i)
    ph.emit()

    if upto < 2:
        dbgcopy()
        return nc
    TC_save = TC
    TC = min(256, S)
    NCH = TC // 128
    NTC = S // TC
    for sp in range(NSUB):
        ph = Phase(nc, f"p2{sp}")
        cst, cd = consts(ph)
        IDN = cv(cst, cd, "ident")
        MUS = cv(cst, cd, "mus")
        MUI = cv(cst, cd, "mui")
        MLS = cv(cst, cd, "mls")
        BO = cv(cst, cd, "bo")
        ONES = cv(cst, cd, "ones")
        W = ph.sb([128, NDC, NB * 128], BF16, "W")
        ph.ldm("gpsimd", "Wld", W.k, [(W.t[:, k, :], wrw_in[sp, k * 128:(k + 1) * 128, :]) for k in range(NDC)])
        mu = ph.sb([128, NB], F32, "mu")
        omu = ph.sb([128, NB], F32, "omu")
        ph.ld("sync", "mu", mu[:, :], mu_in[:, sp * NB:(sp + 1) * NB])
        ph.ts("vector", omu[:, :], mu[:, :], -1.0, ALU.mult, 1.0, ALU.add)
        chp = ph.sb([128, 7 * NHP], F32, "chp")
        ph.ld("sync", "chp", chp[:, :], chp_in[:, :])
        omka = ph.sb([128, NHP], F32, "omka")
        ph.ts("vector", omka[:, :], chp[:, 3 * NHP:4 * NHP], -1.0, ALU.mult, 1.0, ALU.add)
        g8 = ph.sb([128, NHP], F32, "g8")
        ph.ts("vector", g8[:, :], chp[:, 5 * NHP:6 * NHP], 8.0, ALU.mult)
        wdec = ph.sb([128, NHP * 128], F32, "wdec")
        wicl = ph.sb([128, NHP * 128], F32, "wicl")
        wgat = ph.sb([128, NHP * 128], F32, "wgat")
        ph.ld("sync", "wdec", wdec[:, :], wdec_in[:, :])
        ph.ld("sync", "wicl", wicl[:, :], wicl_in[:, :])
        ph.ld("sync", "wgat", wgat[:, :], wgat_in[:, :])
        uTs = [ph.sb([128, NDC, TC], BF16, "uT") for _ in range(2)]
        praw = [ph.sb([128, TC + 1], F32, "praw") for _ in range(2)]
        carry = ph.sb([128, NB], F32, "carry")
        ph.ms("vector", carry[:, :], 0.0)
        tmpa = [ph.sb([128, TC], F32, "tmpa") for _ in range(2)]
        sh = [ph.sb([128, TC], F32, f"sh{j}") for j in range(NB)]
        psw = [ph.ps([128, 512], F32, "psw") for _ in range(3)]
        psq_b = [ph.ps([128, 512], F32, "psq") for _ in range(5)]
        NPQ = int(os.environ.get('NPQ', '5'))
        psq = [psq_b[i % 5].sk(i // 5)[:, (i // 5) * 128:(i // 5 + 1) * 128] for i in range(NPQ)]
        qn = [0]
        wn = [0]

        def nq():
            qn[0] += 1
            return psq[qn[0] % NPQ]

        def nw():
            wn[0] += 1
            return psw[wn[0] % 3]

        def F(key):
            return ph.sb([128, TC], F32, key)

        sg, et, gt, kk, kkn, kp, bv, cs, Wt, Wi, Wp, rt, kt, bt, at_, bon, tq1, yraw, yc = [
            F(n) for n in ["sg", "et", "gt", "kk", "kkn", "kp", "bv", "cs", "Wt", "Wi", "Wp", "rt", "kt", "bt",
                           "at", "bon", "tq1", "yraw", "yc"]]
        yo = [ph.sb([128, TC], BF16, "yo") for _ in range(2)]
        btm = [F("btm"), F("btm")]
        ktm = [F("ktm"), F("ktm")]
        atm = [F("atm"), F("atm")]
        ST = [ph.sb([128, 128], F32, f"ST{h}") for h in range(HP_SUB)]
        for h in range(HP_SUB):
            ph.ms("vector", ST[h][:, :], 0.0)

        def M(key):
            return ph.sb([128, 128], F32, key)

        mats = {}
        for c2 in range(2):
            for h in range(2):
                mats[(c2, h)] = dict(P=[M("P"), M("P")], PT=[M("PT"), M("PT")], T=M("T"), AKA=M("AKA"),
                                     ABR=M("ABR"), AKR=M("AKR"))
        pairm = []
        for c2 in range(2):
            d = dict(VTp=M("VTp"), VTe=M("VTe"), VTo=M("VTo"), BTT=M("BTT"), KTT=M("KTT"), RHS=M("RHS"),
                     SAp=M("SAp"), SAe=M("SAe"), SAo=M("SAo"))
            for n in ["VTe", "VTo", "SAe", "SAo"]:
                ph.ms("gpsimd", d[n][:, :], 0.0)
            pairm.append(d)
        NLEV = 6

        blk_lora = [NB - 3, NB - 2, NB - 1]

        def hp_blocks(hl_):
            return [hl_, HP_SUB + hl_, 2 * HP_SUB + hl_]
        xw, xa, xg = sh[NB - 3], sh[NB - 2], sh[NB - 1]

        def load_uT(t):
            ph.ld("sync", f"uT{t % 2}", uTs[t % 2][:, :, :],
                  uTd[:, :, t * TC:(t + 1) * TC].rearrange("k p t -> p k t"))

        def inproj(t, blocks):
            uT = uTs[t % 2]
            for j in blocks:
                bank = nw()
                for k in range(NDC):
                    ph.mm(bank[:, 0:TC], W[:, k, j * 128:(j + 1) * 128], uT[:, k, :], st=(k == 0), sp=(k == NDC - 1))
                pr = praw[j % 2]
                ph.cp("gpsimd", pr[:, 0:1], carry[:, j:j + 1])
                ph.cp("scalar", pr[:, 1:TC + 1], bank[:, 0:TC])
                ph.cp("gpsimd", carry[:, j:j + 1], pr[:, TC:TC + 1])
                tm = tmpa[j % 2]
                ph.ts("vector", tm[:, :], pr[:, 0:TC], mu[:, j:j + 1], ALU.mult)
                ph.stt(sh[j][:, :], pr[:, 1:TC + 1], omu[:, j:j + 1], tm[:, :], ALU.mult, ALU.add)
                if j == NB - 3:
                    ph.act(xw[:, :], xw[:, :], AF.Tanh)
                if j == NB - 1:
                    ph.act(xg[:, :], xg[:, :], AF.Sigmoid)

        load_uT(0)
        inproj(0, blk_lora + hp_blocks(0))
        for tcI in range(NTC):
            if tcI + 1 < NTC:
                load_uT(tcI + 1)
            for hl in range(HP_SUB):
                hp = sp * HP_SUB + hl
                r_, k_, v_ = sh[hl], sh[HP_SUB + hl], sh[2 * HP_SUB + hl]

                def P_(i):
                    return chp[:, i * NHP + hp:i * NHP + hp + 1]
                w0c, a0c, kkc, kac, rkc, lgc, lbc = [P_(i) for i in range(7)]
                hs = slice(hp * 128, (hp + 1) * 128)
                b = nw()
                ph.mm(b[:, 0:TC], wdec[0:LW, hs], xw[0:LW, :])
                ph.act(sg[:, :], b[:, 0:TC], AF.Sigmoid, bias=w0c)
                b = nw()
                ph.mm(b[:, 0:TC], wicl[0:LA, hs], xa[0:LA, :])
                ph.act(et[:, :], b[:, 0:TC], AF.Sigmoid, bias=a0c)
                b = nw()
                ph.mm(b[:, 0:TC], wgat[0:LG, hs], xg[0:LG, :])
                ph.cp("vector", gt[:, :], b[:, 0:TC])
                if hl + 1 < HP_SUB:
                    inproj(tcI, hp_blocks(hl + 1))
                elif tcI + 1 < NTC and HP_SUB > 1:
                    inproj(tcI + 1, blk_lora + hp_blocks(0))
                ph.act(kk[:, :], k_[:, :], AF.Copy, scale=kkc)
                ph.tt("gpsimd", tq1[:, :], kk[:, :], kk[:, :], ALU.mult)
                b = nw()
                ph.mm(b[:, 0:TC], BO, tq1[:, :])
                ph.rsqrt(tq1[:, :], b[:, 0:TC], 1e-24)
                ph.tt("gpsimd", kkn[:, :], kk[:, :], tq1[:, :], ALU.mult)
                ph.ts("vector", tq1[:, :], et[:, :], kac, ALU.mult, omka[:, hp:hp + 1], ALU.add)
                ph.tt("gpsimd", kp[:, :], k_[:, :], tq1[:, :], ALU.mult)
                ph.tt("gpsimd", bv[:, :], kkn[:, :], et[:, :], ALU.mult)
                for c in range(NCH):
                    sl = slice(c * 128, (c + 1) * 128)
                    ph.op("vector", (lambda o, d0, d1: (lambda e: e.tensor_tensor_scan(o.ap, d0.ap, d1.ap, 0.0, ALU.mult, ALU.add)))(
                        cs[:, sl], ONES, sg[:, sl]), [ONES.k, sg.k], [cs.k])
                ph.act(Wt[:, :], cs[:, :], AF.Exp, scale=-C0)
                ph.act(Wi[:, :], cs[:, :], AF.Exp, scale=C0)
                ph.tt("gpsimd", tq1[:, :], cs[:, :], sg[:, :], ALU.subtract)
                ph.act(Wp[:, :], tq1[:, :], AF.Exp, scale=-C0)
                ph.tt("gpsimd", rt[:, :], r_[:, :], Wt[:, :], ALU.mult)
                ph.tt("gpsimd", kt[:, :], kp[:, :], Wi[:, :], ALU.mult)
                ph.tt("gpsimd", bt[:, :], bv[:, :], Wi[:, :], ALU.mult)
                ph.stt(at_[:, :], kkn[:, :], -1.0, Wp[:, :], ALU.mult, ALU.mult)
                for h in range(2):
                    mcol = Vw(cd["bo"][:, 127 * h:127 * h + 1], cst.k)
                    ph.act(btm[h][:, :], bt[:, :], AF.Copy, scale=mcol)
                    ph.ts("vector", ktm[h][:, :], kt[:, :], mcol, ALU.mult)
                    ph.act(atm[h][:, :], at_[:, :], AF.Copy, scale=mcol)
                ph.tt("gpsimd", tq1[:, :], r_[:, :], kp[:, :], ALU.mult)
                ph.act(tq1[:, :], tq1[:, :], AF.Copy, scale=rkc)
                b = nw()
                ph.mm(b[:, 0:TC], BO, tq1[:, :])
                ph.tt("vector", bon[:, :], b[:, 0:TC], v_[:, :], ALU.mult)
                if CUT == 2:
                    ph.dead = True
                for c in range(NCH):
                    sl = slice(c * 128, (c + 1) * 128)
                    pm = pairm[c % 2]
                    for h in range(2):
                        hb = slice(64 * h, 64 * h + 64)
                        m = mats[(c % 2, h)]
                        if CUT == 31:
                            ph.dead = True
                        q = nq()
                        ph.mm(q, btm[h][:, sl], at_[:, sl])
                        ph.tt("vector", m["P"][0][:, :], q, MUS, ALU.mult)
                        if CUT == 20:
                            ph.dead = True
                        if CUT == 32:
                            ph.dead = True
                        q = nq()
                        ph.mm(q, atm[h][:, sl], bt[:, sl])
                        ph.tt("vector", m["PT"][0][:, :], q, MLS, ALU.mult)
                        if CUT == 33:
                            ph.dead = True
                        q = nq()
                        ph.mm(q, ktm[h][:, sl], at_[:, sl])
                        ph.tt("vector", m["AKA"][:, :], q, MUS, ALU.mult)
                        if CUT == 34:
                            ph.dead = True
                        q = nq()
                        ph.mm(q, btm[h][:, sl], rt[:, sl])
                        ph.tt("vector", m["ABR"][:, :], q, MUI, ALU.mult)
                        if CUT == 35:
                            ph.dead = True
                        q = nq()
                        ph.mm(q, ktm[h][:, sl], rt[:, sl])
                        ph.tt("vector", m["AKR"][:, :], q, MUI, ALU.mult)
                        if CUT == 36:
                            ph.dead = True
                        ph.tt("gpsimd", m["T"][:, :], m["P"][0][:, :], IDN, ALU.add)
                    if CUT == 21:
                        ph.dead = True
                    q = nq()
                    ph.tr(q, v_[:, sl], IDN)
                    ph.cp("scalar", pm["VTp"][:, :], q)
                    ph.cp("gpsimd", pm["VTe"].sk("a")[:, 0:64], pm["VTp"][:, 0:64])
                    ph.cp("gpsimd", pm["VTo"].sk("a")[:, 64:128], pm["VTp"][:, 64:128])
                    q = nq()
                    ph.tr(q, bt[:, sl], IDN)
                    ph.cp("scalar", pm["BTT"][:, :], q)
                    q = nq()
                    ph.tr(q, kt[:, sl], IDN)
                    ph.cp("scalar", pm["KTT"][:, :], q)
                if CUT == 3:
                    ph.dead = True
                assert NCH <= 2
                for lev in range(1, NLEV + 1):
                    a, bnew = (lev - 1) % 2, lev % 2
                    for c in range(NCH):
                        for h in range(2):
                            m = mats[(c % 2, h)]
                            if lev < NLEV:
                                q = nq()
                                ph.mm(q, m["PT"][a][:, :], m["P"][a][:, :])
                                ph.cp("scalar", m["P"][bnew][:, :], q)
                            q = nq()
                            ph.mm(q, m["P"][a][:, :], m["PT"][a][:, :])
                            ph.cp("scalar" if h == 0 else "vector", m["PT"][bnew][:, :], q)
                    for c in range(NCH):
                        for h in range(2):
                            m = mats[(c % 2, h)]
                            q = nq()
                            ph.mm(q, m["PT"][bnew][:, :], m["T"][:, :])
                            ph.tt("vector", m["T"][:, :], m["T"][:, :], q, ALU.add)
                for c in range(NCH):
                    sl = slice(c * 128, (c + 1) * 128)
                    pm = pairm[c % 2]
                    if CUT == 4:
                        ph.dead = True
                    me, mo = mats[(c % 2, 0)], mats[(c % 2, 1)]
                    STh = ST[hl]
                    q = nq()
                    ph.mm(q, at_[:, sl], STh[:, :], st=True, sp=False)
                    ph.mm(q, me["AKA"][:, :], pm["VTe"].sk("a")[:, :], st=False, sp=False)
                    ph.mm(q, mo["AKA"][:, :], pm["VTo"].sk("a")[:, :], st=False, sp=True)
                    ph.cp("scalar", pm["RHS"][:, :], q)
                    q = nq()
                    qa = Vw(q.ap[:, 0:64], q.k)
                    qb = Vw(q.ap[:, 64:128], q.k)
                    ph.mm(qa, me["T"][:, :], pm["RHS"][:, 0:64])
                    ph.mm(qb, mo["T"][:, :], pm["RHS"][:, 64:128])
                    ph.cp("scalar", pm["SAp"][:, :], q)
                    ph.cp("gpsimd", pm["SAe"].sk("a")[:, 0:64], pm["SAp"][:, 0:64])
                    ph.cp("gpsimd", pm["SAo"].sk("a")[:, 64:128], pm["SAp"][:, 64:128])
                    q = nq()
                    ph.mm(q, STh[:, :], rt[:, sl], st=True, sp=False)
                    ph.mm(q, pm["SAe"].sk("a")[:, :], me["ABR"][:, :], st=False, sp=False)
                    ph.mm(q, pm["SAo"].sk("a")[:, :], mo["ABR"][:, :], st=False, sp=False)
                    ph.mm(q, pm["VTe"].sk("a")[:, :], me["AKR"][:, :], st=False, sp=False)
                    ph.mm(q, pm["VTo"].sk("a")[:, :], mo["AKR"][:, :], st=False, sp=True)
                    ph.cp("scalar", yraw[:, sl], q)
                    q = nq()
                    ph.mm(q, IDN, STh[:, :], st=True, sp=False)
                    ph.mm(q, pm["BTT"][:, :], pm["SAp"][:, :], st=False, sp=False)
                    ph.mm(q, pm["KTT"][:, :], pm["VTp"][:, :], st=False, sp=True)
                    ph.stt(STh[:, :], q, Wt[:, c * 128 + 127:c * 128 + 128], BO, ALU.mult, ALU.mult)
                if CUT == 5:
                    ph.dead = True
                b = nw()
                ph.mm(b[:, 0:TC], BO, yraw[:, :])
                ph.stt(yc[:, :], b[:, 0:TC], -1.0 / 64, yraw[:, :], ALU.mult, ALU.add)
                ph.tt("gpsimd", tq1[:, :], yc[:, :], yc[:, :], ALU.mult)
                b = nw()
                ph.mm(b[:, 0:TC], BO, tq1[:, :])
                ph.rsqrt(tq1[:, :], b[:, 0:TC], 64 * LNX_EPS)
                ph.tt("gpsimd", yc[:, :], yc[:, :], tq1[:, :], ALU.mult)
                ph.ts("vector", yc[:, :], yc[:, :], g8[:, hp:hp + 1], ALU.mult, lbc, ALU.add)
                ph.tt("gpsimd", yc[:, :], yc[:, :], bon[:, :], ALU.add)
                yob = yo[(tcI * HP_SUB + hl) % 2]
                ph.tt("vector", yob[:, :], yc[:, :], gt[:, :], ALU.mult)
                ph.stq("sync", f"yo{(tcI * HP_SUB + hl) % 2}", ybuf_t[hp].ap()[:, tcI * TC:(tcI + 1) * TC],
                       yob[:, :], writes=["ybuf"])
            if HP_SUB == 1 and tcI + 1 < NTC:
                inproj(tcI + 1, blk_lora + hp_blocks(0))
        ph.emit()

    if upto < 3:
        dbgcopy()
        return nc
    TC = TC_save
    NCH = TC // 128
    NTC = S // TC
    ph = Phase(nc, "p3")
    cst, cd = consts(ph)
    IDN = cv(cst, cd, "ident")
    identb = ph.sb([128, 128], BF16, "idb")
    ph.cp("vector", identb[:, :], IDN)
    W = ph.sb([128, NDC, NAB * 128], BF16, "W")
    ph.ldm("gpsimd", "Wld", W.k, [(W.t[:, k, :], wat_in[k * 128:(k + 1) * 128, :]) for k in range(NDC)])
    biasm = ph.sb([128, QPC * 256], F32, "biasm")
    ph.ld("sync", "bias", biasm[:, :], bias_in[:, :])
    for h in range(QPC):
        for hf in range(2):
            ph.tt("vector", biasm[:, h * 256 + hf * 128:h * 256 + hf * 128 + 128],
                  biasm[:, h * 256 + hf * 128:h * 256 + hf * 128 + 128],
                  cv(cst, cd, "amc") if hf == 1 else cv(cst, cd, "amp"), ALU.add)
    sinkc = ph.sb([128, QPC], F32, "sink")
    ph.ld("sync", "sink", sinkc[:, :], sink_in[:, :])
    uTs = [ph.sb([128, NDC, TC], BF16, "uT") for _ in range(2)]
    qTm = [[ph.sb([128, TC], BF16, f"qT{i}_{h}") for h in range(2)] for i in range(NQB)]
    kT = ph.sb([128, TC + 128], BF16, "kT")
    vT = ph.sb([128, TC], BF16, "vT")
    VTe = ph.sb([128, NCH + 1, 128], BF16, "VTe")
    VTo = ph.sb([128, NCH + 1, 128], BF16, "VTo")
    ph.ms("vector", VTe[:, :, :], 0.0)
    ph.ms("vector", VTo[:, :, :], 0.0)
    ph.ms("vector", kT[:, :], 0.0)
    psw = [ph.ps([128, 512], F32, "psw") for _ in range(1)]
    class _Sub:
        def __init__(s_, bank, c0, n, key):
            s_.b, s_.c0, s_.n, s_.k = bank, c0, n, bank.k + "/" + key

        def __getitem__(s_, idx):
            rows, cols = idx
            a = s_.c0 + (cols.start or 0)
            bb = s_.c0 + (cols.stop if cols.stop is not None else s_.n)
            return Vw(s_.b.t[rows, a:bb], s_.k)
    pss = [ph.ps([128, 512], F32, "pss") for _ in range(3)]
    pstb = [ph.ps([128, 1024], BF16, "pstb") for _ in range(2)]
    pso = [ph.ps([128, 512], F32, "pso") for _ in range(2)]
    NS = 4
    ssb = [ph.sb([128, 512], F32, "ssb") for _ in range(NS)]
    esb = [ph.sb([128, 512], F32, "esb") for _ in range(NS)]
    pb = [ph.sb([128, 512], BF16, "pb") for _ in range(NS)]
    pTs = [ph.sb([128, 512], BF16, "pTs") for _ in range(NS)]
    sm = [ph.sb([128, 16], F32, "sm") for _ in range(NS)]
    biasm0 = ph.sb([128, QPC * 256], F32, "biasm0")
    ph.cp("vector", biasm0[:, :], biasm[:, :])
    for h in range(QPC):
        ph.ms("vector", biasm0[:, h * 256:h * 256 + 128], -30000.0)
    yb = [ph.sb([128, TC], BF16, "yb") for _ in range(2 * NQB)]
    hn = 0
    wn = 0
    for tcI in range(NTC):
        uT = uTs[tcI % 2]
        ph.ld("sync", f"uT{tcI % 2}", uT[:, :, :],
              uTd[:, :, tcI * TC:(tcI + 1) * TC].rearrange("k p t -> p k t"), reads=["uTd"])
        if tcI > 0:
            ph.cp("gpsimd", kT[:, 0:128], kT[:, TC:TC + 128])
            ph.cp("gpsimd", VTe[:, 0, :], VTe[:, NCH, :])
            ph.cp("gpsimd", VTo[:, 0, :], VTo[:, NCH, :])
        for j in range(NAB):
            bank = psw[0]
            wn += 1
            for k in range(NDC):
                ph.mm(bank[:, 0:TC], W[:, k, j * 128:(j + 1) * 128], uT[:, k, :], st=(k == 0), sp=(k == NDC - 1))
            if j < NQB:
                for h2 in range(2):
                    ph.ts("vector", qTm[j][h2][:, :], bank[:, 0:TC], Vw(cd["bo"][:, 127 * h2:127 * h2 + 1], cst.k), ALU.mult)
            elif j == NQB:
                ph.cp("scalar", kT[:, 128:128 + TC], bank[:, 0:TC])
            else:
                ph.cp("scalar", vT[:, :], bank[:, 0:TC])
        for c in range(NCH):
            pt = pstb[c % 2]
            ph.tr(pt[:, 0:128], vT[:, c * 128:(c + 1) * 128], identb[:, :])
            ph.cp("vector", VTe[:, c + 1, 0:64], pt[:, 0:64])
            ph.cp("vector", VTo[:, c + 1, 64:128], pt[:, 64:128])
        def pair_stages(c, qb, g, bm):
            po, ps_, ptb = pso[g], pss[g], pstb[g]
            sl_ = qb % NS
            s_, e_, p_, pT_, m_ = ssb[sl_], esb[sl_], pb[sl_], pTs[sl_], sm[sl_]
            s3 = Vw(s_.t[:, :].rearrange("p (h k) -> p h k", h=2), s_.k)
            e3 = Vw(e_.t[:, :].rearrange("p (h k) -> p h k", h=2), e_.k)
            sk2 = sinkc[:, 2 * qb:2 * qb + 2]
            ybt = yb[(tcI % 2) * NQB + qb]
            st = []

            def f0():
                for h2 in range(2):
                    ph.mm(ps_[:, h2 * 256:(h2 + 1) * 256], qTm[qb][h2][:, c * 128:(c + 1) * 128], kT[:, c * 128:c * 128 + 256])
            st.append(f0)
            st.append(lambda: ph.stt(s_[:, :], ps_[:, :], 0.125, bm[:, 2 * qb * 256:(2 * qb + 2) * 256], ALU.mult, ALU.add))
            st.append(lambda: ph.op("vector", (lambda o, a: (lambda e: e.reduce_max(o.ap, a.ap, AX.X)))(m_[:, 0:2], s3), [s_.k], [m_.k]))
            st.append(lambda: ph.tt("vector", m_[:, 2:4], m_[:, 0:2], sk2, ALU.max))
            st.append(lambda: ph.ts("vector", m_[:, 4:6], m_[:, 2:4], -1.0, ALU.mult))

            def f5():
                for h2 in range(2):
                    ph.act(e_[:, h2 * 256:(h2 + 1) * 256], s_[:, h2 * 256:(h2 + 1) * 256], AF.Exp, bias=m_[:, 4 + h2:5 + h2])
            st.append(f5)
            st.append(lambda: ph.tt("vector", m_[:, 6:8], sk2, m_[:, 4:6], ALU.add))
            st.append(lambda: ph.act(m_[:, 8:10], m_[:, 6:8], AF.Exp))
            st.append(lambda: ph.op("vector", (lambda o, a: (lambda e: e.reduce_sum(o.ap, a.ap, AX.X)))(m_[:, 10:12], e3), [e_.k], [m_.k]))
            st.append(lambda: ph.tt("vector", m_[:, 12:14], m_[:, 10:12], m_[:, 8:10], ALU.add))
            st.append(lambda: ph.op("vector", (lambda o, a: (lambda e: e.reciprocal(o.ap, a.ap)))(m_[:, 14:16], m_[:, 12:14]), [m_.k], [m_.k]))

            def f11():
                for h2 in range(2):
                    ph.act(p_[:, h2 * 256:(h2 + 1) * 256], e_[:, h2 * 256:(h2 + 1) * 256], AF.Copy, scale=m_[:, 14 + h2:15 + h2])
            st.append(f11)

            def f12():
                for i4 in range(4):
                    ph.tr(ptb[:, i4 * 128:(i4 + 1) * 128], p_[:, i4 * 128:(i4 + 1) * 128], identb[:, :])
            st.append(f12)
            st.append(lambda: ph.cp("vector" if g else "scalar", pT_[:, :], ptb[:, 0:512]))

            def f14():
                for h2 in range(2):
                    VT = VTe if h2 == 0 else VTo
                    ph.mm(po[:, 0:128], VT[:, c, :], pT_[:, (2 * h2) * 128:(2 * h2 + 1) * 128], st=(h2 == 0), sp=False)
                    ph.mm(po[:, 0:128], VT[:, c + 1, :], pT_[:, (2 * h2 + 1) * 128:(2 * h2 + 2) * 128], st=False, sp=(h2 == 1))
            st.append(f14)
            st.append(lambda: ph.cp("scalar", ybt[:, c * 128:(c + 1) * 128], po[:, 0:128]))
            return st

        for c in range(NCH):
            first = (tcI == 0 and c == 0)
            bm = biasm0 if first else biasm
            for q0 in range(0, NQB, 2):
                grp = [pair_stages(c, qb, qb - q0, bm) for qb in range(q0, min(q0 + 2, NQB))]
                for si in range(len(grp[0])):
                    for stl in grp:
                        stl[si]()
        for qb in range(NQB):
            ybt = yb[(tcI % 2) * NQB + qb]
            ph.stq("sync", f"yb{(tcI % 2) * NQB + qb}",
                   ybuf_t[NHP + qb].ap()[:, tcI * TC:(tcI + 1) * TC], ybt[:, :],
                   writes=["ybuf"])
    ph.emit()

    dbgcopy()
    if upto < 4:
        return nc
    cc_sem = nc.alloc_semaphore(name="cc_sem")
    with nc.Block() as block:
        @block.gpsimd
        def _(g):
            for j in range(NRB):
                g.collective_compute("AllGather", ALU.bypass, replica_groups=[[0, 1, 2, 3], [4, 5, 6, 7]],
                                     ins=[ybuf_t[j].ap().opt()], outs=[yall_t[j].ap().opt()]).then_inc(cc_sem)
            g.wait_ge(cc_sem, NRB)
    nc.clear_and_free_semaphores([cc_sem])
    nc.all_engine_barrier()

    if upto < 5:
        return nc
    ph = Phase(nc, "p5")
    cst, cd = consts(ph)
    KC = 4 * YR // 128
    cT = ph.sb([128, KC, NT], BF16, "cT")
    rk_per = YR // 128

    def ldc(e, sem):
        pid = e.partition_id()
        tq = pid % 4
        for j in range(NRB):
            yv = yall_t[j].ap().rearrange("(r p) t -> p r t", p=128)
            e.dma_start(out=cT.t[:, j * 4:(j + 1) * 4, :], in_=yv[:, :, bass.ds(tq * NT, NT)]).then_inc(sem, 16)
    ph.dma("gpsimd", "cTld", ldc, [], [cT.k], n=NRB)
    g1row = modraw[2 * NDC:3 * NDC, :].rearrange("(o a) b -> o (a b)", o=1)
    NTT = NT // 128
    xt = ph.sb([128, D], F32, "xt")
    nst = max(1, D // 512)
    stats = ph.sb([128, nst * 6], F32, "stats")
    mv = ph.sb([128, 2], F32, "mv")
    rs_all = ph.sb([128, NTT], F32, "rs")
    nm_all = ph.sb([128, NTT], F32, "nm")
    for tt_ in range(NTT):
        ph.ld("sync", "xt", xt[:, :], xq_in[tt_ * 128:(tt_ + 1) * 128, :])
        ln_stats(ph, xt, D, stats, mv, rs_all.sk(tt_)[:, tt_:tt_ + 1], nm_all.sk(tt_)[:, tt_:tt_ + 1], LN_EPS)
    DB = min(512, D)
    NDB = D // DB
    Wo = [ph.sb([128, KC, DB], BF16, "Wo") for _ in range(2)]
    psm = [ph.ps([128, DB], F32, "psm") for _ in range(4)]
    xp = [ph.sb([128, DB], F32, "xp") for _ in range(2)]
    zp = [ph.sb([128, DB], F32, "zp") for _ in range(2)]
    mg = [ph.sb([128, DB], F32, "mg") for _ in range(2)]
    gEs = [ph.sb([128, DB], F32, "gE") for _ in range(2)]
    bEs = [ph.sb([128, DB], F32, "bE") for _ in range(2)]
    G1s = [ph.sb([128, DB], F32, "G1") for _ in range(2)]
    D_A = cfg["D_A"]
    HW_ = cfg["HPC"] * 64
    it = 0

    def p5_loads(db):
        Wb = Wo[db % 2]
        gE, bE, G1 = gEs[db % 2], bEs[db % 2], G1s[db % 2]
        dsl = slice(db * DB, (db + 1) * DB)
        ph.ld("sync", f"gE{db % 2}", gE[:, :], lnr_in[0, dsl].partition_broadcast(128))
        ph.ld("sync", f"bE{db % 2}", bE[:, :], lnr_in[1, dsl].partition_broadcast(128))
        ph.ld("sync", f"G1{db % 2}", G1[:, :], g1row[0, dsl].partition_broadcast(128))
        ph.ts("vector", G1[:, :], G1[:, :], 1.0, ALU.add)
        pairs = []
        for kc in range(KC):
            j = kc // 4
            r = kc % 4
            row = (r * HW_ + j * 128) if j * 128 < HW_ else (D_A + r * (cfg["QPC"] * 64) + (j * 128 - HW_))
            pairs.append((Wb.t[:, kc, :], wout_in[row:row + 128, dsl]))
        ph.ldm("gpsimd", f"Wo{db % 2}", Wb.k, pairs)

    p5_loads(0)
    for db in range(NDB):
        if db + 1 < NDB:
            p5_loads(db + 1)
        Wb = Wo[db % 2]
        gE, bE, G1 = gEs[db % 2], bEs[db % 2], G1s[db % 2]
        dsl = slice(db * DB, (db + 1) * DB)
        for tt_ in range(NTT):
            pm_ = psm[it % 4]
            x_ = xp[it % 2]
            z_ = zp[it % 2]
            g_ = mg[it % 2]
            for kc in range(KC):
                ph.mm(pm_[:, :], cT[:, kc, tt_ * 128:(tt_ + 1) * 128], Wb[:, kc, :], st=(kc == 0), sp=(kc == KC - 1))
            ph.ld("sync", f"xp{it % 2}", x_[:, :], xq_in[tt_ * 128:(tt_ + 1) * 128, dsl])
            ph.act(x_[:, :], x_[:, :], AF.Identity, bias=nm_all.sk(tt_)[:, tt_:tt_ + 1], scale=rs_all.sk(tt_)[:, tt_:tt_ + 1])
            ph.tt("vector", x_[:, :], x_[:, :], gE[:, :], ALU.mult)
            ph.tt("vector", x_[:, :], x_[:, :], bE[:, :], ALU.add)
            ph.tt("vector", g_[:, :], pm_[:, :], G1[:, :], ALU.mult)
            ph.stt(z_[:, :], x_[:, :], ALPHA, g_[:, :], ALU.mult, ALU.add)
            ph.stq("sync", f"zst{it % 2}", zbuf[tt_ * 128:(tt_ + 1) * 128, dsl], z_[:, :], writes=["zbuf"])
            it += 1
    ph.emit()

    if upto < 6:
        return nc
    ph = Phase(nc, "p6a")
    cst, cd = consts(ph)
    identb = ph.sb([128, 128], BF16, "idb")
    ph.cp("vector", identb[:, :], cv(cst, cd, "ident"))
    identF = ph.sb([128, 128], F32, "idf")
    ph.cp("vector", identF[:, :], cv(cst, cd, "ident"))
    pstf = ph.ps([128, 128], F32, "pstf")
    modc = load_cols(ph, modraw, 6 * NDC, identF, pstf, "modc")
    lnc = load_cols(ph, lnv_in, 6 * NDC, identF, pstf, "lnc")
    A2 = ph.sb([128, NDC], F32, "A2")
    B2 = ph.sb([128, NDC], F32, "B2")
    t1 = ph.sb([128, NDC], F32, "t1")
    ph.ts("vector", t1[:, :], modc[:, 4 * NDC:5 * NDC], 1.0, ALU.add)
    ph.tt("vector", A2[:, :], t1[:, :], lnc[:, 2 * NDC:3 * NDC], ALU.mult)
    ph.tt("vector", B2[:, :], t1[:, :], lnc[:, 3 * NDC:4 * NDC], ALU.mult)
    ph.tt("vector", B2[:, :], B2[:, :], modc[:, 3 * NDC:4 * NDC], ALU.add)
    g1b = ph.sb([128, D], F32, "g1b")
    b1b = ph.sb([128, D], F32, "b1b")
    ph.ld("sync", "g1b", g1b[:, :], lnr_in[2, :].partition_broadcast(128))
    ph.ld("sync", "b1b", b1b[:, :], lnr_in[3, :].partition_broadcast(128))
    u2s = [ph.sb([128, NDC, 128], BF16, "u2s") for _ in range(2)]
    zt = ph.sb([128, D], F32, "zt")
    zh = ph.sb([128, D], BF16, "zh")
    x1t = ph.sb([128, D], F32, "x1t")
    stats = ph.sb([128, nst * 6], F32, "stats")
    mv = ph.sb([128, 2], F32, "mv")
    rstd = ph.sb([128, 1], F32, "rstd")
    nmr = ph.sb([128, 1], F32, "nmr")
    pstb = [ph.ps([128, 128], BF16, "pstb") for _ in range(7)]
    ti = 0
    for tt_ in range(NTT):
        u2 = u2s[tt_ % 2]
        ph.ld("sync", "zt", zt[:, :], zbuf[tt_ * 128:(tt_ + 1) * 128, :])
        ln_stats(ph, zt, D, stats, mv, rstd, nmr, LN_EPS)
        ph.act(zh[:, :], zt[:, :], AF.Identity, bias=nmr[:, :], scale=rstd[:, :])
        ph.act(x1t[:, :], zt[:, :], AF.Identity, bias=nmr[:, :], scale=rstd[:, :])
        ph.tt("vector", x1t[:, :], x1t[:, :], g1b[:, :], ALU.mult)
        ph.tt("vector", x1t[:, :], x1t[:, :], b1b[:, :], ALU.add)
        ph.stq("sync", "x1st", x1buf[tt_ * 128:(tt_ + 1) * 128, :], x1t[:, :])
        for k in range(NDC):
            pt = pstb[ti % 7]
            ti += 1
            ph.tr(pt[:, :], zh[:, k * 128:(k + 1) * 128], identb[:, :])
            u2k = u2.sk(k % 8)
            if k % 2 == 0:
                ph.ts("vector", u2k[:, k, :], pt[:, :], A2[:, k:k + 1], ALU.mult, B2[:, k:k + 1], ALU.add)
            else:
                ph.act(u2k[:, k, :], pt[:, :], AF.Identity, bias=B2[:, k:k + 1], scale=A2[:, k:k + 1])
        ph.dma("sync", f"u2st{tt_ % 2}",
               (lambda o, i_: (lambda e, sem: e.dma_start(out=o, in_=i_).then_inc(sem, 16)))(
                   u2Td[:, :, tt_ * 128:(tt_ + 1) * 128].rearrange("k p t -> p k t"), u2.t[:, :, :]),
               [u2.k + f"/{q}" for q in range(8)], [])
    ph.emit()

    ph = Phase(nc, "p6b")
    u2T = ph.sb([128, NDC, NT], BF16, "u2T")
    ph.ld("sync", "u2T", u2T[:, :, :], u2Td[:, :, :].rearrange("k p t -> p k t"))
    FB = 512
    NFB = DFF // FB
    TH = min(512, NT)
    NTH = NT // TH
    Wu = [ph.sb([128, NDC, FB], BF16, "Wu") for _ in range(2)]
    psu = [ph.ps([128, 512], F32, "psu") for _ in range(4)]
    hr = [ph.sb([128, TH], F32, "hr") for _ in range(2)]
    hT = [ph.sb([128, FB // 128, NT], BF16, "hT") for _ in range(2)]
    it = 0

    def p6_loads(fb):
        Wb = Wu[fb % 2]
        ph.ldm("gpsimd", f"Wu{fb % 2}", Wb.k,
               [(Wb.t[:, k, :], wup_in[k * 128:(k + 1) * 128, fb * FB:(fb + 1) * FB]) for k in range(NDC)])

    p6_loads(0)
    for fb in range(NFB):
        if fb + 1 < NFB:
            p6_loads(fb + 1)
        Wb = Wu[fb % 2]
        hb_ = hT[fb % 2]
        for c in range(FB // 128):
            for th in range(NTH):
                pu = psu[it % 4]
                h_ = hr[it % 2]
                it += 1
                for k in range(NDC):
                    ph.mm(pu[:, 0:TH], Wb[:, k, c * 128:(c + 1) * 128], u2T[:, k, th * TH:(th + 1) * TH],
                          st=(k == 0), sp=(k == NDC - 1))
                ph.act(h_[:, :], pu[:, 0:TH], AF.Relu)
                ph.tt("vector", hb_[:, c, th * TH:(th + 1) * TH], h_[:, :], h_[:, :], ALU.mult)
        ph.stq("sync", f"hst{fb % 2}", hbuf[fb * (FB // 128):(fb + 1) * (FB // 128), :, :].rearrange("c p t -> p c t"),
               hb_[:, :, :])
    ph.emit()

    if upto < 7:
        return nc
    ph = Phase(nc, "p7")
    DH = D // 2
    DQ = min(512, DH)
    NDQ = DH // DQ
    FBD = 1024
    NFD = DFF // FBD
    KB = FBD // 128
    acc = ph.sb([128, NTT, DH], F32, "acc")
    Wd = [ph.sb([128, KB, DH], BF16, "Wd") for _ in range(2)]
    hTb = [ph.sb([128, KB, NT], BF16, "hTb") for _ in range(2)]
    G2 = ph.sb([128, DH], F32, "G2")
    psd = [ph.ps([128, DQ], F32, "psd") for _ in range(4)]
    x1p = [ph.sb([128, DQ], F32, "x1p") for _ in range(2)]
    tmpds = [ph.sb([128, DQ], F32, "tmpd") for _ in range(2)]
    it = 0
    blocks = [(dh, fd) for dh in range(2) for fd in range(NFD)]

    def p7_loads(bi):
        dh, fd = blocks[bi]
        Wb = Wd[bi % 2]
        hb_ = hTb[bi % 2]
        ph.ldm("gpsimd", f"Wd{bi % 2}", Wb.k,
               [(Wb.t[:, kb, :], wdn_in[fd * FBD + kb * 128:fd * FBD + (kb + 1) * 128, dh * DH:(dh + 1) * DH])
                for kb in range(KB)])
        ph.ldm("sync", f"hTb{bi % 2}", hb_.k, [(hb_.t[:, kb, :], hbuf[fd * KB + kb, :, :]) for kb in range(KB)])

    p7_loads(0)
    for bi, (dh, fd) in enumerate(blocks):
        if bi + 1 < len(blocks):
            p7_loads(bi + 1)
        Wb = Wd[bi % 2]
        hb_ = hTb[bi % 2]
        if fd == 0:
            ph.ld("sync", "G2", G2[:, :],
                  modraw[5 * NDC:6 * NDC, :].rearrange("(o a) b -> o (a b)", o=1)[0, dh * DH:(dh + 1) * DH].partition_broadcast(128))
            ph.ts("vector", G2[:, :], G2[:, :], 1.0, ALU.add)
        for tt_ in range(NTT):
            for dq in range(NDQ):
                pd = psd[it % 4]
                it += 1
                for kb in range(KB):
                    ph.mm(pd[:, :], hb_[:, kb, tt_ * 128:(tt_ + 1) * 128], Wb[:, kb, dq * DQ:(dq + 1) * DQ],
                          st=(kb == 0), sp=(kb == KB - 1))
                a_ = acc.sk(f"{tt_}_{dq}")[:, tt_, dq * DQ:(dq + 1) * DQ]
                if fd == 0:
                    ph.cp("scalar", a_, pd[:, :])
                else:
                    ph.tt("vector", a_, pd[:, :], a_, ALU.add)
        if fd == NFD - 1:
            for tt_ in range(NTT):
                for dq in range(NDQ):
                    a_ = acc.sk(f"{tt_}_{dq}")[:, tt_, dq * DQ:(dq + 1) * DQ]
                    x_ = x1p[(tt_ * NDQ + dq) % 2]
                    c0 = dh * DH + dq * DQ
                    ph.ld("sync", f"x1p{(tt_ * NDQ + dq) % 2}", x_[:, :], x1buf[tt_ * 128:(tt_ + 1) * 128, c0:c0 + DQ])
                    ph.tt("vector", a_, a_, G2[:, dq * DQ:(dq + 1) * DQ], ALU.mult)
                    ph.stt(x_[:, :], x_[:, :], ALPHA, a_, ALU.mult, ALU.add)
                    ph.stq("sync", f"z2st{(tt_ * NDQ + dq) % 2}", z2buf[tt_ * 128:(tt_ + 1) * 128, c0:c0 + DQ], x_[:, :])
    ph.emit()

    ph = Phase(nc, "p8")
    g2b = ph.sb([128, D], F32, "g2b")
    b2b = ph.sb([128, D], F32, "b2b")
    ph.ld("sync", "g2b", g2b[:, :], lnr_in[4, :].partition_broadcast(128))
    ph.ld("sync", "b2b", b2b[:, :], lnr_in[5, :].partition_broadcast(128))
    zts = [ph.sb([128, D], F32, "zt") for _ in range(2)]
    ots = [ph.sb([128, D], F32, "ot") for _ in range(2)]
    stats = ph.sb([128, nst * 6], F32, "stats")
    mv = ph.sb([128, 2], F32, "mv")
    rstd = ph.sb([128, 1], F32, "rstd")
    nmr = ph.sb([128, 1], F32, "nmr")
    for tt_ in range(NTT):
        zt = zts[tt_ % 2]
        ot = ots[tt_ % 2]
        ph.ld("sync", f"zt{tt_ % 2}", zt[:, :], z2buf[tt_ * 128:(tt_ + 1) * 128, :], reads=["z2buf"])
        ln_stats(ph, zt, D, stats, mv, rstd, nmr, LN_EPS)
        ph.act(ot[:, :], zt[:, :], AF.Identity, bias=nmr[:, :], scale=rstd[:, :])
        ph.tt("vector", ot[:, :], ot[:, :], g2b[:, :], ALU.mult)
        ph.tt("vector", ot[:, :], ot[:, :], b2b[:, :], ALU.add)
        ph.stq("sync", f"ost{tt_ % 2}", out_t[tt_ * 128:(tt_ + 1) * 128, :], ot[:, :], writes=["out"])
    ph.op("vector", lambda e: e.memset(nmr.t[:, :], 0.0), ["out"], [nmr.k])
    ph.emit()
    return nc


def t5_bucket(dist):
    n = np.maximum(dist, 0)
    nf = np.maximum(n, 1).astype(np.float32)
    large = 16 + (np.log(nf / 16) / math.log(128 / 16) * (32 - 16)).astype(np.int32)
    large = np.minimum(large, 31)
    return np.where(n < 16, n, large)


def host_inputs(cfg, inp):
    D, S, NDC = cfg["D"], cfg["S"], cfg["NDC"]
    HPC, NHP, QPC, NSUB, HP_SUB, NB, NAB = cfg["HPC"], cfg["NHP"], cfg["QPC"], cfg["NSUB"], cfg["HP_SUB"], cfg["NB"], cfg["NAB"]
    LW, LA, LG, NT = cfg["LW"], cfg["LA"], cfg["LG"], cfg["NT"]
    f = lambda a: np.ascontiguousarray(np.asarray(a, dtype=np.float32))
    x = f(inp["x"])
    c = f(inp["c"])
    w_in = f(inp["w_in"])[0]
    mu_shift = f(inp["mu_shift"])[0]
    lnrows = np.stack([f(inp["ln_emb_g"]), f(inp["ln_emb_b"]), f(inp["ln1_g"])[0], f(inp["ln1_b"])[0],
                       f(inp["ln2_g"])[0], f(inp["ln2_b"])[0]], 0)
    j = np.arange(128)[:, None]
    s_ = np.arange(128)[None, :]
    ident = np.eye(128, dtype=np.float32)
    mus = (j < s_).astype(np.float32)
    mui = (j <= s_).astype(np.float32)
    mls = (j > s_).astype(np.float32)
    bo = np.kron(np.eye(2, dtype=np.float32), np.ones((64, 64), np.float32))
    ones = np.ones((128, 128), np.float32)
    NEG = -30000.0
    amask_prev = np.where(s_ > j, 0.0, NEG).astype(np.float32)
    amask_cur = np.where(s_ <= j, 0.0, NEG).astype(np.float32)
    cst = np.concatenate([ident, mus, mui, mls, bo, ones, amask_prev, amask_cur], 1)
    qi = np.arange(128)[:, None]
    kj = np.arange(256)[None, :]
    bucket = t5_bucket(qi + 128 - kj)
    rpb = f(inp["rpb_table"])
    zero128 = np.zeros((D, 1), np.float32)
    maps = []
    for core in range(8):
        b, hg = core // 4, core % 4
        m = {}
        m["x"] = x[b]
        m["xq"] = np.ascontiguousarray(x[b, hg * NT:(hg + 1) * NT])
        m["ccol"] = np.ascontiguousarray(c[b].reshape(NDC, 128).T)
        MQ = 6 * D // 4
        m["wmod"] = np.ascontiguousarray(f(inp["w_mod"])[0][:, hg * MQ:(hg + 1) * MQ])
        m["bmod"] = np.ascontiguousarray(f(inp["b_mod"])[0][None, hg * MQ:(hg + 1) * MQ])
        m["lnv"] = np.ascontiguousarray(lnrows.reshape(6 * NDC, 128))
        m["lnr"] = lnrows
        wrw = np.zeros((NSUB, D, NB * 128), np.float32)
        mu = np.zeros((128, NSUB * NB), np.float32)
        for sp in range(NSUB):
            blocks = []
            for off in (0, cfg["OFF_K"], cfg["OFF_V"]):
                for hl in range(HP_SUB):
                    hp = sp * HP_SUB + hl
                    c0 = off + (hg * HPC + 2 * hp) * 64
                    blocks.append((c0, 128))
            blocks += [(cfg["OFF_W"], LW), (cfg["OFF_A"], LA), (cfg["OFF_G"], LG)]
            for jb, (c0, n) in enumerate(blocks):
                wrw[sp, :, jb * 128:jb * 128 + n] = w_in[:, c0:c0 + n]
                mu[0:n, sp * NB + jb] = mu_shift[c0:c0 + n]
        m["wrw"] = wrw
        m["mu"] = mu
        wat = np.zeros((D, NAB * 128), np.float32)
        for i in range(QPC // 2):
            c0 = cfg["OFF_Q"] + (hg * QPC + 2 * i) * 64
            wat[:, i * 128:(i + 1) * 128] = w_in[:, c0:c0 + 128]
        kc0 = cfg["OFF_KB"] + hg * 64
        vc0 = cfg["OFF_VB"] + hg * 64
        jb = QPC // 2
        wat[:, jb * 128:jb * 128 + 64] = w_in[:, kc0:kc0 + 64]
        wat[:, jb * 128 + 64:jb * 128 + 128] = w_in[:, kc0:kc0 + 64]
        wat[:, (jb + 1) * 128:(jb + 1) * 128 + 64] = w_in[:, vc0:vc0 + 64]
        wat[:, (jb + 1) * 128 + 64:(jb + 2) * 128] = w_in[:, vc0:vc0 + 64]
        m["wat"] = wat
        chp = np.zeros((128, 7 * NHP), np.float32)
        vecs = [f(inp["w0"])[0], f(inp["a0"])[0], f(inp["k_k"])[0], f(inp["k_a"])[0], f(inp["r_k"])[0].reshape(-1),
                f(inp["lnx_g"])[0], f(inp["lnx_b"])[0]]
        wdec = np.zeros((128, NHP * 128), np.float32)
        wicl = np.zeros((128, NHP * 128), np.float32)
        wgat = np.zeros((128, NHP * 128), np.float32)
        for hp in range(NHP):
            a0_ = (hg * HPC + 2 * hp) * 64
            for i, v in enumerate(vecs):
                chp[:, i * NHP + hp] = v[a0_:a0_ + 128]
            wdec[0:LW, hp * 128:(hp + 1) * 128] = f(inp["w_decay_up"])[0][:, a0_:a0_ + 128]
            wicl[0:LA, hp * 128:(hp + 1) * 128] = f(inp["w_iclr_up"])[0][:, a0_:a0_ + 128]
            wgat[0:LG, hp * 128:(hp + 1) * 128] = f(inp["w_gate_up"])[0][:, a0_:a0_ + 128]
        m["chp"], m["wdec"], m["wicl"], m["wgat"] = chp, wdec, wicl, wgat
        bias = np.zeros((128, QPC * 256), np.float32)
        sinks = np.zeros((128, QPC), np.float32)
        for ql in range(QPC):
            hq = hg * QPC + ql
            bias[:, ql * 256:(ql + 1) * 256] = rpb[bucket, hq]
            sinks[:, ql] = f(inp["attn_sinks"])[0][hq]
        m["biasat"] = bias
        m["sinks"] = sinks
        m["cst"] = cst
        m["wout"] = f(inp["w_out"])[0]
        m["wup"] = f(inp["w_up"])[0]
        m["wdn"] = f(inp["w_down"])[0]
        maps.append(m)
    return maps


_NC_CACHE = {}


def kernel(**inputs):
    cfg = make_cfg()
    if "nc" not in _NC_CACHE:
        _NC_CACHE["nc"] = build(cfg)
    nc = _NC_CACHE["nc"]
    maps = host_inputs(cfg, inputs)
    res = run_bass_kernel_spmd(nc, maps, core_ids=list(range(8)))
    NT = cfg["NT"]
    out = np.zeros((2, cfg["S"], cfg["D"]), np.float32)
    for core in range(8):
        b, q = core // 4, core % 4
        out[b, q * NT:(q + 1) * NT] = res.results[core]["out"]
    return out
```

```python
import math
import os
CUT = int(os.environ.get('KCUT', '99'))
from contextlib import ExitStack
import numpy as np
import concourse.bass as bass
import concourse.mybir as mybir
from concourse.bass_utils import run_bass_kernel_spmd

F32 = mybir.dt.float32
BF16 = mybir.dt.bfloat16
ALU = mybir.AluOpType
AF = mybir.ActivationFunctionType
AX = mybir.AxisListType

LN_EPS = 1e-5
LNX_EPS = 64e-5
ALPHA = 2.0 ** 0.25
C0 = math.exp(-0.5)


def make_cfg(D=4096, S=4096, GQA=8):
    c = dict(D=D, S=S, GQA=GQA)
    c["D_A"] = D // 2
    c["H_A"] = c["D_A"] // 64
    c["H_Q"] = c["H_A"]
    c["H_KV"] = c["H_Q"] // GQA
    assert c["H_KV"] == 4
    c["HPC"] = c["H_A"] // 4
    c["NHP"] = c["HPC"] // 2
    c["QPC"] = c["H_Q"] // 4
    c["LW"] = max(32, int(round(c["D_A"] ** 0.5 * 1.8 / 32)) * 32)
    c["LA"] = c["LW"]
    c["LG"] = max(32, int(round(c["D_A"] ** 0.6 * 0.8 / 32)) * 32)
    c["DFF"] = 4 * D
    c["NDC"] = D // 128
    c["HP_SUB"] = min(2, c["NHP"])
    c["NSUB"] = c["NHP"] // c["HP_SUB"]
    c["NB"] = 3 * c["HP_SUB"] + 3
    c["NAB"] = c["QPC"] // 2 + 2
    c["YR"] = D // 4
    c["NT"] = S // 4
    c["TC"] = min(512, S)
    c["OFF_K"] = c["D_A"]
    c["OFF_V"] = 2 * c["D_A"]
    c["OFF_W"] = 3 * c["D_A"]
    c["OFF_A"] = c["OFF_W"] + c["LW"]
    c["OFF_G"] = c["OFF_A"] + c["LA"]
    c["RWKV_COLS"] = c["OFF_G"] + c["LG"]
    c["OFF_Q"] = c["RWKV_COLS"]
    c["OFF_KB"] = c["OFF_Q"] + c["D_A"]
    c["OFF_VB"] = c["OFF_KB"] + 4 * 64
    c["N_IN"] = c["OFF_VB"] + 4 * 64
    return c


class Vw:
    def __init__(s, ap, k):
        s.ap = ap
        s.k = k


class Buf:
    def __init__(s, t, k):
        s.t = t
        s.k = k

    def __getitem__(s, idx):
        return Vw(s.t[idx], s.k)

    def sk(s, suf):
        return Buf(s.t, s.k + "/" + str(suf))


class Phase:
    ENG = ("tensor", "vector", "scalar", "gpsimd", "sync")

    def __init__(s, nc, name):
        s.nc = nc
        s.name = name
        s.ops = {e: [] for e in s.ENG}
        s.cnt = {e: 0 for e in s.ENG}
        s.dcnt = {}
        s.lastw = {}
        s.rd = {}
        s.mem = ExitStack()
        s.nbuf = 0

    def sb(s, shape, dt, key=None):
        s.nbuf += 1
        nm = f"{s.name}_{key or 'b'}{s.nbuf}"
        return Buf(s.mem.enter_context(s.nc.sbuf_tensor(nm, list(shape), dt)), nm)

    def ps(s, shape, dt, key=None):
        s.nbuf += 1
        nm = f"{s.name}_{key or 'p'}{s.nbuf}"
        return Buf(s.mem.enter_context(s.nc.psum_tensor(nm, list(shape), dt)), nm)

    def _deps(s, reads, writes):
        d = {}

        def add(tok):
            if tok is not None and d.get(tok[0], 0) < tok[1]:
                d[tok[0]] = tok[1]
        for b in reads:
            add(s.lastw.get(b))
        for b in writes:
            add(s.lastw.get(b))
            for k, v in s.rd.get(b, {}).items():
                add((k, v))
        return d

    def _commit(s, tok, reads, writes):
        for b in writes:
            s.lastw[b] = tok
            s.rd[b] = {}
        for b in reads:
            r = s.rd.setdefault(b, {})
            if r.get(tok[0], 0) < tok[1]:
                r[tok[0]] = tok[1]

    def op(s, eng, fn, reads=(), writes=()):
        if getattr(s, "dead", False):
            return
        d = s._deps(reads, writes)
        s.cnt[eng] += 1
        tok = (("E", eng), s.cnt[eng])
        s._commit(tok, reads, writes)
        s.ops[eng].append((fn, d, tok, False))

    def dma(s, eng, key, fn, reads=(), writes=(), n=1):
        if getattr(s, "dead", False):
            return
        d = s._deps(reads, writes)
        c = s.dcnt.get(key, 0) + 16 * n
        s.dcnt[key] = c
        tok = (("D", key), c)
        s._commit(tok, reads, writes)
        s.ops[eng].append((fn, d, tok, True))

    def emit(s):
        nc = s.nc
        sems = {}
        handles = []
        for e in s.ENG:
            h = nc.alloc_semaphore(name=f"{s.name}_e_{e}")
            sems[("E", e)] = h
            handles.append(h)
        for i, k in enumerate(s.dcnt):
            h = nc.alloc_semaphore(name=f"{s.name}_d{i}")
            sems[("D", k)] = h
            handles.append(h)
        with nc.Block() as block:
            for e in s.ENG:
                if not s.ops[e] and e != "sync":
                    continue

                def body(eng, e=e):
                    waited = {}
                    for fn, d, tok, isdma in s.ops[e]:
                        for k, v in d.items():
                            if e == "tensor" and k == ("E", "tensor"):
                                continue
                            if waited.get(k, 0) < v:
                                eng.wait_ge(sems[k], v)
                                waited[k] = v
                        if isdma:
                            fn(eng, sems[tok[0]])
                        else:
                            fn(eng).then_inc(sems[tok[0]], 1)
                    if e == "sync":
                        for k, v in s.dcnt.items():
                            if waited.get(("D", k), 0) < v:
                                eng.wait_ge(sems[("D", k)], v)
                getattr(block, e)(body)
        nc.clear_and_free_semaphores(handles)
        nc.all_engine_barrier()
        s.mem.close()

    def mm(s, out, lhsT, rhs, st=True, sp=True):
        s.op("tensor", lambda e: e.matmul(out.ap, lhsT.ap, rhs.ap, start=st, stop=sp),
             [lhsT.k, rhs.k], [out.k])

    def tr(s, out, in_, ident):
        s.op("tensor", lambda e: e.transpose(out.ap, in_.ap, ident.ap), [in_.k, ident.k], [out.k])

    def act(s, out, in_, func, bias=None, scale=None, extra=()):
        rk = [in_.k] + list(extra)
        kw = {}
        if bias is not None:
            if isinstance(bias, Vw):
                kw["bias"] = bias.ap
                rk.append(bias.k)
            else:
                kw["bias"] = bias
        if scale is not None:
            if isinstance(scale, Vw):
                kw["scale"] = scale.ap
                rk.append(scale.k)
            else:
                kw["scale"] = scale
        s.op("scalar", lambda e: e.activation(out.ap, in_.ap, func, **kw), rk, [out.k])

    def tt(s, eng, out, in0, in1, op):
        s.op(eng, lambda e: e.tensor_tensor(out.ap, in0.ap, in1.ap, op), [in0.k, in1.k], [out.k])

    def ts(s, eng, out, in0, s1, op0, s2=None, op1=None):
        rk = [in0.k]
        a1 = s1
        a2 = s2
        if isinstance(s1, Vw):
            a1 = s1.ap
            rk.append(s1.k)
        if isinstance(s2, Vw):
            a2 = s2.ap
            rk.append(s2.k)
        if op1 is None:
            s.op(eng, lambda e: e.tensor_scalar(out.ap, in0.ap, a1, None, op0), rk, [out.k])
        else:
            s.op(eng, lambda e: e.tensor_scalar(out.ap, in0.ap, a1, a2, op0, op1), rk, [out.k])

    def rsqrt(s, out, in_, addc):
        s.op("scalar", lambda e: e.activation(out.ap, in_.ap, AF.Sqrt, bias=addc), [in_.k], [out.k])
        s.op("vector", lambda e: e.reciprocal(out.ap, out.ap), [out.k], [out.k])

    def stt(s, out, in0, sc, in1, op0, op1):
        rk = [in0.k, in1.k]
        a = sc
        if isinstance(sc, Vw):
            a = sc.ap
            rk.append(sc.k)
        s.op("vector", lambda e: e.scalar_tensor_tensor(out.ap, in0.ap, a, in1.ap, op0, op1), rk, [out.k])

    def cp(s, eng, out, in_):
        if eng == "scalar":
            s.op("scalar", lambda e: e.activation(out.ap, in_.ap, AF.Copy), [in_.k], [out.k])
        else:
            s.op(eng, lambda e: e.tensor_copy(out.ap, in_.ap), [in_.k], [out.k])

    def ms(s, eng, out, val):
        s.op(eng, lambda e: e.memset(out.ap, val), [], [out.k])

    def ld(s, eng, key, out, in_ap, reads=(), **kw):
        s.dma(eng, key, lambda e, sem: e.dma_start(out=out.ap, in_=in_ap, **kw).then_inc(sem, 16),
              list(reads), [out.k])

    def ldm(s, eng, key, out_key, pairs, reads=()):
        def fn(e, sem):
            for o, i in pairs:
                e.dma_start(out=o, in_=i).then_inc(sem, 16)
        s.dma(eng, key, fn, list(reads), [out_key], n=len(pairs))

    def stq(s, eng, key, out_ap, in_, writes=(), **kw):
        s.dma(eng, key, lambda e, sem: e.dma_start(out=out_ap, in_=in_.ap, **kw).then_inc(sem, 16),
              [in_.k], list(writes))


def load_cols(ph, rows_ap, nrows, identf, pst, key):
    out = ph.sb([128, nrows], F32, key)
    r0 = 0
    i = 0
    while r0 < nrows:
        n = min(128, nrows - r0)
        tmp = ph.sb([128, 128], F32, key + "t")
        ph.ld("sync", f"{key}ld{i}", tmp[0:n, :], rows_ap[r0:r0 + n, :])
        ph.tr(pst[:, 0:n], tmp[0:n, :], identf[0:n, 0:n])
        ph.cp("vector", out[:, r0:r0 + n], pst[:, 0:n])
        r0 += n
        i += 1
    return out


def ln_stats(ph, xt, D, stats, mv, rstd, nmr, eps):
    nch = max(1, D // 512)
    w = D // nch
    for i in range(nch):
        ph.op("vector", (lambda o, a: (lambda e: e.bn_stats(o.ap, a.ap)))(stats[:, i * 6:(i + 1) * 6], xt[:, i * w:(i + 1) * w]),
              [xt.k], [stats.k + f"/{i}"])
    ph.op("vector", (lambda o, a: (lambda e: e.bn_aggr(o.ap, a.ap)))(mv[:, :], stats[:, :]),
          [stats.k + f"/{i}" for i in range(nch)], [mv.k])
    rv = rstd if isinstance(rstd, Vw) else rstd[:, :]
    nv = nmr if isinstance(nmr, Vw) else nmr[:, :]
    ph.rsqrt(rv, mv[:, 1:2], eps)
    ph.stt(nv, mv[:, 0:1], -1.0, rv, ALU.mult, ALU.mult)


def build(cfg, debug=False, upto=99):
    D, S = cfg["D"], cfg["S"]
    NDC, NB, NAB, NHP, HP_SUB, NSUB = cfg["NDC"], cfg["NB"], cfg["NAB"], cfg["NHP"], cfg["HP_SUB"], cfg["NSUB"]
    QPC, LW, LA, LG, DFF, YR, NT, TC = cfg["QPC"], cfg["LW"], cfg["LA"], cfg["LG"], cfg["DFF"], cfg["YR"], cfg["NT"], cfg["TC"]
    NCH = TC // 128
    NTC = S // TC
    NQB = QPC // 2
    nc = bass.Bass("TRN2", target_bir_lowering=False)

    def din(name, shape, dt=F32):
        return nc.dram_tensor(name, list(shape), dt, kind="ExternalInput").ap()

    x_in = din("x", [S, D])
    xq_in = din("xq", [NT, D])
    ccol_in = din("ccol", [128, NDC])
    wmod_in = din("wmod", [D, 6 * D // 4])
    bmod_in = din("bmod", [1, 6 * D // 4])
    lnv_in = din("lnv", [6 * NDC, 128])
    lnr_in = din("lnr", [6, D])
    wrw_in = din("wrw", [NSUB, D, NB * 128])
    wat_in = din("wat", [D, NAB * 128])
    mu_in = din("mu", [128, NSUB * NB])
    chp_in = din("chp", [128, 7 * NHP])
    wdec_in = din("wdec", [128, NHP * 128])
    wicl_in = din("wicl", [128, NHP * 128])
    wgat_in = din("wgat", [128, NHP * 128])
    bias_in = din("biasat", [128, QPC * 256])
    sink_in = din("sinks", [128, QPC])
    cst_in = din("cst", [128, 8 * 128])
    wout_in = din("wout", [D, D])
    wup_in = din("wup", [D, DFF])
    wdn_in = din("wdn", [DFF, D])
    out_t = nc.dram_tensor("out", [NT, D], F32, kind="ExternalOutput").ap()

    modraw_t = nc.dram_tensor("modraw", [6 * NDC, 128], F32)
    modraw = modraw_t.ap()
    modpart_t = nc.dram_tensor("modpart", [6 * NDC // 4, 128], F32)
    modpart = modpart_t.ap()
    uTd = nc.dram_tensor("uTd", [NDC, 128, S], BF16).ap()
    NRB = YR // 128
    ybuf_t = [nc.dram_tensor(f"ybuf{j}", [128, S], BF16) for j in range(NRB)]
    yall_t = [nc.dram_tensor(f"yall{j}", [4 * 128, S], BF16) for j in range(NRB)]
    ydbg = nc.dram_tensor("ydbg", [YR, S], BF16, kind="ExternalOutput").ap() if debug else None
    zbuf = nc.dram_tensor("zbuf", [NT, D], F32, **(dict(kind="ExternalOutput") if debug else dict())).ap()
    x1buf = nc.dram_tensor("x1buf", [NT, D], F32, **(dict(kind="ExternalOutput") if debug else dict())).ap()
    hbuf = nc.dram_tensor("hbuf", [DFF // 128, 128, NT], BF16, **(dict(kind="ExternalOutput") if debug else dict())).ap()
    z2buf = nc.dram_tensor("z2buf", [NT, D], F32, **(dict(kind="ExternalOutput") if debug else dict())).ap()
    if debug:
        d7h = nc.dram_tensor("d7h", [128, (1024 // 128), NT], BF16, kind="ExternalOutput").ap()
        d7w = nc.dram_tensor("d7w", [128, (1024 // 128), D // 2], BF16, kind="ExternalOutput").ap()
        d7p = nc.dram_tensor("d7p", [128, min(512, D // 2)], F32, kind="ExternalOutput").ap()
        d7acc = nc.dram_tensor("d7acc", [128, D // 2], F32, kind="ExternalOutput").ap()
    u2Td = nc.dram_tensor("u2Td", [NDC, 128, NT], BF16, **(dict(kind="ExternalOutput") if debug else dict())).ap()

    def dbgcopy():
        if debug and not _dbgdone:
            _dbgdone.append(1)
            dbg_sem0 = nc.alloc_semaphore(name="dbg_sem0")
            with nc.Block() as block:
                @block.gpsimd
                def _(g):
                    for j in range(NRB):
                        g.dma_start(out=ydbg[j * 128:(j + 1) * 128, :], in_=ybuf_t[j].ap()).then_inc(dbg_sem0, 16)
                    g.wait_ge(dbg_sem0, 16 * NRB)
            nc.clear_and_free_semaphores([dbg_sem0])
            nc.all_engine_barrier()
    _dbgdone = []

    def consts(ph):
        c = ph.sb([128, 8 * 128], F32, "cst")
        ph.ld("sync", "cst", c[:, :], cst_in[:, :])
        names = ["ident", "mus", "mui", "mls", "bo", "ones", "amp", "amc"]
        d = {n: c.t[:, i * 128:(i + 1) * 128] for i, n in enumerate(names)}
        return c, d

    def cv(c, d, n, sl=None):
        ap = d[n]
        if sl is not None:
            ap = ap[sl]
        return Vw(ap, c.k)

    ph = Phase(nc, "p0")
    ccol = ph.sb([128, NDC], F32)
    cond = ph.sb([128, NDC], BF16)
    ph.ld("sync", "ccol", ccol[:, :], ccol_in[:, :])
    ph.act(cond[:, :], ccol[:, :], AF.Silu)
    MQ = 6 * D // 4
    GW = 2048 if MQ % 2048 == 0 else 1536
    assert MQ % GW == 0 and GW % 512 == 0
    NG = MQ // GW
    NQ_ = GW // 512
    wms = [ph.sb([128, GW], BF16, "wm") for _ in range(4)]
    psb = [[ph.ps([128, 512], F32, "pm") for _ in range(NQ_)] for _ in range(2)]
    bms = [ph.sb([1, GW], F32, "bm") for _ in range(2)]
    mds = [ph.sb([1, GW], F32, "md") for _ in range(2)]
    it = 0
    for g in range(NG):
        pb = psb[g % 2]
        ph.ld("sync", f"bm{g % 2}", bms[g % 2][:, :], bmod_in[0:1, g * GW:(g + 1) * GW])
        for k in range(NDC):
            wm = wms[it % 4]
            ph.ld("gpsimd", f"wm{it % 4}", wm[:, :], wmod_in[k * 128:(k + 1) * 128, g * GW:(g + 1) * GW])
            it += 1
            for q in range(NQ_):
                ph.mm(pb[q][0:1, :], cond[:, k:k + 1], wm[:, q * 512:(q + 1) * 512], st=(k == 0), sp=(k == NDC - 1))
        for q in range(NQ_):
            ph.tt("vector", mds[g % 2][0:1, q * 512:(q + 1) * 512], pb[q][0:1, :],
                  bms[g % 2][0:1, q * 512:(q + 1) * 512], ALU.add)
        ph.stq("sync", f"mdst{g % 2}",
               modpart[g * (GW // 128):(g + 1) * (GW // 128), :].rearrange("(o a) b -> o (a b)", o=1),
               mds[g % 2][0:1, :])
    ph.emit()
    cc0 = nc.alloc_semaphore(name="cc0_sem")
    with nc.Block() as block:
        @block.gpsimd
        def _(g):
            g.collective_compute("AllGather", ALU.bypass, replica_groups=[[0, 1, 2, 3], [4, 5, 6, 7]],
                                 ins=[modpart_t.ap().opt()], outs=[modraw_t.ap().opt()]).then_inc(cc0)
            g.wait_ge(cc0, 1)
    nc.clear_and_free_semaphores([cc0])
    nc.all_engine_barrier()

    if upto < 1:
        return nc
    ph = Phase(nc, "p1")
    cst, cd = consts(ph)
    identb = ph.sb([128, 128], BF16, "idb")
    ph.cp("vector", identb[:, :], cv(cst, cd, "ident"))
    pstf = ph.ps([128, 128], F32, "pstf")
    identF = ph.sb([128, 128], F32, "idf")
    ph.cp("vector", identF[:, :], cv(cst, cd, "ident"))
    modc = load_cols(ph, modraw, 6 * NDC, identF, pstf, "modc")
    lnc = load_cols(ph, lnv_in, 6 * NDC, identF, pstf, "lnc")
    A1 = ph.sb([128, NDC], F32, "A1")
    B1 = ph.sb([128, NDC], F32, "B1")
    t1 = ph.sb([128, NDC], F32, "t1")
    ph.ts("vector", t1[:, :], modc[:, NDC:2 * NDC], 1.0, ALU.add)
    ph.tt("vector", A1[:, :], t1[:, :], lnc[:, 0:NDC], ALU.mult)
    ph.tt("vector", B1[:, :], t1[:, :], lnc[:, NDC:2 * NDC], ALU.mult)
    ph.tt("vector", B1[:, :], B1[:, :], modc[:, 0:NDC], ALU.add)
    xts = [ph.sb([128, D], F32, "xt") for _ in range(2)]
    xhs = [ph.sb([128, D], BF16, "xh") for _ in range(2)]
    uTt = [ph.sb([128, NDC, TC], BF16, "uTt") for _ in range(2)]
    nst = max(1, D // 512)
    stats = ph.sb([128, nst * 6], F32, "stats")
    mv = ph.sb([128, 2], F32, "mv")
    rstd = ph.sb([128, 1], F32, "rstd")
    nmr = ph.sb([128, 1], F32, "nmr")
    pstb = [ph.ps([128, 128], BF16, "pstb") for _ in range(7)]
    lnt = [(ph.sb([128, nst * 6], F32, "stats"), ph.sb([128, 2], F32, "mv"), ph.sb([128, 1], F32, "rstd"),
            ph.sb([128, 1], F32, "nmr")) for _ in range(2)]
    ti = [0]
    NTL = NTC * NCH

    def p1_A(i):
        xt = xts[i % 2]
        xh = xhs[i % 2]
        ph.ld("sync", f"xt{i % 2}", xt[:, :], x_in[i * 128:(i + 1) * 128, :])
        stats, mv, rstd, nmr = lnt[i % 2]
        ln_stats(ph, xt, D, stats, mv, rstd, nmr, LN_EPS)
        ph.act(xh[:, :], xt[:, :], AF.Identity, bias=nmr[:, :], scale=rstd[:, :])

    def p1_B(i):
        tcI, c = i // NCH, i % NCH
        ub = uTt[tcI % 2]
        xh = xhs[i % 2]
        for k in range(NDC):
            pt = pstb[ti[0] % 7]
            ti[0] += 1
            ph.tr(pt[:, :], xh[:, k * 128:(k + 1) * 128], identb[:, :])
            ubk = ub.sk(k % 8)
            if k % 2 == 0:
                ph.ts("vector", ubk[:, k, c * 128:(c + 1) * 128], pt[:, :], A1[:, k:k + 1], ALU.mult,
                      B1[:, k:k + 1], ALU.add)
            else:
                ph.act(ubk[:, k, c * 128:(c + 1) * 128], pt[:, :], AF.Identity, bias=B1[:, k:k + 1],
                       scale=A1[:, k:k + 1])
        if c == NCH - 1:
            ph.dma("gpsimd", f"uTst{tcI % 2}",
                   (lambda o, i_: (lambda e, sem: e.dma_start(out=o, in_=i_).then_inc(sem, 16)))(
                       uTd[:, :, tcI * TC:(tcI + 1) * TC].rearrange("k p t -> p k t"), ub.t[:, :, :]),
                   [ub.k + f"/{q}" for q in range(8)], [])

    p1_A(0)
    for i in range(NTL):
        if i + 1 < NTL:
            p1_A(i + 1)
        p1_B(i)
    ph.emit()

    if upto < 2:
        dbgcopy()
        return nc
    TC_save = TC
    TC = min(256, S)
    NCH = TC // 128
    NTC = S // TC
    for sp in range(NSUB):
        ph = Phase(nc, f"p2{sp}")
        cst, cd = consts(ph)
        IDN = cv(cst, cd, "ident")
        MUS = cv(cst, cd, "mus")
        MUI = cv(cst, cd, "mui")
        MLS = cv(cst, cd, "mls")
        BO = cv(cst, cd, "bo")
        ONES = cv(cst, cd, "ones")
        W = ph.sb([128, NDC, NB * 128], BF16, "W")
        ph.ldm("gpsimd", "Wld", W.k, [(W.t[:, k, :], wrw_in[sp, k * 128:(k + 1) * 128, :]) for k in range(NDC)])
        mu = ph.sb([128, NB], F32, "mu")
        omu = ph.sb([128, NB], F32, "omu")
        ph.ld("sync", "mu", mu[:, :], mu_in[:, sp * NB:(sp + 1) * NB])
        ph.ts("vector", omu[:, :], mu[:, :], -1.0, ALU.mult, 1.0, ALU.add)
        chp = ph.sb([128, 7 * NHP], F32, "chp")
        ph.ld("sync", "chp", chp[:, :], chp_in[:, :])
        omka = ph.sb([128, NHP], F32, "omka")
        ph.ts("vector", omka[:, :], chp[:, 3 * NHP:4 * NHP], -1.0, ALU.mult, 1.0, ALU.add)
        g8 = ph.sb([128, NHP], F32, "g8")
        ph.ts("vector", g8[:, :], chp[:, 5 * NHP:6 * NHP], 8.0, ALU.mult)
        wdec = ph.sb([128, NHP * 128], F32, "wdec")
        wicl = ph.sb([128, NHP * 128], F32, "wicl")
        wgat = ph.sb([128, NHP * 128], F32, "wgat")
        ph.ld("sync", "wdec", wdec[:, :], wdec_in[:, :])
        ph.ld("sync", "wicl", wicl[:, :], wicl_in[:, :])
        ph.ld("sync", "wgat", wgat[:, :], wgat_in[:, :])
        uTs = [ph.sb([128, NDC, TC], BF16, "uT") for _ in range(2)]
        praw = [ph.sb([128, TC + 1], F32, "praw") for _ in range(2)]
        carry = ph.sb([128, NB], F32, "carry")
        ph.ms("vector", carry[:, :], 0.0)
        tmpa = [ph.sb([128, TC], F32, "tmpa") for _ in range(2)]
        sh = [ph.sb([128, TC], F32, f"sh{j}") for j in range(NB)]
        psw = [ph.ps([128, 512], F32, "psw") for _ in range(3)]
        psq_b = [ph.ps([128, 512], F32, "psq") for _ in range(5)]
        NPQ = int(os.environ.get('NPQ', '5'))
        psq = [psq_b[i % 5].sk(i // 5)[:, (i // 5) * 128:(i // 5 + 1) * 128] for i in range(NPQ)]
        qn = [0]
        wn = [0]

        def nq():
            qn[0] += 1
            return psq[qn[0] % NPQ]

        def nw():
            wn[0] += 1
            return psw[wn[0] % 3]

        def F(key):
            return ph.sb([128, TC], F32, key)

        sg, et, gt, kk, kkn, kp, bv, cs, Wt, Wi, Wp, rt, kt, bt, at_, bon, tq1, yraw, yc = [
            F(n) for n in ["sg", "et", "gt", "kk", "kkn", "kp", "bv", "cs", "Wt", "Wi", "Wp", "rt", "kt", "bt",
                           "at", "bon", "tq1", "yraw", "yc"]]
        yo = [ph.sb([128, TC], BF16, "yo") for _ in range(2)]
        btm = [F("btm"), F("btm")]
        ktm = [F("ktm"), F("ktm")]
        atm = [F("atm"), F("atm")]
        ST = [ph.sb([128, 128], F32, f"ST{h}") for h in range(HP_SUB)]
        for h in range(HP_SUB):
            ph.ms("vector", ST[h][:, :], 0.0)

        def M(key):
            return ph.sb([128, 128], F32, key)

        mats = {}
        for c2 in range(2):
            for h in range(2):
                mats[(c2, h)] = dict(P=[M("P"), M("P")], PT=[M("PT"), M("PT")], T=M("T"), AKA=M("AKA"),
                                     ABR=M("ABR"), AKR=M("AKR"))
        pairm = []
        for c2 in range(2):
            d = dict(VTp=M("VTp"), VTe=M("VTe"), VTo=M("VTo"), BTT=M("BTT"), KTT=M("KTT"), RHS=M("RHS"),
                     SAp=M("SAp"), SAe=M("SAe"), SAo=M("SAo"))
            for n in ["VTe", "VTo", "SAe", "SAo"]:
                ph.ms("gpsimd", d[n][:, :], 0.0)
            pairm.append(d)
        NLEV = 6

        blk_lora = [NB - 3, NB - 2, NB - 1]

        def hp_blocks(hl_):
            return [hl_, HP_SUB + hl_, 2 * HP_SUB + hl_]
        xw, xa, xg = sh[NB - 3], sh[NB - 2], sh[NB - 1]

        def load_uT(t):
            ph.ld("sync", f"uT{t % 2}", uTs[t % 2][:, :, :],
                  uTd[:, :, t * TC:(t + 1) * TC].rearrange("k p t -> p k t"))

        def inproj(t, blocks):
            uT = uTs[t % 2]
            for j in blocks:
                bank = nw()
                for k in range(NDC):
                    ph.mm(bank[:, 0:TC], W[:, k, j * 128:(j + 1) * 128], uT[:, k, :], st=(k == 0), sp=(k == NDC - 1))
                pr = praw[j % 2]
                ph.cp("gpsimd", pr[:, 0:1], carry[:, j:j + 1])
                ph.cp("scalar", pr[:, 1:TC + 1], bank[:, 0:TC])
                ph.cp("gpsimd", carry[:, j:j + 1], pr[:, TC:TC + 1])
                tm = tmpa[j % 2]
                ph.ts("vector", tm[:, :], pr[:, 0:TC], mu[:, j:j + 1], ALU.mult)
                ph.stt(sh[j][:, :], pr[:, 1:TC + 1], omu[:, j:j + 1], tm[:, :], ALU.mult, ALU.add)
                if j == NB - 3:
                    ph.act(xw[:, :], xw[:, :], AF.Tanh)
                if j == NB - 1:
                    ph.act(xg[:, :], xg[:, :], AF.Sigmoid)

        load_uT(0)
        inproj(0, blk_lora + hp_blocks(0))
        for tcI in range(NTC):
            if tcI + 1 < NTC:
                load_uT(tcI + 1)
            for hl in range(HP_SUB):
                hp = sp * HP_SUB + hl
                r_, k_, v_ = sh[hl], sh[HP_SUB + hl], sh[2 * HP_SUB + hl]

                def P_(i):
                    return chp[:, i * NHP + hp:i * NHP + hp + 1]
                w0c, a0c, kkc, kac, rkc, lgc, lbc = [P_(i) for i in range(7)]
                hs = slice(hp * 128, (hp + 1) * 128)
                b = nw()
                ph.mm(b[:, 0:TC], wdec[0:LW, hs], xw[0:LW, :])
                ph.act(sg[:, :], b[:, 0:TC], AF.Sigmoid, bias=w0c)
                b = nw()
                ph.mm(b[:, 0:TC], wicl[0:LA, hs], xa[0:LA, :])
                ph.act(et[:, :], b[:, 0:TC], AF.Sigmoid, bias=a0c)
                b = nw()
                ph.mm(b[:, 0:TC], wgat[0:LG, hs], xg[0:LG, :])
                ph.cp("vector", gt[:, :], b[:, 0:TC])
                if hl + 1 < HP_SUB:
                    inproj(tcI, hp_blocks(hl + 1))
                elif tcI + 1 < NTC and HP_SUB > 1:
                    inproj(tcI + 1, blk_lora + hp_blocks(0))
                ph.act(kk[:, :], k_[:, :], AF.Copy, scale=kkc)
                ph.tt("gpsimd", tq1[:, :], kk[:, :], kk[:, :], ALU.mult)
                b = nw()
                ph.mm(b[:, 0:TC], BO, tq1[:, :])
                ph.rsqrt(tq1[:, :], b[:, 0:TC], 1e-24)
                ph.tt("gpsimd", kkn[:, :], kk[:, :], tq1[:, :], ALU.mult)
                ph.ts("vector", tq1[:, :], et[:, :], kac, ALU.mult, omka[:, hp:hp + 1], ALU.add)
                ph.tt("gpsimd", kp[:, :], k_[:, :], tq1[:, :], ALU.mult)
                ph.tt("gpsimd", bv[:, :], kkn[:, :], et[:, :], ALU.mult)
                for c in range(NCH):
                    sl = slice(c * 128, (c + 1) * 128)
                    ph.op("vector", (lambda o, d0, d1: (lambda e: e.tensor_tensor_scan(o.ap, d0.ap, d1.ap, 0.0, ALU.mult, ALU.add)))(
                        cs[:, sl], ONES, sg[:, sl]), [ONES.k, sg.k], [cs.k])
                ph.act(Wt[:, :], cs[:, :], AF.Exp, scale=-C0)
                ph.act(Wi[:, :], cs[:, :], AF.Exp, scale=C0)
                ph.tt("gpsimd", tq1[:, :], cs[:, :], sg[:, :], ALU.subtract)
                ph.act(Wp[:, :], tq1[:, :], AF.Exp, scale=-C0)
                ph.tt("gpsimd", rt[:, :], r_[:, :], Wt[:, :], ALU.mult)
                ph.tt("gpsimd", kt[:, :], kp[:, :], Wi[:, :], ALU.mult)
                ph.tt("gpsimd", bt[:, :], bv[:, :], Wi[:, :], ALU.mult)
                ph.stt(at_[:, :], kkn[:, :], -1.0, Wp[:, :], ALU.mult, ALU.mult)
                for h in range(2):
                    mcol = Vw(cd["bo"][:, 127 * h:127 * h + 1], cst.k)
                    ph.act(btm[h][:, :], bt[:, :], AF.Copy, scale=mcol)
                    ph.ts("vector", ktm[h][:, :], kt[:, :], mcol, ALU.mult)
                    ph.act(atm[h][:, :], at_[:, :], AF.Copy, scale=mcol)
                ph.tt("gpsimd", tq1[:, :], r_[:, :], kp[:, :], ALU.mult)
                ph.act(tq1[:, :], tq1[:, :], AF.Copy, scale=rkc)
                b = nw()
                ph.mm(b[:, 0:TC], BO, tq1[:, :])
                ph.tt("vector", bon[:, :], b[:, 0:TC], v_[:, :], ALU.mult)
                if CUT == 2:
                    ph.dead = True
                for c in range(NCH):
                    sl = slice(c * 128, (c + 1) * 128)
                    pm = pairm[c % 2]
                    for h in range(2):
                        hb = slice(64 * h, 64 * h + 64)
                        m = mats[(c % 2, h)]
                        if CUT == 31:
                            ph.dead = True
                        q = nq()
                        ph.mm(q, btm[h][:, sl], at_[:, sl])
                        ph.tt("vector", m["P"][0][:, :], q, MUS, ALU.mult)
                        if CUT == 20:
                            ph.dead = True
                        if CUT == 32:
                            ph.dead = True
                        q = nq()
                        ph.mm(q, atm[h][:, sl], bt[:, sl])
                        ph.tt("vector", m["PT"][0][:, :], q, MLS, ALU.mult)
                        if CUT == 33:
                            ph.dead = True
                        q = nq()
                        ph.mm(q, ktm[h][:, sl], at_[:, sl])
                        ph.tt("vector", m["AKA"][:, :], q, MUS, ALU.mult)
                        if CUT == 34:
                            ph.dead = True
                        q = nq()
                        ph.mm(q, btm[h][:, sl], rt[:, sl])
                        ph.tt("vector", m["ABR"][:, :], q, MUI, ALU.mult)
                        if CUT == 35:
                            ph.dead = True
                        q = nq()
                        ph.mm(q, ktm[h][:, sl], rt[:, sl])
                        ph.tt("vector", m["AKR"][:, :], q, MUI, ALU.mult)
                        if CUT == 36:
                            ph.dead = True
                        ph.tt("gpsimd", m["T"][:, :], m["P"][0][:, :], IDN, ALU.add)
                    if CUT == 21:
                        ph.dead = True
                    q = nq()
                    ph.tr(q, v_[:, sl], IDN)
                    ph.cp("scalar", pm["VTp"][:, :], q)
                    ph.cp("gpsimd", pm["VTe"].sk("a")[:, 0:64], pm["VTp"][:, 0:64])
                    ph.cp("gpsimd", pm["VTo"].sk("a")[:, 64:128], pm["VTp"][:, 64:128])
                    q = nq()
                    ph.tr(q, bt[:, sl], IDN)
                    ph.cp("scalar", pm["BTT"][:, :], q)
                    q = nq()
                    ph.tr(q, kt[:, sl], IDN)
                    ph.cp("scalar", pm["KTT"][:, :], q)
                if CUT == 3:
                    ph.dead = True
                assert NCH <= 2
                for lev in range(1, NLEV + 1):
                    a, bnew = (lev - 1) % 2, lev % 2
                    for c in range(NCH):
                        for h in range(2):
                            m = mats[(c % 2, h)]
                            if lev < NLEV:
                                q = nq()
                                ph.mm(q, m["PT"][a][:, :], m["P"][a][:, :])
                                ph.cp("scalar", m["P"][bnew][:, :], q)
                            q = nq()
                            ph.mm(q, m["P"][a][:, :], m["PT"][a][:, :])
                            ph.cp("scalar" if h == 0 else "vector", m["PT"][bnew][:, :], q)
                    for c in range(NCH):
                        for h in range(2):
                            m = mats[(c % 2, h)]
                            q = nq()
                            ph.mm(q, m["PT"][bnew][:, :], m["T"][:, :])
                            ph.tt("vector", m["T"][:, :], m["T"][:, :], q, ALU.add)
                for c in range(NCH):
                    sl = slice(c * 128, (c + 1) * 128)
                    pm = pairm[c % 2]
                    if CUT == 4:
                        ph.dead = True
                    me, mo = mats[(c % 2, 0)], mats[(c % 2, 1)]
                    STh = ST[hl]
                    q = nq()
                    ph.mm(q, at_[:, sl], STh[:, :], st=True, sp=False)
                    ph.mm(q, me["AKA"][:, :], pm["VTe"].sk("a")[:, :], st=False, sp=False)
                    ph.mm(q, mo["AKA"][:, :], pm["VTo"].sk("a")[:, :], st=False, sp=True)
                    ph.cp("scalar", pm["RHS"][:, :], q)
                    q = nq()
                    qa = Vw(q.ap[:, 0:64], q.k)
                    qb = Vw(q.ap[:, 64:128], q.k)
                    ph.mm(qa, me["T"][:, :], pm["RHS"][:, 0:64])
                    ph.mm(qb, mo["T"][:, :], pm["RHS"][:, 64:128])
                    ph.cp("scalar", pm["SAp"][:, :], q)
                    ph.cp("gpsimd", pm["SAe"].sk("a")[:, 0:64], pm["SAp"][:, 0:64])
                    ph.cp("gpsimd", pm["SAo"].sk("a")[:, 64:128], pm["SAp"][:, 64:128])
                    q = nq()
                    ph.mm(q, STh[:, :], rt[:, sl], st=True, sp=False)
                    ph.mm(q, pm["SAe"].sk("a")[:, :], me["ABR"][:, :], st=False, sp=False)
                    ph.mm(q, pm["SAo"].sk("a")[:, :], mo["ABR"][:, :], st=False, sp=False)
                    ph.mm(q, pm["VTe"].sk("a")[:, :], me["AKR"][:, :], st=False, sp=False)
                    ph.mm(q, pm["VTo"].sk("a")[:, :], mo["AKR"][:, :], st=False, sp=True)
                    ph.cp("scalar", yraw[:, sl], q)
                    q = nq()
                    ph.mm(q, IDN, STh[:, :], st=True, sp=False)
                    ph.mm(q, pm["BTT"][:, :], pm["SAp"][:, :], st=False, sp=False)
                    ph.mm(q, pm["KTT"][:, :], pm["VTp"][:, :], st=False, sp=True)
                    ph.stt(STh[:, :], q, Wt[:, c * 128 + 127:c * 128 + 128], BO, ALU.mult, ALU.mult)
                if CUT == 5:
                    ph.dead = True
                b = nw()
                ph.mm(b[:, 0:TC], BO, yraw[:, :])
                ph.stt(yc[:, :], b[:, 0:TC], -1.0 / 64, yraw[:, :], ALU.mult, ALU.add)
                ph.tt("gpsimd", tq1[:, :], yc[:, :], yc[:, :], ALU.mult)
                b = nw()
                ph.mm(b[:, 0:TC], BO, tq1[:, :])
                ph.rsqrt(tq1[:, :], b[:, 0:TC], 64 * LNX_EPS)
                ph.tt("gpsimd", yc[:, :], yc[:, :], tq1[:, :], ALU.mult)
                ph.ts("vector", yc[:, :], yc[:, :], g8[:, hp:hp + 1], ALU.mult, lbc, ALU.add)
                ph.tt("gpsimd", yc[:, :], yc[:, :], bon[:, :], ALU.add)
                yob = yo[(tcI * HP_SUB + hl) % 2]
                ph.tt("vector", yob[:, :], yc[:, :], gt[:, :], ALU.mult)
                ph.stq("sync", f"yo{(tcI * HP_SUB + hl) % 2}", ybuf_t[hp].ap()[:, tcI * TC:(tcI + 1) * TC],
                       yob[:, :])
            if HP_SUB == 1 and tcI + 1 < NTC:
                inproj(tcI + 1, blk_lora + hp_blocks(0))
        ph.emit()

    if upto < 3:
        dbgcopy()
        return nc
    TC = TC_save
    NCH = TC // 128
    NTC = S // TC
    ph = Phase(nc, "p3")
    cst, cd = consts(ph)
    IDN = cv(cst, cd, "ident")
    identb = ph.sb([128, 128], BF16, "idb")
    ph.cp("vector", identb[:, :], IDN)
    W = ph.sb([128, NDC, NAB * 128], BF16, "W")
    ph.ldm("gpsimd", "Wld", W.k, [(W.t[:, k, :], wat_in[k * 128:(k + 1) * 128, :]) for k in range(NDC)])
    biasm = ph.sb([128, QPC * 256], F32, "biasm")
    ph.ld("sync", "bias", biasm[:, :], bias_in[:, :])
    for h in range(QPC):
        for hf in range(2):
            ph.tt("vector", biasm[:, h * 256 + hf * 128:h * 256 + hf * 128 + 128],
                  biasm[:, h * 256 + hf * 128:h * 256 + hf * 128 + 128],
                  cv(cst, cd, "amc") if hf == 1 else cv(cst, cd, "amp"), ALU.add)
    sinkc = ph.sb([128, QPC], F32, "sink")
    ph.ld("sync", "sink", sinkc[:, :], sink_in[:, :])
    uTs = [ph.sb([128, NDC, TC], BF16, "uT") for _ in range(2)]
    qTm = [[ph.sb([128, TC], BF16, f"qT{i}_{h}") for h in range(2)] for i in range(NQB)]
    kT = ph.sb([128, TC + 128], BF16, "kT")
    vT = ph.sb([128, TC], BF16, "vT")
    VTe = ph.sb([128, NCH + 1, 128], BF16, "VTe")
    VTo = ph.sb([128, NCH + 1, 128], BF16, "VTo")
    ph.ms("vector", VTe[:, :, :], 0.0)
    ph.ms("vector", VTo[:, :, :], 0.0)
    ph.ms("vector", kT[:, :], 0.0)
    psw = [ph.ps([128, 512], F32, "psw") for _ in range(1)]
    class _Sub:
        def __init__(s_, bank, c0, n, key):
            s_.b, s_.c0, s_.n, s_.k = bank, c0, n, bank.k + "/" + key

        def __getitem__(s_, idx):
            rows, cols = idx
            a = s_.c0 + (cols.start or 0)
            bb = s_.c0 + (cols.stop if cols.stop is not None else s_.n)
            return Vw(s_.b.t[rows, a:bb], s_.k)
    pss = [ph.ps([128, 512], F32, "pss") for _ in range(3)]
    pstb = [ph.ps([128, 1024], BF16, "pstb") for _ in range(2)]
    pso = [ph.ps([128, 512], F32, "pso") for _ in range(2)]
    NS = 4
    ssb = [ph.sb([128, 512], F32, "ssb") for _ in range(NS)]
    esb = [ph.sb([128, 512], F32, "esb") for _ in range(NS)]
    pb = [ph.sb([128, 512], BF16, "pb") for _ in range(NS)]
    pTs = [ph.sb([128, 512], BF16, "pTs") for _ in range(NS)]
    sm = [ph.sb([128, 16], F32, "sm") for _ in range(NS)]
    biasm0 = ph.sb([128, QPC * 256], F32, "biasm0")
    ph.cp("vector", biasm0[:, :], biasm[:, :])
    for h in range(QPC):
        ph.ms("vector", biasm0[:, h * 256:h * 256 + 128], -30000.0)
    yb = [ph.sb([128, TC], BF16, "yb") for _ in range(2 * NQB)]
    hn = 0
    wn = 0
    for tcI in range(NTC):
        uT = uTs[tcI % 2]
        ph.ld("sync", f"uT{tcI % 2}", uT[:, :, :],
              uTd[:, :, tcI * TC:(tcI + 1) * TC].rearrange("k p t -> p k t"), reads=["uTd"])
        if tcI > 0:
            ph.cp("gpsimd", kT[:, 0:128], kT[:, TC:TC + 128])
            ph.cp("gpsimd", VTe[:, 0, :], VTe[:, NCH, :])
            ph.cp("gpsimd", VTo[:, 0, :], VTo[:, NCH, :])
        for j in range(NAB):
            bank = psw[0]
            wn += 1
            for k in range(NDC):
                ph.mm(bank[:, 0:TC], W[:, k, j * 128:(j + 1) * 128], uT[:, k, :], st=(k == 0), sp=(k == NDC - 1))
            if j < NQB:
                for h2 in range(2):
                    ph.ts("vector", qTm[j][h2][:, :], bank[:, 0:TC], Vw(cd["bo"][:, 127 * h2:127 * h2 + 1], cst.k), ALU.mult)
            elif j == NQB:
                ph.cp("scalar", kT[:, 128:128 + TC], bank[:, 0:TC])
            else:
                ph.cp("scalar", vT[:, :], bank[:, 0:TC])
        for c in range(NCH):
            pt = pstb[c % 2]
            ph.tr(pt[:, 0:128], vT[:, c * 128:(c + 1) * 128], identb[:, :])
            ph.cp("vector", VTe[:, c + 1, 0:64], pt[:, 0:64])
            ph.cp("vector", VTo[:, c + 1, 64:128], pt[:, 64:128])
        def pair_stages(c, qb, g, bm):
            po, ps_, ptb = pso[g], pss[g], pstb[g]
            sl_ = qb % NS
            s_, e_, p_, pT_, m_ = ssb[sl_], esb[sl_], pb[sl_], pTs[sl_], sm[sl_]
            s3 = Vw(s_.t[:, :].rearrange("p (h k) -> p h k", h=2), s_.k)
            e3 = Vw(e_.t[:, :].rearrange("p (h k) -> p h k", h=2), e_.k)
            sk2 = sinkc[:, 2 * qb:2 * qb + 2]
            ybt = yb[(tcI % 2) * NQB + qb]
            st = []

            def f0():
                for h2 in range(2):
                    ph.mm(ps_[:, h2 * 256:(h2 + 1) * 256], qTm[qb][h2][:, c * 128:(c + 1) * 128], kT[:, c * 128:c * 128 + 256])
            st.append(f0)
            st.append(lambda: ph.stt(s_[:, :], ps_[:, :], 0.125, bm[:, 2 * qb * 256:(2 * qb + 2) * 256], ALU.mult, ALU.add))
            st.append(lambda: ph.op("vector", (lambda o, a: (lambda e: e.reduce_max(o.ap, a.ap, AX.X)))(m_[:, 0:2], s3), [s_.k], [m_.k]))
            st.append(lambda: ph.tt("vector", m_[:, 2:4], m_[:, 0:2], sk2, ALU.max))
            st.append(lambda: ph.ts("vector", m_[:, 4:6], m_[:, 2:4], -1.0, ALU.mult))

            def f5():
                for h2 in range(2):
                    ph.act(e_[:, h2 * 256:(h2 + 1) * 256], s_[:, h2 * 256:(h2 + 1) * 256], AF.Exp, bias=m_[:, 4 + h2:5 + h2])
            st.append(f5)
            st.append(lambda: ph.tt("vector", m_[:, 6:8], sk2, m_[:, 4:6], ALU.add))
            st.append(lambda: ph.act(m_[:, 8:10], m_[:, 6:8], AF.Exp))
            st.append(lambda: ph.op("vector", (lambda o, a: (lambda e: e.reduce_sum(o.ap, a.ap, AX.X)))(m_[:, 10:12], e3), [e_.k], [m_.k]))
            st.append(lambda: ph.tt("vector", m_[:, 12:14], m_[:, 10:12], m_[:, 8:10], ALU.add))
            st.append(lambda: ph.op("vector", (lambda o, a: (lambda e: e.reciprocal(o.ap, a.ap)))(m_[:, 14:16], m_[:, 12:14]), [m_.k], [m_.k]))

            def f11():
                for h2 in range(2):
                    ph.act(p_[:, h2 * 256:(h2 + 1) * 256], e_[:, h2 * 256:(h2 + 1) * 256], AF.Copy, scale=m_[:, 14 + h2:15 + h2])
            st.append(f11)

            def f12():
                for i4 in range(4):
                    ph.tr(ptb[:, i4 * 128:(i4 + 1) * 128], p_[:, i4 * 128:(i4 + 1) * 128], identb[:, :])
            st.append(f12)
            st.append(lambda: ph.cp("vector" if g else "scalar", pT_[:, :], ptb[:, 0:512]))

            def f14():
                for h2 in range(2):
                    VT = VTe if h2 == 0 else VTo
                    ph.mm(po[:, 0:128], VT[:, c, :], pT_[:, (2 * h2) * 128:(2 * h2 + 1) * 128], st=(h2 == 0), sp=False)
                    ph.mm(po[:, 0:128], VT[:, c + 1, :], pT_[:, (2 * h2 + 1) * 128:(2 * h2 + 2) * 128], st=False, sp=(h2 == 1))
            st.append(f14)
            st.append(lambda: ph.cp("scalar", ybt[:, c * 128:(c + 1) * 128], po[:, 0:128]))
            return st

        for c in range(NCH):
            first = (tcI == 0 and c == 0)
            bm = biasm0 if first else biasm
            for q0 in range(0, NQB, 2):
                grp = [pair_stages(c, qb, qb - q0, bm) for qb in range(q0, min(q0 + 2, NQB))]
                for si in range(len(grp[0])):
                    for stl in grp:
                        stl[si]()
        for qb in range(NQB):
            ybt = yb[(tcI % 2) * NQB + qb]
            ph.stq("sync", f"yb{(tcI % 2) * NQB + qb}",
                   ybuf_t[NHP + qb].ap()[:, tcI * TC:(tcI + 1) * TC], ybt[:, :])
    ph.emit()

    dbgcopy()
    if upto < 4:
        return nc
    cc_sem = nc.alloc_semaphore(name="cc_sem")
    with nc.Block() as block:
        @block.gpsimd
        def _(g):
            for j in range(NRB):
                g.collective_compute("AllGather", ALU.bypass, replica_groups=[[0, 1, 2, 3], [4, 5, 6, 7]],
                                     ins=[ybuf_t[j].ap().opt()], outs=[yall_t[j].ap().opt()]).then_inc(cc_sem)
            g.wait_ge(cc_sem, NRB)
    nc.clear_and_free_semaphores([cc_sem])
    nc.all_engine_barrier()

    if upto < 5:
        return nc
    ph = Phase(nc, "p5")
    cst, cd = consts(ph)
    KC = 4 * YR // 128
    cT = ph.sb([128, KC, NT], BF16, "cT")
    rk_per = YR // 128

    def ldc(e, sem):
        pid = e.partition_id()
        tq = pid % 4
        for j in range(NRB):
            yv = yall_t[j].ap().rearrange("(r p) t -> p r t", p=128)
            e.dma_start(out=cT.t[:, j * 4:(j + 1) * 4, :], in_=yv[:, :, bass.ds(tq * NT, NT)]).then_inc(sem, 16)
    ph.dma("gpsimd", "cTld", ldc, [], [cT.k], n=NRB)
    g1row = modraw[2 * NDC:3 * NDC, :].rearrange("(o a) b -> o (a b)", o=1)
    NTT = NT // 128
    xt = ph.sb([128, D], F32, "xt")
    nst = max(1, D // 512)
    stats = ph.sb([128, nst * 6], F32, "stats")
    mv = ph.sb([128, 2], F32, "mv")
    rs_all = ph.sb([128, NTT], F32, "rs")
    nm_all = ph.sb([128, NTT], F32, "nm")
    for tt_ in range(NTT):
        ph.ld("sync", "xt", xt[:, :], xq_in[tt_ * 128:(tt_ + 1) * 128, :])
        ln_stats(ph, xt, D, stats, mv, rs_all.sk(tt_)[:, tt_:tt_ + 1], nm_all.sk(tt_)[:, tt_:tt_ + 1], LN_EPS)
    DB = min(512, D)
    NDB = D // DB
    Wo = [ph.sb([128, KC, DB], BF16, "Wo") for _ in range(2)]
    psm = [ph.ps([128, DB], F32, "psm") for _ in range(8)]
    xp = [ph.sb([128, DB], F32, "xp") for _ in range(2)]
    zp = [ph.sb([128, DB], F32, "zp") for _ in range(2)]
    mg = [ph.sb([128, DB], F32, "mg") for _ in range(2)]
    gEs = [ph.sb([128, DB], F32, "gE") for _ in range(2)]
    bEs = [ph.sb([128, DB], F32, "bE") for _ in range(2)]
    G1s = [ph.sb([128, DB], F32, "G1") for _ in range(2)]
    D_A = cfg["D_A"]
    HW_ = cfg["HPC"] * 64
    it = 0

    def p5_loads(db):
        Wb = Wo[db % 2]
        gE, bE, G1 = gEs[db % 2], bEs[db % 2], G1s[db % 2]
        dsl = slice(db * DB, (db + 1) * DB)
        ph.ld("sync", f"gE{db % 2}", gE[:, :], lnr_in[0, dsl].partition_broadcast(128))
        ph.ld("sync", f"bE{db % 2}", bE[:, :], lnr_in[1, dsl].partition_broadcast(128))
        ph.ld("sync", f"G1{db % 2}", G1[:, :], g1row[0, dsl].partition_broadcast(128))
        ph.ts("vector", G1[:, :], G1[:, :], 1.0, ALU.add)
        pairs = []
        for kc in range(KC):
            j = kc // 4
            r = kc % 4
            row = (r * HW_ + j * 128) if j * 128 < HW_ else (D_A + r * (cfg["QPC"] * 64) + (j * 128 - HW_))
            pairs.append((Wb.t[:, kc, :], wout_in[row:row + 128, dsl]))
        ph.ldm("gpsimd", f"Wo{db % 2}", Wb.k, pairs)

    p5_loads(0)
    for db in range(NDB):
        if db + 1 < NDB:
            p5_loads(db + 1)
        Wb = Wo[db % 2]
        gE, bE, G1 = gEs[db % 2], bEs[db % 2], G1s[db % 2]
        dsl = slice(db * DB, (db + 1) * DB)
        for tt_ in range(NTT):
            pm_ = psm[it % 8]
            x_ = xp[it % 2]
            z_ = zp[it % 2]
            g_ = mg[it % 2]
            for kc in range(KC):
                ph.mm(pm_[:, :], cT[:, kc, tt_ * 128:(tt_ + 1) * 128], Wb[:, kc, :], st=(kc == 0), sp=(kc == KC - 1))
            ph.ld("sync", f"xp{it % 2}", x_[:, :], xq_in[tt_ * 128:(tt_ + 1) * 128, dsl])
            ph.act(x_[:, :], x_[:, :], AF.Identity, bias=nm_all.sk(tt_)[:, tt_:tt_ + 1], scale=rs_all.sk(tt_)[:, tt_:tt_ + 1])
            ph.tt("vector", x_[:, :], x_[:, :], gE[:, :], ALU.mult)
            ph.tt("vector", x_[:, :], x_[:, :], bE[:, :], ALU.add)
            ph.tt("vector", g_[:, :], pm_[:, :], G1[:, :], ALU.mult)
            ph.stt(z_[:, :], x_[:, :], ALPHA, g_[:, :], ALU.mult, ALU.add)
            ph.stq("gpsimd", f"zst{it % 2}", zbuf[tt_ * 128:(tt_ + 1) * 128, dsl], z_[:, :])
            it += 1
    ph.emit()

    if upto < 6:
        return nc
    ph = Phase(nc, "p6a")
    cst, cd = consts(ph)
    identb = ph.sb([128, 128], BF16, "idb")
    ph.cp("vector", identb[:, :], cv(cst, cd, "ident"))
    identF = ph.sb([128, 128], F32, "idf")
    ph.cp("vector", identF[:, :], cv(cst, cd, "ident"))
    pstf = ph.ps([128, 128], F32, "pstf")
    modc = load_cols(ph, modraw, 6 * NDC, identF, pstf, "modc")
    lnc = load_cols(ph, lnv_in, 6 * NDC, identF, pstf, "lnc")
    A2 = ph.sb([128, NDC], F32, "A2")
    B2 = ph.sb([128, NDC], F32, "B2")
    t1 = ph.sb([128, NDC], F32, "t1")
    ph.ts("vector", t1[:, :], modc[:, 4 * NDC:5 * NDC], 1.0, ALU.add)
    ph.tt("vector", A2[:, :], t1[:, :], lnc[:, 2 * NDC:3 * NDC], ALU.mult)
    ph.tt("vector", B2[:, :], t1[:, :], lnc[:, 3 * NDC:4 * NDC], ALU.mult)
    ph.tt("vector", B2[:, :], B2[:, :], modc[:, 3 * NDC:4 * NDC], ALU.add)
    g1b = ph.sb([128, D], F32, "g1b")
    b1b = ph.sb([128, D], F32, "b1b")
    ph.ld("sync", "g1b", g1b[:, :], lnr_in[2, :].partition_broadcast(128))
    ph.ld("sync", "b1b", b1b[:, :], lnr_in[3, :].partition_broadcast(128))
    u2s = [ph.sb([128, NDC, 128], BF16, "u2s") for _ in range(2)]
    zt = ph.sb([128, D], F32, "zt")
    zh = ph.sb([128, D], BF16, "zh")
    x1t = ph.sb([128, D], F32, "x1t")
    stats = ph.sb([128, nst * 6], F32, "stats")
    mv = ph.sb([128, 2], F32, "mv")
    rstd = ph.sb([128, 1], F32, "rstd")
    nmr = ph.sb([128, 1], F32, "nmr")
    pstb = [ph.ps([128, 128], BF16, "pstb") for _ in range(7)]
    ti = 0
    for tt_ in range(NTT):
        u2 = u2s[tt_ % 2]
        ph.ld("sync", "zt", zt[:, :], zbuf[tt_ * 128:(tt_ + 1) * 128, :])
        ln_stats(ph, zt, D, stats, mv, rstd, nmr, LN_EPS)
        ph.act(zh[:, :], zt[:, :], AF.Identity, bias=nmr[:, :], scale=rstd[:, :])
        ph.act(x1t[:, :], zt[:, :], AF.Identity, bias=nmr[:, :], scale=rstd[:, :])
        ph.tt("vector", x1t[:, :], x1t[:, :], g1b[:, :], ALU.mult)
        ph.tt("vector", x1t[:, :], x1t[:, :], b1b[:, :], ALU.add)
        ph.stq("sync", "x1st", x1buf[tt_ * 128:(tt_ + 1) * 128, :], x1t[:, :])
        for k in range(NDC):
            pt = pstb[ti % 7]
            ti += 1
            ph.tr(pt[:, :], zh[:, k * 128:(k + 1) * 128], identb[:, :])
            u2k = u2.sk(k % 8)
            if k % 2 == 0:
                ph.ts("vector", u2k[:, k, :], pt[:, :], A2[:, k:k + 1], ALU.mult, B2[:, k:k + 1], ALU.add)
            else:
                ph.act(u2k[:, k, :], pt[:, :], AF.Identity, bias=B2[:, k:k + 1], scale=A2[:, k:k + 1])
        ph.dma("sync", f"u2st{tt_ % 2}",
               (lambda o, i_: (lambda e, sem: e.dma_start(out=o, in_=i_).then_inc(sem, 16)))(
                   u2Td[:, :, tt_ * 128:(tt_ + 1) * 128].rearrange("k p t -> p k t"), u2.t[:, :, :]),
               [u2.k + f"/{q}" for q in range(8)], [])
    ph.emit()

    ph = Phase(nc, "p6b")
    u2T = ph.sb([128, NDC, NT], BF16, "u2T")
    ph.ld("sync", "u2T", u2T[:, :, :], u2Td[:, :, :].rearrange("k p t -> p k t"))
    FB = 512
    NFB = DFF // FB
    TH = min(512, NT)
    NTH = NT // TH
    Wu = [ph.sb([128, NDC, FB], BF16, "Wu") for _ in range(2)]
    psu = [ph.ps([128, 512], F32, "psu") for _ in range(4)]
    hr = [ph.sb([128, TH], F32, "hr") for _ in range(2)]
    hT = [ph.sb([128, FB // 128, NT], BF16, "hT") for _ in range(2)]
    it = 0

    def p6_loads(fb):
        Wb = Wu[fb % 2]
        ph.ldm("gpsimd", f"Wu{fb % 2}", Wb.k,
               [(Wb.t[:, k, :], wup_in[k * 128:(k + 1) * 128, fb * FB:(fb + 1) * FB]) for k in range(NDC)])

    p6_loads(0)
    for fb in range(NFB):
        if fb + 1 < NFB:
            p6_loads(fb + 1)
        Wb = Wu[fb % 2]
        hb_ = hT[fb % 2]
        for c in range(FB // 128):
            for th in range(NTH):
                pu = psu[it % 4]
                h_ = hr[it % 2]
                it += 1
                for k in range(NDC):
                    ph.mm(pu[:, 0:TH], Wb[:, k, c * 128:(c + 1) * 128], u2T[:, k, th * TH:(th + 1) * TH],
                          st=(k == 0), sp=(k == NDC - 1))
                ph.act(h_[:, :], pu[:, 0:TH], AF.Relu)
                ph.tt("vector", hb_[:, c, th * TH:(th + 1) * TH], h_[:, :], h_[:, :], ALU.mult)
        ph.stq("sync", f"hst{fb % 2}", hbuf[fb * (FB // 128):(fb + 1) * (FB // 128), :, :].rearrange("c p t -> p c t"),
               hb_[:, :, :])
    ph.emit()

    if upto < 7:
        return nc
    ph = Phase(nc, "p7")
    DH = D // 2
    DQ = min(512, DH)
    NDQ = DH // DQ
    FBD = 1024
    NFD = DFF // FBD
    KB = FBD // 128
    acc = ph.sb([128, NTT, DH], F32, "acc")
    Wd = [ph.sb([128, KB, DH], BF16, "Wd") for _ in range(2)]
    hTb = [ph.sb([128, KB, NT], BF16, "hTb") for _ in range(2)]
    G2 = ph.sb([128, DH], F32, "G2")
    psd = [ph.ps([128, DQ], F32, "psd") for _ in range(8)]
    x1p = [ph.sb([128, DQ], F32, "x1p") for _ in range(2)]
    tmpds = [ph.sb([128, DQ], F32, "tmpd") for _ in range(2)]
    it = 0
    blocks = [(dh, fd) for dh in range(2) for fd in range(NFD)]

    def p7_loads(bi):
        dh, fd = blocks[bi]
        Wb = Wd[bi % 2]
        hb_ = hTb[bi % 2]
        ph.ldm("gpsimd", f"Wd{bi % 2}", Wb.k,
               [(Wb.t[:, kb, :], wdn_in[fd * FBD + kb * 128:fd * FBD + (kb + 1) * 128, dh * DH:(dh + 1) * DH])
                for kb in range(KB)])
        ph.ldm("sync", f"hTb{bi % 2}", hb_.k, [(hb_.t[:, kb, :], hbuf[fd * KB + kb, :, :]) for kb in range(KB)])

    p7_loads(0)
    for bi, (dh, fd) in enumerate(blocks):
        if bi + 1 < len(blocks):
            p7_loads(bi + 1)
        Wb = Wd[bi % 2]
        hb_ = hTb[bi % 2]
        if fd == 0:
            ph.ld("sync", "G2", G2[:, :],
                  modraw[5 * NDC:6 * NDC, :].rearrange("(o a) b -> o (a b)", o=1)[0, dh * DH:(dh + 1) * DH].partition_broadcast(128))
            ph.ts("vector", G2[:, :], G2[:, :], 1.0, ALU.add)
        for tt_ in range(NTT):
            for dq in range(NDQ):
                pd = psd[it % 8]
                it += 1
                for kb in range(KB):
                    ph.mm(pd[:, :], hb_[:, kb, tt_ * 128:(tt_ + 1) * 128], Wb[:, kb, dq * DQ:(dq + 1) * DQ],
                          st=(kb == 0), sp=(kb == KB - 1))
                a_ = acc.sk(f"{tt_}_{dq}")[:, tt_, dq * DQ:(dq + 1) * DQ]
                if fd == 0:
                    ph.cp("scalar", a_, pd[:, :])
                else:
                    ph.tt("vector", a_, pd[:, :], a_, ALU.add)
        if fd == NFD - 1:
            for tt_ in range(NTT):
                for dq in range(NDQ):
                    a_ = acc.sk(f"{tt_}_{dq}")[:, tt_, dq * DQ:(dq + 1) * DQ]
                    x_ = x1p[(tt_ * NDQ + dq) % 2]
                    c0 = dh * DH + dq * DQ
                    ph.ld("sync", f"x1p{(tt_ * NDQ + dq) % 2}", x_[:, :], x1buf[tt_ * 128:(tt_ + 1) * 128, c0:c0 + DQ])
                    ph.tt("vector", a_, a_, G2[:, dq * DQ:(dq + 1) * DQ], ALU.mult)
                    ph.stt(x_[:, :], x_[:, :], ALPHA, a_, ALU.mult, ALU.add)
                    ph.stq("sync", f"z2st{(tt_ * NDQ + dq) % 2}", z2buf[tt_ * 128:(tt_ + 1) * 128, c0:c0 + DQ], x_[:, :])
    ph.emit()

    ph = Phase(nc, "p8")
    g2b = ph.sb([128, D], F32, "g2b")
    b2b = ph.sb([128, D], F32, "b2b")
    ph.ld("sync", "g2b", g2b[:, :], lnr_in[4, :].partition_broadcast(128))
    ph.ld("sync", "b2b", b2b[:, :], lnr_in[5, :].partition_broadcast(128))
    zts = [ph.sb([128, D], F32, "zt") for _ in range(2)]
    ots = [ph.sb([128, D], F32, "ot") for _ in range(2)]
    stats = ph.sb([128, nst * 6], F32, "stats")
    mv = ph.sb([128, 2], F32, "mv")
    rstd = ph.sb([128, 1], F32, "rstd")
    nmr = ph.sb([128, 1], F32, "nmr")
    for tt_ in range(NTT):
        zt = zts[tt_ % 2]
        ot = ots[tt_ % 2]
        ph.ld("sync", f"zt{tt_ % 2}", zt[:, :], z2buf[tt_ * 128:(tt_ + 1) * 128, :], reads=["z2buf"])
        ln_stats(ph, zt, D, stats, mv, rstd, nmr, LN_EPS)
        ph.act(ot[:, :], zt[:, :], AF.Identity, bias=nmr[:, :], scale=rstd[:, :])
        ph.tt("vector", ot[:, :], ot[:, :], g2b[:, :], ALU.mult)
        ph.tt("vector", ot[:, :], ot[:, :], b2b[:, :], ALU.add)
        ph.stq("gpsimd", f"ost{tt_ % 2}", out_t[tt_ * 128:(tt_ + 1) * 128, :], ot[:, :])
    ph.op("vector", lambda e: e.memset(nmr.t[:, :], 0.0), ["out"], [nmr.k])
    ph.emit()
    return nc


def t5_bucket(dist):
    n = np.maximum(dist, 0)
    nf = np.maximum(n, 1).astype(np.float32)
    large = 16 + (np.log(nf / 16) / math.log(128 / 16) * (32 - 16)).astype(np.int32)
    large = np.minimum(large, 31)
    return np.where(n < 16, n, large)


def host_inputs(cfg, inp):
    D, S, NDC = cfg["D"], cfg["S"], cfg["NDC"]
    HPC, NHP, QPC, NSUB, HP_SUB, NB, NAB = cfg["HPC"], cfg["NHP"], cfg["QPC"], cfg["NSUB"], cfg["HP_SUB"], cfg["NB"], cfg["NAB"]
    LW, LA, LG, NT = cfg["LW"], cfg["LA"], cfg["LG"], cfg["NT"]
    f = lambda a: np.ascontiguousarray(np.asarray(a, dtype=np.float32))
    x = f(inp["x"])
    c = f(inp["c"])
    w_in = f(inp["w_in"])[0]
    mu_shift = f(inp["mu_shift"])[0]
    lnrows = np.stack([f(inp["ln_emb_g"]), f(inp["ln_emb_b"]), f(inp["ln1_g"])[0], f(inp["ln1_b"])[0],
                       f(inp["ln2_g"])[0], f(inp["ln2_b"])[0]], 0)
    j = np.arange(128)[:, None]
    s_ = np.arange(128)[None, :]
    ident = np.eye(128, dtype=np.float32)
    mus = (j < s_).astype(np.float32)
    mui = (j <= s_).astype(np.float32)
    mls = (j > s_).astype(np.float32)
    bo = np.kron(np.eye(2, dtype=np.float32), np.ones((64, 64), np.float32))
    ones = np.ones((128, 128), np.float32)
    NEG = -30000.0
    amask_prev = np.where(s_ > j, 0.0, NEG).astype(np.float32)
    amask_cur = np.where(s_ <= j, 0.0, NEG).astype(np.float32)
    cst = np.concatenate([ident, mus, mui, mls, bo, ones, amask_prev, amask_cur], 1)
    qi = np.arange(128)[:, None]
    kj = np.arange(256)[None, :]
    bucket = t5_bucket(qi + 128 - kj)
    rpb = f(inp["rpb_table"])
    zero128 = np.zeros((D, 1), np.float32)
    maps = []
    for core in range(8):
        b, hg = core // 4, core % 4
        m = {}
        m["x"] = x[b]
        m["xq"] = np.ascontiguousarray(x[b, hg * NT:(hg + 1) * NT])
        m["ccol"] = np.ascontiguousarray(c[b].reshape(NDC, 128).T)
        MQ = 6 * D // 4
        m["wmod"] = np.ascontiguousarray(f(inp["w_mod"])[0][:, hg * MQ:(hg + 1) * MQ])
        m["bmod"] = np.ascontiguousarray(f(inp["b_mod"])[0][None, hg * MQ:(hg + 1) * MQ])
        m["lnv"] = np.ascontiguousarray(lnrows.reshape(6 * NDC, 128))
        m["lnr"] = lnrows
        wrw = np.zeros((NSUB, D, NB * 128), np.float32)
        mu = np.zeros((128, NSUB * NB), np.float32)
        for sp in range(NSUB):
            blocks = []
            for off in (0, cfg["OFF_K"], cfg["OFF_V"]):
                for hl in range(HP_SUB):
                    hp = sp * HP_SUB + hl
                    c0 = off + (hg * HPC + 2 * hp) * 64
                    blocks.append((c0, 128))
            blocks += [(cfg["OFF_W"], LW), (cfg["OFF_A"], LA), (cfg["OFF_G"], LG)]
            for jb, (c0, n) in enumerate(blocks):
                wrw[sp, :, jb * 128:jb * 128 + n] = w_in[:, c0:c0 + n]
                mu[0:n, sp * NB + jb] = mu_shift[c0:c0 + n]
        m["wrw"] = wrw
        m["mu"] = mu
        wat = np.zeros((D, NAB * 128), np.float32)
        for i in range(QPC // 2):
            c0 = cfg["OFF_Q"] + (hg * QPC + 2 * i) * 64
            wat[:, i * 128:(i + 1) * 128] = w_in[:, c0:c0 + 128]
        kc0 = cfg["OFF_KB"] + hg * 64
        vc0 = cfg["OFF_VB"] + hg * 64
        jb = QPC // 2
        wat[:, jb * 128:jb * 128 + 64] = w_in[:, kc0:kc0 + 64]
        wat[:, jb * 128 + 64:jb * 128 + 128] = w_in[:, kc0:kc0 + 64]
        wat[:, (jb + 1) * 128:(jb + 1) * 128 + 64] = w_in[:, vc0:vc0 + 64]
        wat[:, (jb + 1) * 128 + 64:(jb + 2) * 128] = w_in[:, vc0:vc0 + 64]
        m["wat"] = wat
        chp = np.zeros((128, 7 * NHP), np.float32)
        vecs = [f(inp["w0"])[0], f(inp["a0"])[0], f(inp["k_k"])[0], f(inp["k_a"])[0], f(inp["r_k"])[0].reshape(-1),
                f(inp["lnx_g"])[0], f(inp["lnx_b"])[0]]
        wdec = np.zeros((128, NHP * 128), np.float32)
        wicl = np.zeros((128, NHP * 128), np.float32)
        wgat = np.zeros((128, NHP * 128), np.float32)
        for hp in range(NHP):
            a0_ = (hg * HPC + 2 * hp) * 64
            for i, v in enumerate(vecs):
                chp[:, i * NHP + hp] = v[a0_:a0_ + 128]
            wdec[0:LW, hp * 128:(hp + 1) * 128] = f(inp["w_decay_up"])[0][:, a0_:a0_ + 128]
            wicl[0:LA, hp * 128:(hp + 1) * 128] = f(inp["w_iclr_up"])[0][:, a0_:a0_ + 128]
            wgat[0:LG, hp * 128:(hp + 1) * 128] = f(inp["w_gate_up"])[0][:, a0_:a0_ + 128]
        m["chp"], m["wdec"], m["wicl"], m["wgat"] = chp, wdec, wicl, wgat
        bias = np.zeros((128, QPC * 256), np.float32)
        sinks = np.zeros((128, QPC), np.float32)
        for ql in range(QPC):
            hq = hg * QPC + ql
            bias[:, ql * 256:(ql + 1) * 256] = rpb[bucket, hq]
            sinks[:, ql] = f(inp["attn_sinks"])[0][hq]
        m["biasat"] = bias
        m["sinks"] = sinks
        m["cst"] = cst
        m["wout"] = f(inp["w_out"])[0]
        m["wup"] = f(inp["w_up"])[0]
        m["wdn"] = f(inp["w_down"])[0]
        maps.append(m)
    return maps


_NC_CACHE = {}


def kernel(**inputs):
    cfg = make_cfg()
    if "nc" not in _NC_CACHE:
        _NC_CACHE["nc"] = build(cfg)
    nc = _NC_CACHE["nc"]
    maps = host_inputs(cfg, inputs)
    res = run_bass_kernel_spmd(nc, maps, core_ids=list(range(8)))
    NT = cfg["NT"]
    out = np.zeros((2, cfg["S"], cfg["D"]), np.float32)
    for core in range(8):
        b, q = core // 4, core % 4
        out[b, q * NT:(q + 1) * NT] = res.results[core]["out"]
    return out
```
